# Optimizing a Trainium2 kernel written in Bass

```python
import math
import jax, jax.numpy as jnp
from jax import lax
import numpy as np

D_MODEL = 1024
BATCH = 8
SEQ = 4096
DEPTH = 2

GRID_W = 64
CTX_LEN = 256

MLA_HEADS = 8
QK_NOPE = 64
QK_ROPE = 32
V_HEAD = 64
Q_LORA = 384
KV_LORA = 256
AXIS_ROPE = QK_ROPE // 2
ROPE_BASE = 10000.0
Q_BLOCK = 128

FNET_GROUPS = 4
FNET_GROUP_W = 64
FNET_W = FNET_GROUPS * FNET_GROUP_W

HYENA_W = 256
HYENA_CONV = 3
FILTER_EMB = 33
FILTER_ORDER = 64
DECAY_TARGET = 1e-2
FAST_DECAY_PCT = 0.3
SLOW_DECAY_PCT = 1.5

MLA_W = MLA_HEADS * V_HEAD
MIX_W = MLA_W + FNET_W + HYENA_W
OFF_Q = 0
OFF_KV = OFF_Q + Q_LORA
OFF_KR = OFF_KV + KV_LORA
OFF_F = OFF_KR + QK_ROPE
OFF_H = OFF_F + FNET_W
IN_W = OFF_H + 3 * HYENA_W
D_FF = 4 * D_MODEL
N_MOD = 6
EPS = 1e-6

kernel_name = "hymba_mla_fnet_hyena_dit_trunk"


def rms_norm(x, g):
    xf = x.astype(jnp.float32)
    y = xf * lax.rsqrt(jnp.mean(xf * xf, axis=-1, keepdims=True) + EPS)
    return (y * g.astype(jnp.float32)).astype(x.dtype)


def modulation(cvec, w_mod, b_mod):
    m = (jax.nn.silu(cvec) @ w_mod + b_mod).reshape(-1, 1, N_MOD * D_MODEL)
    return jnp.split(m, N_MOD, axis=-1)


def modulate(h, shift, scale):
    return h * (1.0 + scale) + shift


def grid_rope_tables(n_tokens, dtype):
    rows = n_tokens // GRID_W
    row = jnp.repeat(jnp.arange(rows, dtype=jnp.float32), GRID_W)
    col = jnp.tile(jnp.arange(GRID_W, dtype=jnp.float32), rows)
    inv = ROPE_BASE ** (-jnp.arange(0, AXIS_ROPE, 2, dtype=jnp.float32) / AXIS_ROPE)
    ang = jnp.concatenate([row[:, None] * inv, col[:, None] * inv], axis=-1)
    return jnp.cos(ang).astype(dtype), jnp.sin(ang).astype(dtype)


def _rotate_half(v, cos, sin):
    half = v.shape[-1] // 2
    v1, v2 = v[..., :half], v[..., half:]
    return jnp.concatenate([v1 * cos - v2 * sin, v1 * sin + v2 * cos], axis=-1)


def apply_axial_rope(x, cos, sin):
    n = AXIS_ROPE // 2
    return jnp.concatenate([
        _rotate_half(x[..., :AXIS_ROPE], cos[..., :n], sin[..., :n]),
        _rotate_half(x[..., AXIS_ROPE:], cos[..., n:], sin[..., n:])], axis=-1)


def mla_queries(p, q_norm_g, w_uq, rope):
    B, L, _ = p.shape
    cq = rms_norm(p[..., OFF_Q:OFF_KV], q_norm_g)
    q = (cq @ w_uq).reshape(B, L, MLA_HEADS, QK_NOPE + QK_ROPE)
    q_nope, q_rope = q[..., :QK_NOPE], q[..., QK_NOPE:]
    if rope is not None:
        cos, sin = rope
        q_rope = apply_axial_rope(q_rope, cos[:, None, :], sin[:, None, :])
    return q_nope, q_rope


def mla_keys_values(p_kv, kv_norm_g, w_ukv, rope):
    B, L, _ = p_kv.shape
    ckv = rms_norm(p_kv[..., :KV_LORA], kv_norm_g)
    kv = (ckv @ w_ukv).reshape(B, L, MLA_HEADS, QK_NOPE + V_HEAD)
    k_rope = p_kv[..., KV_LORA:]
    if rope is not None:
        k_rope = apply_axial_rope(k_rope, *rope)
    return kv[..., :QK_NOPE], k_rope, kv[..., QK_NOPE:]


def mla_attention(q_nope, q_rope, k_nope, k_rope, v):
    B, L, H, _ = q_nope.shape
    nb = L // Q_BLOCK
    scale = 1.0 / math.sqrt(QK_NOPE + QK_ROPE)

    def to_blocks(t):
        return jnp.moveaxis(t.reshape(B, nb, Q_BLOCK, *t.shape[2:]), 1, 0)

    def block(qs):
        qn, qr = qs
        s = (jnp.einsum('bqhd,bkhd->bhqk', qn, k_nope)
             + jnp.einsum('bqhr,bkr->bhqk', qr, k_rope))
        prob = jax.nn.softmax(s.astype(jnp.float32) * scale, axis=-1).astype(v.dtype)
        return jnp.einsum('bhqk,bkhd->bqhd', prob, v)

    o = lax.map(block, (to_blocks(q_nope), to_blocks(q_rope)))
    return jnp.moveaxis(o, 0, 1).reshape(B, L, H * V_HEAD)


def fourier_mix(u):
    B, L, _ = u.shape
    g = u.reshape(B, L, FNET_GROUPS, FNET_GROUP_W).astype(jnp.float32)
    f = jnp.fft.fftn(g, axes=(1, 3), norm="ortho").real
    return f.reshape(B, L, FNET_W).astype(u.dtype)


def short_conv(u, w, b):
    L = u.shape[1]
    pad = (HYENA_CONV - 1) // 2
    up = jnp.pad(u, ((0, 0), (pad, HYENA_CONV - 1 - pad), (0, 0)))
    out = b
    for k in range(HYENA_CONV):
        out = out + up[:, k:k + L] * w[k]
    return out


def hyena_filters(L, w1, b1, freq, w2, b2, w3):
    f32 = jnp.float32
    t = jnp.linspace(0.0, 1.0, L, dtype=f32)[:, None]
    bands = (FILTER_EMB - 1) // 2
    fr = jnp.linspace(1e-4, bands - 1, bands, dtype=f32)
    ang = 2.0 * math.pi * jnp.arange(L, dtype=f32)[:, None] / L * fr
    z = jnp.concatenate([t, jnp.cos(ang), -jnp.sin(ang)], axis=-1)
    h = jnp.sin(freq * (z.astype(w1.dtype) @ w1 + b1))
    h = jnp.sin(freq * (h @ w2 + b2))
    h = (h @ w3).astype(f32)
    min_decay = math.log(DECAY_TARGET) / SLOW_DECAY_PCT
    max_decay = math.log(DECAY_TARGET) / FAST_DECAY_PCT
    deltas = jnp.abs(jnp.linspace(min_decay, max_decay, HYENA_W, dtype=f32))
    decay = jnp.exp(-t * deltas)
    h = h * jnp.tile(decay, (1, 2))
    h_fwd, h_bwd = h[:, :HYENA_W], h[:, HYENA_W:]
    k = jnp.concatenate([h_fwd, jnp.zeros((1, HYENA_W), f32), h_bwd[:0:-1]], axis=0)
    return k / jnp.sum(jnp.abs(k), axis=0, keepdims=True)


def hyena_mix(u, conv_w, conv_b, w1, b1, freq, w2, b2, w3, d_bias):
    B, L, _ = u.shape
    u = short_conv(u, conv_w, conv_b)
    x0, x1, v = jnp.split(u, 3, axis=-1)
    z = (x1 * v).astype(jnp.float32)
    k = hyena_filters(L, w1, b1, freq, w2, b2, w3)
    zf = jnp.fft.rfft(z, n=2 * L, axis=1)
    kf = jnp.fft.rfft(k, n=2 * L, axis=0)
    y = jnp.fft.irfft(zf * kf[None], n=2 * L, axis=1)[:, :L] + d_bias.astype(jnp.float32) * z
    return (x0.astype(jnp.float32) * y).astype(u.dtype)


def heads_out(att, p, hy, w_out):
    f = fourier_mix(p[..., OFF_F:OFF_H])
    hz = hyena_mix(p[..., OFF_H:], *hy)
    return jnp.concatenate([att, f, hz], axis=-1) @ w_out


def sq_relu_mlp(h, w1, w2):
    return jnp.square(jax.nn.relu(h @ w1)) @ w2


def setup_inputs(seed: int = 0) -> dict:
    key = jax.random.key(seed)
    ks = jax.random.split(key, 26)

    def nrm(k, shape, scale):
        return jax.random.normal(k, shape, jnp.float32) * scale

    D = D_MODEL
    return {
        "x": nrm(ks[0], (BATCH, SEQ, D), 1.0),
        "c": nrm(ks[1], (BATCH, D), 1.0),
        "ctx": nrm(ks[2], (BATCH, CTX_LEN, D), 1.0),
        "c_ctx": nrm(ks[3], (D,), 1.0),
        "norm1_g": 1.0 + nrm(ks[4], (DEPTH, D), 0.05),
        "norm2_g": 1.0 + nrm(ks[5], (DEPTH, D), 0.05),
        "w_mod": nrm(ks[6], (DEPTH, D, N_MOD * D), 0.5 * D ** -0.5),
        "b_mod": nrm(ks[7], (DEPTH, N_MOD * D), 0.01),
        "w_in": nrm(ks[8], (DEPTH, D, IN_W), D ** -0.5),
        "q_norm_g": 1.0 + nrm(ks[9], (DEPTH, Q_LORA), 0.05),
        "kv_norm_g": 1.0 + nrm(ks[10], (DEPTH, KV_LORA), 0.05),
        "w_uq": nrm(ks[11], (DEPTH, Q_LORA, MLA_HEADS * (QK_NOPE + QK_ROPE)), Q_LORA ** -0.5),
        "w_ukv": nrm(ks[12], (DEPTH, KV_LORA, MLA_HEADS * (QK_NOPE + V_HEAD)), KV_LORA ** -0.5),
        "hy_conv_w": nrm(ks[13], (DEPTH, HYENA_CONV, 3 * HYENA_W), HYENA_CONV ** -0.5),
        "hy_conv_b": nrm(ks[14], (DEPTH, 3 * HYENA_W), 0.01),
        "hy_w1": nrm(ks[15], (DEPTH, FILTER_EMB, FILTER_ORDER), FILTER_EMB ** -0.5),
        "hy_b1": nrm(ks[16], (DEPTH, FILTER_ORDER), 0.1),
        "hy_freq": 1.0 + nrm(ks[17], (DEPTH, FILTER_ORDER), 0.05),
        "hy_w2": nrm(ks[18], (DEPTH, FILTER_ORDER, FILTER_ORDER), FILTER_ORDER ** -0.5),
        "hy_b2": nrm(ks[19], (DEPTH, FILTER_ORDER), 0.1),
        "hy_w3": nrm(ks[20], (DEPTH, FILTER_ORDER, 2 * HYENA_W), FILTER_ORDER ** -0.5),
        "hy_d": nrm(ks[21], (DEPTH, HYENA_W), 0.5),
        "w_out": nrm(ks[22], (DEPTH, MIX_W, D), MIX_W ** -0.5),
        "w_mlp1": nrm(ks[23], (DEPTH, D, D_FF), D ** -0.5),
        "w_mlp2": nrm(ks[24], (DEPTH, D_FF, D), D_FF ** -0.5),
        "final_norm_g": 1.0 + nrm(ks[25], (D,), 0.05),
    }


def reference(x, c, ctx, c_ctx, norm1_g, norm2_g, w_mod, b_mod, w_in, q_norm_g, kv_norm_g,
              w_uq, w_ukv, hy_conv_w, hy_conv_b, hy_w1, hy_b1, hy_freq, hy_w2, hy_b2, hy_w3,
              hy_d, w_out, w_mlp1, w_mlp2, final_norm_g):
    L = x.shape[1]
    rope = grid_rope_tables(L, x.dtype)
    xc = ctx
    for l in range(DEPTH):
        last = l == DEPTH - 1
        hy = (hy_conv_w[l], hy_conv_b[l], hy_w1[l], hy_b1[l], hy_freq[l],
              hy_w2[l], hy_b2[l], hy_w3[l], hy_d[l])
        sh1, sc1, g1, sh2, sc2, g2 = modulation(c, w_mod[l], b_mod[l])
        csh1, csc1, cg1, csh2, csc2, cg2 = modulation(c_ctx, w_mod[l], b_mod[l])

        hx = modulate(rms_norm(x, norm1_g[l]), sh1, sc1)
        hc = modulate(rms_norm(xc, norm1_g[l]), csh1, csc1)
        px = hx @ w_in[l]
        if last:
            pc_kv = hc @ w_in[l][:, OFF_KV:OFF_F]
        else:
            pc = hc @ w_in[l]
            pc_kv = pc[..., OFF_KV:OFF_F]
        kn_c, kr_c, v_c = mla_keys_values(pc_kv, kv_norm_g[l], w_ukv[l], None)
        kn_x, kr_x, v_x = mla_keys_values(px[..., OFF_KV:OFF_F], kv_norm_g[l], w_ukv[l], rope)
        qn_x, qr_x = mla_queries(px, q_norm_g[l], w_uq[l], rope)
        att_x = mla_attention(qn_x, qr_x,
                              jnp.concatenate([kn_x, kn_c], axis=1),
                              jnp.concatenate([kr_x, kr_c], axis=1),
                              jnp.concatenate([v_x, v_c], axis=1))
        x = x + g1 * heads_out(att_x, px, hy, w_out[l])

        x = x + g2 * sq_relu_mlp(modulate(rms_norm(x, norm2_g[l]), sh2, sc2), w_mlp1[l], w_mlp2[l])

        if not last:
            qn_c, qr_c = mla_queries(pc, q_norm_g[l], w_uq[l], None)
            att_c = mla_attention(qn_c, qr_c, kn_c, kr_c, v_c)
            xc = xc + cg1 * heads_out(att_c, pc, hy, w_out[l])
            xc = xc + cg2 * sq_relu_mlp(modulate(rms_norm(xc, norm2_g[l]), csh2, csc2),
                                        w_mlp1[l], w_mlp2[l])
    return rms_norm(x, final_norm_g)
```

```python
import math
import numpy as np
import ml_dtypes
from contextlib import ExitStack
import concourse.bass as bass
import concourse.mybir as mybir
from concourse.bass_utils import run_bass_kernel_spmd

F32 = mybir.dt.float32
BF16 = mybir.dt.bfloat16
AF = mybir.ActivationFunctionType
ALU = mybir.AluOpType

D = 1024
L = 4096
LC = 256
T = L + LC
NT = T // 128
DEPTH = 2
H = 8
OFF_Q, OFF_KV, OFF_KR, OFF_F, OFF_H, IN_W = 0, 384, 640, 672, 928, 1696
EPS = 1e-6
NCOL = 64
C_N1G, C_N2G, C_QG, C_KVG, C_HCW, C_HCB, C_HD, C_HB1, C_HFR, C_HB2 = 0, 8, 16, 19, 21, 39, 45, 47, 48, 49


class _Op:
    __slots__ = ('eng', 'calls', 'deps', 'signal', 'sem', 'val', 'dma', 'waits', 'pre')


class Sched:
    NDMA = 8
    ATTR = [('pe', 'tensor'), ('act', 'scalar'), ('dve', 'vector'), ('pool', 'gpsimd'), ('sp', 'sync')]

    def __init__(self, nc, es):
        self.nc = nc
        self.sem = {e: es.enter_context(nc.semaphore('s_' + e)) for e in ['pe', 'act', 'dve', 'pool']}
        self.dsem = {q: [es.enter_context(nc.semaphore('d_%s%d' % (q, i))) for i in range(self.NDMA)]
                     for q in ['sp', 'pool', 'act']}
        self.cnt = {e: 0 for e in self.sem}
        self.dcnt = {q: 0 for q in self.dsem}
        self._reset()

    def _reset(self):
        self.ops = {e: [] for e, _ in self.ATTR}
        self.last_w = {}
        self.readers = {}

    def multi(self, eng, calls, r=(), w=(), dma=False):
        op = _Op()
        op.eng = eng; op.calls = calls; op.dma = dma; op.signal = False; op.pre = None
        deps = []
        for k in r:
            d = self.last_w.get(k)
            if d is not None: deps.append(d)
        for k in w:
            d = self.last_w.get(k)
            if d is not None: deps.append(d)
            deps.extend(self.readers.get(k, ()))
        if eng == 'pe':
            deps = [d for d in deps if d.eng != 'pe' or d.dma]
        op.deps = [d for d in deps if d is not op]
        for d in op.deps: d.signal = True
        for k in r: self.readers.setdefault(k, []).append(op)
        for k in w:
            self.last_w[k] = op
            self.readers[k] = []
        self.ops[eng].append(op)
        return op

    def op(self, eng, name, *args, r=(), w=(), **kw):
        return self.multi(eng, [(name, args, kw)], r=r, w=w)

    def dma(self, q, out, in_, r=(), w=(), **kw):
        return self.multi(q, [('dma_start', (out, in_), kw)], r=r, w=w, dma=True)

    def flush(self):
        nc = self.nc
        for e, _ in self.ATTR:
            ops = self.ops[e]
            for op in reversed(ops):
                if not op.dma:
                    op.signal = True
                    break
            for op in ops:
                if op.dma:
                    j = self.dcnt[e]; self.dcnt[e] += 1
                    op.sem = self.dsem[e][j % self.NDMA]
                    op.val = 16 * (j // self.NDMA + 1)
                    op.pre = (op.sem, op.val - 16) if op.val > 16 else None
                elif op.signal:
                    self.cnt[e] += 1
                    op.sem = self.sem[e]; op.val = self.cnt[e]
        finals = {}
        for e, _ in self.ATTR:
            seen = {}
            for op in self.ops[e]:
                need = {}
                if op.pre is not None: need[id(op.pre[0])] = op.pre
                for d in op.deps:
                    cur = need.get(id(d.sem))
                    if cur is None or cur[1] < d.val: need[id(d.sem)] = (d.sem, d.val)
                op.waits = []
                for k, (s, v) in need.items():
                    if seen.get(k, -1) < v:
                        seen[k] = v; op.waits.append((s, v))
                if op.dma or op.signal:
                    cur = finals.get(id(op.sem))
                    if cur is None or cur[1] < op.val: finals[id(op.sem)] = (op.sem, op.val)
        with nc.Block() as blk:
            for e, attr in self.ATTR:
                ops = self.ops[e]
                if not ops and e != 'sp': continue

                def body(eng, ops=ops, e=e):
                    for op in ops:
                        for (s, v) in op.waits: eng.wait_ge(s, v)
                        inst = None
                        for (name, args, kw) in op.calls:
                            inst = getattr(eng, name)(*args, **kw)
                        if op.dma: inst.then_inc(op.sem, 16)
                        elif op.signal: inst.then_inc(op.sem, 1)
                    if e == 'sp':
                        for (s, v) in finals.values(): eng.wait_ge(s, v)
                getattr(blk, attr)(body)
        self._reset()


def _bf(a):
    return np.ascontiguousarray(a.astype(ml_dtypes.bfloat16))


def _f32(a):
    return np.ascontiguousarray(a.astype(np.float32))


ROPE_PERM = np.array(list(range(8, 16)) + list(range(0, 8)) + list(range(24, 32)) + list(range(16, 24)))


def _rope_tables():
    rows = L // 64
    row = np.repeat(np.arange(rows, dtype=np.float64), 64)
    col = np.tile(np.arange(64, dtype=np.float64), rows)
    inv = 10000.0 ** (-np.arange(0, 16, 2, dtype=np.float64) / 16)
    ar = row[None, :] * inv[:, None]
    ac = col[None, :] * inv[:, None]
    cos = np.ones((32, T)); sin = np.zeros((32, T))
    cos[0:8, :L] = np.cos(ar); cos[8:16, :L] = np.cos(ar); cos[16:24, :L] = np.cos(ac); cos[24:32, :L] = np.cos(ac)
    sin[0:8, :L] = -np.sin(ar); sin[8:16, :L] = np.sin(ar); sin[16:24, :L] = -np.sin(ac); sin[24:32, :L] = np.sin(ac)
    return _f32(cos), _f32(sin)


def _fnet_tables(Ls, L1, L2):
    w = np.arange(64)
    ph = 2 * np.pi * np.outer(w, w) / 64
    cw = np.zeros((128, 128)); sw = np.zeros((128, 128))
    for g in range(2):
        cw[g * 64:(g + 1) * 64, g * 64:(g + 1) * 64] = np.cos(ph)
        sw[g * 64:(g + 1) * 64, g * 64:(g + 1) * 64] = np.sin(ph)
    csw = np.concatenate([cw, sw], axis=1)
    a = np.arange(L1)
    p1 = 2 * np.pi * np.outer(a, a) / L1
    wr, wi = np.cos(p1), -np.sin(p1)
    w1 = np.concatenate([wr, wi], axis=1)
    w2 = np.concatenate([wi, -wr], axis=1)
    p2 = 2 * np.pi * np.outer(np.arange(L2), np.arange(Ls)) / Ls
    sc = 1.0 / math.sqrt(Ls * 64)
    return _bf(csw), _bf(w1), _bf(w2), _bf(np.cos(p2) * sc), _bf(np.sin(p2) * sc)


def _hyena_tables(Ls):
    N = 2 * Ls
    N1 = N // 128
    a = np.arange(N1)
    p1 = 2 * np.pi * np.outer(a, a) / N1
    w1 = np.concatenate([np.cos(p1), -np.sin(p1)], axis=1)
    p2 = 2 * np.pi * np.outer(np.arange(128), np.arange(N)) / N
    tr, ti, nti = np.cos(p2), -np.sin(p2), np.sin(p2)
    p3 = 2 * np.pi * np.outer(np.arange(128), np.arange(128)) / 128
    i1a = np.concatenate([np.cos(p3), np.sin(p3)], axis=1)
    i1b = np.concatenate([-np.sin(p3), np.cos(p3)], axis=1)
    p4 = 2 * np.pi * np.outer(np.arange(N1), np.arange(Ls)) / N
    er, nei = np.cos(p4) / N, -np.sin(p4) / N
    j = np.arange(N)
    tau = np.where(j < Ls, j, N - j).astype(np.float64)
    tau[Ls] = 0
    tl = np.linspace(0.0, 1.0, Ls)
    t = tl[tau.astype(np.int64)]
    fr = np.linspace(1e-4, 15.0, 16)
    ang = 2.0 * np.pi * tau[:, None] / Ls * fr[None, :]
    z = np.concatenate([t[:, None], np.cos(ang), -np.sin(ang)], axis=1)
    negt = -t.copy()
    negt[Ls] = -1e4
    tcol = negt.reshape(N1, 128)
    deltas = np.abs(np.linspace(math.log(1e-2) / 1.5, math.log(1e-2) / 0.3, 256))
    drow = np.broadcast_to(deltas[None, :], (128, 256))
    return dict(N1=N1, w1=_bf(w1), tr=_bf(tr), ti=_bf(ti), nti=_bf(nti), i1a=_bf(i1a), i1b=_bf(i1b),
                er=_bf(er), nei=_bf(nei), zT=_f32(z.T), tcol=_f32(tcol), drow=_f32(drow))


_CONST = None


def _consts():
    global _CONST
    if _CONST is not None:
        return _CONST
    c = {}
    c['ident_bf'] = _bf(np.eye(128))
    c['ident_f'] = _f32(np.eye(128))
    c['ones_f'] = _f32(np.ones((128, 128)))
    c['ropeC'], c['ropeS'] = _rope_tables()
    for nm, Ls, L1, L2 in (('x', L, 64, 64), ('c', LC, 4, 64)):
        csw, w1, w2, tr, sn = _fnet_tables(Ls, L1, L2)
        c['fn_csw'] = csw
        c['fn_w1' + nm], c['fn_w2' + nm], c['fn_tr' + nm], c['fn_sn' + nm] = w1, w2, tr, sn
        ht = _hyena_tables(Ls)
        for k, v in ht.items():
            if k != 'N1':
                c['hy_%s%s' % (k, nm)] = v
    _CONST = c
    return c


def _colpack(v, n):
    return np.asarray(v, np.float32).reshape(n, 128).T


def _prep(inp):
    c = dict(_consts())
    sh = {}
    w_in = np.asarray(inp['w_in'], np.float32)
    sh['w_inx'] = np.ascontiguousarray(np.concatenate([w_in, w_in[:, :, OFF_KR + ROPE_PERM]], axis=2))
    w_uq = np.asarray(inp['w_uq'], np.float32)
    sh['w_uq'] = np.ascontiguousarray(w_uq)
    wq = w_uq.reshape(DEPTH, 384, H, 96)
    sh['w_uqp'] = np.ascontiguousarray(
        np.concatenate([wq[..., :64], wq[..., 64:][..., ROPE_PERM]], axis=-1).reshape(DEPTH, 384, 768))
    wkv = np.asarray(inp['w_ukv'], np.float32).reshape(DEPTH, 256, H, 128)
    sh['w_ukvx'] = np.ascontiguousarray(
        np.concatenate([wkv[..., :64].reshape(DEPTH, 256, 512), wkv[..., 64:].reshape(DEPTH, 256, 512)], axis=2))
    for k in ('w_mod', 'w_out', 'w_mlp1', 'w_mlp2', 'hy_w1', 'hy_w2', 'hy_w3'):
        sh[k] = np.ascontiguousarray(np.asarray(inp[k], np.float32))
    sh['b_mod2'] = np.ascontiguousarray(np.repeat(np.asarray(inp['b_mod'], np.float32)[:, None, :], 2, axis=1))
    cols = np.zeros((DEPTH, 128, NCOL), np.float32)
    for l in range(DEPTH):
        cols[l, :, C_N1G:C_N1G + 8] = _colpack(inp['norm1_g'][l], 8)
        cols[l, :, C_N2G:C_N2G + 8] = _colpack(inp['norm2_g'][l], 8)
        cols[l, :, C_QG:C_QG + 3] = _colpack(inp['q_norm_g'][l], 3)
        cols[l, :, C_KVG:C_KVG + 2] = _colpack(inp['kv_norm_g'][l], 2)
        for tap in range(3):
            cols[l, :, C_HCW + tap * 6:C_HCW + tap * 6 + 6] = _colpack(inp['hy_conv_w'][l, tap], 6)
        cols[l, :, C_HCB:C_HCB + 6] = _colpack(inp['hy_conv_b'][l], 6)
        cols[l, :, C_HD:C_HD + 2] = _colpack(inp['hy_d'][l], 2)
        cols[l, :64, C_HB1] = inp['hy_b1'][l]
        cols[l, :64, C_HFR] = inp['hy_freq'][l]
        cols[l, :64, C_HB2] = inp['hy_b2'][l]
    sh['cols'] = cols
    sh['fgrow'] = np.ascontiguousarray(np.broadcast_to(np.asarray(inp['final_norm_g'], np.float32)[None, :], (128, D)))
    sh.update(c)
    per = []
    x = np.asarray(inp['x'], np.float32); ctx = np.asarray(inp['ctx'], np.float32)
    cc = np.asarray(inp['c'], np.float32); c_ctx = np.asarray(inp['c_ctx'], np.float32)
    for b in range(8):
        d = dict(sh)
        d['xin'] = np.ascontiguousarray(np.concatenate([x[b], ctx[b]], axis=0))
        cv = np.stack([_colpack(cc[b], 8), _colpack(c_ctx, 8)], axis=-1)
        d['cvec'] = np.ascontiguousarray(cv)
        per.append(d)
    return per


GROUPS = [(g * 512, 512, 0) for g in range(8)] + [(L, LC, 1)]
SCALE = 1.0 / math.sqrt(96.0)


class Prog:
    def __init__(self, debug=False):
        self.debug = debug
        self.nc = nc = bass.Bass("TRN2", target_bir_lowering=False)
        self.es = ExitStack()
        self.S = Sched(nc, self.es)
        self._in = {}
        okind = dict(kind="ExternalOutput") if debug else {}
        scr = lambda n, shp, dt=F32: nc.dram_tensor(n, list(shp), dt, **okind).ap()
        self.spec = dict(
            xin=([T, D], F32), cvec=([128, 8, 2], F32), w_mod=([DEPTH, D, 6 * D], F32), b_mod2=([DEPTH, 2, 6 * D], F32),
            w_inx=([DEPTH, D, 1728], F32), w_uq=([DEPTH, 384, 768], F32), w_uqp=([DEPTH, 384, 768], F32),
            w_ukvx=([DEPTH, 256, 1024], F32), w_out=([DEPTH, D, D], F32), w_mlp1=([DEPTH, D, 4 * D], F32),
            w_mlp2=([DEPTH, 4 * D, D], F32), hy_w1=([DEPTH, 33, 64], F32), hy_w2=([DEPTH, 64, 64], F32),
            hy_w3=([DEPTH, 64, 512], F32), cols=([DEPTH, 128, NCOL], F32), fgrow=([128, D], F32),
            ident_bf=([128, 128], BF16), ident_f=([128, 128], F32), ones_f=([128, 128], F32),
            ropeC=([32, T], F32), ropeS=([32, T], F32), fn_csw=([128, 256], BF16))
        self.fn = {}; self.hy = {}
        for nm, Ls, L1 in (('x', L, 64), ('c', LC, 4)):
            N = 2 * Ls; N1 = N // 128
            self.spec.update({'fn_w1' + nm: ([L1, 2 * L1], BF16), 'fn_w2' + nm: ([L1, 2 * L1], BF16),
                              'fn_tr' + nm: ([64, Ls], BF16), 'fn_sn' + nm: ([64, Ls], BF16),
                              'hy_w1' + nm: ([N1, 2 * N1], BF16), 'hy_tr' + nm: ([128, N], BF16),
                              'hy_ti' + nm: ([128, N], BF16), 'hy_nti' + nm: ([128, N], BF16),
                              'hy_i1a' + nm: ([128, 256], BF16), 'hy_i1b' + nm: ([128, 256], BF16),
                              'hy_er' + nm: ([N1, Ls], BF16), 'hy_nei' + nm: ([N1, Ls], BF16),
                              'hy_zT' + nm: ([33, N], F32), 'hy_tcol' + nm: ([N1, 128], F32),
                              'hy_drow' + nm: ([128, 256], F32)})
            self.fn[nm] = dict(L1=L1, Ls=Ls)
            self.hy[nm] = dict(N1=N1, Ls=Ls, N=N, kf=scr('kf' + nm, [128, N1, 2, 256], BF16),
                               ssum=scr('ssum' + nm, [128, 2]))
        self.y = nc.dram_tensor('y', [L, D], F32, kind="ExternalOutput").ap()
        self.xres = scr('xres', [T, D]); self.mrow = scr('mrow', [2, 6 * D])
        self.pxT = scr('pxT', [14, 128, T], BF16); self.mixT = scr('mixT', [8, 128, T], BF16)
        self.hx0 = scr('hx0', [2, 128, T], BF16); self.hz = scr('hz', [2, 128, T], BF16)
        self.xmid = scr('xmid', [T, D]); self.h2T = scr('h2T', [8, 128, T], BF16)
        self.hyY = {}; self.hyF = {}
        for nm_ in ('x', 'c'):
            n1_ = self.hy[nm_]['N1']
            self.hyY[nm_] = scr('hyY' + nm_, [128, 256, 2 * n1_], BF16)
            self.hyF[nm_] = scr('hyF' + nm_, [128, n1_, 2, 128], BF16)
        self.modc = self.es.enter_context(nc.sbuf_tensor('modc', [128, 2, 4, 8], F32))
        self.cols = self.es.enter_context(nc.sbuf_tensor('colsb', [128, NCOL], F32))

    def uq(self, n):
        self._uq = getattr(self, '_uq', 0) + 1
        return '%s_%d' % (n, self._uq)

    def I(self, name):
        if name not in self._in:
            shp, dt = self.spec[name]
            self._in[name] = self.nc.dram_tensor(name, list(shp), dt, kind="ExternalInput").ap()
        return self._in[name]

    def p0_mod(self, l):
        nc, S = self.nc, self.S
        with ExitStack() as es:
            sb = lambda n, shp, dt=F32: es.enter_context(nc.sbuf_tensor(self.uq(n), shp, dt))
            cv = sb('cv', [128, 8, 2]); sc = sb('sc', [128, 8, 2])
            wt = [sb('wt%d' % i, [128, 8, 512]) for i in range(2)]
            msb = sb('msb', [2, 6 * D]); bm = sb('bm', [2, 6 * D])
            ps = [es.enter_context(nc.psum_tensor(self.uq('ps%d' % i), [2, 512], F32)) for i in range(2)]
            S.dma('sp', cv[:], self.I('cvec'), w=['cv'])
            S.dma('sp', bm[:], self.I('b_mod2')[l], w=['bm'])
            S.dma('sp', self.cols[:], self.I('cols')[l], w=['cols'])
            S.op('act', 'activation', sc[:], cv[:], AF.Silu, r=['cv'], w=['sc'])
            wv = self.I('w_mod')[l].rearrange("(k p) n -> p k n", p=128)
            for n in range(12):
                b = n % 2
                S.dma('sp', wt[b][:], wv[:, :, n * 512:(n + 1) * 512], w=[('wt', b)])
                S.multi('pe', [('matmul', (ps[b][:], sc[:, k, :], wt[b][:, k, :]), dict(start=(k == 0), stop=(k == 7)))
                               for k in range(8)], r=['sc', ('wt', b)], w=[('ps', b)])
                S.op('dve', 'tensor_tensor', msb[:, n * 512:(n + 1) * 512], ps[b][:], bm[:, n * 512:(n + 1) * 512],
                     ALU.add, r=[('ps', b), 'bm'], w=['msb'])
            S.dma('sp', self.mrow, msb[:], r=['msb'])
            S.flush()
            mT = sb('mT', [96, 128]); idf = sb('idf', [128, 128]); mcol = sb('mcol', [128, 96])
            pm = es.enter_context(nc.psum_tensor(self.uq('pm'), [128, 96], F32))
            S.dma('sp', mT[:], self.mrow.rearrange("s (j p) -> (s j) p", p=128), w=['mT'])
            S.dma('sp', idf[:], self.I('ident_f'), w=['idf'])
            S.op('pe', 'transpose', pm[:], mT[:], idf[0:96, 0:96], r=['mT', 'idf'], w=['pm'])
            S.op('dve', 'tensor_copy', mcol[:], pm[:], r=['pm'], w=['mcol'])
            for s in range(2):
                o = s * 48
                S.op('dve', 'scalar_tensor_tensor', self.modc[:, s, 0, :], mcol[:, o + 8:o + 16], 1.0,
                     self.cols[:, C_N1G:C_N1G + 8], ALU.add, ALU.mult, r=['mcol', 'cols'], w=['modc'])
                S.op('dve', 'tensor_copy', self.modc[:, s, 1, :], mcol[:, o:o + 8], r=['mcol'], w=['modc'])
                S.op('dve', 'scalar_tensor_tensor', self.modc[:, s, 2, :], mcol[:, o + 32:o + 40], 1.0,
                     self.cols[:, C_N2G:C_N2G + 8], ALU.add, ALU.mult, r=['mcol', 'cols'], w=['modc'])
                S.op('dve', 'tensor_copy', self.modc[:, s, 3, :], mcol[:, o + 24:o + 32], r=['mcol'], w=['modc'])
            S.flush()

    def norm_tiles(self, xt, nj, s, which, tl, key, n):
        S = self.S
        ss, rstd, xn, junk, pt, hT, idb = tl['ss'], tl['rstd'], tl['xn'], tl['junk'], tl['pt'], tl['hT'], tl['idb']
        for j in range(nj):
            S.op('act', 'activation', junk[:], xt[:, j, :], AF.Square, accum_out=ss[:, j:j + 1],
                 r=[key], w=['junk', tl['k'] + 'ss'])
        S.op('act', 'activation', rstd[:, 0:nj], ss[:, 0:nj], AF.Sqrt, bias=tl['eps'][:, 0:1], scale=1.0 / D,
             r=[tl['k'] + 'ss', 'eps'], w=[tl['k'] + 'rstd'])
        S.op('dve', 'reciprocal', rstd[:, 0:nj], rstd[:, 0:nj], r=[tl['k'] + 'rstd'], w=[tl['k'] + 'rstd'])
        for j in range(nj):
            S.op('dve', 'tensor_scalar', xn[:, j, :], xt[:, j, :], rstd[:, j:j + 1], None, ALU.mult,
                 r=[key, tl['k'] + 'rstd'], w=[tl['k'] + 'xn'])
        for k in range(8):
            pb = k % 2
            S.multi('pe', [('transpose', (pt[pb][:, j * 128:(j + 1) * 128], xn[:, j, k * 128:(k + 1) * 128], idb[:]), {})
                           for j in range(nj)], r=[tl['k'] + 'xn', 'idb'], w=[('pt', pb)])
            if k % 2 == 0:
                S.op('dve', 'tensor_scalar', hT[:, k, 0:n], pt[pb][:, 0:n], self.modc[:, s, which, k:k + 1],
                     self.modc[:, s, which + 1, k:k + 1], ALU.mult, ALU.add, r=[('pt', pb), 'modc'], w=[tl['k'] + 'hT'])
            else:
                S.op('act', 'activation', hT[:, k, 0:n], pt[pb][:, 0:n], AF.Identity,
                     bias=self.modc[:, s, which + 1, k:k + 1], scale=self.modc[:, s, which, k:k + 1],
                     r=[('pt', pb), 'modc'], w=[tl['k'] + 'hT'])

    def p1_inproj(self, l):
        nc, S = self.nc, self.S
        src = self.I('xin') if l == 0 else self.xres
        with ExitStack() as es:
            sb = lambda n, shp, dt=F32: es.enter_context(nc.sbuf_tensor(self.uq(n), shp, dt))
            pst = lambda n, shp, dt=F32: es.enter_context(nc.psum_tensor(self.uq(n), shp, dt))
            win = sb('win', [128, 8, 1728], BF16)
            S.dma('pool', win[:], self.I('w_inx')[l].rearrange("(k p) n -> p k n", p=128), w=['win'])
            idb = sb('idb', [128, 128], BF16); S.dma('sp', idb[:], self.I('ident_bf'), w=['idb'])
            onesf = sb('onesf', [128, 128]); S.dma('sp', onesf[:], self.I('ones_f'), w=['onesf'])
            rc = sb('rc', [32, T]); rs = sb('rs', [32, T])
            S.dma('sp', rc[:], self.I('ropeC'), w=['rc']); S.dma('sp', rs[:], self.I('ropeS'), w=['rs'])
            eps = sb('eps', [128, 1]); S.op('dve', 'memset', eps[:], EPS, w=['eps'])
            xg = [sb('xg%d' % i, [128, 4, D]) for i in range(2)]
            tls = []
            junk = sb('junk', [128, D], BF16)
            pt = [pst('pt%d' % i, [128, 512], BF16) for i in range(2)]
            for i in range(2):
                tls.append(dict(k='t%d' % i, ss=sb('ss%d' % i, [128, 4]), rstd=sb('rstd%d' % i, [128, 4]),
                                xn=sb('xn%d' % i, [128, 4, D], BF16), junk=junk, pt=pt,
                                hT=sb('hT%d' % i, [128, 8, 512], BF16), idb=idb, eps=eps))
            ost = [sb('ost%d' % i, [128, 14, 512], BF16) for i in range(2)]
            for i in range(2):
                S.op('pool', 'memset', ost[i][:, 13, :], 0.0, w=[('ost', i)])
            c32 = sb('c32', [128, 5, 512]); sq = sb('sq', [128, 5, 512]); rq = sb('rq', [128, 512])
            t1 = sb('t1', [32, 512]); t2 = sb('t2', [32, 512])
            po = [pst('po%d' % i, [128, 512]) for i in range(3)]
            pn = pst('pn', [128, 512])
            npo = [0]

            def proj(col0, ncols, hT, n, b):
                i = npo[0] % 3; npo[0] += 1
                S.multi('pe', [('matmul', (po[i][0:ncols, 0:n], win[:, k, col0:col0 + ncols], hT[:, k, 0:n]),
                                dict(start=(k == 0), stop=(k == 7))) for k in range(8)],
                        r=['win', 't%dhT' % b], w=[('po', i)])
                return i

            def front(gi):
                t0, n, s = GROUPS[gi]
                b = gi % 2; nj = n // 128; tl = tls[b]
                S.dma('sp', xg[b][:, 0:nj, :], src[t0:t0 + n, :].rearrange("(j p) d -> p j d", p=128), w=[('xg', b)])
                self.norm_tiles(xg[b], nj, s, 0, tl, ('xg', b), n)

            def back(gi):
                t0, n, s = GROUPS[gi]
                b = gi % 2; nj = n // 128; tl = tls[b]
                hT = tl['hT']
                okey = ('ost', b)
                for (c0, nch, gcol, ci0, i0) in ((OFF_Q, 3, C_QG, 0, 0), (OFF_KV, 2, C_KVG, 3, 3)):
                    for c in range(nch):
                        i = proj(c0 + c * 128, 128, hT, n, b)
                        S.op('act', 'activation', c32[:, i0 + c, 0:n], po[i][:, 0:n], AF.Copy,
                             r=[('po', i)], w=[('c32', i0 + c)])
                        S.op('act', 'activation', sq[:, i0 + c, 0:n], po[i][:, 0:n], AF.Square,
                             r=[('po', i)], w=[('sq', i0 + c)])
                    S.multi('pe', [('matmul', (pn[:, 0:n], onesf[:], sq[:, i0 + c, 0:n]),
                                    dict(start=(c == 0), stop=(c == nch - 1))) for c in range(nch)],
                            r=['onesf'] + [('sq', i0 + c) for c in range(nch)], w=['pn'])
                    S.op('act', 'activation', rq[:, 0:n], pn[:, 0:n], AF.Sqrt, bias=eps[:, 0:1], scale=1.0 / (128 * nch),
                         r=['pn', 'eps'], w=['rq'])
                    S.op('dve', 'reciprocal', rq[:, 0:n], rq[:, 0:n], r=['rq'], w=['rq'])
                    for c in range(nch):
                        S.op('dve', 'scalar_tensor_tensor', ost[b][:, ci0 + c, 0:n], c32[:, i0 + c, 0:n],
                             self.cols[:, gcol + c:gcol + c + 1], rq[:, 0:n], ALU.mult, ALU.mult,
                             r=[('c32', i0 + c), 'rq', 'cols'], w=[okey])
                ia = proj(OFF_KR, 32, hT, n, b)
                ib = proj(IN_W, 32, hT, n, b)
                S.op('dve', 'tensor_tensor', t1[:, 0:n], po[ia][0:32, 0:n], rc[:, t0:t0 + n], ALU.mult,
                     r=[('po', ia), 'rc'], w=['t1'])
                S.op('dve', 'tensor_tensor', t2[:, 0:n], po[ib][0:32, 0:n], rs[:, t0:t0 + n], ALU.mult,
                     r=[('po', ib), 'rs'], w=['t2'])
                S.op('dve', 'tensor_tensor', ost[b][0:32, 13, 0:n], t1[:, 0:n], t2[:, 0:n], ALU.add,
                     r=['t1', 't2'], w=[okey])
                for c in range(8):
                    i = proj(OFF_F + c * 128, 128, hT, n, b)
                    if c % 2 == 0:
                        S.op('act', 'activation', ost[b][:, 5 + c, 0:n], po[i][:, 0:n], AF.Copy, r=[('po', i)], w=[okey])
                    else:
                        S.op('dve', 'tensor_copy', ost[b][:, 5 + c, 0:n], po[i][:, 0:n], r=[('po', i)], w=[okey])
                S.dma('sp', self.pxT[:, :, t0:t0 + n].rearrange("c p t -> p c t"), ost[b][:, :, 0:n], r=[okey])

            front(0)
            for gi in range(len(GROUPS)):
                if gi + 1 < len(GROUPS):
                    front(gi + 1)
                back(gi)
            S.flush()

    def p2_attn(self, l):
        nc, S = self.nc, self.S
        with ExitStack() as es:
            sb = lambda n, shp, dt=F32: es.enter_context(nc.sbuf_tensor(self.uq(n), shp, dt))
            pst = lambda n, shp, dt=F32: es.enter_context(nc.psum_tensor(self.uq(n), shp, dt))
            cqn = sb('cqn', [128, 3, T], BF16); ckvn = sb('ckvn', [128, 2, T], BF16)
            S.dma('sp', cqn[:], self.pxT[0:3].rearrange("c p t -> p c t"), w=['cqn'])
            S.dma('sp', ckvn[:], self.pxT[3:5].rearrange("c p t -> p c t"), w=['ckvn'])
            wuq = sb('wuq', [128, 3, 768], BF16); wuqp = sb('wuqp', [128, 3, 768], BF16)
            wukv = sb('wukv', [128, 2, 1024], BF16)
            S.dma('pool', wuq[:], self.I('w_uq')[l].rearrange("(k p) n -> p k n", p=128), w=['wuq'])
            S.dma('pool', wuqp[:], self.I('w_uqp')[l].rearrange("(k p) n -> p k n", p=128), w=['wuqp'])
            S.dma('pool', wukv[:], self.I('w_ukvx')[l].rearrange("(k p) n -> p k n", p=128), w=['wukv'])
            onesf = sb('onesf', [128, 128]); S.dma('sp', onesf[:], self.I('ones_f'), w=['onesf'])
            tc_ = sb('tabc', [96, T]); ts_ = sb('tabs', [96, T])
            S.dma('sp', tc_[64:96, :], self.I('ropeC'), w=['tabc']); S.dma('sp', ts_[64:96, :], self.I('ropeS'), w=['tabs'])
            kt = [sb('kt%d' % i, [96, T], BF16) for i in range(2)]
            qt = [sb('qt%d' % i, [96, T], BF16) for i in range(2)]
            for b in range(2):
                S.dma('sp', kt[b][64:96, :], self.pxT[13, 0:32, :], w=[('ktr', b)])
            vh = [sb('vh%d' % i, [128, NT, 128], BF16) for i in range(2)]
            for i in range(2):
                S.op('pool', 'memset', vh[i][:], 1.0, w=[('vh', i)])
            t1 = sb('t1', [96, 512]); t2 = sb('t2', [96, 512])
            ptl = [sb('ptl%d' % i, [128, 2, 512], BF16) for i in range(3)]
            rr = sb('rr', [96, 512]); osb = [sb('osb%d' % i, [64, 512]) for i in range(2)]; ost = sb('ost', [64, T], BF16)
            sel = sb('sel', [128, 128], BF16)
            S.op('pool', 'memset', sel[:], 0.0, w=['sel'])
            S.op('pool', 'memset', sel[64:65, :], 1.0, w=['sel'])
            rhi = [sb('rhi%d' % i, [128, 512], BF16) for i in range(2)]; rlo = [sb('rlo%d' % i, [128, 512], BF16) for i in range(2)]
            for i in range(2):
                S.op('pool', 'memset', rhi[i][:], 0.0, w=[('rhi', i)])
                S.op('pool', 'memset', rlo[i][:], 0.0, w=[('rlo', i)])
            ps = [pst('ps%d' % i, [128, 1024]) for i in range(2)]
            po = [pst('po%d' % i, [128, 512]) for i in range(2)]
            pq = pst('pq', [128, 512]); pq2 = pst('pq2', [128, 512]); pk = pq; pb = pq2
            LA = 2

            def project_chunks(h):
                b = h % 2
                chunks = []
                for gi, (t0, n, s) in enumerate(GROUPS):
                    def cA(gi=gi, t0=t0, n=n):
                        S.multi('pe', [('matmul', (pq[0:96, 0:n], wuq[:, c, h * 96:(h + 1) * 96], cqn[:, c, t0:t0 + n]),
                                        dict(start=(c == 0), stop=(c == 2))) for c in range(3)],
                                r=['cqn', 'wuq'], w=['pq'])
                        S.multi('pe', [('matmul', (pq2[0:96, 0:n], wuqp[:, c, h * 96:(h + 1) * 96], cqn[:, c, t0:t0 + n]),
                                        dict(start=(c == 0), stop=(c == 2))) for c in range(3)],
                                r=['cqn', 'wuqp'], w=['pq2'])
                        S.op('dve', 'tensor_copy', qt[b][0:64, t0:t0 + n], pq[0:64, 0:n], r=['pq'], w=[('qt', b, gi)])
                        S.op('dve', 'tensor_tensor', t1[64:96, 0:n], pq[64:96, 0:n], tc_[64:96, t0:t0 + n], ALU.mult,
                             r=['pq', 'tabc'], w=['t1'])
                        S.op('dve', 'tensor_tensor', t2[64:96, 0:n], pq2[64:96, 0:n], ts_[64:96, t0:t0 + n], ALU.mult,
                             r=['pq2', 'tabs'], w=['t2'])
                        S.op('dve', 'tensor_tensor', qt[b][64:96, t0:t0 + n], t1[64:96, 0:n], t2[64:96, 0:n], ALU.add,
                             r=['t1', 't2'], w=[('qt', b, gi)])

                    def cB(gi=gi, t0=t0, n=n):
                        S.multi('pe', [('matmul', (pk[:, 0:n], wukv[:, c, h * 64:h * 64 + 128], ckvn[:, c, t0:t0 + n]),
                                        dict(start=(c == 0), stop=(c == 1))) for c in range(2)],
                                r=['ckvn', 'wukv'], w=['pq'])
                        S.op('dve', 'tensor_copy', kt[b][0:64, t0:t0 + n], pk[0:64, 0:n], r=['pq'], w=[('kt', b, gi)])
                    chunks += [cA, cB]
                for i0 in range(0, NT, 8):
                    def cV(i0=i0):
                        nt_ = min(8, NT - i0)
                        calls = []
                        for ii_ in range(nt_):
                            i_ = i0 + ii_
                            calls += [('matmul', (pq2[:, ii_ * 64:(ii_ + 1) * 64], ckvn[:, c, i_ * 128:(i_ + 1) * 128],
                                                  wukv[:, c, 512 + h * 64:512 + (h + 1) * 64]), dict(start=(c == 0), stop=(c == 1)))
                                      for c in range(2)]
                        S.multi('pe', calls, r=['ckvn', 'wukv'], w=['pq2'])
                        S.op('dve', 'tensor_copy', vh[b][:, i0:i0 + nt_, 0:64],
                             pq2[:, 0:nt_ * 64].rearrange("p (a d) -> p a d", d=64), r=['pq2'], w=[('vh', b)])
                    chunks.append(cV)
                return chunks

            def project(h):
                for c_ in project_chunks(h):
                    c_()

            project(0)
            for h in range(H):
                b = h % 2
                seq = []
                for gi, (q0, nq, s) in enumerate(GROUPS):
                    tiles = list(range(NT)) if s == 0 else [32, 33]
                    npair = len(tiles) // 2
                    for ii in range(npair):
                        seq.append((gi, q0, nq, ii, tiles[2 * ii], npair))

                def qk(e):
                    gi, q0, nq, ii, i, np_ = seq[e]
                    sl = e % 2
                    S.multi('pe', [('matmul', (ps[sl][:, a_ * 512:a_ * 512 + nq], kt[b][0:96, (i + a_) * 128:(i + a_ + 1) * 128],
                                               qt[b][0:96, q0:q0 + nq]), dict(start=True, stop=True)) for a_ in range(2)],
                            r=[('kt', b, i // 4), ('kt', b, (i + 1) // 4), ('ktr', b), ('qt', b, gi)], w=[('ps', sl)])

                pending = []
                nxt = project_chunks(h + 1) if h + 1 < H else []
                cstep = max(1, (len(seq) - 12) // max(1, len(nxt)))
                for e in range(min(LA, len(seq))):
                    qk(e)
                for e in range(len(seq)):
                    gi, q0, nq, ii, i, np_ = seq[e]
                    sl = e % 2; sl2 = e % 3; ob = gi % 2
                    S.op('act', 'activation', ptl[sl2][:, :, 0:nq], ps[sl][:, :].rearrange("p (a n) -> p a n", a=2)[:, :, 0:nq],
                         AF.Exp, scale=SCALE, r=[('ps', sl)], w=[('ptl', sl2)])
                    if e + LA < len(seq):
                        qk(e + LA)
                    S.multi('pe', [('matmul', (po[ob][:, 0:nq], vh[b][:, i + a_, :], ptl[sl2][:, a_, 0:nq]),
                                    dict(start=(ii == 0 and a_ == 0), stop=(ii == np_ - 1 and a_ == 1))) for a_ in range(2)],
                            r=[('ptl', sl2), ('vh', b)], w=[('po', ob)])
                    if ii == np_ - 1:
                        fb = gi % 2
                        S.op('dve', 'reciprocal', rr[64:65, 0:nq], po[ob][64:65, 0:nq], r=[('po', ob)], w=['rr'])
                        S.op('dve', 'tensor_copy', rhi[fb][64:65, 0:nq], rr[64:65, 0:nq], r=['rr'], w=[('rhi', fb)])
                        S.op('dve', 'tensor_tensor', rlo[fb][64:65, 0:nq], rr[64:65, 0:nq], rhi[fb][64:65, 0:nq], ALU.subtract,
                             r=['rr', ('rhi', fb)], w=[('rlo', fb)])
                        S.op('dve', 'tensor_copy', osb[fb][:, 0:nq], po[ob][0:64, 0:nq], r=[('po', ob)], w=[('osb', fb)])
                        pending.append((e + 3, fb, q0, nq))
                    while pending and (pending[0][0] <= e or e == len(seq) - 1):
                        _, fb, fq0, fnq = pending.pop(0)
                        S.multi('pe', [('matmul', (pb[:, 0:fnq], sel[:, :], rhi[fb][:, 0:fnq]), dict(start=True, stop=False)),
                                       ('matmul', (pb[:, 0:fnq], sel[:, :], rlo[fb][:, 0:fnq]), dict(start=False, stop=True))],
                                r=[('rhi', fb), ('rlo', fb), 'sel'], w=['pq2'])
                        S.op('dve', 'tensor_tensor', ost[:, fq0:fq0 + fnq], osb[fb][:, 0:fnq], pb[0:64, 0:fnq], ALU.mult,
                             r=[('osb', fb), 'pq2'], w=['ost'])
                    if nxt and e >= 4 and (e - 4) % cstep == 0:
                        nxt.pop(0)()
                while nxt:
                    nxt.pop(0)()
                S.dma('sp', self.mixT[h // 2, (h % 2) * 64:(h % 2) * 64 + 64, :], ost[:, :], r=['ost'])
            S.flush()

    def p5_out_mlp(self, l):
        nc, S = self.nc, self.S
        last = (l == DEPTH - 1)
        src = self.I('xin') if l == 0 else self.xres
        groups = GROUPS[:8] if last else GROUPS
        with ExitStack() as es:
            sb = lambda n, shp, dt=F32: es.enter_context(nc.sbuf_tensor(self.uq(n), shp, dt))
            pst = lambda n, shp, dt=F32: es.enter_context(nc.psum_tensor(self.uq(n), shp, dt))
            wout = sb('wout', [128, 8, D], BF16)
            S.dma('pool', wout[:], self.I('w_out')[l].rearrange("(k p) n -> p k n", p=128), w=['wout'])
            idb = sb('idb', [128, 128], BF16); S.dma('sp', idb[:], self.I('ident_bf'), w=['idb'])
            eps = sb('eps', [128, 1]); S.op('dve', 'memset', eps[:], EPS, w=['eps'])
            g1r = sb('g1r', [128, D])
            mixt = [sb('mixt%d' % i, [128, 8, 512], BF16) for i in range(2)]
            xt = [sb('xt%d' % i, [128, 4, D]) for i in range(2)]
            tmp = [sb('tmp%d' % i, [128, 512]) for i in range(2)]
            junk = sb('junk', [128, D], BF16)
            pt = [pst('pt%d' % i, [128, 512], BF16) for i in range(2)]
            tls = [dict(k='n%d' % i, ss=sb('ss%d' % i, [128, 4]), rstd=sb('rstd%d' % i, [128, 4]),
                        xn=sb('xn%d' % i, [128, 4, D], BF16), junk=junk, pt=pt,
                        hT=sb('hT%d' % i, [128, 8, 512], BF16), idb=idb, eps=eps) for i in range(2)]
            pso = [pst('pso%d' % i, [128, 512]) for i in range(4)]
            st = dict(cur_s=-1, np_=0)

            def stA(gi):
                t0, n, s = groups[gi]
                b = gi % 2; nj = n // 128
                if s != st['cur_s']:
                    S.dma('sp', g1r[:], self.mrow[s, 2 * D:3 * D].partition_broadcast(128), w=['g1r']); st['cur_s'] = s
                S.dma('sp', mixt[b][:, :, 0:n], self.mixT[:, :, t0:t0 + n].rearrange("c p t -> p c t"), w=[('mixt', b)])
                S.dma('sp', xt[b][:, 0:nj, :], src[t0:t0 + n, :].rearrange("(j p) d -> p j d", p=128), w=[('xt', b)])
                for j in range(nj):
                    for hh in range(2):
                        pi = st['np_'] % 4; st['np_'] += 1
                        S.multi('pe', [('matmul', (pso[pi][:, :], mixt[b][:, c, j * 128:(j + 1) * 128], wout[:, c, hh * 512:(hh + 1) * 512]),
                                        dict(start=(c == 0), stop=(c == 7))) for c in range(8)],
                                r=[('mixt', b), 'wout'], w=[('pso', pi)])
                        S.op('dve', 'tensor_tensor', tmp[pi % 2][:, :], pso[pi][:, :], g1r[:, hh * 512:(hh + 1) * 512], ALU.mult,
                             r=[('pso', pi), 'g1r'], w=[('tmp', pi % 2)])
                        S.op('pool', 'tensor_tensor', xt[b][:, j, hh * 512:(hh + 1) * 512], tmp[pi % 2][:, :],
                             xt[b][:, j, hh * 512:(hh + 1) * 512], ALU.add, r=[('tmp', pi % 2), ('xt', b)], w=[('xt', b)])
                S.dma('sp', self.xmid[t0:t0 + n, :].rearrange("(j p) d -> p j d", p=128), xt[b][:, 0:nj, :], r=[('xt', b)])

            def stB(gi):
                t0, n, s = groups[gi]
                b = gi % 2; nj = n // 128
                self.norm_tiles(xt[b], nj, s, 2, tls[b], ('xt', b), n)
                S.dma('sp', self.h2T[:, :, t0:t0 + n].rearrange("c p t -> p c t"), tls[b]['hT'][:, :, 0:n], r=['n%dhT' % b])

            stA(0)
            for gi in range(len(groups)):
                if gi + 1 < len(groups):
                    stA(gi + 1)
                stB(gi)
            S.flush()
        with ExitStack() as es:
            sb = lambda n, shp, dt=F32: es.enter_context(nc.sbuf_tensor(self.uq(n), shp, dt))
            pst = lambda n, shp, dt=F32: es.enter_context(nc.psum_tensor(self.uq(n), shp, dt))
            w1 = sb('w1', [128, 8, 4 * D], BF16); w2 = sb('w2', [128, 32, D], BF16)
            for hh in range(2):
                S.dma('pool', w1[:, :, hh * 2048:(hh + 1) * 2048],
                      self.I('w_mlp1')[l].rearrange("(k p) n -> p k n", p=128)[:, :, hh * 2048:(hh + 1) * 2048], w=[('w1', hh)])
            for q in range(4):
                S.dma('pool', w2[:, q * 8:(q + 1) * 8, :],
                      self.I('w_mlp2')[l].rearrange("(k p) n -> p k n", p=128)[:, q * 8:(q + 1) * 8, :], w=[('w2', q)])
            g2r = sb('g2r', [128, D])
            if last:
                fg = sb('fg', [128, D]); S.dma('sp', fg[:], self.I('fgrow'), w=['fg'])
                eps = sb('eps', [128, 1]); S.op('dve', 'memset', eps[:], EPS, w=['eps'])
                junk = sb('junk', [128, D], BF16); fss = sb('fss', [128, 4]); frs = sb('frs', [128, 4])
            hT = sb('hT', [128, 8, 512], BF16); fT = sb('fT', [128, 32, 512], BF16)
            x1 = sb('x1', [128, 4, D]); sq = [sb('sq%d' % i, [128, 512]) for i in range(2)]
            tmp = [sb('tmp%d' % i, [128, 512]) for i in range(2)]
            psf = [pst('psf%d' % i, [128, 512]) for i in range(3)]
            ps2 = [pst('ps2%d' % i, [128, 512]) for i in range(3)]
            cur_s = -1; n2 = 0
            t0_, n_, s_ = groups[0]
            S.dma('sp', hT[:, :, 0:n_], self.h2T[:, :, t0_:t0_ + n_].rearrange("c p t -> p c t"), w=['hT'])
            for gi, (t0, n, s) in enumerate(groups):
                nj = n // 128
                if s != cur_s:
                    S.dma('sp', g2r[:], self.mrow[s, 5 * D:6 * D].partition_broadcast(128), w=['g2r']); cur_s = s
                S.dma('sp', x1[:, 0:nj, :], self.xmid[t0:t0 + n, :].rearrange("(j p) d -> p j d", p=128),
                      w=[('x1', j) for j in range(nj)])
                for j in range(32):
                    pf = psf[j % 3]
                    S.multi('pe', [('matmul', (pf[:, 0:n], w1[:, k, j * 128:(j + 1) * 128], hT[:, k, 0:n]),
                                    dict(start=(k == 0), stop=(k == 7))) for k in range(8)],
                            r=[('w1', j // 16), 'hT'], w=[('psf', j % 3)])
                    S.op('act', 'activation', sq[j % 2][:, 0:n], pf[:, 0:n], AF.Square, r=[('psf', j % 3)], w=[('sq', j % 2)])
                    S.op('dve', 'scalar_tensor_tensor', fT[:, j, 0:n], pf[:, 0:n], 0.0, sq[j % 2][:, 0:n],
                         ALU.is_gt, ALU.mult, r=[('psf', j % 3), ('sq', j % 2)], w=[('fT', j)])
                if gi + 1 < len(groups):
                    t0n, nn, sn_ = groups[gi + 1]
                    S.dma('sp', hT[:, :, 0:nn], self.h2T[:, :, t0n:t0n + nn].rearrange("c p t -> p c t"), w=['hT'])
                for j in range(nj):
                    for hh in range(2):
                        pi = n2 % 3; n2 += 1
                        S.multi('pe', [('matmul', (ps2[pi][:, :], fT[:, q, j * 128:(j + 1) * 128], w2[:, q, hh * 512:(hh + 1) * 512]),
                                        dict(start=(q == 0), stop=(q == 31))) for q in range(32)],
                                r=[('fT', q) for q in range(32)] + [('w2', q) for q in range(4)], w=[('ps2', pi)])
                        S.op('dve', 'tensor_tensor', tmp[pi % 2][:, :], ps2[pi][:, :], g2r[:, hh * 512:(hh + 1) * 512], ALU.mult,
                             r=[('ps2', pi), 'g2r'], w=[('tmp', pi % 2)])
                        S.op('pool', 'tensor_tensor', x1[:, j, hh * 512:(hh + 1) * 512], tmp[pi % 2][:, :],
                             x1[:, j, hh * 512:(hh + 1) * 512], ALU.add, r=[('tmp', pi % 2), ('x1', j)], w=[('x1', j)])
                    tt = t0 + j * 128
                    if not last:
                        S.dma('sp', self.xres[tt:tt + 128, :], x1[:, j, :], r=[('x1', j)])
                    else:
                        S.op('act', 'activation', junk[:], x1[:, j, :], AF.Square, accum_out=fss[:, j:j + 1],
                             r=[('x1', j)], w=['junk', ('fss', j)])
                        S.op('act', 'activation', frs[:, j:j + 1], fss[:, j:j + 1], AF.Sqrt, bias=eps[:, 0:1], scale=1.0 / D,
                             r=[('fss', j), 'eps'], w=[('frs', j)])
                        S.op('dve', 'reciprocal', frs[:, j:j + 1], frs[:, j:j + 1], r=[('frs', j)], w=[('frs', j)])
                        S.op('dve', 'scalar_tensor_tensor', x1[:, j, :], x1[:, j, :], frs[:, j:j + 1], fg[:], ALU.mult, ALU.mult,
                             r=[('x1', j), ('frs', j), 'fg'], w=[('x1', j)])
                        S.dma('sp', self.y[tt:tt + 128, :], x1[:, j, :], r=[('x1', j)])
            S.flush()

    def p3_fnet(self, l):
        for nm, t0 in (('x', 0), ('c', L)):
            if nm == 'c' and l == DEPTH - 1:
                continue
            self._fnet(l, nm, t0)

    def _fnet(self, l, nm, t0):
        nc, S = self.nc, self.S
        L1 = self.fn[nm]['L1']; Ls = self.fn[nm]['Ls']; L2 = 64
        with ExitStack() as es:
            sb = lambda n, shp, dt=F32: es.enter_context(nc.sbuf_tensor(self.uq(n), shp, dt))
            pst = lambda n, shp, dt=F32: es.enter_context(nc.psum_tensor(self.uq(n), shp, dt))
            uf = sb('uf', [128, 2, Ls], BF16)
            S.dma('sp', uf[:], self.pxT[5:7, :, t0:t0 + Ls].rearrange("c p t -> p c t"), w=['uf'])
            csw = sb('csw', [128, 256], BF16); S.dma('sp', csw[:], self.I('fn_csw'), w=['csw'])
            w1 = sb('w1', [L1, 2 * L1], BF16); w2 = sb('w2', [L1, 2 * L1], BF16)
            S.dma('sp', w1[:], self.I('fn_w1' + nm), w=['w1']); S.dma('sp', w2[:], self.I('fn_w2' + nm), w=['w2'])
            tr = sb('tr', [64, Ls], BF16); sn = sb('sn', [64, Ls], BF16)
            S.dma('sp', tr[:], self.I('fn_tr' + nm), w=['tr']); S.dma('sp', sn[:], self.I('fn_sn' + nm), w=['sn'])
            U = sb('U', [L1, L2, 512], BF16)
            Y = sb('Y', [64, 2, L1, 256], BF16)
            pu = [pst('pu%d' % i, [128, 512]) for i in range(2)]
            py = [pst('py%d' % i, [128, 512]) for i in range(2)]
            pf = [pst('pf%d' % i, [128, 512]) for i in range(2)]
            for l2 in range(L2):
                b = l2 % 2
                S.multi('pe', [('matmul', (pu[b][0:L1, c * 256:(c + 1) * 256], uf[:, c, l2::L2], csw[:, :]),
                                dict(start=True, stop=True)) for c in range(2)], r=['uf', 'csw'], w=[('pu', b)])
                if b == 0:
                    S.op('act', 'activation', U[:, l2, :], pu[b][0:L1, :], AF.Copy, r=[('pu', b)], w=['U'])
                else:
                    S.op('dve', 'tensor_copy', U[:, l2, :], pu[b][0:L1, :], r=[('pu', b)], w=['U'])
            cpb = 512 // (2 * L1)
            nb = 0
            for ch0 in range(0, 256, cpb):
                b = nb % 2; nb += 1
                calls = []
                for chl in range(cpb):
                    ch = ch0 + chl; c = ch // 128; i = ch % 128
                    o = py[b][0:64, chl * 2 * L1:(chl + 1) * 2 * L1]
                    calls.append(('matmul', (o, U[:, :, c * 256 + i], w1[:, :]), dict(start=True, stop=False)))
                    calls.append(('matmul', (o, U[:, :, c * 256 + 128 + i], w2[:, :]), dict(start=False, stop=True)))
                S.multi('pe', calls, r=['U', 'w1', 'w2'], w=[('py', b)])
                for r_ in range(2):
                    src = py[b][0:64, 0:cpb * 2 * L1].rearrange("p (c r a) -> p c r a", c=cpb, r=2)[:, :, r_, :]
                    dst = Y[:, r_, :, ch0:ch0 + cpb].rearrange("p a c -> p c a")
                    if r_ == 0:
                        S.op('act', 'activation', dst, src, AF.Copy, r=[('py', b)], w=['Y'])
                    else:
                        S.op('dve', 'tensor_copy', dst, src, r=[('py', b)], w=['Y'])
            na = min(8, L1)
            nb = 0
            for cc in range(2):
                for a0 in range(0, L1, na):
                    b = nb % 2; nb += 1
                    calls = []
                    for al in range(na):
                        a_ = a0 + al
                        o = pf[b][:, al * 64:(al + 1) * 64]
                        calls.append(('matmul', (o, Y[:, 0, a_, cc * 128:(cc + 1) * 128], tr[:, a_::L1]), dict(start=True, stop=False)))
                        calls.append(('matmul', (o, Y[:, 1, a_, cc * 128:(cc + 1) * 128], sn[:, a_::L1]), dict(start=False, stop=True)))
                    S.multi('pe', calls, r=['Y', 'tr', 'sn'], w=[('pf', b)])
                    dst = uf[:, cc, :].rearrange("p (b a) -> p a b", a=L1)[:, a0:a0 + na, :]
                    src = pf[b][:, 0:na * 64].rearrange("p (a b) -> p a b", a=na)
                    if b == 0:
                        S.op('act', 'activation', dst, src, AF.Copy, r=[('pf', b), 'U'], w=['uf'])
                    else:
                        S.op('dve', 'tensor_copy', dst, src, r=[('pf', b), 'U'], w=['uf'])
            S.dma('sp', self.mixT[4:6, :, t0:t0 + Ls].rearrange("c p t -> p c t"), uf[:], r=['uf'])
            S.flush()

    def _hy_f1(self, src, krows, N1, w1h, hyY, es_sb, pst):
        S = self.S
        cpb = 512 // (2 * N1)
        py = [pst('hpy%d' % i, [128, 512]) for i in range(2)]
        nbank = 256 // cpb
        GB = min(4, nbank)
        stg = [es_sb('hstg%d' % i, [128, GB, 512], BF16) for i in range(2)]
        for nb in range(nbank):
            ch0 = nb * cpb
            b = nb % 2; sb_ = (nb // GB) % 2; g = nb % GB
            S.multi('pe', [('matmul', (py[b][:, chl * 2 * N1:(chl + 1) * 2 * N1], src[0:krows, :, ch0 + chl], w1h[0:krows, :]),
                            dict(start=True, stop=True)) for chl in range(cpb)], r=['hsrc', 'w1h'], w=[('hpy', b)])
            if b == 0:
                S.op('act', 'activation', stg[sb_][:, g, :], py[b][:, :], AF.Copy, r=[('hpy', b)], w=[('hstg', sb_)])
            else:
                S.op('dve', 'tensor_copy', stg[sb_][:, g, :], py[b][:, :], r=[('hpy', b)], w=[('hstg', sb_)])
            if g == GB - 1:
                c0 = (nb - GB + 1) * cpb
                S.dma('sp', hyY[:, c0:c0 + GB * cpb, :].rearrange("p c x -> p (c x)"),
                      stg[sb_][:, :, :].rearrange("p g x -> p (g x)"), r=[('hstg', sb_)])

    def _hy_f2(self, Yz, ncol, N1, tr, ti, nti, pz, f1):
        S = self.S
        calls = [('matmul', (pz[:, 0:ncol], tr[:, f1::N1], Yz[:, :, f1]), dict(start=True, stop=False)),
                 ('matmul', (pz[:, 0:ncol], nti[:, f1::N1], Yz[:, :, N1 + f1]), dict(start=False, stop=True)),
                 ('matmul', (pz[:, ncol:2 * ncol], ti[:, f1::N1], Yz[:, :, f1]), dict(start=True, stop=False)),
                 ('matmul', (pz[:, ncol:2 * ncol], tr[:, f1::N1], Yz[:, :, N1 + f1]), dict(start=False, stop=True))]
        return calls

    def _range_sin(self, dst, ps, fcol, fbcol, tl, n):
        S = self.S
        a, t, k = tl['a'], tl['t'], tl['k']
        S.op('dve', 'tensor_scalar', a[:, 0:n], ps, fcol, fbcol, ALU.mult, ALU.add, r=[tl['ps'], 'fcols'], w=['rs_a'])
        S.op('dve', 'tensor_scalar', t[:, 0:n], a[:, 0:n], 1.0 / (2 * math.pi), 12582912.0, ALU.mult, ALU.add, r=['rs_a'], w=['rs_t'])
        S.op('dve', 'tensor_scalar', k[:, 0:n], t[:, 0:n], -12582912.0, None, ALU.add, r=['rs_t'], w=['rs_k'])
        S.op('dve', 'scalar_tensor_tensor', a[:, 0:n], k[:, 0:n], -2 * math.pi, a[:, 0:n], ALU.mult, ALU.add, r=['rs_k', 'rs_a'], w=['rs_a'])
        S.op('dve', 'tensor_scalar', a[:, 0:n], a[:, 0:n], -3.14159, 3.14159, ALU.max, ALU.min, r=['rs_a'], w=['rs_a'])
        S.op('act', 'activation', dst, a[:, 0:n], AF.Sin, r=['rs_a'], w=[tl['dst']])

    def pk_filters(self, l):
        for nm in ('x', 'c'):
            if nm == 'c' and l == DEPTH - 1:
                continue
            self._filters(l, nm)

    def _filters(self, l, nm):
        nc, S = self.nc, self.S
        hy = self.hy[nm]; N1 = hy['N1']; Ls = hy['Ls']; N = hy['N']
        hyY = self.hyY[nm]
        with ExitStack() as es:
            sb = lambda n, shp, dt=F32: es.enter_context(nc.sbuf_tensor(self.uq(n), shp, dt))
            pst = lambda n, shp, dt=F32: es.enter_context(nc.psum_tensor(self.uq(n), shp, dt))
            ks = sb('ks', [N1, 128, 256], BF16)
            w1h = sb('w1h', [N1, 2 * N1], BF16); S.dma('sp', w1h[:], self.I('hy_w1' + nm), w=['w1h'])
            with ExitStack() as es2:
                sb2 = lambda n, shp, dt=F32: es2.enter_context(nc.sbuf_tensor(self.uq(n), shp, dt))
                ps2 = lambda n, shp, dt=F32: es2.enter_context(nc.psum_tensor(self.uq(n), shp, dt))
                wa = sb2('wa', [33, 64]); wb = sb2('wb', [64, 64]); wc = sb2('wc', [64, 512], BF16)
                S.dma('sp', wa[:], self.I('hy_w1')[l], w=['wa']); S.dma('sp', wb[:], self.I('hy_w2')[l], w=['wb'])
                S.dma('pool', wc[:], self.I('hy_w3')[l], w=['wc'])
                fc = sb2('fc', [64, 4])
                S.op('dve', 'tensor_copy', fc[:, 0:1], self.cols[0:64, C_HFR:C_HFR + 1], r=['cols'], w=['fcols'])
                S.op('dve', 'tensor_tensor', fc[:, 1:2], self.cols[0:64, C_HFR:C_HFR + 1], self.cols[0:64, C_HB1:C_HB1 + 1], ALU.mult, r=['cols'], w=['fcols'])
                S.op('dve', 'tensor_tensor', fc[:, 2:3], self.cols[0:64, C_HFR:C_HFR + 1], self.cols[0:64, C_HB2:C_HB2 + 1], ALU.mult, r=['cols'], w=['fcols'])
                zt = [sb2('zt%d' % i, [33, 512]) for i in range(2)]
                h1 = sb2('h1', [64, 512])
                hA = sb2('hA', [64, N], BF16); hB = sb2('hB', [64, N], BF16)
                tl = dict(a=sb2('rsa', [64, 512]), t=sb2('rst', [64, 512]), k=sb2('rsk', [64, 512]))
                tcol = sb2('tcol', [N1, 128]); S.dma('sp', tcol[:], self.I('hy_tcol' + nm), w=['tcol'])
                drow = sb2('drow', [128, 256]); S.dma('sp', drow[:], self.I('hy_drow' + nm), w=['drow'])
                dec = [sb2('dec%d' % i, [N1, 256]) for i in range(2)]
                kabs = sb2('kabs', [N1, 32 * 256], BF16); red = sb2('red', [N1, 4, 256])
                onf = sb2('onf', [128, 1]); S.op('dve', 'memset', onf[:], 1.0, w=['onf'])
                sres = sb2('sres', [128, 2])
                pa = ps2('pa', [64, 512]); pb_ = ps2('pb', [64, 512])
                pk = [ps2('pk%d' % i, [128, 512]) for i in range(2)]
                pS = ps2('pS', [128, 2])
                nblk = N // 512
                for blk in range(nblk):
                    b = blk % 2
                    S.dma('sp', zt[b][:], self.I('hy_zT' + nm)[:, blk * 512:(blk + 1) * 512], w=[('zt', b)])
                    S.op('pe', 'matmul', pa[:, :], wa[:, :], zt[b][:, :], start=True, stop=True, r=['wa', ('zt', b)], w=['pa'])
                    tl.update(ps='pa', dst='h1')
                    self._range_sin(h1[:, :], pa[:, :], fc[:, 0:1], fc[:, 1:2], tl, 512)
                    S.op('pe', 'matmul', pb_[:, :], wb[:, :], h1[:, :], start=True, stop=True, r=['wb', 'h1'], w=['pb'])
                    tl.update(ps='pb', dst='hA')
                    self._range_sin(hA[:, blk * 512:(blk + 1) * 512], pb_[:, :], fc[:, 0:1], fc[:, 2:3], tl, 512)
                S.op('pool', 'tensor_copy', hB[:, Ls:N], hA[:, Ls:N], r=['hA'], w=['hB'])
                S.op('pool', 'memset', hB[:, 0:Ls], 0.0, w=['hB'])
                S.op('pool', 'memset', hA[:, Ls:N], 0.0, r=['hB'], w=['hA'])
                for s2 in range(128):
                    b = s2 % 2
                    S.multi('pe', [('matmul', (pk[b][0:N1, 0:256], hA[:, s2::128], wc[:, 0:256]), dict(start=True, stop=False)),
                                   ('matmul', (pk[b][0:N1, 0:256], hB[:, s2::128], wc[:, 256:512]), dict(start=False, stop=True))],
                            r=['hA', 'hB', 'wc'], w=[('pk', b)])
                    S.op('act', 'activation', dec[b][:, :], drow[0:N1, :], AF.Exp, scale=tcol[:, s2:s2 + 1],
                         r=['drow', 'tcol'], w=[('dec', b)])
                    S.op('dve', 'tensor_tensor', ks[:, s2, :], pk[b][0:N1, 0:256], dec[b][:, :], ALU.mult,
                         r=[('pk', b), ('dec', b)], w=['hsrc'])
                for q in range(4):
                    S.op('act', 'activation', kabs[:, :], ks[:, q * 32:(q + 1) * 32, :].rearrange("p s c -> p (s c)"), AF.Abs,
                         r=['hsrc'], w=['kabs'])
                    S.op('dve', 'tensor_reduce', red[:, q, :], kabs[:, :].rearrange("p (s c) -> p c s", c=256),
                         mybir.AxisListType.X, ALU.add, r=['kabs'], w=['red'])
                for cc in range(2):
                    S.multi('pe', [('matmul', (pS[:, cc:cc + 1], red[:, q, cc * 128:(cc + 1) * 128], onf[0:N1, 0:1]),
                                    dict(start=(q == 0), stop=(q == 3))) for q in range(4)], r=['red', 'onf'], w=['pS'])
                S.op('dve', 'tensor_copy', sres[:, :], pS[:, :], r=['pS'], w=['sres'])
                S.dma('sp', hy['ssum'], sres[:, :], r=['sres'])
                self._hy_f1(ks, N1, N1, w1h, hyY, sb2, ps2)
                S.flush()
        with ExitStack() as es:
            sb = lambda n, shp, dt=F32: es.enter_context(nc.sbuf_tensor(self.uq(n), shp, dt))
            pst = lambda n, shp, dt=F32: es.enter_context(nc.psum_tensor(self.uq(n), shp, dt))
            Yz = sb('Yz', [128, 256, 2 * N1], BF16); S.dma('sp', Yz[:], hyY, w=['Yz'])
            tr = sb('tr', [128, N], BF16); ti = sb('ti', [128, N], BF16); nti = sb('nti', [128, N], BF16)
            S.dma('sp', tr[:], self.I('hy_tr' + nm), w=['tr']); S.dma('sp', ti[:], self.I('hy_ti' + nm), w=['ti'])
            S.dma('sp', nti[:], self.I('hy_nti' + nm), w=['nti'])
            FB = min(8, N1)
            kst = [sb('kst%d' % i, [128, FB, 512], BF16) for i in range(2)]
            pz = [pst('pz%d' % i, [128, 512]) for i in range(2)]
            for f1 in range(N1):
                b = f1 % 2; kb = (f1 // FB) % 2
                S.multi('pe', self._hy_f2(Yz, 256, N1, tr, ti, nti, pz[b], f1), r=['Yz', 'tr', 'ti', 'nti'], w=[('pz', b)])
                if b == 0:
                    S.op('act', 'activation', kst[kb][:, f1 % FB, :], pz[b][:, :], AF.Copy, r=[('pz', b)], w=[('kst', kb)])
                else:
                    S.op('dve', 'tensor_copy', kst[kb][:, f1 % FB, :], pz[b][:, :], r=[('pz', b)], w=[('kst', kb)])
                if f1 % FB == FB - 1:
                    f0 = f1 - FB + 1
                    S.dma('sp', hy['kf'][:, f0:f0 + FB, :, :].rearrange("p f r c -> p (f r c)"),
                          kst[kb][:, :, :].rearrange("p f x -> p (f x)"), r=[('kst', kb)])
            S.flush()

    def p4_hyena(self, l):
        for nm, t0 in (('x', 0), ('c', L)):
            if nm == 'c' and l == DEPTH - 1:
                continue
            self._hyena(l, nm, t0)

    def _hyena(self, l, nm, t0):
        nc, S = self.nc, self.S
        hy = self.hy[nm]; N1 = hy['N1']; Ls = hy['Ls']; N = hy['N']; K1 = N1 // 2
        hyY = self.hyY[nm]; hyF = self.hyF[nm]
        nblk = max(1, Ls // 512); bw = Ls // nblk
        with ExitStack() as es:
            sb = lambda n, shp, dt=F32: es.enter_context(nc.sbuf_tensor(self.uq(n), shp, dt))
            pst = lambda n, shp, dt=F32: es.enter_context(nc.psum_tensor(self.uq(n), shp, dt))
            uh = sb('uh', [128, 6, Ls], BF16)
            S.dma('sp', uh[:], self.pxT[7:13, :, t0:t0 + Ls].rearrange("c p t -> p c t"), w=['uh'])
            idb = sb('idb', [128, 128], BF16); S.dma('sp', idb[:], self.I('ident_bf'), w=['idb'])
            w1h = sb('w1h', [N1, 2 * N1], BF16); S.dma('sp', w1h[:], self.I('hy_w1' + nm), w=['w1h'])
            zT = sb('zT', [128, 2, Ls], BF16); x0 = sb('x0', [128, 2, Ls], BF16)
            o = [sb('o%d' % c, [128, bw]) for c in range(6)]
            zs = sb('zs', [K1, 128, 256], BF16)
            cw = lambda tap, c: self.cols[:, C_HCW + tap * 6 + c:C_HCW + tap * 6 + c + 1]
            for blk in range(nblk):
                c0 = blk * bw
                for c in range(6):
                    S.op('act', 'activation', o[c][:, :], uh[:, c, c0:c0 + bw], AF.Identity,
                         bias=self.cols[:, C_HCB + c:C_HCB + c + 1], scale=cw(1, c), r=['uh', 'cols'], w=[('o', c)])
                    lo = 1 if blk == 0 else 0
                    S.op('dve', 'scalar_tensor_tensor', o[c][:, lo:bw], uh[:, c, c0 + lo - 1:c0 + bw - 1], cw(0, c), o[c][:, lo:bw],
                         ALU.mult, ALU.add, r=['uh', 'cols', ('o', c)], w=[('o', c)])
                    hi = bw - 1 if blk == nblk - 1 else bw
                    S.op('dve', 'scalar_tensor_tensor', o[c][:, 0:hi], uh[:, c, c0 + 1:c0 + hi + 1], cw(2, c), o[c][:, 0:hi],
                         ALU.mult, ALU.add, r=['uh', 'cols', ('o', c)], w=[('o', c)])
                for cc in range(2):
                    S.op('pool', 'tensor_copy', x0[:, cc, c0:c0 + bw], o[cc][:, :], r=[('o', cc)], w=['x0'])
                    S.op('pool', 'tensor_tensor', zT[:, cc, c0:c0 + bw], o[2 + cc][:, :], o[4 + cc][:, :], ALU.mult,
                         r=[('o', 2 + cc), ('o', 4 + cc)], w=['zT'])
            S.dma('sp', self.hx0[:, :, t0:t0 + Ls].rearrange("c p t -> p c t"), x0[:], r=['x0'])
            S.dma('sp', self.hz[:, :, t0:t0 + Ls].rearrange("c p t -> p c t"), zT[:], r=['zT'])
            ptr = [pst('ptr%d' % i, [128, 1024], BF16) for i in range(2)]
            nb = 0
            for s20 in range(0, 128, 4):
                b = nb % 2; nb += 1
                calls = []
                for sl in range(4):
                    for cc in range(2):
                        calls.append(('transpose', (ptr[b][0:K1, (sl * 2 + cc) * 128:(sl * 2 + cc + 1) * 128],
                                                    zT[:, cc, s20 + sl::128], idb[:, :]), {}))
                S.multi('pe', calls, r=['zT', 'idb'], w=[('ptr', b)])
                dst = zs[:, s20:s20 + 4, :].rearrange("p a c -> p (a c)")
                if b == 0:
                    S.op('act', 'activation', dst, ptr[b][0:K1, :], AF.Copy, r=[('ptr', b)], w=['hsrc'])
                else:
                    S.op('dve', 'tensor_copy', dst, ptr[b][0:K1, :], r=[('ptr', b)], w=['hsrc'])
            self._hy_f1(zs, K1, N1, w1h, hyY, sb, pst)
            S.flush()
        for cc in range(2):
            with ExitStack() as es:
                sb = lambda n, shp, dt=F32: es.enter_context(nc.sbuf_tensor(self.uq(n), shp, dt))
                pst = lambda n, shp, dt=F32: es.enter_context(nc.psum_tensor(self.uq(n), shp, dt))
                Yz = sb('Yz', [128, 128, 2 * N1], BF16); S.dma('sp', Yz[:], hyY[:, cc * 128:(cc + 1) * 128, :], w=['Yz'])
                tr = sb('tr', [128, N], BF16); ti = sb('ti', [128, N], BF16); nti = sb('nti', [128, N], BF16)
                S.dma('sp', tr[:], self.I('hy_tr' + nm), w=['tr']); S.dma('sp', ti[:], self.I('hy_ti' + nm), w=['ti'])
                S.dma('sp', nti[:], self.I('hy_nti' + nm), w=['nti'])
                kf = sb('kf', [128, N1, 2, 256], BF16); S.dma('sp', kf[:], hy['kf'], w=['kf'])
                Yf = sb('Yf', [128, N1, 2, 128], BF16)
                FB = min(4, N1)
                ta = [sb('ta%d' % i, [128, FB, 2, 128]) for i in range(2)]; tb = [sb('tb%d' % i, [128, FB, 2, 128]) for i in range(2)]
                pz = [pst('pz%d' % i, [128, FB * 256]) for i in range(2)]
                ccs = slice(cc * 128, (cc + 1) * 128)
                for st_ in range(N1 // FB):
                    f0 = st_ * FB; b = st_ % 2
                    calls = []
                    for fl in range(FB):
                        calls += self._hy_f2(Yz, 128, N1, tr, ti, nti, pz[b][:, fl * 256:(fl + 1) * 256], f0 + fl)
                    S.multi('pe', calls, r=['Yz', 'tr', 'ti', 'nti'], w=[('pz', b)])
                    pzv = pz[b][:, :].rearrange("p (f r c) -> p f r c", f=FB, r=2)
                    S.op('dve', 'tensor_tensor', ta[b][:, :, :, :], pzv, kf[:, f0:f0 + FB, :, ccs], ALU.mult, r=[('pz', b), 'kf'], w=[('ta', b)])
                    S.op('dve', 'tensor_tensor', tb[b][:, :, 0, :], pzv[:, :, 0, :], kf[:, f0:f0 + FB, 1, ccs], ALU.mult, r=[('pz', b), 'kf'], w=[('tb', b)])
                    S.op('dve', 'tensor_tensor', tb[b][:, :, 1, :], pzv[:, :, 1, :], kf[:, f0:f0 + FB, 0, ccs], ALU.mult, r=[('pz', b), 'kf'], w=[('tb', b)])
                    S.op('pool', 'tensor_tensor', Yf[:, f0:f0 + FB, 0, :], ta[b][:, :, 0, :], ta[b][:, :, 1, :], ALU.subtract, r=[('ta', b)], w=['Yf'])
                    S.op('pool', 'tensor_tensor', Yf[:, f0:f0 + FB, 1, :], tb[b][:, :, 0, :], tb[b][:, :, 1, :], ALU.add, r=[('tb', b)], w=['Yf'])
                S.dma('sp', hyF, Yf[:], r=['Yf'])
                S.flush()
            with ExitStack() as es:
                sb = lambda n, shp, dt=F32: es.enter_context(nc.sbuf_tensor(self.uq(n), shp, dt))
                pst = lambda n, shp, dt=F32: es.enter_context(nc.psum_tensor(self.uq(n), shp, dt))
                Yf = sb('Yf', [128, N1, 2, 128], BF16); S.dma('sp', Yf[:], hyF, w=['Yf'])
                i1a = sb('i1a', [128, 256], BF16); i1b = sb('i1b', [128, 256], BF16)
                S.dma('sp', i1a[:], self.I('hy_i1a' + nm), w=['i1a']); S.dma('sp', i1b[:], self.I('hy_i1b' + nm), w=['i1b'])
                er = sb('er', [N1, Ls], BF16); nei = sb('nei', [N1, Ls], BF16)
                S.dma('sp', er[:], self.I('hy_er' + nm), w=['er']); S.dma('sp', nei[:], self.I('hy_nei' + nm), w=['nei'])
                V = sb('V', [N1, 128, 2, 128], BF16)
                yT = sb('yT', [128, Ls]); x0h = sb('x0h', [128, Ls], BF16); zh = sb('zh', [128, Ls], BF16)
                S.dma('sp', x0h[:], self.hx0[cc, :, t0:t0 + Ls], w=['x0h']); S.dma('sp', zh[:], self.hz[cc, :, t0:t0 + Ls], w=['zh'])
                ssb = sb('ssb', [128, 2]); S.dma('sp', ssb[:], hy['ssum'], w=['ssb'])
                S.op('dve', 'reciprocal', ssb[:, :], ssb[:, :], r=['ssb'], w=['ssb'])
                res = sb('res', [128, Ls], BF16)
                pv = [pst('pv%d' % i, [128, 512]) for i in range(2)]
                py = [pst('py%d' % i, [128, 512]) for i in range(2)]
                nb = 0
                for ch0 in range(0, 128, 2):
                    b = nb % 2; nb += 1
                    calls = []
                    for chl in range(2):
                        o_ = pv[b][0:N1, chl * 256:(chl + 1) * 256]
                        calls.append(('matmul', (o_, Yf[:, :, 0, ch0 + chl], i1a[:, :]), dict(start=True, stop=False)))
                        calls.append(('matmul', (o_, Yf[:, :, 1, ch0 + chl], i1b[:, :]), dict(start=False, stop=True)))
                    S.multi('pe', calls, r=['Yf', 'i1a', 'i1b'], w=[('pv', b)])
                    dst = V[:, ch0:ch0 + 2, :, :].rearrange("p c r a -> p (c r a)")
                    if b == 0:
                        S.op('act', 'activation', dst, pv[b][0:N1, :], AF.Copy, r=[('pv', b)], w=['V'])
                    else:
                        S.op('dve', 'tensor_copy', dst, pv[b][0:N1, :], r=[('pv', b)], w=['V'])
                nta = min(128, 512 // K1)
                nb = 0
                for ta0 in range(0, 128, nta):
                    b = nb % 2; nb += 1
                    calls = []
                    for al in range(nta):
                        ta_ = ta0 + al
                        o_ = py[b][:, al * K1:(al + 1) * K1]
                        calls.append(('matmul', (o_, V[:, :, 0, ta_], er[:, ta_::128]), dict(start=True, stop=False)))
                        calls.append(('matmul', (o_, V[:, :, 1, ta_], nei[:, ta_::128]), dict(start=False, stop=True)))
                    S.multi('pe', calls, r=['V', 'er', 'nei'], w=[('py', b)])
                    dst = yT[:, :].rearrange("p (b a) -> p a b", a=128)[:, ta0:ta0 + nta, :]
                    src = py[b][:, 0:nta * K1].rearrange("p (a b) -> p a b", b=K1)
                    if b == 0:
                        S.op('act', 'activation', dst, src, AF.Copy, r=[('py', b)], w=['yT'])
                    else:
                        S.op('dve', 'tensor_copy', dst, src, r=[('py', b)], w=['yT'])
                S.op('act', 'activation', yT[:, :], yT[:, :], AF.Identity, scale=ssb[:, cc:cc + 1], r=['yT', 'ssb'], w=['yT'])
                S.op('dve', 'scalar_tensor_tensor', yT[:, :], zh[:, :], self.cols[:, C_HD + cc:C_HD + cc + 1], yT[:, :],
                     ALU.mult, ALU.add, r=['zh', 'yT', 'cols'], w=['yT'])
                S.op('dve', 'tensor_tensor', res[:, :], yT[:, :], x0h[:, :], ALU.mult, r=['yT', 'x0h'], w=['res'])
                S.dma('sp', self.mixT[6 + cc, :, t0:t0 + Ls], res[:, :], r=['res'])
                S.flush()


def build_program(debug=False, stop_after=None):
    P = Prog(debug=debug)
    steps = []
    for l in range(DEPTH):
        steps += [('p0', l), ('p1', l), ('pk', l), ('p2', l), ('p3', l), ('p4', l), ('p5', l)]
    for (nm, l) in steps:
        fn = getattr(P, {'p0': 'p0_mod', 'p1': 'p1_inproj', 'pk': 'pk_filters', 'p2': 'p2_attn',
                         'p3': 'p3_fnet', 'p4': 'p4_hyena', 'p5': 'p5_out_mlp'}[nm])
        fn(l)
        if stop_after is not None and (nm, l) == stop_after:
            break
    P.es.close()
    P.nc.used_inputs = list(P._in.keys())
    return P.nc


_PROG = None


def kernel(**inputs):
    global _PROG
    per = _prep(inputs)
    if _PROG is None:
        _PROG = build_program()
    per = [{k: d[k] for k in _PROG.used_inputs} for d in per]
    res = run_bass_kernel_spmd(_PROG, per, core_ids=list(range(8)))
    return np.stack([np.asarray(r['y'], np.float32) for r in res.results], axis=0)
```

```python
import math
import numpy as np
import ml_dtypes
from contextlib import ExitStack
import concourse.bass as bass
import concourse.mybir as mybir
from concourse.bass_utils import run_bass_kernel_spmd

F32 = mybir.dt.float32
BF16 = mybir.dt.bfloat16
AF = mybir.ActivationFunctionType
ALU = mybir.AluOpType

D = 1024
L = 4096
LC = 256
T = L + LC
NT = T // 128
DEPTH = 2
H = 8
OFF_Q, OFF_KV, OFF_KR, OFF_F, OFF_H, IN_W = 0, 384, 640, 672, 928, 1696
EPS = 1e-6
NCOL = 64
C_N1G, C_N2G, C_QG, C_KVG, C_HCW, C_HCB, C_HD, C_HB1, C_HFR, C_HB2 = 0, 8, 16, 19, 21, 39, 45, 47, 48, 49


class _Op:
    __slots__ = ('eng', 'calls', 'deps', 'signal', 'sem', 'val', 'dma', 'waits', 'pre')


class Sched:
    NDMA = 8
    ATTR = [('pe', 'tensor'), ('act', 'scalar'), ('dve', 'vector'), ('pool', 'gpsimd'), ('sp', 'sync')]

    def __init__(self, nc, es):
        self.nc = nc
        self.sem = {e: es.enter_context(nc.semaphore('s_' + e)) for e in ['pe', 'act', 'dve', 'pool']}
        self.dsem = {q: [es.enter_context(nc.semaphore('d_%s%d' % (q, i))) for i in range(self.NDMA)]
                     for q in ['sp', 'pool', 'act']}
        self.cnt = {e: 0 for e in self.sem}
        self.dcnt = {q: 0 for q in self.dsem}
        self._reset()

    def _reset(self):
        self.ops = {e: [] for e, _ in self.ATTR}
        self.last_w = {}
        self.readers = {}

    def multi(self, eng, calls, r=(), w=(), dma=False):
        op = _Op()
        op.eng = eng; op.calls = calls; op.dma = dma; op.signal = False; op.pre = None
        deps = []
        for k in r:
            d = self.last_w.get(k)
            if d is not None: deps.append(d)
        for k in w:
            d = self.last_w.get(k)
            if d is not None: deps.append(d)
            deps.extend(self.readers.get(k, ()))
        if eng == 'pe':
            deps = [d for d in deps if d.eng != 'pe' or d.dma]
        op.deps = [d for d in deps if d is not op]
        for d in op.deps: d.signal = True
        for k in r: self.readers.setdefault(k, []).append(op)
        for k in w:
            self.last_w[k] = op
            self.readers[k] = []
        self.ops[eng].append(op)
        return op

    def op(self, eng, name, *args, r=(), w=(), **kw):
        return self.multi(eng, [(name, args, kw)], r=r, w=w)

    def dma(self, q, out, in_, r=(), w=(), **kw):
        return self.multi(q, [('dma_start', (out, in_), kw)], r=r, w=w, dma=True)

    def flush(self):
        nc = self.nc
        for e, _ in self.ATTR:
            ops = self.ops[e]
            for op in reversed(ops):
                if not op.dma:
                    op.signal = True
                    break
            for op in ops:
                if op.dma:
                    j = self.dcnt[e]; self.dcnt[e] += 1
                    op.sem = self.dsem[e][j % self.NDMA]
                    op.val = 16 * (j // self.NDMA + 1)
                    op.pre = (op.sem, op.val - 16) if op.val > 16 else None
                elif op.signal:
                    self.cnt[e] += 1
                    op.sem = self.sem[e]; op.val = self.cnt[e]
        finals = {}
        for e, _ in self.ATTR:
            seen = {}
            for op in self.ops[e]:
                need = {}
                if op.pre is not None: need[id(op.pre[0])] = op.pre
                for d in op.deps:
                    cur = need.get(id(d.sem))
                    if cur is None or cur[1] < d.val: need[id(d.sem)] = (d.sem, d.val)
                op.waits = []
                for k, (s, v) in need.items():
                    if seen.get(k, -1) < v:
                        seen[k] = v; op.waits.append((s, v))
                if op.dma or op.signal:
                    cur = finals.get(id(op.sem))
                    if cur is None or cur[1] < op.val: finals[id(op.sem)] = (op.sem, op.val)
        with nc.Block() as blk:
            for e, attr in self.ATTR:
                ops = self.ops[e]
                if not ops and e != 'sp': continue

                def body(eng, ops=ops, e=e):
                    for op in ops:
                        for (s, v) in op.waits: eng.wait_ge(s, v)
                        inst = None
                        for (name, args, kw) in op.calls:
                            inst = getattr(eng, name)(*args, **kw)
                        if op.dma: inst.then_inc(op.sem, 16)
                        elif op.signal: inst.then_inc(op.sem, 1)
                    if e == 'sp':
                        for (s, v) in finals.values(): eng.wait_ge(s, v)
                getattr(blk, attr)(body)
        self._reset()


def _bf(a):
    return np.ascontiguousarray(a.astype(ml_dtypes.bfloat16))


def _f32(a):
    return np.ascontiguousarray(a.astype(np.float32))


ROPE_PERM = np.array(list(range(8, 16)) + list(range(0, 8)) + list(range(24, 32)) + list(range(16, 24)))


def _rope_tables():
    rows = L // 64
    row = np.repeat(np.arange(rows, dtype=np.float64), 64)
    col = np.tile(np.arange(64, dtype=np.float64), rows)
    inv = 10000.0 ** (-np.arange(0, 16, 2, dtype=np.float64) / 16)
    ar = row[None, :] * inv[:, None]
    ac = col[None, :] * inv[:, None]
    cos = np.ones((32, T)); sin = np.zeros((32, T))
    cos[0:8, :L] = np.cos(ar); cos[8:16, :L] = np.cos(ar); cos[16:24, :L] = np.cos(ac); cos[24:32, :L] = np.cos(ac)
    sin[0:8, :L] = -np.sin(ar); sin[8:16, :L] = np.sin(ar); sin[16:24, :L] = -np.sin(ac); sin[24:32, :L] = np.sin(ac)
    return _f32(cos), _f32(sin)


def _fnet_tables(Ls, L1, L2):
    w = np.arange(64)
    ph = 2 * np.pi * np.outer(w, w) / 64
    cw = np.zeros((128, 128)); sw = np.zeros((128, 128))
    for g in range(2):
        cw[g * 64:(g + 1) * 64, g * 64:(g + 1) * 64] = np.cos(ph)
        sw[g * 64:(g + 1) * 64, g * 64:(g + 1) * 64] = np.sin(ph)
    csw = np.concatenate([cw, sw], axis=1)
    a = np.arange(L1)
    p1 = 2 * np.pi * np.outer(a, a) / L1
    wr, wi = np.cos(p1), -np.sin(p1)
    w1 = np.concatenate([wr, wi], axis=1)
    w2 = np.concatenate([wi, -wr], axis=1)
    p2 = 2 * np.pi * np.outer(np.arange(L2), np.arange(Ls)) / Ls
    sc = 1.0 / math.sqrt(Ls * 64)
    return _bf(csw), _bf(w1), _bf(w2), _bf(np.cos(p2) * sc), _bf(np.sin(p2) * sc)


def _hyena_tables(Ls):
    N = 2 * Ls
    N1 = N // 128
    a = np.arange(N1)
    p1 = 2 * np.pi * np.outer(a, a) / N1
    w1 = np.concatenate([np.cos(p1), -np.sin(p1)], axis=1)
    p2 = 2 * np.pi * np.outer(np.arange(128), np.arange(N)) / N
    tr, ti, nti = np.cos(p2), -np.sin(p2), np.sin(p2)
    p3 = 2 * np.pi * np.outer(np.arange(128), np.arange(128)) / 128
    i1a = np.concatenate([np.cos(p3), np.sin(p3)], axis=1)
    i1b = np.concatenate([-np.sin(p3), np.cos(p3)], axis=1)
    p4 = 2 * np.pi * np.outer(np.arange(N1), np.arange(Ls)) / N
    er, nei = np.cos(p4) / N, -np.sin(p4) / N
    j = np.arange(N)
    tau = np.where(j < Ls, j, N - j).astype(np.float64)
    tau[Ls] = 0
    tl = np.linspace(0.0, 1.0, Ls)
    t = tl[tau.astype(np.int64)]
    fr = np.linspace(1e-4, 15.0, 16)
    ang = 2.0 * np.pi * tau[:, None] / Ls * fr[None, :]
    z = np.concatenate([t[:, None], np.cos(ang), -np.sin(ang)], axis=1)
    negt = -t.copy()
    negt[Ls] = -1e4
    tcol = negt.reshape(N1, 128)
    deltas = np.abs(np.linspace(math.log(1e-2) / 1.5, math.log(1e-2) / 0.3, 256))
    drow = np.broadcast_to(deltas[None, :], (128, 256))
    return dict(N1=N1, w1=_bf(w1), tr=_bf(tr), ti=_bf(ti), nti=_bf(nti), i1a=_bf(i1a), i1b=_bf(i1b),
                er=_bf(er), nei=_bf(nei), zT=_f32(z.T), tcol=_f32(tcol), drow=_f32(drow))


_CONST = None


def _consts():
    global _CONST
    if _CONST is not None:
        return _CONST
    c = {}
    c['ident_bf'] = _bf(np.eye(128))
    c['ident_f'] = _f32(np.eye(128))
    c['ones_f'] = _f32(np.ones((128, 128)))
    c['ropeC'], c['ropeS'] = _rope_tables()
    for nm, Ls, L1, L2 in (('x', L, 64, 64), ('c', LC, 4, 64)):
        csw, w1, w2, tr, sn = _fnet_tables(Ls, L1, L2)
        c['fn_csw'] = csw
        c['fn_w1' + nm], c['fn_w2' + nm], c['fn_tr' + nm], c['fn_sn' + nm] = w1, w2, tr, sn
        ht = _hyena_tables(Ls)
        for k, v in ht.items():
            if k != 'N1':
                c['hy_%s%s' % (k, nm)] = v
    _CONST = c
    return c


def _colpack(v, n):
    return np.asarray(v, np.float32).reshape(n, 128).T


def _prep(inp):
    c = dict(_consts())
    sh = {}
    w_in = np.asarray(inp['w_in'], np.float32)
    sh['w_inx'] = np.ascontiguousarray(np.concatenate([w_in, w_in[:, :, OFF_KR + ROPE_PERM]], axis=2))
    w_uq = np.asarray(inp['w_uq'], np.float32)
    sh['w_uq'] = np.ascontiguousarray(w_uq)
    wq = w_uq.reshape(DEPTH, 384, H, 96)
    sh['w_uqp'] = np.ascontiguousarray(
        np.concatenate([wq[..., :64], wq[..., 64:][..., ROPE_PERM]], axis=-1).reshape(DEPTH, 384, 768))
    wkv = np.asarray(inp['w_ukv'], np.float32).reshape(DEPTH, 256, H, 128)
    sh['w_ukvx'] = np.ascontiguousarray(
        np.concatenate([wkv[..., :64].reshape(DEPTH, 256, 512), wkv[..., 64:].reshape(DEPTH, 256, 512)], axis=2))
    for k in ('w_mod', 'w_out', 'w_mlp1', 'w_mlp2', 'hy_w1', 'hy_w2', 'hy_w3'):
        sh[k] = np.ascontiguousarray(np.asarray(inp[k], np.float32))
    sh['b_mod2'] = np.ascontiguousarray(np.repeat(np.asarray(inp['b_mod'], np.float32)[:, None, :], 2, axis=1))
    cols = np.zeros((DEPTH, 128, NCOL), np.float32)
    for l in range(DEPTH):
        cols[l, :, C_N1G:C_N1G + 8] = _colpack(inp['norm1_g'][l], 8)
        cols[l, :, C_N2G:C_N2G + 8] = _colpack(inp['norm2_g'][l], 8)
        cols[l, :, C_QG:C_QG + 3] = _colpack(inp['q_norm_g'][l], 3)
        cols[l, :, C_KVG:C_KVG + 2] = _colpack(inp['kv_norm_g'][l], 2)
        for tap in range(3):
            cols[l, :, C_HCW + tap * 6:C_HCW + tap * 6 + 6] = _colpack(inp['hy_conv_w'][l, tap], 6)
        cols[l, :, C_HCB:C_HCB + 6] = _colpack(inp['hy_conv_b'][l], 6)
        cols[l, :, C_HD:C_HD + 2] = _colpack(inp['hy_d'][l], 2)
        cols[l, :64, C_HB1] = inp['hy_b1'][l]
        cols[l, :64, C_HFR] = inp['hy_freq'][l]
        cols[l, :64, C_HB2] = inp['hy_b2'][l]
    sh['cols'] = cols
    sh['fgrow'] = np.ascontiguousarray(np.broadcast_to(np.asarray(inp['final_norm_g'], np.float32)[None, :], (128, D)))
    sh.update(c)
    per = []
    x = np.asarray(inp['x'], np.float32); ctx = np.asarray(inp['ctx'], np.float32)
    cc = np.asarray(inp['c'], np.float32); c_ctx = np.asarray(inp['c_ctx'], np.float32)
    for b in range(8):
        d = dict(sh)
        d['xin'] = np.ascontiguousarray(np.concatenate([x[b], ctx[b]], axis=0))
        cv = np.stack([_colpack(cc[b], 8), _colpack(c_ctx, 8)], axis=-1)
        d['cvec'] = np.ascontiguousarray(cv)
        per.append(d)
    return per


GROUPS = [(g * 512, 512, 0) for g in range(8)] + [(L, LC, 1)]
SCALE = 1.0 / math.sqrt(96.0)


class Prog:
    def __init__(self, debug=False):
        self.debug = debug
        self.nc = nc = bass.Bass("TRN2", target_bir_lowering=False)
        self.es = ExitStack()
        self.S = Sched(nc, self.es)
        self._in = {}
        okind = dict(kind="ExternalOutput") if debug else {}
        scr = lambda n, shp, dt=F32: nc.dram_tensor(n, list(shp), dt, **okind).ap()
        self.spec = dict(
            xin=([T, D], F32), cvec=([128, 8, 2], F32), w_mod=([DEPTH, D, 6 * D], F32), b_mod2=([DEPTH, 2, 6 * D], F32),
            w_inx=([DEPTH, D, 1728], F32), w_uq=([DEPTH, 384, 768], F32), w_uqp=([DEPTH, 384, 768], F32),
            w_ukvx=([DEPTH, 256, 1024], F32), w_out=([DEPTH, D, D], F32), w_mlp1=([DEPTH, D, 4 * D], F32),
            w_mlp2=([DEPTH, 4 * D, D], F32), hy_w1=([DEPTH, 33, 64], F32), hy_w2=([DEPTH, 64, 64], F32),
            hy_w3=([DEPTH, 64, 512], F32), cols=([DEPTH, 128, NCOL], F32), fgrow=([128, D], F32),
            ident_bf=([128, 128], BF16), ident_f=([128, 128], F32), ones_f=([128, 128], F32),
            ropeC=([32, T], F32), ropeS=([32, T], F32), fn_csw=([128, 256], BF16))
        self.fn = {}; self.hy = {}
        for nm, Ls, L1 in (('x', L, 64), ('c', LC, 4)):
            N = 2 * Ls; N1 = N // 128
            self.spec.update({'fn_w1' + nm: ([L1, 2 * L1], BF16), 'fn_w2' + nm: ([L1, 2 * L1], BF16),
                              'fn_tr' + nm: ([64, Ls], BF16), 'fn_sn' + nm: ([64, Ls], BF16),
                              'hy_w1' + nm: ([N1, 2 * N1], BF16), 'hy_tr' + nm: ([128, N], BF16),
                              'hy_ti' + nm: ([128, N], BF16), 'hy_nti' + nm: ([128, N], BF16),
                              'hy_i1a' + nm: ([128, 256], BF16), 'hy_i1b' + nm: ([128, 256], BF16),
                              'hy_er' + nm: ([N1, Ls], BF16), 'hy_nei' + nm: ([N1, Ls], BF16),
                              'hy_zT' + nm: ([33, N], F32), 'hy_tcol' + nm: ([N1, 128], F32),
                              'hy_drow' + nm: ([128, 256], F32)})
            self.fn[nm] = dict(L1=L1, Ls=Ls)
            self.hy[nm] = dict(N1=N1, Ls=Ls, N=N, kf=scr('kf' + nm, [128, N1, 2, 256], BF16),
                               ssum=scr('ssum' + nm, [128, 2]))
        self.y = nc.dram_tensor('y', [L, D], F32, kind="ExternalOutput").ap()
        self.xres = scr('xres', [T, D]); self.mrow = scr('mrow', [2, 6 * D])
        self.pxT = scr('pxT', [14, 128, T], BF16); self.mixT = scr('mixT', [8, 128, T], BF16)
        self.hx0 = scr('hx0', [2, 128, T], BF16); self.hz = scr('hz', [2, 128, T], BF16)
        self.xmid = scr('xmid', [T, D]); self.h2T = scr('h2T', [8, 128, T], BF16)
        self.hyY = {}; self.hyF = {}
        for nm_ in ('x', 'c'):
            n1_ = self.hy[nm_]['N1']
            self.hyY[nm_] = scr('hyY' + nm_, [128, 256, 2 * n1_], BF16)
            self.hyF[nm_] = scr('hyF' + nm_, [128, n1_, 2, 128], BF16)
        self.modc = self.es.enter_context(nc.sbuf_tensor('modc', [128, 2, 4, 8], F32))
        self.cols = self.es.enter_context(nc.sbuf_tensor('colsb', [128, NCOL], F32))

    def uq(self, n):
        self._uq = getattr(self, '_uq', 0) + 1
        return '%s_%d' % (n, self._uq)

    def I(self, name):
        if name not in self._in:
            shp, dt = self.spec[name]
            self._in[name] = self.nc.dram_tensor(name, list(shp), dt, kind="ExternalInput").ap()
        return self._in[name]

    def p0_mod(self, l):
        nc, S = self.nc, self.S
        with ExitStack() as es:
            sb = lambda n, shp, dt=F32: es.enter_context(nc.sbuf_tensor(self.uq(n), shp, dt))
            cv = sb('cv', [128, 8, 2]); sc = sb('sc', [128, 8, 2])
            wt = [sb('wt%d' % i, [128, 8, 512]) for i in range(2)]
            msb = sb('msb', [2, 6 * D]); bm = sb('bm', [2, 6 * D])
            ps = [es.enter_context(nc.psum_tensor(self.uq('ps%d' % i), [2, 512], F32)) for i in range(2)]
            S.dma('sp', cv[:], self.I('cvec'), w=['cv'])
            S.dma('sp', bm[:], self.I('b_mod2')[l], w=['bm'])
            S.dma('sp', self.cols[:], self.I('cols')[l], w=['cols'])
            S.op('act', 'activation', sc[:], cv[:], AF.Silu, r=['cv'], w=['sc'])
            wv = self.I('w_mod')[l].rearrange("(k p) n -> p k n", p=128)
            for n in range(12):
                b = n % 2
                S.dma('sp', wt[b][:], wv[:, :, n * 512:(n + 1) * 512], w=[('wt', b)])
                S.multi('pe', [('matmul', (ps[b][:], sc[:, k, :], wt[b][:, k, :]), dict(start=(k == 0), stop=(k == 7)))
                               for k in range(8)], r=['sc', ('wt', b)], w=[('ps', b)])
                S.op('dve', 'tensor_tensor', msb[:, n * 512:(n + 1) * 512], ps[b][:], bm[:, n * 512:(n + 1) * 512],
                     ALU.add, r=[('ps', b), 'bm'], w=['msb'])
            S.dma('sp', self.mrow, msb[:], r=['msb'])
            S.flush()
            mT = sb('mT', [96, 128]); idf = sb('idf', [128, 128]); mcol = sb('mcol', [128, 96])
            pm = es.enter_context(nc.psum_tensor(self.uq('pm'), [128, 96], F32))
            S.dma('sp', mT[:], self.mrow.rearrange("s (j p) -> (s j) p", p=128), w=['mT'])
            S.dma('sp', idf[:], self.I('ident_f'), w=['idf'])
            S.op('pe', 'transpose', pm[:], mT[:], idf[0:96, 0:96], r=['mT', 'idf'], w=['pm'])
            S.op('dve', 'tensor_copy', mcol[:], pm[:], r=['pm'], w=['mcol'])
            for s in range(2):
                o = s * 48
                S.op('dve', 'scalar_tensor_tensor', self.modc[:, s, 0, :], mcol[:, o + 8:o + 16], 1.0,
                     self.cols[:, C_N1G:C_N1G + 8], ALU.add, ALU.mult, r=['mcol', 'cols'], w=['modc'])
                S.op('dve', 'tensor_copy', self.modc[:, s, 1, :], mcol[:, o:o + 8], r=['mcol'], w=['modc'])
                S.op('dve', 'scalar_tensor_tensor', self.modc[:, s, 2, :], mcol[:, o + 32:o + 40], 1.0,
                     self.cols[:, C_N2G:C_N2G + 8], ALU.add, ALU.mult, r=['mcol', 'cols'], w=['modc'])
                S.op('dve', 'tensor_copy', self.modc[:, s, 3, :], mcol[:, o + 24:o + 32], r=['mcol'], w=['modc'])
            S.flush()

    def norm_tiles(self, xt, nj, s, which, tl, key, n):
        S = self.S
        ss, rstd, xn, junk, pt, hT, idb = tl['ss'], tl['rstd'], tl['xn'], tl['junk'], tl['pt'], tl['hT'], tl['idb']
        for j in range(nj):
            S.op('act', 'activation', junk[:], xt[:, j, :], AF.Square, accum_out=ss[:, j:j + 1],
                 r=[key], w=['junk', tl['k'] + 'ss'])
        S.op('act', 'activation', rstd[:, 0:nj], ss[:, 0:nj], AF.Sqrt, bias=tl['eps'][:, 0:1], scale=1.0 / D,
             r=[tl['k'] + 'ss', 'eps'], w=[tl['k'] + 'rstd'])
        S.op('dve', 'reciprocal', rstd[:, 0:nj], rstd[:, 0:nj], r=[tl['k'] + 'rstd'], w=[tl['k'] + 'rstd'])
        for j in range(nj):
            S.op('dve', 'tensor_scalar', xn[:, j, :], xt[:, j, :], rstd[:, j:j + 1], None, ALU.mult,
                 r=[key, tl['k'] + 'rstd'], w=[tl['k'] + 'xn'])
        for k in range(8):
            pb = k % 2
            S.multi('pe', [('transpose', (pt[pb][:, j * 128:(j + 1) * 128], xn[:, j, k * 128:(k + 1) * 128], idb[:]), {})
                           for j in range(nj)], r=[tl['k'] + 'xn', 'idb'], w=[('pt', pb)])
            if k % 2 == 0:
                S.op('dve', 'tensor_scalar', hT[:, k, 0:n], pt[pb][:, 0:n], self.modc[:, s, which, k:k + 1],
                     self.modc[:, s, which + 1, k:k + 1], ALU.mult, ALU.add, r=[('pt', pb), 'modc'], w=[tl['k'] + 'hT'])
            else:
                S.op('act', 'activation', hT[:, k, 0:n], pt[pb][:, 0:n], AF.Identity,
                     bias=self.modc[:, s, which + 1, k:k + 1], scale=self.modc[:, s, which, k:k + 1],
                     r=[('pt', pb), 'modc'], w=[tl['k'] + 'hT'])

    def p1_inproj(self, l):
        nc, S = self.nc, self.S
        src = self.I('xin') if l == 0 else self.xres
        with ExitStack() as es:
            sb = lambda n, shp, dt=F32: es.enter_context(nc.sbuf_tensor(self.uq(n), shp, dt))
            pst = lambda n, shp, dt=F32: es.enter_context(nc.psum_tensor(self.uq(n), shp, dt))
            win = sb('win', [128, 8, 1728], BF16)
            S.dma('pool', win[:], self.I('w_inx')[l].rearrange("(k p) n -> p k n", p=128), w=['win'])
            idb = sb('idb', [128, 128], BF16); S.dma('sp', idb[:], self.I('ident_bf'), w=['idb'])
            onesf = sb('onesf', [128, 128]); S.dma('sp', onesf[:], self.I('ones_f'), w=['onesf'])
            rc = sb('rc', [32, T]); rs = sb('rs', [32, T])
            S.dma('sp', rc[:], self.I('ropeC'), w=['rc']); S.dma('sp', rs[:], self.I('ropeS'), w=['rs'])
            eps = sb('eps', [128, 1]); S.op('dve', 'memset', eps[:], EPS, w=['eps'])
            xg = [sb('xg%d' % i, [128, 4, D]) for i in range(2)]
            tls = []
            junk = sb('junk', [128, D], BF16)
            pt = [pst('pt%d' % i, [128, 512], BF16) for i in range(2)]
            for i in range(2):
                tls.append(dict(k='t%d' % i, ss=sb('ss%d' % i, [128, 4]), rstd=sb('rstd%d' % i, [128, 4]),
                                xn=sb('xn%d' % i, [128, 4, D], BF16), junk=junk, pt=pt,
                                hT=sb('hT%d' % i, [128, 8, 512], BF16), idb=idb, eps=eps))
            ost = [sb('ost%d' % i, [128, 14, 512], BF16) for i in range(2)]
            for i in range(2):
                S.op('pool', 'memset', ost[i][:, 13, :], 0.0, w=[('ost', i)])
            c32 = sb('c32', [128, 5, 512]); sq = sb('sq', [128, 5, 512]); rq = sb('rq', [128, 512])
            t1 = sb('t1', [32, 512]); t2 = sb('t2', [32, 512])
            po = [pst('po%d' % i, [128, 512]) for i in range(3)]
            pn = pst('pn', [128, 512])
            npo = [0]

            def proj(col0, ncols, hT, n, b):
                i = npo[0] % 3; npo[0] += 1
                S.multi('pe', [('matmul', (po[i][0:ncols, 0:n], win[:, k, col0:col0 + ncols], hT[:, k, 0:n]),
                                dict(start=(k == 0), stop=(k == 7))) for k in range(8)],
                        r=['win', 't%dhT' % b], w=[('po', i)])
                return i

            def front(gi):
                t0, n, s = GROUPS[gi]
                b = gi % 2; nj = n // 128; tl = tls[b]
                S.dma('sp', xg[b][:, 0:nj, :], src[t0:t0 + n, :].rearrange("(j p) d -> p j d", p=128), w=[('xg', b)])
                self.norm_tiles(xg[b], nj, s, 0, tl, ('xg', b), n)

            def back(gi):
                t0, n, s = GROUPS[gi]
                b = gi % 2; nj = n // 128; tl = tls[b]
                hT = tl['hT']
                okey = ('ost', b)
                for (c0, nch, gcol, ci0, i0) in ((OFF_Q, 3, C_QG, 0, 0), (OFF_KV, 2, C_KVG, 3, 3)):
                    for c in range(nch):
                        i = proj(c0 + c * 128, 128, hT, n, b)
                        S.op('act', 'activation', c32[:, i0 + c, 0:n], po[i][:, 0:n], AF.Copy,
                             r=[('po', i)], w=[('c32', i0 + c)])
                        S.op('act', 'activation', sq[:, i0 + c, 0:n], po[i][:, 0:n], AF.Square,
                             r=[('po', i)], w=[('sq', i0 + c)])
                    S.multi('pe', [('matmul', (pn[:, 0:n], onesf[:], sq[:, i0 + c, 0:n]),
                                    dict(start=(c == 0), stop=(c == nch - 1))) for c in range(nch)],
                            r=['onesf'] + [('sq', i0 + c) for c in range(nch)], w=['pn'])
                    S.op('act', 'activation', rq[:, 0:n], pn[:, 0:n], AF.Sqrt, bias=eps[:, 0:1], scale=1.0 / (128 * nch),
                         r=['pn', 'eps'], w=['rq'])
                    S.op('dve', 'reciprocal', rq[:, 0:n], rq[:, 0:n], r=['rq'], w=['rq'])
                    for c in range(nch):
                        S.op('dve', 'scalar_tensor_tensor', ost[b][:, ci0 + c, 0:n], c32[:, i0 + c, 0:n],
                             self.cols[:, gcol + c:gcol + c + 1], rq[:, 0:n], ALU.mult, ALU.mult,
                             r=[('c32', i0 + c), 'rq', 'cols'], w=[okey])
                ia = proj(OFF_KR, 32, hT, n, b)
                ib = proj(IN_W, 32, hT, n, b)
                S.op('dve', 'tensor_tensor', t1[:, 0:n], po[ia][0:32, 0:n], rc[:, t0:t0 + n], ALU.mult,
                     r=[('po', ia), 'rc'], w=['t1'])
                S.op('dve', 'tensor_tensor', t2[:, 0:n], po[ib][0:32, 0:n], rs[:, t0:t0 + n], ALU.mult,
                     r=[('po', ib), 'rs'], w=['t2'])
                S.op('dve', 'tensor_tensor', ost[b][0:32, 13, 0:n], t1[:, 0:n], t2[:, 0:n], ALU.add,
                     r=['t1', 't2'], w=[okey])
                for c in range(8):
                    i = proj(OFF_F + c * 128, 128, hT, n, b)
                    if c % 2 == 0:
                        S.op('act', 'activation', ost[b][:, 5 + c, 0:n], po[i][:, 0:n], AF.Copy, r=[('po', i)], w=[okey])
                    else:
                        S.op('dve', 'tensor_copy', ost[b][:, 5 + c, 0:n], po[i][:, 0:n], r=[('po', i)], w=[okey])
                S.dma('sp', self.pxT[:, :, t0:t0 + n].rearrange("c p t -> p c t"), ost[b][:, :, 0:n], r=[okey])

            front(0)
            for gi in range(len(GROUPS)):
                if gi + 1 < len(GROUPS):
                    front(gi + 1)
                back(gi)
            S.flush()

    def p2_attn(self, l):
        nc, S = self.nc, self.S
        with ExitStack() as es:
            sb = lambda n, shp, dt=F32: es.enter_context(nc.sbuf_tensor(self.uq(n), shp, dt))
            pst = lambda n, shp, dt=F32: es.enter_context(nc.psum_tensor(self.uq(n), shp, dt))
            cqn = sb('cqn', [128, 3, T], BF16); ckvn = sb('ckvn', [128, 2, T], BF16)
            S.dma('sp', cqn[:], self.pxT[0:3].rearrange("c p t -> p c t"), w=['cqn'])
            S.dma('sp', ckvn[:], self.pxT[3:5].rearrange("c p t -> p c t"), w=['ckvn'])
            wuq = sb('wuq', [128, 3, 768], BF16); wuqp = sb('wuqp', [128, 3, 768], BF16)
            wukv = sb('wukv', [128, 2, 1024], BF16)
            S.dma('pool', wuq[:], self.I('w_uq')[l].rearrange("(k p) n -> p k n", p=128), w=['wuq'])
            S.dma('pool', wuqp[:], self.I('w_uqp')[l].rearrange("(k p) n -> p k n", p=128), w=['wuqp'])
            S.dma('pool', wukv[:], self.I('w_ukvx')[l].rearrange("(k p) n -> p k n", p=128), w=['wukv'])
            onesf = sb('onesf', [128, 128]); S.dma('sp', onesf[:], self.I('ones_f'), w=['onesf'])
            tc_ = sb('tabc', [96, T]); ts_ = sb('tabs', [96, T])
            S.dma('sp', tc_[64:96, :], self.I('ropeC'), w=['tabc']); S.dma('sp', ts_[64:96, :], self.I('ropeS'), w=['tabs'])
            kt = [sb('kt%d' % i, [96, T], BF16) for i in range(2)]
            qt = [sb('qt%d' % i, [96, T], BF16) for i in range(2)]
            for b in range(2):
                S.dma('sp', kt[b][64:96, :], self.pxT[13, 0:32, :], w=[('ktr', b)])
            vh = [sb('vh%d' % i, [128, NT, 128], BF16) for i in range(2)]
            for i in range(2):
                S.op('pool', 'memset', vh[i][:], 1.0, w=[('vh', i)])
            t1 = sb('t1', [96, 512]); t2 = sb('t2', [96, 512])
            ptl = [sb('ptl%d' % i, [128, 2, 512], BF16) for i in range(3)]
            rr = sb('rr', [96, 512]); osb = [sb('osb%d' % i, [64, 512]) for i in range(2)]; ost = sb('ost', [64, T], BF16)
            sel = sb('sel', [128, 128], BF16); rb = sb('rb', [64, 512])
            S.op('pool', 'memset', sel[:], 0.0, w=['sel'])
            S.op('pool', 'memset', sel[64:65, :], 1.0, w=['sel'])
            rhi = [sb('rhi%d' % i, [128, 512], BF16) for i in range(2)]; rlo = [sb('rlo%d' % i, [128, 512], BF16) for i in range(2)]
            for i in range(2):
                S.op('pool', 'memset', rhi[i][:], 0.0, w=[('rhi', i)])
                S.op('pool', 'memset', rlo[i][:], 0.0, w=[('rlo', i)])
            ps = [pst('ps%d' % i, [128, 1024]) for i in range(2)]
            po = [pst('po%d' % i, [128, 512]) for i in range(2)]
            pq = pst('pq', [128, 512]); pq2 = pst('pq2', [128, 512]); pk = pq; pb = pq2
            LA = 2

            def project_chunks(h):
                b = h % 2
                chunks = []
                for gi, (t0, n, s) in enumerate(GROUPS):
                    def cA(gi=gi, t0=t0, n=n):
                        S.multi('pe', [('matmul', (pq[0:96, 0:n], wuq[:, c, h * 96:(h + 1) * 96], cqn[:, c, t0:t0 + n]),
                                        dict(start=(c == 0), stop=(c == 2))) for c in range(3)],
                                r=['cqn', 'wuq'], w=['pq'])
                        S.multi('pe', [('matmul', (pq2[0:96, 0:n], wuqp[:, c, h * 96:(h + 1) * 96], cqn[:, c, t0:t0 + n]),
                                        dict(start=(c == 0), stop=(c == 2))) for c in range(3)],
                                r=['cqn', 'wuqp'], w=['pq2'])
                        S.op('dve', 'tensor_copy', qt[b][0:64, t0:t0 + n], pq[0:64, 0:n], r=['pq'], w=[('qt', b, gi)])
                        S.op('dve', 'tensor_tensor', t1[64:96, 0:n], pq[64:96, 0:n], tc_[64:96, t0:t0 + n], ALU.mult,
                             r=['pq', 'tabc'], w=['t1'])
                        S.op('dve', 'tensor_tensor', t2[64:96, 0:n], pq2[64:96, 0:n], ts_[64:96, t0:t0 + n], ALU.mult,
                             r=['pq2', 'tabs'], w=['t2'])
                        S.op('dve', 'tensor_tensor', qt[b][64:96, t0:t0 + n], t1[64:96, 0:n], t2[64:96, 0:n], ALU.add,
                             r=['t1', 't2'], w=[('qt', b, gi)])

                    def cB(gi=gi, t0=t0, n=n):
                        S.multi('pe', [('matmul', (pk[:, 0:n], wukv[:, c, h * 64:h * 64 + 128], ckvn[:, c, t0:t0 + n]),
                                        dict(start=(c == 0), stop=(c == 1))) for c in range(2)],
                                r=['ckvn', 'wukv'], w=['pq'])
                        S.op('dve', 'tensor_copy', kt[b][0:64, t0:t0 + n], pk[0:64, 0:n], r=['pq'], w=[('kt', b, gi)])
                    chunks += [cA, cB]
                for i0 in range(0, NT, 8):
                    def cV(i0=i0):
                        nt_ = min(8, NT - i0)
                        calls = []
                        for ii_ in range(nt_):
                            i_ = i0 + ii_
                            calls += [('matmul', (pq2[:, ii_ * 64:(ii_ + 1) * 64], ckvn[:, c, i_ * 128:(i_ + 1) * 128],
                                                  wukv[:, c, 512 + h * 64:512 + (h + 1) * 64]), dict(start=(c == 0), stop=(c == 1)))
                                      for c in range(2)]
                        S.multi('pe', calls, r=['ckvn', 'wukv'], w=['pq2'])
                        S.op('dve', 'tensor_copy', vh[b][:, i0:i0 + nt_, 0:64],
                             pq2[:, 0:nt_ * 64].rearrange("p (a d) -> p a d", d=64), r=['pq2'], w=[('vh', b)])
                    chunks.append(cV)
                return chunks

            def project(h):
                for c_ in project_chunks(h):
                    c_()

            project(0)
            for h in range(H):
                b = h % 2
                seq = []
                for gi, (q0, nq, s) in enumerate(GROUPS):
                    tiles = list(range(NT)) if s == 0 else [32, 33]
                    npair = len(tiles) // 2
                    for ii in range(npair):
                        seq.append((gi, q0, nq, ii, tiles[2 * ii], npair))

                def qk(e):
                    gi, q0, nq, ii, i, np_ = seq[e]
                    sl = e % 2
                    S.multi('pe', [('matmul', (ps[sl][:, a_ * 512:a_ * 512 + nq], kt[b][0:96, (i + a_) * 128:(i + a_ + 1) * 128],
                                               qt[b][0:96, q0:q0 + nq]), dict(start=True, stop=True)) for a_ in range(2)],
                            r=[('kt', b, i // 4), ('kt', b, (i + 1) // 4), ('ktr', b), ('qt', b, gi)], w=[('ps', sl)])

                pending = []
                nxt = project_chunks(h + 1) if h + 1 < H else []
                cstep = max(1, (len(seq) - 12) // max(1, len(nxt)))
                for e in range(min(LA, len(seq))):
                    qk(e)
                for e in range(len(seq)):
                    gi, q0, nq, ii, i, np_ = seq[e]
                    sl = e % 2; sl2 = e % 3; ob = gi % 2
                    S.op('act', 'activation', ptl[sl2][:, :, 0:nq], ps[sl][:, :].rearrange("p (a n) -> p a n", a=2)[:, :, 0:nq],
                         AF.Exp, scale=SCALE, r=[('ps', sl)], w=[('ptl', sl2)])
                    if e + LA < len(seq):
                        qk(e + LA)
                    S.multi('pe', [('matmul', (po[ob][:, 0:nq], vh[b][:, i + a_, :], ptl[sl2][:, a_, 0:nq]),
                                    dict(start=(ii == 0 and a_ == 0), stop=(ii == np_ - 1 and a_ == 1))) for a_ in range(2)],
                            r=[('ptl', sl2), ('vh', b)], w=[('po', ob)])
                    if ii == np_ - 1:
                        fb = gi % 2
                        S.op('dve', 'tensor_copy', rhi[fb][64:65, 0:nq], po[ob][64:65, 0:nq], r=[('po', ob)], w=[('rhi', fb)])
                        S.op('dve', 'tensor_tensor', rlo[fb][64:65, 0:nq], po[ob][64:65, 0:nq], rhi[fb][64:65, 0:nq], ALU.subtract,
                             r=[('po', ob), ('rhi', fb)], w=[('rlo', fb)])
                        S.op('dve', 'tensor_copy', osb[fb][:, 0:nq], po[ob][0:64, 0:nq], r=[('po', ob)], w=[('osb', fb)])
                        pending.append((e + 3, fb, q0, nq))
                    while pending and (pending[0][0] <= e or e == len(seq) - 1):
                        _, fb, fq0, fnq = pending.pop(0)
                        S.multi('pe', [('matmul', (pb[:, 0:fnq], sel[:, :], rhi[fb][:, 0:fnq]), dict(start=True, stop=False)),
                                       ('matmul', (pb[:, 0:fnq], sel[:, :], rlo[fb][:, 0:fnq]), dict(start=False, stop=True))],
                                r=[('rhi', fb), ('rlo', fb), 'sel'], w=['pq2'])
                        S.op('dve', 'reciprocal', rb[:, 0:fnq], pb[0:64, 0:fnq], r=['pq2'], w=['rb'])
                        S.op('dve', 'tensor_tensor', ost[:, fq0:fq0 + fnq], osb[fb][:, 0:fnq], rb[:, 0:fnq], ALU.mult,
                             r=[('osb', fb), 'rb'], w=['ost'])
                    if nxt and e >= 4 and (e - 4) % cstep == 0:
                        nxt.pop(0)()
                while nxt:
                    nxt.pop(0)()
                S.dma('sp', self.mixT[h // 2, (h % 2) * 64:(h % 2) * 64 + 64, :], ost[:, :], r=['ost'])
            S.flush()

    def p5_out_mlp(self, l):
        nc, S = self.nc, self.S
        last = (l == DEPTH - 1)
        src = self.I('xin') if l == 0 else self.xres
        groups = GROUPS[:8] if last else GROUPS
        with ExitStack() as es:
            sb = lambda n, shp, dt=F32: es.enter_context(nc.sbuf_tensor(self.uq(n), shp, dt))
            pst = lambda n, shp, dt=F32: es.enter_context(nc.psum_tensor(self.uq(n), shp, dt))
            wout = sb('wout', [128, 8, D], BF16)
            S.dma('pool', wout[:], self.I('w_out')[l].rearrange("(k p) n -> p k n", p=128), w=['wout'])
            idb = sb('idb', [128, 128], BF16); S.dma('sp', idb[:], self.I('ident_bf'), w=['idb'])
            eps = sb('eps', [128, 1]); S.op('dve', 'memset', eps[:], EPS, w=['eps'])
            g1r = sb('g1r', [128, D])
            mixt = [sb('mixt%d' % i, [128, 8, 512], BF16) for i in range(2)]
            xt = [sb('xt%d' % i, [128, 4, D]) for i in range(2)]
            tmp = [sb('tmp%d' % i, [128, 512]) for i in range(2)]
            junk = sb('junk', [128, D], BF16)
            pt = [pst('pt%d' % i, [128, 512], BF16) for i in range(2)]
            tls = [dict(k='n%d' % i, ss=sb('ss%d' % i, [128, 4]), rstd=sb('rstd%d' % i, [128, 4]),
                        xn=sb('xn%d' % i, [128, 4, D], BF16), junk=junk, pt=pt,
                        hT=sb('hT%d' % i, [128, 8, 512], BF16), idb=idb, eps=eps) for i in range(2)]
            pso = [pst('pso%d' % i, [128, 512]) for i in range(4)]
            st = dict(cur_s=-1, np_=0)

            def stA(gi):
                t0, n, s = groups[gi]
                b = gi % 2; nj = n // 128
                if s != st['cur_s']:
                    S.dma('sp', g1r[:], self.mrow[s, 2 * D:3 * D].partition_broadcast(128), w=['g1r']); st['cur_s'] = s
                S.dma('sp', mixt[b][:, :, 0:n], self.mixT[:, :, t0:t0 + n].rearrange("c p t -> p c t"), w=[('mixt', b)])
                S.dma('sp', xt[b][:, 0:nj, :], src[t0:t0 + n, :].rearrange("(j p) d -> p j d", p=128), w=[('xt', b)])
                for j in range(nj):
                    for hh in range(2):
                        pi = st['np_'] % 4; st['np_'] += 1
                        S.multi('pe', [('matmul', (pso[pi][:, :], mixt[b][:, c, j * 128:(j + 1) * 128], wout[:, c, hh * 512:(hh + 1) * 512]),
                                        dict(start=(c == 0), stop=(c == 7))) for c in range(8)],
                                r=[('mixt', b), 'wout'], w=[('pso', pi)])
                        S.op('dve', 'tensor_tensor', tmp[pi % 2][:, :], pso[pi][:, :], g1r[:, hh * 512:(hh + 1) * 512], ALU.mult,
                             r=[('pso', pi), 'g1r'], w=[('tmp', pi % 2)])
                        S.op('pool', 'tensor_tensor', xt[b][:, j, hh * 512:(hh + 1) * 512], tmp[pi % 2][:, :],
                             xt[b][:, j, hh * 512:(hh + 1) * 512], ALU.add, r=[('tmp', pi % 2), ('xt', b)], w=[('xt', b)])
                S.dma('sp', self.xmid[t0:t0 + n, :].rearrange("(j p) d -> p j d", p=128), xt[b][:, 0:nj, :], r=[('xt', b)])

            def stB(gi):
                t0, n, s = groups[gi]
                b = gi % 2; nj = n // 128
                self.norm_tiles(xt[b], nj, s, 2, tls[b], ('xt', b), n)
                S.dma('sp', self.h2T[:, :, t0:t0 + n].rearrange("c p t -> p c t"), tls[b]['hT'][:, :, 0:n], r=['n%dhT' % b])

            stA(0)
            for gi in range(len(groups)):
                if gi + 1 < len(groups):
                    stA(gi + 1)
                stB(gi)
            S.flush()
        with ExitStack() as es:
            sb = lambda n, shp, dt=F32: es.enter_context(nc.sbuf_tensor(self.uq(n), shp, dt))
            pst = lambda n, shp, dt=F32: es.enter_context(nc.psum_tensor(self.uq(n), shp, dt))
            w1 = sb('w1', [128, 8, 4 * D], BF16); w2 = sb('w2', [128, 32, D], BF16)
            for hh in range(2):
                S.dma('pool', w1[:, :, hh * 2048:(hh + 1) * 2048],
                      self.I('w_mlp1')[l].rearrange("(k p) n -> p k n", p=128)[:, :, hh * 2048:(hh + 1) * 2048], w=[('w1', hh)])
            for q in range(4):
                S.dma('pool', w2[:, q * 8:(q + 1) * 8, :],
                      self.I('w_mlp2')[l].rearrange("(k p) n -> p k n", p=128)[:, q * 8:(q + 1) * 8, :], w=[('w2', q)])
            g2r = sb('g2r', [128, D])
            if last:
                fg = sb('fg', [128, D]); S.dma('sp', fg[:], self.I('fgrow'), w=['fg'])
                eps = sb('eps', [128, 1]); S.op('dve', 'memset', eps[:], EPS, w=['eps'])
                junk = sb('junk', [128, D], BF16); fss = sb('fss', [128, 4]); frs = sb('frs', [128, 4])
            hT = sb('hT', [128, 8, 512], BF16); fT = sb('fT', [128, 32, 512], BF16)
            x1 = sb('x1', [128, 4, D]); sq = [sb('sq%d' % i, [128, 512]) for i in range(2)]
            tmp = [sb('tmp%d' % i, [128, 512]) for i in range(2)]
            psf = [pst('psf%d' % i, [128, 512]) for i in range(3)]
            ps2 = [pst('ps2%d' % i, [128, 512]) for i in range(3)]
            cur_s = -1; n2 = 0
            t0_, n_, s_ = groups[0]
            S.dma('sp', hT[:, :, 0:n_], self.h2T[:, :, t0_:t0_ + n_].rearrange("c p t -> p c t"), w=['hT'])
            for gi, (t0, n, s) in enumerate(groups):
                nj = n // 128
                if s != cur_s:
                    S.dma('sp', g2r[:], self.mrow[s, 5 * D:6 * D].partition_broadcast(128), w=['g2r']); cur_s = s
                S.dma('sp', x1[:, 0:nj, :], self.xmid[t0:t0 + n, :].rearrange("(j p) d -> p j d", p=128),
                      w=[('x1', j) for j in range(nj)])
                for j in range(32):
                    pf = psf[j % 3]
                    S.multi('pe', [('matmul', (pf[:, 0:n], w1[:, k, j * 128:(j + 1) * 128], hT[:, k, 0:n]),
                                    dict(start=(k == 0), stop=(k == 7))) for k in range(8)],
                            r=[('w1', j // 16), 'hT'], w=[('psf', j % 3)])
                    S.op('act', 'activation', sq[j % 2][:, 0:n], pf[:, 0:n], AF.Square, r=[('psf', j % 3)], w=[('sq', j % 2)])
                    S.op('dve', 'scalar_tensor_tensor', fT[:, j, 0:n], pf[:, 0:n], 0.0, sq[j % 2][:, 0:n],
                         ALU.is_gt, ALU.mult, r=[('psf', j % 3), ('sq', j % 2)], w=[('fT', j)])
                if gi + 1 < len(groups):
                    t0n, nn, sn_ = groups[gi + 1]
                    S.dma('sp', hT[:, :, 0:nn], self.h2T[:, :, t0n:t0n + nn].rearrange("c p t -> p c t"), w=['hT'])
                for j in range(nj):
                    for hh in range(2):
                        pi = n2 % 3; n2 += 1
                        S.multi('pe', [('matmul', (ps2[pi][:, :], fT[:, q, j * 128:(j + 1) * 128], w2[:, q, hh * 512:(hh + 1) * 512]),
                                        dict(start=(q == 0), stop=(q == 31))) for q in range(32)],
                                r=[('fT', q) for q in range(32)] + [('w2', q) for q in range(4)], w=[('ps2', pi)])
                        S.op('dve', 'tensor_tensor', tmp[pi % 2][:, :], ps2[pi][:, :], g2r[:, hh * 512:(hh + 1) * 512], ALU.mult,
                             r=[('ps2', pi), 'g2r'], w=[('tmp', pi % 2)])
                        S.op('pool', 'tensor_tensor', x1[:, j, hh * 512:(hh + 1) * 512], tmp[pi % 2][:, :],
                             x1[:, j, hh * 512:(hh + 1) * 512], ALU.add, r=[('tmp', pi % 2), ('x1', j)], w=[('x1', j)])
                    tt = t0 + j * 128
                    if not last:
                        S.dma('sp', self.xres[tt:tt + 128, :], x1[:, j, :], r=[('x1', j)])
                    else:
                        S.op('act', 'activation', junk[:], x1[:, j, :], AF.Square, accum_out=fss[:, j:j + 1],
                             r=[('x1', j)], w=['junk', ('fss', j)])
                        S.op('act', 'activation', frs[:, j:j + 1], fss[:, j:j + 1], AF.Sqrt, bias=eps[:, 0:1], scale=1.0 / D,
                             r=[('fss', j), 'eps'], w=[('frs', j)])
                        S.op('dve', 'reciprocal', frs[:, j:j + 1], frs[:, j:j + 1], r=[('frs', j)], w=[('frs', j)])
                        S.op('dve', 'scalar_tensor_tensor', x1[:, j, :], x1[:, j, :], frs[:, j:j + 1], fg[:], ALU.mult, ALU.mult,
                             r=[('x1', j), ('frs', j), 'fg'], w=[('x1', j)])
                        S.dma('sp', self.y[tt:tt + 128, :], x1[:, j, :], r=[('x1', j)])
            S.flush()

    def p3_fnet(self, l):
        for nm, t0 in (('x', 0), ('c', L)):
            if nm == 'c' and l == DEPTH - 1:
                continue
            self._fnet(l, nm, t0)

    def _fnet(self, l, nm, t0):
        nc, S = self.nc, self.S
        L1 = self.fn[nm]['L1']; Ls = self.fn[nm]['Ls']; L2 = 64
        with ExitStack() as es:
            sb = lambda n, shp, dt=F32: es.enter_context(nc.sbuf_tensor(self.uq(n), shp, dt))
            pst = lambda n, shp, dt=F32: es.enter_context(nc.psum_tensor(self.uq(n), shp, dt))
            uf = sb('uf', [128, 2, Ls], BF16)
            S.dma('sp', uf[:], self.pxT[5:7, :, t0:t0 + Ls].rearrange("c p t -> p c t"), w=['uf'])
            csw = sb('csw', [128, 256], BF16); S.dma('sp', csw[:], self.I('fn_csw'), w=['csw'])
            w1 = sb('w1', [L1, 2 * L1], BF16); w2 = sb('w2', [L1, 2 * L1], BF16)
            S.dma('sp', w1[:], self.I('fn_w1' + nm), w=['w1']); S.dma('sp', w2[:], self.I('fn_w2' + nm), w=['w2'])
            tr = sb('tr', [64, Ls], BF16); sn = sb('sn', [64, Ls], BF16)
            S.dma('sp', tr[:], self.I('fn_tr' + nm), w=['tr']); S.dma('sp', sn[:], self.I('fn_sn' + nm), w=['sn'])
            U = sb('U', [L1, L2, 512], BF16)
            Y = sb('Y', [64, 2, L1, 256], BF16)
            pu = [pst('pu%d' % i, [128, 512]) for i in range(2)]
            py = [pst('py%d' % i, [128, 512]) for i in range(2)]
            pf = [pst('pf%d' % i, [128, 512]) for i in range(2)]
            for l2 in range(L2):
                b = l2 % 2
                S.multi('pe', [('matmul', (pu[b][0:L1, c * 256:(c + 1) * 256], uf[:, c, l2::L2], csw[:, :]),
                                dict(start=True, stop=True)) for c in range(2)], r=['uf', 'csw'], w=[('pu', b)])
                if b == 0:
                    S.op('act', 'activation', U[:, l2, :], pu[b][0:L1, :], AF.Copy, r=[('pu', b)], w=['U'])
                else:
                    S.op('dve', 'tensor_copy', U[:, l2, :], pu[b][0:L1, :], r=[('pu', b)], w=['U'])
            cpb = 512 // (2 * L1)
            nb = 0
            for ch0 in range(0, 256, cpb):
                b = nb % 2; nb += 1
                calls = []
                for chl in range(cpb):
                    ch = ch0 + chl; c = ch // 128; i = ch % 128
                    o = py[b][0:64, chl * 2 * L1:(chl + 1) * 2 * L1]
                    calls.append(('matmul', (o, U[:, :, c * 256 + i], w1[:, :]), dict(start=True, stop=False)))
                    calls.append(('matmul', (o, U[:, :, c * 256 + 128 + i], w2[:, :]), dict(start=False, stop=True)))
                S.multi('pe', calls, r=['U', 'w1', 'w2'], w=[('py', b)])
                for r_ in range(2):
                    src = py[b][0:64, 0:cpb * 2 * L1].rearrange("p (c r a) -> p c r a", c=cpb, r=2)[:, :, r_, :]
                    dst = Y[:, r_, :, ch0:ch0 + cpb].rearrange("p a c -> p c a")
                    if r_ == 0:
                        S.op('act', 'activation', dst, src, AF.Copy, r=[('py', b)], w=['Y'])
                    else:
                        S.op('dve', 'tensor_copy', dst, src, r=[('py', b)], w=['Y'])
            na = min(8, L1)
            nb = 0
            for cc in range(2):
                for a0 in range(0, L1, na):
                    b = nb % 2; nb += 1
                    calls = []
                    for al in range(na):
                        a_ = a0 + al
                        o = pf[b][:, al * 64:(al + 1) * 64]
                        calls.append(('matmul', (o, Y[:, 0, a_, cc * 128:(cc + 1) * 128], tr[:, a_::L1]), dict(start=True, stop=False)))
                        calls.append(('matmul', (o, Y[:, 1, a_, cc * 128:(cc + 1) * 128], sn[:, a_::L1]), dict(start=False, stop=True)))
                    S.multi('pe', calls, r=['Y', 'tr', 'sn'], w=[('pf', b)])
                    dst = uf[:, cc, :].rearrange("p (b a) -> p a b", a=L1)[:, a0:a0 + na, :]
                    src = pf[b][:, 0:na * 64].rearrange("p (a b) -> p a b", a=na)
                    if b == 0:
                        S.op('act', 'activation', dst, src, AF.Copy, r=[('pf', b), 'U'], w=['uf'])
                    else:
                        S.op('dve', 'tensor_copy', dst, src, r=[('pf', b), 'U'], w=['uf'])
            S.dma('sp', self.mixT[4:6, :, t0:t0 + Ls].rearrange("c p t -> p c t"), uf[:], r=['uf'])
            S.flush()

    def _hy_f1(self, src, krows, N1, w1h, hyY, es_sb, pst):
        S = self.S
        cpb = 512 // (2 * N1)
        py = [pst('hpy%d' % i, [128, 512]) for i in range(2)]
        nbank = 256 // cpb
        GB = min(4, nbank)
        stg = [es_sb('hstg%d' % i, [128, GB, 512], BF16) for i in range(2)]
        for nb in range(nbank):
            ch0 = nb * cpb
            b = nb % 2; sb_ = (nb // GB) % 2; g = nb % GB
            S.multi('pe', [('matmul', (py[b][:, chl * 2 * N1:(chl + 1) * 2 * N1], src[0:krows, :, ch0 + chl], w1h[0:krows, :]),
                            dict(start=True, stop=True)) for chl in range(cpb)], r=['hsrc', 'w1h'], w=[('hpy', b)])
            if b == 0:
                S.op('act', 'activation', stg[sb_][:, g, :], py[b][:, :], AF.Copy, r=[('hpy', b)], w=[('hstg', sb_)])
            else:
                S.op('dve', 'tensor_copy', stg[sb_][:, g, :], py[b][:, :], r=[('hpy', b)], w=[('hstg', sb_)])
            if g == GB - 1:
                c0 = (nb - GB + 1) * cpb
                S.dma('sp', hyY[:, c0:c0 + GB * cpb, :].rearrange("p c x -> p (c x)"),
                      stg[sb_][:, :, :].rearrange("p g x -> p (g x)"), r=[('hstg', sb_)])

    def _hy_f2(self, Yz, ncol, N1, tr, ti, nti, pz, f1):
        S = self.S
        calls = [('matmul', (pz[:, 0:ncol], tr[:, f1::N1], Yz[:, :, f1]), dict(start=True, stop=False)),
                 ('matmul', (pz[:, 0:ncol], nti[:, f1::N1], Yz[:, :, N1 + f1]), dict(start=False, stop=True)),
                 ('matmul', (pz[:, ncol:2 * ncol], ti[:, f1::N1], Yz[:, :, f1]), dict(start=True, stop=False)),
                 ('matmul', (pz[:, ncol:2 * ncol], tr[:, f1::N1], Yz[:, :, N1 + f1]), dict(start=False, stop=True))]
        return calls

    def _range_sin(self, dst, ps, fcol, fbcol, tl, n):
        S = self.S
        a, t, k = tl['a'], tl['t'], tl['k']
        S.op('dve', 'tensor_scalar', a[:, 0:n], ps, fcol, fbcol, ALU.mult, ALU.add, r=[tl['ps'], 'fcols'], w=['rs_a'])
        S.op('dve', 'tensor_scalar', t[:, 0:n], a[:, 0:n], 1.0 / (2 * math.pi), 12582912.0, ALU.mult, ALU.add, r=['rs_a'], w=['rs_t'])
        S.op('dve', 'tensor_scalar', k[:, 0:n], t[:, 0:n], -12582912.0, None, ALU.add, r=['rs_t'], w=['rs_k'])
        S.op('dve', 'scalar_tensor_tensor', a[:, 0:n], k[:, 0:n], -2 * math.pi, a[:, 0:n], ALU.mult, ALU.add, r=['rs_k', 'rs_a'], w=['rs_a'])
        S.op('dve', 'tensor_scalar', a[:, 0:n], a[:, 0:n], -3.14159, 3.14159, ALU.max, ALU.min, r=['rs_a'], w=['rs_a'])
        S.op('act', 'activation', dst, a[:, 0:n], AF.Sin, r=['rs_a'], w=[tl['dst']])

    def pk_filters(self, l):
        for nm in ('x', 'c'):
            if nm == 'c' and l == DEPTH - 1:
                continue
            self._filters(l, nm)

    def _filters(self, l, nm):
        nc, S = self.nc, self.S
        hy = self.hy[nm]; N1 = hy['N1']; Ls = hy['Ls']; N = hy['N']
        hyY = self.hyY[nm]
        with ExitStack() as es:
            sb = lambda n, shp, dt=F32: es.enter_context(nc.sbuf_tensor(self.uq(n), shp, dt))
            pst = lambda n, shp, dt=F32: es.enter_context(nc.psum_tensor(self.uq(n), shp, dt))
            ks = sb('ks', [N1, 128, 256], BF16)
            w1h = sb('w1h', [N1, 2 * N1], BF16); S.dma('sp', w1h[:], self.I('hy_w1' + nm), w=['w1h'])
            with ExitStack() as es2:
                sb2 = lambda n, shp, dt=F32: es2.enter_context(nc.sbuf_tensor(self.uq(n), shp, dt))
                ps2 = lambda n, shp, dt=F32: es2.enter_context(nc.psum_tensor(self.uq(n), shp, dt))
                wa = sb2('wa', [33, 64]); wb = sb2('wb', [64, 64]); wc = sb2('wc', [64, 512], BF16)
                S.dma('sp', wa[:], self.I('hy_w1')[l], w=['wa']); S.dma('sp', wb[:], self.I('hy_w2')[l], w=['wb'])
                S.dma('pool', wc[:], self.I('hy_w3')[l], w=['wc'])
                fc = sb2('fc', [64, 4])
                S.op('dve', 'tensor_copy', fc[:, 0:1], self.cols[0:64, C_HFR:C_HFR + 1], r=['cols'], w=['fcols'])
                S.op('dve', 'tensor_tensor', fc[:, 1:2], self.cols[0:64, C_HFR:C_HFR + 1], self.cols[0:64, C_HB1:C_HB1 + 1], ALU.mult, r=['cols'], w=['fcols'])
                S.op('dve', 'tensor_tensor', fc[:, 2:3], self.cols[0:64, C_HFR:C_HFR + 1], self.cols[0:64, C_HB2:C_HB2 + 1], ALU.mult, r=['cols'], w=['fcols'])
                zt = [sb2('zt%d' % i, [33, 512]) for i in range(2)]
                h1 = sb2('h1', [64, 512])
                hA = sb2('hA', [64, N], BF16); hB = sb2('hB', [64, N], BF16)
                tl = dict(a=sb2('rsa', [64, 512]), t=sb2('rst', [64, 512]), k=sb2('rsk', [64, 512]))
                tcol = sb2('tcol', [N1, 128]); S.dma('sp', tcol[:], self.I('hy_tcol' + nm), w=['tcol'])
                drow = sb2('drow', [128, 256]); S.dma('sp', drow[:], self.I('hy_drow' + nm), w=['drow'])
                dec = [sb2('dec%d' % i, [N1, 256]) for i in range(2)]
                kabs = sb2('kabs', [N1, 32 * 256], BF16); red = sb2('red', [N1, 4, 256])
                onf = sb2('onf', [128, 1]); S.op('dve', 'memset', onf[:], 1.0, w=['onf'])
                sres = sb2('sres', [128, 2])
                pa = ps2('pa', [64, 512]); pb_ = ps2('pb', [64, 512])
                pk = [ps2('pk%d' % i, [128, 512]) for i in range(2)]
                pS = ps2('pS', [128, 2])
                nblk = N // 512
                for blk in range(nblk):
                    b = blk % 2
                    S.dma('sp', zt[b][:], self.I('hy_zT' + nm)[:, blk * 512:(blk + 1) * 512], w=[('zt', b)])
                    S.op('pe', 'matmul', pa[:, :], wa[:, :], zt[b][:, :], start=True, stop=True, r=['wa', ('zt', b)], w=['pa'])
                    tl.update(ps='pa', dst='h1')
                    self._range_sin(h1[:, :], pa[:, :], fc[:, 0:1], fc[:, 1:2], tl, 512)
                    S.op('pe', 'matmul', pb_[:, :], wb[:, :], h1[:, :], start=True, stop=True, r=['wb', 'h1'], w=['pb'])
                    tl.update(ps='pb', dst='hA')
                    self._range_sin(hA[:, blk * 512:(blk + 1) * 512], pb_[:, :], fc[:, 0:1], fc[:, 2:3], tl, 512)
                S.op('pool', 'tensor_copy', hB[:, Ls:N], hA[:, Ls:N], r=['hA'], w=['hB'])
                S.op('pool', 'memset', hB[:, 0:Ls], 0.0, w=['hB'])
                S.op('pool', 'memset', hA[:, Ls:N], 0.0, r=['hB'], w=['hA'])
                for s2 in range(128):
                    b = s2 % 2
                    S.multi('pe', [('matmul', (pk[b][0:N1, 0:256], hA[:, s2::128], wc[:, 0:256]), dict(start=True, stop=False)),
                                   ('matmul', (pk[b][0:N1, 0:256], hB[:, s2::128], wc[:, 256:512]), dict(start=False, stop=True))],
                            r=['hA', 'hB', 'wc'], w=[('pk', b)])
                    S.op('act', 'activation', dec[b][:, :], drow[0:N1, :], AF.Exp, scale=tcol[:, s2:s2 + 1],
                         r=['drow', 'tcol'], w=[('dec', b)])
                    S.op('dve', 'tensor_tensor', ks[:, s2, :], pk[b][0:N1, 0:256], dec[b][:, :], ALU.mult,
                         r=[('pk', b), ('dec', b)], w=['hsrc'])
                for q in range(4):
                    S.op('act', 'activation', kabs[:, :], ks[:, q * 32:(q + 1) * 32, :].rearrange("p s c -> p (s c)"), AF.Abs,
                         r=['hsrc'], w=['kabs'])
                    S.op('dve', 'tensor_reduce', red[:, q, :], kabs[:, :].rearrange("p (s c) -> p c s", c=256),
                         mybir.AxisListType.X, ALU.add, r=['kabs'], w=['red'])
                for cc in range(2):
                    S.multi('pe', [('matmul', (pS[:, cc:cc + 1], red[:, q, cc * 128:(cc + 1) * 128], onf[0:N1, 0:1]),
                                    dict(start=(q == 0), stop=(q == 3))) for q in range(4)], r=['red', 'onf'], w=['pS'])
                S.op('dve', 'tensor_copy', sres[:, :], pS[:, :], r=['pS'], w=['sres'])
                S.dma('sp', hy['ssum'], sres[:, :], r=['sres'])
                self._hy_f1(ks, N1, N1, w1h, hyY, sb2, ps2)
                S.flush()
        with ExitStack() as es:
            sb = lambda n, shp, dt=F32: es.enter_context(nc.sbuf_tensor(self.uq(n), shp, dt))
            pst = lambda n, shp, dt=F32: es.enter_context(nc.psum_tensor(self.uq(n), shp, dt))
            Yz = sb('Yz', [128, 256, 2 * N1], BF16); S.dma('sp', Yz[:], hyY, w=['Yz'])
            tr = sb('tr', [128, N], BF16); ti = sb('ti', [128, N], BF16); nti = sb('nti', [128, N], BF16)
            S.dma('sp', tr[:], self.I('hy_tr' + nm), w=['tr']); S.dma('sp', ti[:], self.I('hy_ti' + nm), w=['ti'])
            S.dma('sp', nti[:], self.I('hy_nti' + nm), w=['nti'])
            FB = min(8, N1)
            kst = [sb('kst%d' % i, [128, FB, 512], BF16) for i in range(2)]
            pz = [pst('pz%d' % i, [128, 512]) for i in range(2)]
            for f1 in range(N1):
                b = f1 % 2; kb = (f1 // FB) % 2
                S.multi('pe', self._hy_f2(Yz, 256, N1, tr, ti, nti, pz[b], f1), r=['Yz', 'tr', 'ti', 'nti'], w=[('pz', b)])
                if b == 0:
                    S.op('act', 'activation', kst[kb][:, f1 % FB, :], pz[b][:, :], AF.Copy, r=[('pz', b)], w=[('kst', kb)])
                else:
                    S.op('dve', 'tensor_copy', kst[kb][:, f1 % FB, :], pz[b][:, :], r=[('pz', b)], w=[('kst', kb)])
                if f1 % FB == FB - 1:
                    f0 = f1 - FB + 1
                    S.dma('sp', hy['kf'][:, f0:f0 + FB, :, :].rearrange("p f r c -> p (f r c)"),
                          kst[kb][:, :, :].rearrange("p f x -> p (f x)"), r=[('kst', kb)])
            S.flush()

    def p4_hyena(self, l):
        for nm, t0 in (('x', 0), ('c', L)):
            if nm == 'c' and l == DEPTH - 1:
                continue
            self._hyena(l, nm, t0)

    def _hyena(self, l, nm, t0):
        nc, S = self.nc, self.S
        hy = self.hy[nm]; N1 = hy['N1']; Ls = hy['Ls']; N = hy['N']; K1 = N1 // 2
        hyY = self.hyY[nm]; hyF = self.hyF[nm]
        nblk = max(1, Ls // 512); bw = Ls // nblk
        with ExitStack() as es:
            sb = lambda n, shp, dt=F32: es.enter_context(nc.sbuf_tensor(self.uq(n), shp, dt))
            pst = lambda n, shp, dt=F32: es.enter_context(nc.psum_tensor(self.uq(n), shp, dt))
            uh = sb('uh', [128, 6, Ls], BF16)
            S.dma('sp', uh[:], self.pxT[7:13, :, t0:t0 + Ls].rearrange("c p t -> p c t"), w=['uh'])
            idb = sb('idb', [128, 128], BF16); S.dma('sp', idb[:], self.I('ident_bf'), w=['idb'])
            w1h = sb('w1h', [N1, 2 * N1], BF16); S.dma('sp', w1h[:], self.I('hy_w1' + nm), w=['w1h'])
            zT = sb('zT', [128, 2, Ls], BF16); x0 = sb('x0', [128, 2, Ls], BF16)
            o = [sb('o%d' % c, [128, bw]) for c in range(6)]
            zs = sb('zs', [K1, 128, 256], BF16)
            cw = lambda tap, c: self.cols[:, C_HCW + tap * 6 + c:C_HCW + tap * 6 + c + 1]
            for blk in range(nblk):
                c0 = blk * bw
                for c in range(6):
                    S.op('act', 'activation', o[c][:, :], uh[:, c, c0:c0 + bw], AF.Identity,
                         bias=self.cols[:, C_HCB + c:C_HCB + c + 1], scale=cw(1, c), r=['uh', 'cols'], w=[('o', c)])
                    lo = 1 if blk == 0 else 0
                    S.op('dve', 'scalar_tensor_tensor', o[c][:, lo:bw], uh[:, c, c0 + lo - 1:c0 + bw - 1], cw(0, c), o[c][:, lo:bw],
                         ALU.mult, ALU.add, r=['uh', 'cols', ('o', c)], w=[('o', c)])
                    hi = bw - 1 if blk == nblk - 1 else bw
                    S.op('dve', 'scalar_tensor_tensor', o[c][:, 0:hi], uh[:, c, c0 + 1:c0 + hi + 1], cw(2, c), o[c][:, 0:hi],
                         ALU.mult, ALU.add, r=['uh', 'cols', ('o', c)], w=[('o', c)])
                for cc in range(2):
                    S.op('pool', 'tensor_copy', x0[:, cc, c0:c0 + bw], o[cc][:, :], r=[('o', cc)], w=['x0'])
                    S.op('pool', 'tensor_tensor', zT[:, cc, c0:c0 + bw], o[2 + cc][:, :], o[4 + cc][:, :], ALU.mult,
                         r=[('o', 2 + cc), ('o', 4 + cc)], w=['zT'])
            S.dma('sp', self.hx0[:, :, t0:t0 + Ls].rearrange("c p t -> p c t"), x0[:], r=['x0'])
            S.dma('sp', self.hz[:, :, t0:t0 + Ls].rearrange("c p t -> p c t"), zT[:], r=['zT'])
            ptr = [pst('ptr%d' % i, [128, 1024], BF16) for i in range(2)]
            nb = 0
            for s20 in range(0, 128, 4):
                b = nb % 2; nb += 1
                calls = []
                for sl in range(4):
                    for cc in range(2):
                        calls.append(('transpose', (ptr[b][0:K1, (sl * 2 + cc) * 128:(sl * 2 + cc + 1) * 128],
                                                    zT[:, cc, s20 + sl::128], idb[:, :]), {}))
                S.multi('pe', calls, r=['zT', 'idb'], w=[('ptr', b)])
                dst = zs[:, s20:s20 + 4, :].rearrange("p a c -> p (a c)")
                if b == 0:
                    S.op('act', 'activation', dst, ptr[b][0:K1, :], AF.Copy, r=[('ptr', b)], w=['hsrc'])
                else:
                    S.op('dve', 'tensor_copy', dst, ptr[b][0:K1, :], r=[('ptr', b)], w=['hsrc'])
            self._hy_f1(zs, K1, N1, w1h, hyY, sb, pst)
            S.flush()
        for cc in range(2):
            with ExitStack() as es:
                sb = lambda n, shp, dt=F32: es.enter_context(nc.sbuf_tensor(self.uq(n), shp, dt))
                pst = lambda n, shp, dt=F32: es.enter_context(nc.psum_tensor(self.uq(n), shp, dt))
                Yz = sb('Yz', [128, 128, 2 * N1], BF16); S.dma('sp', Yz[:], hyY[:, cc * 128:(cc + 1) * 128, :], w=['Yz'])
                tr = sb('tr', [128, N], BF16); ti = sb('ti', [128, N], BF16); nti = sb('nti', [128, N], BF16)
                S.dma('sp', tr[:], self.I('hy_tr' + nm), w=['tr']); S.dma('sp', ti[:], self.I('hy_ti' + nm), w=['ti'])
                S.dma('sp', nti[:], self.I('hy_nti' + nm), w=['nti'])
                kf = sb('kf', [128, N1, 2, 256], BF16); S.dma('sp', kf[:], hy['kf'], w=['kf'])
                Yf = sb('Yf', [128, N1, 2, 128], BF16)
                FB = min(4, N1)
                ta = [sb('ta%d' % i, [128, FB, 2, 128]) for i in range(2)]; tb = [sb('tb%d' % i, [128, FB, 2, 128]) for i in range(2)]
                pz = [pst('pz%d' % i, [128, FB * 256]) for i in range(2)]
                ccs = slice(cc * 128, (cc + 1) * 128)
                for st_ in range(N1 // FB):
                    f0 = st_ * FB; b = st_ % 2
                    calls = []
                    for fl in range(FB):
                        calls += self._hy_f2(Yz, 128, N1, tr, ti, nti, pz[b][:, fl * 256:(fl + 1) * 256], f0 + fl)
                    S.multi('pe', calls, r=['Yz', 'tr', 'ti', 'nti'], w=[('pz', b)])
                    pzv = pz[b][:, :].rearrange("p (f r c) -> p f r c", f=FB, r=2)
                    S.op('dve', 'tensor_tensor', ta[b][:, :, :, :], pzv, kf[:, f0:f0 + FB, :, ccs], ALU.mult, r=[('pz', b), 'kf'], w=[('ta', b)])
                    S.op('dve', 'tensor_tensor', tb[b][:, :, 0, :], pzv[:, :, 0, :], kf[:, f0:f0 + FB, 1, ccs], ALU.mult, r=[('pz', b), 'kf'], w=[('tb', b)])
                    S.op('dve', 'tensor_tensor', tb[b][:, :, 1, :], pzv[:, :, 1, :], kf[:, f0:f0 + FB, 0, ccs], ALU.mult, r=[('pz', b), 'kf'], w=[('tb', b)])
                    S.op('pool', 'tensor_tensor', Yf[:, f0:f0 + FB, 0, :], ta[b][:, :, 0, :], ta[b][:, :, 1, :], ALU.subtract, r=[('ta', b)], w=['Yf'])
                    S.op('pool', 'tensor_tensor', Yf[:, f0:f0 + FB, 1, :], tb[b][:, :, 0, :], tb[b][:, :, 1, :], ALU.add, r=[('tb', b)], w=['Yf'])
                S.dma('sp', hyF, Yf[:], r=['Yf'])
                S.flush()
            with ExitStack() as es:
                sb = lambda n, shp, dt=F32: es.enter_context(nc.sbuf_tensor(self.uq(n), shp, dt))
                pst = lambda n, shp, dt=F32: es.enter_context(nc.psum_tensor(self.uq(n), shp, dt))
                Yf = sb('Yf', [128, N1, 2, 128], BF16); S.dma('sp', Yf[:], hyF, w=['Yf'])
                i1a = sb('i1a', [128, 256], BF16); i1b = sb('i1b', [128, 256], BF16)
                S.dma('sp', i1a[:], self.I('hy_i1a' + nm), w=['i1a']); S.dma('sp', i1b[:], self.I('hy_i1b' + nm), w=['i1b'])
                er = sb('er', [N1, Ls], BF16); nei = sb('nei', [N1, Ls], BF16)
                S.dma('sp', er[:], self.I('hy_er' + nm), w=['er']); S.dma('sp', nei[:], self.I('hy_nei' + nm), w=['nei'])
                V = sb('V', [N1, 128, 2, 128], BF16)
                yT = sb('yT', [128, Ls]); x0h = sb('x0h', [128, Ls], BF16); zh = sb('zh', [128, Ls], BF16)
                S.dma('sp', x0h[:], self.hx0[cc, :, t0:t0 + Ls], w=['x0h']); S.dma('sp', zh[:], self.hz[cc, :, t0:t0 + Ls], w=['zh'])
                ssb = sb('ssb', [128, 2]); S.dma('sp', ssb[:], hy['ssum'], w=['ssb'])
                S.op('dve', 'reciprocal', ssb[:, :], ssb[:, :], r=['ssb'], w=['ssb'])
                res = sb('res', [128, Ls], BF16)
                pv = [pst('pv%d' % i, [128, 512]) for i in range(2)]
                py = [pst('py%d' % i, [128, 512]) for i in range(2)]
                nb = 0
                for ch0 in range(0, 128, 2):
                    b = nb % 2; nb += 1
                    calls = []
                    for chl in range(2):
                        o_ = pv[b][0:N1, chl * 256:(chl + 1) * 256]
                        calls.append(('matmul', (o_, Yf[:, :, 0, ch0 + chl], i1a[:, :]), dict(start=True, stop=False)))
                        calls.append(('matmul', (o_, Yf[:, :, 1, ch0 + chl], i1b[:, :]), dict(start=False, stop=True)))
                    S.multi('pe', calls, r=['Yf', 'i1a', 'i1b'], w=[('pv', b)])
                    dst = V[:, ch0:ch0 + 2, :, :].rearrange("p c r a -> p (c r a)")
                    if b == 0:
                        S.op('act', 'activation', dst, pv[b][0:N1, :], AF.Copy, r=[('pv', b)], w=['V'])
                    else:
                        S.op('dve', 'tensor_copy', dst, pv[b][0:N1, :], r=[('pv', b)], w=['V'])
                nta = min(128, 512 // K1)
                nb = 0
                for ta0 in range(0, 128, nta):
                    b = nb % 2; nb += 1
                    calls = []
                    for al in range(nta):
                        ta_ = ta0 + al
                        o_ = py[b][:, al * K1:(al + 1) * K1]
                        calls.append(('matmul', (o_, V[:, :, 0, ta_], er[:, ta_::128]), dict(start=True, stop=False)))
                        calls.append(('matmul', (o_, V[:, :, 1, ta_], nei[:, ta_::128]), dict(start=False, stop=True)))
                    S.multi('pe', calls, r=['V', 'er', 'nei'], w=[('py', b)])
                    dst = yT[:, :].rearrange("p (b a) -> p a b", a=128)[:, ta0:ta0 + nta, :]
                    src = py[b][:, 0:nta * K1].rearrange("p (a b) -> p a b", b=K1)
                    if b == 0:
                        S.op('act', 'activation', dst, src, AF.Copy, r=[('py', b)], w=['yT'])
                    else:
                        S.op('dve', 'tensor_copy', dst, src, r=[('py', b)], w=['yT'])
                S.op('act', 'activation', yT[:, :], yT[:, :], AF.Identity, scale=ssb[:, cc:cc + 1], r=['yT', 'ssb'], w=['yT'])
                S.op('dve', 'scalar_tensor_tensor', yT[:, :], zh[:, :], self.cols[:, C_HD + cc:C_HD + cc + 1], yT[:, :],
                     ALU.mult, ALU.add, r=['zh', 'yT', 'cols'], w=['yT'])
                S.op('dve', 'tensor_tensor', res[:, :], yT[:, :], x0h[:, :], ALU.mult, r=['yT', 'x0h'], w=['res'])
                S.dma('sp', self.mixT[6 + cc, :, t0:t0 + Ls], res[:, :], r=['res'])
                S.flush()


def build_program(debug=False, stop_after=None):
    P = Prog(debug=debug)
    steps = []
    for l in range(DEPTH):
        steps += [('p0', l), ('p1', l), ('pk', l), ('p2', l), ('p3', l), ('p4', l), ('p5', l)]
    for (nm, l) in steps:
        fn = getattr(P, {'p0': 'p0_mod', 'p1': 'p1_inproj', 'pk': 'pk_filters', 'p2': 'p2_attn',
                         'p3': 'p3_fnet', 'p4': 'p4_hyena', 'p5': 'p5_out_mlp'}[nm])
        fn(l)
        if stop_after is not None and (nm, l) == stop_after:
            break
    P.es.close()
    P.nc.used_inputs = list(P._in.keys())
    return P.nc


_PROG = None


def kernel(**inputs):
    global _PROG
    per = _prep(inputs)
    if _PROG is None:
        _PROG = build_program()
    per = [{k: d[k] for k in _PROG.used_inputs} for d in per]
    res = run_bass_kernel_spmd(_PROG, per, core_ids=list(range(8)))
    return np.stack([np.asarray(r['y'], np.float32) for r in res.results], axis=0)
```

```python
import math
import numpy as np
import ml_dtypes
from contextlib import ExitStack
import concourse.bass as bass
import concourse.mybir as mybir
from concourse.bass_utils import run_bass_kernel_spmd

F32 = mybir.dt.float32
BF16 = mybir.dt.bfloat16
AF = mybir.ActivationFunctionType
ALU = mybir.AluOpType

D = 1024
L = 4096
LC = 256
T = L + LC
NT = T // 128
DEPTH = 2
H = 8
OFF_Q, OFF_KV, OFF_KR, OFF_F, OFF_H, IN_W = 0, 384, 640, 672, 928, 1696
EPS = 1e-6
NCOL = 64
C_N1G, C_N2G, C_QG, C_KVG, C_HCW, C_HCB, C_HD, C_HB1, C_HFR, C_HB2 = 0, 8, 16, 19, 21, 39, 45, 47, 48, 49


class _Op:
    __slots__ = ('eng', 'calls', 'deps', 'signal', 'sem', 'val', 'dma', 'waits', 'pre')


class Sched:
    NDMA = 8
    ATTR = [('pe', 'tensor'), ('act', 'scalar'), ('dve', 'vector'), ('pool', 'gpsimd'), ('sp', 'sync')]

    def __init__(self, nc, es):
        self.nc = nc
        self.sem = {e: es.enter_context(nc.semaphore('s_' + e)) for e in ['pe', 'act', 'dve', 'pool']}
        self.dsem = {q: [es.enter_context(nc.semaphore('d_%s%d' % (q, i))) for i in range(self.NDMA)]
                     for q in ['sp', 'pool', 'act']}
        self.cnt = {e: 0 for e in self.sem}
        self.dcnt = {q: 0 for q in self.dsem}
        self._reset()

    def _reset(self):
        self.ops = {e: [] for e, _ in self.ATTR}
        self.last_w = {}
        self.readers = {}

    def multi(self, eng, calls, r=(), w=(), dma=False):
        op = _Op()
        op.eng = eng; op.calls = calls; op.dma = dma; op.signal = False; op.pre = None
        deps = []
        for k in r:
            d = self.last_w.get(k)
            if d is not None: deps.append(d)
        for k in w:
            d = self.last_w.get(k)
            if d is not None: deps.append(d)
            deps.extend(self.readers.get(k, ()))
        if eng == 'pe':
            deps = [d for d in deps if d.eng != 'pe' or d.dma]
        op.deps = [d for d in deps if d is not op]
        for d in op.deps: d.signal = True
        for k in r: self.readers.setdefault(k, []).append(op)
        for k in w:
            self.last_w[k] = op
            self.readers[k] = []
        self.ops[eng].append(op)
        return op

    def op(self, eng, name, *args, r=(), w=(), **kw):
        return self.multi(eng, [(name, args, kw)], r=r, w=w)

    def dma(self, q, out, in_, r=(), w=(), **kw):
        return self.multi(q, [('dma_start', (out, in_), kw)], r=r, w=w, dma=True)

    def flush(self):
        nc = self.nc
        for e, _ in self.ATTR:
            ops = self.ops[e]
            for op in reversed(ops):
                if not op.dma:
                    op.signal = True
                    break
            for op in ops:
                if op.dma:
                    j = self.dcnt[e]; self.dcnt[e] += 1
                    op.sem = self.dsem[e][j % self.NDMA]
                    op.val = 16 * (j // self.NDMA + 1)
                    op.pre = (op.sem, op.val - 16) if op.val > 16 else None
                elif op.signal:
                    self.cnt[e] += 1
                    op.sem = self.sem[e]; op.val = self.cnt[e]
        finals = {}
        for e, _ in self.ATTR:
            seen = {}
            for op in self.ops[e]:
                need = {}
                if op.pre is not None: need[id(op.pre[0])] = op.pre
                for d in op.deps:
                    cur = need.get(id(d.sem))
                    if cur is None or cur[1] < d.val: need[id(d.sem)] = (d.sem, d.val)
                op.waits = []
                for k, (s, v) in need.items():
                    if seen.get(k, -1) < v:
                        seen[k] = v; op.waits.append((s, v))
                if op.dma or op.signal:
                    cur = finals.get(id(op.sem))
                    if cur is None or cur[1] < op.val: finals[id(op.sem)] = (op.sem, op.val)
        with nc.Block() as blk:
            for e, attr in self.ATTR:
                ops = self.ops[e]
                if not ops and e != 'sp': continue

                def body(eng, ops=ops, e=e):
                    for op in ops:
                        for (s, v) in op.waits: eng.wait_ge(s, v)
                        inst = None
                        for (name, args, kw) in op.calls:
                            inst = getattr(eng, name)(*args, **kw)
                        if op.dma: inst.then_inc(op.sem, 16)
                        elif op.signal: inst.then_inc(op.sem, 1)
                    if e == 'sp':
                        for (s, v) in finals.values(): eng.wait_ge(s, v)
                getattr(blk, attr)(body)
        self._reset()


def _bf(a):
    return np.ascontiguousarray(a.astype(ml_dtypes.bfloat16))


def _f32(a):
    return np.ascontiguousarray(a.astype(np.float32))


ROPE_PERM = np.array(list(range(8, 16)) + list(range(0, 8)) + list(range(24, 32)) + list(range(16, 24)))


def _rope_tables():
    rows = L // 64
    row = np.repeat(np.arange(rows, dtype=np.float64), 64)
    col = np.tile(np.arange(64, dtype=np.float64), rows)
    inv = 10000.0 ** (-np.arange(0, 16, 2, dtype=np.float64) / 16)
    ar = row[None, :] * inv[:, None]
    ac = col[None, :] * inv[:, None]
    cos = np.ones((32, T)); sin = np.zeros((32, T))
    cos[0:8, :L] = np.cos(ar); cos[8:16, :L] = np.cos(ar); cos[16:24, :L] = np.cos(ac); cos[24:32, :L] = np.cos(ac)
    sin[0:8, :L] = -np.sin(ar); sin[8:16, :L] = np.sin(ar); sin[16:24, :L] = -np.sin(ac); sin[24:32, :L] = np.sin(ac)
    return _f32(cos), _f32(sin)


def _fnet_tables(Ls, L1, L2):
    w = np.arange(64)
    ph = 2 * np.pi * np.outer(w, w) / 64
    cw = np.zeros((128, 128)); sw = np.zeros((128, 128))
    for g in range(2):
        cw[g * 64:(g + 1) * 64, g * 64:(g + 1) * 64] = np.cos(ph)
        sw[g * 64:(g + 1) * 64, g * 64:(g + 1) * 64] = np.sin(ph)
    csw = np.concatenate([cw, sw], axis=1)
    a = np.arange(L1)
    p1 = 2 * np.pi * np.outer(a, a) / L1
    wr, wi = np.cos(p1), -np.sin(p1)
    w1 = np.concatenate([wr, wi], axis=1)
    w2 = np.concatenate([wi, -wr], axis=1)
    p2 = 2 * np.pi * np.outer(np.arange(L2), np.arange(Ls)) / Ls
    sc = 1.0 / math.sqrt(Ls * 64)
    return _bf(csw), _bf(w1), _bf(w2), _bf(np.cos(p2) * sc), _bf(np.sin(p2) * sc)


def _hyena_tables(Ls):
    N = 2 * Ls
    N1 = N // 128
    a = np.arange(N1)
    p1 = 2 * np.pi * np.outer(a, a) / N1
    w1 = np.concatenate([np.cos(p1), -np.sin(p1)], axis=1)
    p2 = 2 * np.pi * np.outer(np.arange(128), np.arange(N)) / N
    tr, ti, nti = np.cos(p2), -np.sin(p2), np.sin(p2)
    p3 = 2 * np.pi * np.outer(np.arange(128), np.arange(128)) / 128
    i1a = np.concatenate([np.cos(p3), np.sin(p3)], axis=1)
    i1b = np.concatenate([-np.sin(p3), np.cos(p3)], axis=1)
    p4 = 2 * np.pi * np.outer(np.arange(N1), np.arange(Ls)) / N
    er, nei = np.cos(p4) / N, -np.sin(p4) / N
    j = np.arange(N)
    tau = np.where(j < Ls, j, N - j).astype(np.float64)
    tau[Ls] = 0
    tl = np.linspace(0.0, 1.0, Ls)
    t = tl[tau.astype(np.int64)]
    fr = np.linspace(1e-4, 15.0, 16)
    ang = 2.0 * np.pi * tau[:, None] / Ls * fr[None, :]
    z = np.concatenate([t[:, None], np.cos(ang), -np.sin(ang)], axis=1)
    negt = -t.copy()
    negt[Ls] = -1e4
    tcol = negt.reshape(N1, 128)
    deltas = np.abs(np.linspace(math.log(1e-2) / 1.5, math.log(1e-2) / 0.3, 256))
    drow = np.broadcast_to(deltas[None, :], (128, 256))
    return dict(N1=N1, w1=_bf(w1), tr=_bf(tr), ti=_bf(ti), nti=_bf(nti), i1a=_bf(i1a), i1b=_bf(i1b),
                er=_bf(er), nei=_bf(nei), zT=_f32(z.T), tcol=_f32(tcol), drow=_f32(drow))


_CONST = None


def _consts():
    global _CONST
    if _CONST is not None:
        return _CONST
    c = {}
    c['ident_bf'] = _bf(np.eye(128))
    c['ident_f'] = _f32(np.eye(128))
    c['ones_f'] = _f32(np.ones((128, 128)))
    c['ropeC'], c['ropeS'] = _rope_tables()
    for nm, Ls, L1, L2 in (('x', L, 64, 64), ('c', LC, 4, 64)):
        csw, w1, w2, tr, sn = _fnet_tables(Ls, L1, L2)
        c['fn_csw'] = csw
        c['fn_w1' + nm], c['fn_w2' + nm], c['fn_tr' + nm], c['fn_sn' + nm] = w1, w2, tr, sn
        ht = _hyena_tables(Ls)
        for k, v in ht.items():
            if k != 'N1':
                c['hy_%s%s' % (k, nm)] = v
    _CONST = c
    return c


def _colpack(v, n):
    return np.asarray(v, np.float32).reshape(n, 128).T


def _prep(inp):
    c = dict(_consts())
    sh = {}
    w_in = np.asarray(inp['w_in'], np.float32)
    sh['w_inx'] = np.ascontiguousarray(np.concatenate([w_in, w_in[:, :, OFF_KR + ROPE_PERM]], axis=2))
    w_uq = np.asarray(inp['w_uq'], np.float32)
    sh['w_uq'] = np.ascontiguousarray(w_uq)
    wq = w_uq.reshape(DEPTH, 384, H, 96)
    sh['w_uqp'] = np.ascontiguousarray(
        np.concatenate([wq[..., :64], wq[..., 64:][..., ROPE_PERM]], axis=-1).reshape(DEPTH, 384, 768))
    wkv = np.asarray(inp['w_ukv'], np.float32).reshape(DEPTH, 256, H, 128)
    sh['w_ukvx'] = np.ascontiguousarray(
        np.concatenate([wkv[..., :64].reshape(DEPTH, 256, 512), wkv[..., 64:].reshape(DEPTH, 256, 512)], axis=2))
    for k in ('w_mod', 'w_out', 'w_mlp1', 'w_mlp2', 'hy_w1', 'hy_w2', 'hy_w3'):
        sh[k] = np.ascontiguousarray(np.asarray(inp[k], np.float32))
    sh['b_mod2'] = np.ascontiguousarray(np.repeat(np.asarray(inp['b_mod'], np.float32)[:, None, :], 2, axis=1))
    cols = np.zeros((DEPTH, 128, NCOL), np.float32)
    for l in range(DEPTH):
        cols[l, :, C_N1G:C_N1G + 8] = _colpack(inp['norm1_g'][l], 8)
        cols[l, :, C_N2G:C_N2G + 8] = _colpack(inp['norm2_g'][l], 8)
        cols[l, :, C_QG:C_QG + 3] = _colpack(inp['q_norm_g'][l], 3)
        cols[l, :, C_KVG:C_KVG + 2] = _colpack(inp['kv_norm_g'][l], 2)
        for tap in range(3):
            cols[l, :, C_HCW + tap * 6:C_HCW + tap * 6 + 6] = _colpack(inp['hy_conv_w'][l, tap], 6)
        cols[l, :, C_HCB:C_HCB + 6] = _colpack(inp['hy_conv_b'][l], 6)
        cols[l, :, C_HD:C_HD + 2] = _colpack(inp['hy_d'][l], 2)
        cols[l, :64, C_HB1] = inp['hy_b1'][l]
        cols[l, :64, C_HFR] = inp['hy_freq'][l]
        cols[l, :64, C_HB2] = inp['hy_b2'][l]
    sh['cols'] = cols
    sh['fgrow'] = np.ascontiguousarray(np.broadcast_to(np.asarray(inp['final_norm_g'], np.float32)[None, :], (128, D)))
    sh.update(c)
    per = []
    x = np.asarray(inp['x'], np.float32); ctx = np.asarray(inp['ctx'], np.float32)
    cc = np.asarray(inp['c'], np.float32); c_ctx = np.asarray(inp['c_ctx'], np.float32)
    for b in range(8):
        d = dict(sh)
        d['xin'] = np.ascontiguousarray(np.concatenate([x[b], ctx[b]], axis=0))
        cv = np.stack([_colpack(cc[b], 8), _colpack(c_ctx, 8)], axis=-1)
        d['cvec'] = np.ascontiguousarray(cv)
        per.append(d)
    return per


GROUPS = [(g * 512, 512, 0) for g in range(8)] + [(L, LC, 1)]
SCALE = 1.0 / math.sqrt(96.0)


class Prog:
    def __init__(self, debug=False):
        self.debug = debug
        self.nc = nc = bass.Bass("TRN2", target_bir_lowering=False)
        self.es = ExitStack()
        self.S = Sched(nc, self.es)
        self._in = {}
        okind = dict(kind="ExternalOutput") if debug else {}
        scr = lambda n, shp, dt=F32: nc.dram_tensor(n, list(shp), dt, **okind).ap()
        self.spec = dict(
            xin=([T, D], F32), cvec=([128, 8, 2], F32), w_mod=([DEPTH, D, 6 * D], F32), b_mod2=([DEPTH, 2, 6 * D], F32),
            w_inx=([DEPTH, D, 1728], F32), w_uq=([DEPTH, 384, 768], F32), w_uqp=([DEPTH, 384, 768], F32),
            w_ukvx=([DEPTH, 256, 1024], F32), w_out=([DEPTH, D, D], F32), w_mlp1=([DEPTH, D, 4 * D], F32),
            w_mlp2=([DEPTH, 4 * D, D], F32), hy_w1=([DEPTH, 33, 64], F32), hy_w2=([DEPTH, 64, 64], F32),
            hy_w3=([DEPTH, 64, 512], F32), cols=([DEPTH, 128, NCOL], F32), fgrow=([128, D], F32),
            ident_bf=([128, 128], BF16), ident_f=([128, 128], F32), ones_f=([128, 128], F32),
            ropeC=([32, T], F32), ropeS=([32, T], F32), fn_csw=([128, 256], BF16))
        self.fn = {}; self.hy = {}
        for nm, Ls, L1 in (('x', L, 64), ('c', LC, 4)):
            N = 2 * Ls; N1 = N // 128
            self.spec.update({'fn_w1' + nm: ([L1, 2 * L1], BF16), 'fn_w2' + nm: ([L1, 2 * L1], BF16),
                              'fn_tr' + nm: ([64, Ls], BF16), 'fn_sn' + nm: ([64, Ls], BF16),
                              'hy_w1' + nm: ([N1, 2 * N1], BF16), 'hy_tr' + nm: ([128, N], BF16),
                              'hy_ti' + nm: ([128, N], BF16), 'hy_nti' + nm: ([128, N], BF16),
                              'hy_i1a' + nm: ([128, 256], BF16), 'hy_i1b' + nm: ([128, 256], BF16),
                              'hy_er' + nm: ([N1, Ls], BF16), 'hy_nei' + nm: ([N1, Ls], BF16),
                              'hy_zT' + nm: ([33, N], F32), 'hy_tcol' + nm: ([N1, 128], F32),
                              'hy_drow' + nm: ([128, 256], F32)})
            self.fn[nm] = dict(L1=L1, Ls=Ls)
            self.hy[nm] = dict(N1=N1, Ls=Ls, N=N, kf=scr('kf' + nm, [128, N1, 2, 256], BF16),
                               ssum=scr('ssum' + nm, [128, 2]))
        self.y = nc.dram_tensor('y', [L, D], F32, kind="ExternalOutput").ap()
        self.xres = scr('xres', [T, D]); self.mrow = scr('mrow', [2, 6 * D])
        self.pxT = scr('pxT', [14, 128, T], BF16); self.mixT = scr('mixT', [8, 128, T], BF16)
        self.hx0 = scr('hx0', [2, 128, T], BF16); self.hz = scr('hz', [2, 128, T], BF16)
        self.xmid = scr('xmid', [T, D]); self.h2T = scr('h2T', [8, 128, T], BF16)
        self.hyY = {}; self.hyF = {}
        for nm_ in ('x', 'c'):
            n1_ = self.hy[nm_]['N1']
            self.hyY[nm_] = scr('hyY' + nm_, [128, 256, 2 * n1_], BF16)
            self.hyF[nm_] = scr('hyF' + nm_, [128, n1_, 2, 128], BF16)
        self.modc = self.es.enter_context(nc.sbuf_tensor('modc', [128, 2, 4, 8], F32))
        self.cols = self.es.enter_context(nc.sbuf_tensor('colsb', [128, NCOL], F32))

    def uq(self, n):
        self._uq = getattr(self, '_uq', 0) + 1
        return '%s_%d' % (n, self._uq)

    def I(self, name):
        if name not in self._in:
            shp, dt = self.spec[name]
            self._in[name] = self.nc.dram_tensor(name, list(shp), dt, kind="ExternalInput").ap()
        return self._in[name]

    def p0_mod(self, l):
        nc, S = self.nc, self.S
        with ExitStack() as es:
            sb = lambda n, shp, dt=F32: es.enter_context(nc.sbuf_tensor(self.uq(n), shp, dt))
            cv = sb('cv', [128, 8, 2]); sc = sb('sc', [128, 8, 2])
            wt = [sb('wt%d' % i, [128, 8, 512]) for i in range(2)]
            msb = sb('msb', [2, 6 * D]); bm = sb('bm', [2, 6 * D])
            ps = [es.enter_context(nc.psum_tensor(self.uq('ps%d' % i), [2, 512], F32)) for i in range(2)]
            S.dma('sp', cv[:], self.I('cvec'), w=['cv'])
            S.dma('sp', bm[:], self.I('b_mod2')[l], w=['bm'])
            S.dma('sp', self.cols[:], self.I('cols')[l], w=['cols'])
            S.op('act', 'activation', sc[:], cv[:], AF.Silu, r=['cv'], w=['sc'])
            wv = self.I('w_mod')[l].rearrange("(k p) n -> p k n", p=128)
            for n in range(12):
                b = n % 2
                S.dma('sp', wt[b][:], wv[:, :, n * 512:(n + 1) * 512], w=[('wt', b)])
                S.multi('pe', [('matmul', (ps[b][:], sc[:, k, :], wt[b][:, k, :]), dict(start=(k == 0), stop=(k == 7)))
                               for k in range(8)], r=['sc', ('wt', b)], w=[('ps', b)])
                S.op('dve', 'tensor_tensor', msb[:, n * 512:(n + 1) * 512], ps[b][:], bm[:, n * 512:(n + 1) * 512],
                     ALU.add, r=[('ps', b), 'bm'], w=['msb'])
            S.dma('sp', self.mrow, msb[:], r=['msb'])
            S.flush()
            mT = sb('mT', [96, 128]); idf = sb('idf', [128, 128]); mcol = sb('mcol', [128, 96])
            pm = es.enter_context(nc.psum_tensor(self.uq('pm'), [128, 96], F32))
            S.dma('sp', mT[:], self.mrow.rearrange("s (j p) -> (s j) p", p=128), w=['mT'])
            S.dma('sp', idf[:], self.I('ident_f'), w=['idf'])
            S.op('pe', 'transpose', pm[:], mT[:], idf[0:96, 0:96], r=['mT', 'idf'], w=['pm'])
            S.op('dve', 'tensor_copy', mcol[:], pm[:], r=['pm'], w=['mcol'])
            for s in range(2):
                o = s * 48
                S.op('dve', 'scalar_tensor_tensor', self.modc[:, s, 0, :], mcol[:, o + 8:o + 16], 1.0,
                     self.cols[:, C_N1G:C_N1G + 8], ALU.add, ALU.mult, r=['mcol', 'cols'], w=['modc'])
                S.op('dve', 'tensor_copy', self.modc[:, s, 1, :], mcol[:, o:o + 8], r=['mcol'], w=['modc'])
                S.op('dve', 'scalar_tensor_tensor', self.modc[:, s, 2, :], mcol[:, o + 32:o + 40], 1.0,
                     self.cols[:, C_N2G:C_N2G + 8], ALU.add, ALU.mult, r=['mcol', 'cols'], w=['modc'])
                S.op('dve', 'tensor_copy', self.modc[:, s, 3, :], mcol[:, o + 24:o + 32], r=['mcol'], w=['modc'])
            S.flush()

    def norm_tiles(self, xt, nj, s, which, tl, key, n):
        S = self.S
        ss, rstd, xn, junk, pt, hT, idb = tl['ss'], tl['rstd'], tl['xn'], tl['junk'], tl['pt'], tl['hT'], tl['idb']
        for j in range(nj):
            S.op('act', 'activation', junk[:], xt[:, j, :], AF.Square, accum_out=ss[:, j:j + 1],
                 r=[key], w=['junk', tl['k'] + 'ss'])
        S.op('act', 'activation', rstd[:, 0:nj], ss[:, 0:nj], AF.Sqrt, bias=tl['eps'][:, 0:1], scale=1.0 / D,
             r=[tl['k'] + 'ss', 'eps'], w=[tl['k'] + 'rstd'])
        S.op('dve', 'reciprocal', rstd[:, 0:nj], rstd[:, 0:nj], r=[tl['k'] + 'rstd'], w=[tl['k'] + 'rstd'])
        for j in range(nj):
            S.op('dve', 'tensor_scalar', xn[:, j, :], xt[:, j, :], rstd[:, j:j + 1], None, ALU.mult,
                 r=[key, tl['k'] + 'rstd'], w=[tl['k'] + 'xn'])
        for k in range(8):
            pb = k % 2
            S.multi('pe', [('transpose', (pt[pb][:, j * 128:(j + 1) * 128], xn[:, j, k * 128:(k + 1) * 128], idb[:]), {})
                           for j in range(nj)], r=[tl['k'] + 'xn', 'idb'], w=[('pt', pb)])
            if k % 2 == 0:
                S.op('dve', 'tensor_scalar', hT[:, k, 0:n], pt[pb][:, 0:n], self.modc[:, s, which, k:k + 1],
                     self.modc[:, s, which + 1, k:k + 1], ALU.mult, ALU.add, r=[('pt', pb), 'modc'], w=[tl['k'] + 'hT'])
            else:
                S.op('act', 'activation', hT[:, k, 0:n], pt[pb][:, 0:n], AF.Identity,
                     bias=self.modc[:, s, which + 1, k:k + 1], scale=self.modc[:, s, which, k:k + 1],
                     r=[('pt', pb), 'modc'], w=[tl['k'] + 'hT'])

    def p1_inproj(self, l):
        nc, S = self.nc, self.S
        src = self.I('xin') if l == 0 else self.xres
        with ExitStack() as es:
            sb = lambda n, shp, dt=F32: es.enter_context(nc.sbuf_tensor(self.uq(n), shp, dt))
            pst = lambda n, shp, dt=F32: es.enter_context(nc.psum_tensor(self.uq(n), shp, dt))
            win = sb('win', [128, 8, 1728], BF16)
            S.dma('pool', win[:], self.I('w_inx')[l].rearrange("(k p) n -> p k n", p=128), w=['win'])
            idb = sb('idb', [128, 128], BF16); S.dma('sp', idb[:], self.I('ident_bf'), w=['idb'])
            onesf = sb('onesf', [128, 128]); S.dma('sp', onesf[:], self.I('ones_f'), w=['onesf'])
            rc = sb('rc', [32, T]); rs = sb('rs', [32, T])
            S.dma('sp', rc[:], self.I('ropeC'), w=['rc']); S.dma('sp', rs[:], self.I('ropeS'), w=['rs'])
            eps = sb('eps', [128, 1]); S.op('dve', 'memset', eps[:], EPS, w=['eps'])
            xg = [sb('xg%d' % i, [128, 4, D]) for i in range(2)]
            tls = []
            junk = sb('junk', [128, D], BF16)
            pt = [pst('pt%d' % i, [128, 512], BF16) for i in range(2)]
            for i in range(2):
                tls.append(dict(k='t%d' % i, ss=sb('ss%d' % i, [128, 4]), rstd=sb('rstd%d' % i, [128, 4]),
                                xn=sb('xn%d' % i, [128, 4, D], BF16), junk=junk, pt=pt,
                                hT=sb('hT%d' % i, [128, 8, 512], BF16), idb=idb, eps=eps))
            ost = [sb('ost%d' % i, [128, 14, 512], BF16) for i in range(2)]
            for i in range(2):
                S.op('pool', 'memset', ost[i][:, 13, :], 0.0, w=[('ost', i)])
            c32 = sb('c32', [128, 5, 512]); sq = sb('sq', [128, 5, 512]); rq = sb('rq', [128, 512])
            t1 = sb('t1', [32, 512]); t2 = sb('t2', [32, 512])
            po = [pst('po%d' % i, [128, 512]) for i in range(3)]
            pn = pst('pn', [128, 512])
            npo = [0]

            def proj(col0, ncols, hT, n, b):
                i = npo[0] % 3; npo[0] += 1
                S.multi('pe', [('matmul', (po[i][0:ncols, 0:n], win[:, k, col0:col0 + ncols], hT[:, k, 0:n]),
                                dict(start=(k == 0), stop=(k == 7))) for k in range(8)],
                        r=['win', 't%dhT' % b], w=[('po', i)])
                return i

            def front(gi):
                t0, n, s = GROUPS[gi]
                b = gi % 2; nj = n // 128; tl = tls[b]
                S.dma('sp', xg[b][:, 0:nj, :], src[t0:t0 + n, :].rearrange("(j p) d -> p j d", p=128), w=[('xg', b)])
                self.norm_tiles(xg[b], nj, s, 0, tl, ('xg', b), n)

            def back(gi):
                t0, n, s = GROUPS[gi]
                b = gi % 2; nj = n // 128; tl = tls[b]
                hT = tl['hT']
                okey = ('ost', b)
                for (c0, nch, gcol, ci0, i0) in ((OFF_Q, 3, C_QG, 0, 0), (OFF_KV, 2, C_KVG, 3, 3)):
                    for c in range(nch):
                        i = proj(c0 + c * 128, 128, hT, n, b)
                        S.op('act', 'activation', c32[:, i0 + c, 0:n], po[i][:, 0:n], AF.Copy,
                             r=[('po', i)], w=[('c32', i0 + c)])
                        S.op('act', 'activation', sq[:, i0 + c, 0:n], po[i][:, 0:n], AF.Square,
                             r=[('po', i)], w=[('sq', i0 + c)])
                    S.multi('pe', [('matmul', (pn[:, 0:n], onesf[:], sq[:, i0 + c, 0:n]),
                                    dict(start=(c == 0), stop=(c == nch - 1))) for c in range(nch)],
                            r=['onesf'] + [('sq', i0 + c) for c in range(nch)], w=['pn'])
                    S.op('act', 'activation', rq[:, 0:n], pn[:, 0:n], AF.Sqrt, bias=eps[:, 0:1], scale=1.0 / (128 * nch),
                         r=['pn', 'eps'], w=['rq'])
                    S.op('dve', 'reciprocal', rq[:, 0:n], rq[:, 0:n], r=['rq'], w=['rq'])
                    for c in range(nch):
                        S.op('dve', 'scalar_tensor_tensor', ost[b][:, ci0 + c, 0:n], c32[:, i0 + c, 0:n],
                             self.cols[:, gcol + c:gcol + c + 1], rq[:, 0:n], ALU.mult, ALU.mult,
                             r=[('c32', i0 + c), 'rq', 'cols'], w=[okey])
                ia = proj(OFF_KR, 32, hT, n, b)
                ib = proj(IN_W, 32, hT, n, b)
                S.op('dve', 'tensor_tensor', t1[:, 0:n], po[ia][0:32, 0:n], rc[:, t0:t0 + n], ALU.mult,
                     r=[('po', ia), 'rc'], w=['t1'])
                S.op('dve', 'tensor_tensor', t2[:, 0:n], po[ib][0:32, 0:n], rs[:, t0:t0 + n], ALU.mult,
                     r=[('po', ib), 'rs'], w=['t2'])
                S.op('dve', 'tensor_tensor', ost[b][0:32, 13, 0:n], t1[:, 0:n], t2[:, 0:n], ALU.add,
                     r=['t1', 't2'], w=[okey])
                for c in range(8):
                    i = proj(OFF_F + c * 128, 128, hT, n, b)
                    if c % 2 == 0:
                        S.op('act', 'activation', ost[b][:, 5 + c, 0:n], po[i][:, 0:n], AF.Copy, r=[('po', i)], w=[okey])
                    else:
                        S.op('dve', 'tensor_copy', ost[b][:, 5 + c, 0:n], po[i][:, 0:n], r=[('po', i)], w=[okey])
                S.dma('sp', self.pxT[:, :, t0:t0 + n].rearrange("c p t -> p c t"), ost[b][:, :, 0:n], r=[okey])

            front(0)
            for gi in range(len(GROUPS)):
                if gi + 1 < len(GROUPS):
                    front(gi + 1)
                back(gi)
            S.flush()

    def p2_attn(self, l):
        nc, S = self.nc, self.S
        with ExitStack() as es:
            sb = lambda n, shp, dt=F32: es.enter_context(nc.sbuf_tensor(self.uq(n), shp, dt))
            pst = lambda n, shp, dt=F32: es.enter_context(nc.psum_tensor(self.uq(n), shp, dt))
            cqn = sb('cqn', [128, 3, T], BF16); ckvn = sb('ckvn', [128, 2, T], BF16)
            S.dma('sp', cqn[:], self.pxT[0:3].rearrange("c p t -> p c t"), w=['cqn'])
            S.dma('sp', ckvn[:], self.pxT[3:5].rearrange("c p t -> p c t"), w=['ckvn'])
            wuq = sb('wuq', [128, 3, 768], BF16); wuqp = sb('wuqp', [128, 3, 768], BF16)
            wukv = sb('wukv', [128, 2, 1024], BF16)
            S.dma('pool', wuq[:], self.I('w_uq')[l].rearrange("(k p) n -> p k n", p=128), w=['wuq'])
            S.dma('pool', wuqp[:], self.I('w_uqp')[l].rearrange("(k p) n -> p k n", p=128), w=['wuqp'])
            S.dma('pool', wukv[:], self.I('w_ukvx')[l].rearrange("(k p) n -> p k n", p=128), w=['wukv'])
            onesf = sb('onesf', [128, 128]); S.dma('sp', onesf[:], self.I('ones_f'), w=['onesf'])
            tc_ = sb('tabc', [96, T]); ts_ = sb('tabs', [96, T])
            S.dma('sp', tc_[64:96, :], self.I('ropeC'), w=['tabc']); S.dma('sp', ts_[64:96, :], self.I('ropeS'), w=['tabs'])
            kt = [sb('kt%d' % i, [96, T], BF16) for i in range(2)]
            qt = [sb('qt%d' % i, [96, T], BF16) for i in range(2)]
            for b in range(2):
                S.dma('sp', kt[b][64:96, :], self.pxT[13, 0:32, :], w=[('ktr', b)])
            vh = [sb('vh%d' % i, [128, NT, 128], BF16) for i in range(2)]
            for i in range(2):
                S.op('pool', 'memset', vh[i][:], 1.0, w=[('vh', i)])
            t1 = sb('t1', [96, 512]); t2 = sb('t2', [96, 512])
            ptl = [sb('ptl%d' % i, [128, 2, 512], BF16) for i in range(3)]
            rr = sb('rr', [96, 512]); osb = [sb('osb%d' % i, [64, 512]) for i in range(2)]; ost = sb('ost', [64, T], BF16)
            sel = sb('sel', [128, 128], BF16); rb = sb('rb', [64, 512])
            S.op('pool', 'memset', sel[:], 0.0, w=['sel'])
            S.op('pool', 'memset', sel[64:65, :], 1.0, w=['sel'])
            rhi = [sb('rhi%d' % i, [128, 512], BF16) for i in range(2)]; rlo = [sb('rlo%d' % i, [128, 512], BF16) for i in range(2)]
            for i in range(2):
                S.op('pool', 'memset', rhi[i][:], 0.0, w=[('rhi', i)])
                S.op('pool', 'memset', rlo[i][:], 0.0, w=[('rlo', i)])
            ps = [pst('ps%d' % i, [128, 1024]) for i in range(2)]
            po = [pst('po%d' % i, [128, 512]) for i in range(2)]
            pq = pst('pq', [128, 512]); pq2 = pst('pq2', [128, 512]); pk = pq; pb = pq2
            LA = 2

            def project_chunks(h):
                b = h % 2
                chunks = []
                for gi, (t0, n, s) in enumerate(GROUPS):
                    def cA(gi=gi, t0=t0, n=n):
                        S.multi('pe', [('matmul', (pq[0:96, 0:n], wuq[:, c, h * 96:(h + 1) * 96], cqn[:, c, t0:t0 + n]),
                                        dict(start=(c == 0), stop=(c == 2))) for c in range(3)],
                                r=['cqn', 'wuq'], w=['pq'])
                        S.multi('pe', [('matmul', (pq2[0:96, 0:n], wuqp[:, c, h * 96:(h + 1) * 96], cqn[:, c, t0:t0 + n]),
                                        dict(start=(c == 0), stop=(c == 2))) for c in range(3)],
                                r=['cqn', 'wuqp'], w=['pq2'])
                        S.op('dve', 'tensor_copy', qt[b][0:64, t0:t0 + n], pq[0:64, 0:n], r=['pq'], w=[('qt', b, gi)])
                        S.op('dve', 'tensor_tensor', t1[64:96, 0:n], pq[64:96, 0:n], tc_[64:96, t0:t0 + n], ALU.mult,
                             r=['pq', 'tabc'], w=['t1'])
                        S.op('dve', 'tensor_tensor', t2[64:96, 0:n], pq2[64:96, 0:n], ts_[64:96, t0:t0 + n], ALU.mult,
                             r=['pq2', 'tabs'], w=['t2'])
                        S.op('dve', 'tensor_tensor', qt[b][64:96, t0:t0 + n], t1[64:96, 0:n], t2[64:96, 0:n], ALU.add,
                             r=['t1', 't2'], w=[('qt', b, gi)])

                    def cB(gi=gi, t0=t0, n=n):
                        S.multi('pe', [('matmul', (pk[:, 0:n], wukv[:, c, h * 64:h * 64 + 128], ckvn[:, c, t0:t0 + n]),
                                        dict(start=(c == 0), stop=(c == 1))) for c in range(2)],
                                r=['ckvn', 'wukv'], w=['pq'])
                        S.op('dve', 'tensor_copy', kt[b][0:64, t0:t0 + n], pk[0:64, 0:n], r=['pq'], w=[('kt', b, gi)])
                    chunks += [cA, cB]
                for i0 in range(0, NT, 8):
                    def cV(i0=i0):
                        nt_ = min(8, NT - i0)
                        calls = []
                        for ii_ in range(nt_):
                            i_ = i0 + ii_
                            calls += [('matmul', (pq2[:, ii_ * 64:(ii_ + 1) * 64], ckvn[:, c, i_ * 128:(i_ + 1) * 128],
                                                  wukv[:, c, 512 + h * 64:512 + (h + 1) * 64]), dict(start=(c == 0), stop=(c == 1)))
                                      for c in range(2)]
                        S.multi('pe', calls, r=['ckvn', 'wukv'], w=['pq2'])
                        S.op('dve', 'tensor_copy', vh[b][:, i0:i0 + nt_, 0:64],
                             pq2[:, 0:nt_ * 64].rearrange("p (a d) -> p a d", d=64), r=['pq2'], w=[('vh', b)])
                    chunks.append(cV)
                return chunks

            def project(h):
                for c_ in project_chunks(h):
                    c_()

            project(0)
            for h in range(H):
                b = h % 2
                seq = []
                for gi, (q0, nq, s) in enumerate(GROUPS):
                    tiles = list(range(NT)) if s == 0 else [32, 33]
                    npair = len(tiles) // 2
                    for ii in range(npair):
                        seq.append((gi, q0, nq, ii, tiles[2 * ii], npair))

                def qk(e):
                    gi, q0, nq, ii, i, np_ = seq[e]
                    sl = e % 2
                    S.multi('pe', [('matmul', (ps[sl][:, a_ * 512:a_ * 512 + nq], kt[b][0:96, (i + a_) * 128:(i + a_ + 1) * 128],
                                               qt[b][0:96, q0:q0 + nq]), dict(start=True, stop=True)) for a_ in range(2)],
                            r=[('kt', b, i // 4), ('kt', b, (i + 1) // 4), ('ktr', b), ('qt', b, gi)], w=[('ps', sl)])

                pending = []
                nxt = project_chunks(h + 1) if h + 1 < H else []
                cstep = max(1, (len(seq) - 12) // max(1, len(nxt)))
                for e in range(min(LA, len(seq))):
                    qk(e)
                for e in range(len(seq)):
                    gi, q0, nq, ii, i, np_ = seq[e]
                    sl = e % 2; sl2 = e % 3; ob = gi % 2
                    S.op('act', 'activation', ptl[sl2][:, :, 0:nq], ps[sl][:, :].rearrange("p (a n) -> p a n", a=2)[:, :, 0:nq],
                         AF.Exp, scale=SCALE, r=[('ps', sl)], w=[('ptl', sl2)])
                    if e + LA < len(seq):
                        qk(e + LA)
                    S.multi('pe', [('matmul', (po[ob][:, 0:nq], vh[b][:, i + a_, :], ptl[sl2][:, a_, 0:nq]),
                                    dict(start=(ii == 0 and a_ == 0), stop=(ii == np_ - 1 and a_ == 1))) for a_ in range(2)],
                            r=[('ptl', sl2), ('vh', b)], w=[('po', ob)])
                    if ii == np_ - 1:
                        fb = gi % 2
                        S.op('dve', 'tensor_copy', rhi[fb][64:65, 0:nq], po[ob][64:65, 0:nq], r=[('po', ob)], w=[('rhi', fb)])
                        S.op('dve', 'tensor_tensor', rlo[fb][64:65, 0:nq], po[ob][64:65, 0:nq], rhi[fb][64:65, 0:nq], ALU.subtract,
                             r=[('po', ob), ('rhi', fb)], w=[('rlo', fb)])
                        S.op('dve', 'tensor_copy', osb[fb][:, 0:nq], po[ob][0:64, 0:nq], r=[('po', ob)], w=[('osb', fb)])
                        pending.append((e + 3, fb, q0, nq))
                    while pending and (pending[0][0] <= e or e == len(seq) - 1):
                        _, fb, fq0, fnq = pending.pop(0)
                        S.multi('pe', [('matmul', (pb[:, 0:fnq], sel[:, :], rhi[fb][:, 0:fnq]), dict(start=True, stop=False)),
                                       ('matmul', (pb[:, 0:fnq], sel[:, :], rlo[fb][:, 0:fnq]), dict(start=False, stop=True))],
                                r=[('rhi', fb), ('rlo', fb), 'sel'], w=['pq2'])
                        S.op('dve', 'reciprocal', rb[:, 0:fnq], pb[0:64, 0:fnq], r=['pq2'], w=['rb'])
                        S.op('dve', 'tensor_tensor', ost[:, fq0:fq0 + fnq], osb[fb][:, 0:fnq], rb[:, 0:fnq], ALU.mult,
                             r=[('osb', fb), 'rb'], w=['ost'])
                    if nxt and e >= 4 and (e - 4) % cstep == 0:
                        nxt.pop(0)()
                while nxt:
                    nxt.pop(0)()
                S.dma('sp', self.mixT[h // 2, (h % 2) * 64:(h % 2) * 64 + 64, :], ost[:, :], r=['ost'])
            S.flush()

    def p5_out_mlp(self, l):
        nc, S = self.nc, self.S
        last = (l == DEPTH - 1)
        src = self.I('xin') if l == 0 else self.xres
        groups = GROUPS[:8] if last else GROUPS
        with ExitStack() as es:
            sb = lambda n, shp, dt=F32: es.enter_context(nc.sbuf_tensor(self.uq(n), shp, dt))
            pst = lambda n, shp, dt=F32: es.enter_context(nc.psum_tensor(self.uq(n), shp, dt))
            wout = sb('wout', [128, 8, D], BF16)
            S.dma('pool', wout[:], self.I('w_out')[l].rearrange("(k p) n -> p k n", p=128), w=['wout'])
            idb = sb('idb', [128, 128], BF16); S.dma('sp', idb[:], self.I('ident_bf'), w=['idb'])
            eps = sb('eps', [128, 1]); S.op('dve', 'memset', eps[:], EPS, w=['eps'])
            g1r = sb('g1r', [128, D])
            mixt = [sb('mixt%d' % i, [128, 8, 512], BF16) for i in range(2)]
            xt = [sb('xt%d' % i, [128, 4, D]) for i in range(2)]
            tmp = [sb('tmp%d' % i, [128, 512]) for i in range(2)]
            junk = sb('junk', [128, D], BF16)
            pt = [pst('pt%d' % i, [128, 512], BF16) for i in range(2)]
            tls = [dict(k='n%d' % i, ss=sb('ss%d' % i, [128, 4]), rstd=sb('rstd%d' % i, [128, 4]),
                        xn=sb('xn%d' % i, [128, 4, D], BF16), junk=junk, pt=pt,
                        hT=sb('hT%d' % i, [128, 8, 512], BF16), idb=idb, eps=eps) for i in range(2)]
            pso = [pst('pso%d' % i, [128, 512]) for i in range(4)]
            st = dict(cur_s=-1, np_=0)

            def stA(gi):
                t0, n, s = groups[gi]
                b = gi % 2; nj = n // 128
                if s != st['cur_s']:
                    S.dma('sp', g1r[:], self.mrow[s, 2 * D:3 * D].partition_broadcast(128), w=['g1r']); st['cur_s'] = s
                S.dma('sp', mixt[b][:, :, 0:n], self.mixT[:, :, t0:t0 + n].rearrange("c p t -> p c t"), w=[('mixt', b)])
                S.dma('sp', xt[b][:, 0:nj, :], src[t0:t0 + n, :].rearrange("(j p) d -> p j d", p=128), w=[('xt', b)])
                for j in range(nj):
                    for hh in range(2):
                        pi = st['np_'] % 4; st['np_'] += 1
                        S.multi('pe', [('matmul', (pso[pi][:, :], mixt[b][:, c, j * 128:(j + 1) * 128], wout[:, c, hh * 512:(hh + 1) * 512]),
                                        dict(start=(c == 0), stop=(c == 7))) for c in range(8)],
                                r=[('mixt', b), 'wout'], w=[('pso', pi)])
                        S.op('dve', 'tensor_tensor', tmp[pi % 2][:, :], pso[pi][:, :], g1r[:, hh * 512:(hh + 1) * 512], ALU.mult,
                             r=[('pso', pi), 'g1r'], w=[('tmp', pi % 2)])
                        S.op('pool', 'tensor_tensor', xt[b][:, j, hh * 512:(hh + 1) * 512], tmp[pi % 2][:, :],
                             xt[b][:, j, hh * 512:(hh + 1) * 512], ALU.add, r=[('tmp', pi % 2), ('xt', b)], w=[('xt', b)])
                S.dma('sp', self.xmid[t0:t0 + n, :].rearrange("(j p) d -> p j d", p=128), xt[b][:, 0:nj, :], r=[('xt', b)])

            def stB(gi):
                t0, n, s = groups[gi]
                b = gi % 2; nj = n // 128
                self.norm_tiles(xt[b], nj, s, 2, tls[b], ('xt', b), n)
                S.dma('sp', self.h2T[:, :, t0:t0 + n].rearrange("c p t -> p c t"), tls[b]['hT'][:, :, 0:n], r=['n%dhT' % b])

            stA(0)
            for gi in range(len(groups)):
                if gi + 1 < len(groups):
                    stA(gi + 1)
                stB(gi)
            S.flush()
        with ExitStack() as es:
            sb = lambda n, shp, dt=F32: es.enter_context(nc.sbuf_tensor(self.uq(n), shp, dt))
            pst = lambda n, shp, dt=F32: es.enter_context(nc.psum_tensor(self.uq(n), shp, dt))
            w1 = sb('w1', [128, 8, 4 * D], BF16); w2 = sb('w2', [128, 32, D], BF16)
            for hh in range(2):
                S.dma('pool', w1[:, :, hh * 2048:(hh + 1) * 2048],
                      self.I('w_mlp1')[l].rearrange("(k p) n -> p k n", p=128)[:, :, hh * 2048:(hh + 1) * 2048], w=[('w1', hh)])
            for q in range(4):
                S.dma('pool', w2[:, q * 8:(q + 1) * 8, :],
                      self.I('w_mlp2')[l].rearrange("(k p) n -> p k n", p=128)[:, q * 8:(q + 1) * 8, :], w=[('w2', q)])
            g2r = sb('g2r', [128, D])
            if last:
                fg = sb('fg', [128, D]); S.dma('sp', fg[:], self.I('fgrow'), w=['fg'])
                eps = sb('eps', [128, 1]); S.op('dve', 'memset', eps[:], EPS, w=['eps'])
                junk = sb('junk', [128, D], BF16); fss = sb('fss', [128, 4]); frs = sb('frs', [128, 4])
            hT = sb('hT', [128, 8, 512], BF16); fT = sb('fT', [128, 32, 512], BF16)
            x1 = sb('x1', [128, 4, D]); sq = [sb('sq%d' % i, [128, 512]) for i in range(2)]
            tmp = [sb('tmp%d' % i, [128, 512]) for i in range(2)]
            psf = [pst('psf%d' % i, [128, 512]) for i in range(3)]
            ps2 = [pst('ps2%d' % i, [128, 512]) for i in range(3)]
            cur_s = -1; n2 = 0
            t0_, n_, s_ = groups[0]
            S.dma('sp', hT[:, :, 0:n_], self.h2T[:, :, t0_:t0_ + n_].rearrange("c p t -> p c t"), w=['hT'])
            for gi, (t0, n, s) in enumerate(groups):
                nj = n // 128
                if s != cur_s:
                    S.dma('sp', g2r[:], self.mrow[s, 5 * D:6 * D].partition_broadcast(128), w=['g2r']); cur_s = s
                S.dma('sp', x1[:, 0:nj, :], self.xmid[t0:t0 + n, :].rearrange("(j p) d -> p j d", p=128),
                      w=[('x1', j) for j in range(nj)])
                for j in range(32):
                    pf = psf[j % 3]
                    S.multi('pe', [('matmul', (pf[:, 0:n], w1[:, k, j * 128:(j + 1) * 128], hT[:, k, 0:n]),
                                    dict(start=(k == 0), stop=(k == 7))) for k in range(8)],
                            r=[('w1', j // 16), 'hT'], w=[('psf', j % 3)])
                    S.op('act', 'activation', sq[j % 2][:, 0:n], pf[:, 0:n], AF.Square, r=[('psf', j % 3)], w=[('sq', j % 2)])
                    S.op('dve', 'scalar_tensor_tensor', fT[:, j, 0:n], pf[:, 0:n], 0.0, sq[j % 2][:, 0:n],
                         ALU.is_gt, ALU.mult, r=[('psf', j % 3), ('sq', j % 2)], w=[('fT', j)])
                if gi + 1 < len(groups):
                    t0n, nn, sn_ = groups[gi + 1]
                    S.dma('sp', hT[:, :, 0:nn], self.h2T[:, :, t0n:t0n + nn].rearrange("c p t -> p c t"), w=['hT'])
                for j in range(nj):
                    for hh in range(2):
                        pi = n2 % 3; n2 += 1
                        S.multi('pe', [('matmul', (ps2[pi][:, :], fT[:, q, j * 128:(j + 1) * 128], w2[:, q, hh * 512:(hh + 1) * 512]),
                                        dict(start=(q == 0), stop=(q == 31))) for q in range(32)],
                                r=[('fT', q) for q in range(32)] + [('w2', q) for q in range(4)], w=[('ps2', pi)])
                        S.op('dve', 'tensor_tensor', tmp[pi % 2][:, :], ps2[pi][:, :], g2r[:, hh * 512:(hh + 1) * 512], ALU.mult,
                             r=[('ps2', pi), 'g2r'], w=[('tmp', pi % 2)])
                        S.op('pool', 'tensor_tensor', x1[:, j, hh * 512:(hh + 1) * 512], tmp[pi % 2][:, :],
                             x1[:, j, hh * 512:(hh + 1) * 512], ALU.add, r=[('tmp', pi % 2), ('x1', j)], w=[('x1', j)])
                    tt = t0 + j * 128
                    if not last:
                        S.dma('sp', self.xres[tt:tt + 128, :], x1[:, j, :], r=[('x1', j)])
                    else:
                        S.op('act', 'activation', junk[:], x1[:, j, :], AF.Square, accum_out=fss[:, j:j + 1],
                             r=[('x1', j)], w=['junk', ('fss', j)])
                        S.op('act', 'activation', frs[:, j:j + 1], fss[:, j:j + 1], AF.Sqrt, bias=eps[:, 0:1], scale=1.0 / D,
                             r=[('fss', j), 'eps'], w=[('frs', j)])
                        S.op('dve', 'reciprocal', frs[:, j:j + 1], frs[:, j:j + 1], r=[('frs', j)], w=[('frs', j)])
                        S.op('dve', 'scalar_tensor_tensor', x1[:, j, :], x1[:, j, :], frs[:, j:j + 1], fg[:], ALU.mult, ALU.mult,
                             r=[('x1', j), ('frs', j), 'fg'], w=[('x1', j)])
                        S.dma('sp', self.y[tt:tt + 128, :], x1[:, j, :], r=[('x1', j)])
            S.flush()

    def p3_fnet(self, l):
        for nm, t0 in (('x', 0), ('c', L)):
            if nm == 'c' and l == DEPTH - 1:
                continue
            self._fnet(l, nm, t0)

    def _fnet(self, l, nm, t0):
        nc, S = self.nc, self.S
        L1 = self.fn[nm]['L1']; Ls = self.fn[nm]['Ls']; L2 = 64
        with ExitStack() as es:
            sb = lambda n, shp, dt=F32: es.enter_context(nc.sbuf_tensor(self.uq(n), shp, dt))
            pst = lambda n, shp, dt=F32: es.enter_context(nc.psum_tensor(self.uq(n), shp, dt))
            uf = sb('uf', [128, 2, Ls], BF16)
            S.dma('sp', uf[:], self.pxT[5:7, :, t0:t0 + Ls].rearrange("c p t -> p c t"), w=['uf'])
            csw = sb('csw', [128, 256], BF16); S.dma('sp', csw[:], self.I('fn_csw'), w=['csw'])
            w1 = sb('w1', [L1, 2 * L1], BF16); w2 = sb('w2', [L1, 2 * L1], BF16)
            S.dma('sp', w1[:], self.I('fn_w1' + nm), w=['w1']); S.dma('sp', w2[:], self.I('fn_w2' + nm), w=['w2'])
            tr = sb('tr', [64, Ls], BF16); sn = sb('sn', [64, Ls], BF16)
            S.dma('sp', tr[:], self.I('fn_tr' + nm), w=['tr']); S.dma('sp', sn[:], self.I('fn_sn' + nm), w=['sn'])
            U = sb('U', [L1, L2, 512], BF16)
            Y = sb('Y', [64, 2, 256, L1], BF16)
            pu = [pst('pu%d' % i, [128, 512]) for i in range(2)]
            py = [pst('py%d' % i, [128, 512]) for i in range(2)]
            pf = [pst('pf%d' % i, [128, 512]) for i in range(2)]
            for l2 in range(L2):
                b = l2 % 2
                S.multi('pe', [('matmul', (pu[b][0:L1, c * 256:(c + 1) * 256], uf[:, c, l2::L2], csw[:, :]),
                                dict(start=True, stop=True)) for c in range(2)], r=['uf', 'csw'], w=[('pu', b)])
                if b == 0:
                    S.op('act', 'activation', U[:, l2, :], pu[b][0:L1, :], AF.Copy, r=[('pu', b)], w=['U'])
                else:
                    S.op('dve', 'tensor_copy', U[:, l2, :], pu[b][0:L1, :], r=[('pu', b)], w=['U'])
            cpb = 512 // (2 * L1)
            nb = 0
            for ch0 in range(0, 256, cpb):
                b = nb % 2; nb += 1
                calls = []
                for chl in range(cpb):
                    ch = ch0 + chl; c = ch // 128; i = ch % 128
                    o = py[b][0:64, chl * 2 * L1:(chl + 1) * 2 * L1]
                    calls.append(('matmul', (o, U[:, :, c * 256 + i], w1[:, :]), dict(start=True, stop=False)))
                    calls.append(('matmul', (o, U[:, :, c * 256 + 128 + i], w2[:, :]), dict(start=False, stop=True)))
                S.multi('pe', calls, r=['U', 'w1', 'w2'], w=[('py', b)])
                for r_ in range(2):
                    src = py[b][0:64, 0:cpb * 2 * L1].rearrange("p (c r a) -> p c r a", c=cpb, r=2)[:, :, r_, :]
                    dst = Y[:, r_, ch0:ch0 + cpb, :]
                    if r_ == 0:
                        S.op('act', 'activation', dst, src, AF.Copy, r=[('py', b)], w=['Y'])
                    else:
                        S.op('dve', 'tensor_copy', dst, src, r=[('py', b)], w=['Y'])
            na = min(8, L1)
            nb = 0
            for cc in range(2):
                for a0 in range(0, L1, na):
                    b = nb % 2; nb += 1
                    calls = []
                    for al in range(na):
                        a_ = a0 + al
                        o = pf[b][:, al * 64:(al + 1) * 64]
                        calls.append(('matmul', (o, Y[:, 0, cc * 128:(cc + 1) * 128, a_], tr[:, a_::L1]), dict(start=True, stop=False)))
                        calls.append(('matmul', (o, Y[:, 1, cc * 128:(cc + 1) * 128, a_], sn[:, a_::L1]), dict(start=False, stop=True)))
                    S.multi('pe', calls, r=['Y', 'tr', 'sn'], w=[('pf', b)])
                    dst = uf[:, cc, :].rearrange("p (b a) -> p a b", a=L1)[:, a0:a0 + na, :]
                    src = pf[b][:, 0:na * 64].rearrange("p (a b) -> p a b", a=na)
                    if b == 0:
                        S.op('act', 'activation', dst, src, AF.Copy, r=[('pf', b), 'U'], w=['uf'])
                    else:
                        S.op('dve', 'tensor_copy', dst, src, r=[('pf', b), 'U'], w=['uf'])
            S.dma('sp', self.mixT[4:6, :, t0:t0 + Ls].rearrange("c p t -> p c t"), uf[:], r=['uf'])
            S.flush()

    def _hy_f1(self, src, krows, N1, w1h, hyY, es_sb, pst):
        S = self.S
        cpb = 512 // (2 * N1)
        py = [pst('hpy%d' % i, [128, 512]) for i in range(2)]
        nbank = 256 // cpb
        GB = min(4, nbank)
        stg = [es_sb('hstg%d' % i, [128, GB, 512], BF16) for i in range(2)]
        for nb in range(nbank):
            ch0 = nb * cpb
            b = nb % 2; sb_ = (nb // GB) % 2; g = nb % GB
            S.multi('pe', [('matmul', (py[b][:, chl * 2 * N1:(chl + 1) * 2 * N1], src[0:krows, :, ch0 + chl], w1h[0:krows, :]),
                            dict(start=True, stop=True)) for chl in range(cpb)], r=['hsrc', 'w1h'], w=[('hpy', b)])
            if b == 0:
                S.op('act', 'activation', stg[sb_][:, g, :], py[b][:, :], AF.Copy, r=[('hpy', b)], w=[('hstg', sb_)])
            else:
                S.op('dve', 'tensor_copy', stg[sb_][:, g, :], py[b][:, :], r=[('hpy', b)], w=[('hstg', sb_)])
            if g == GB - 1:
                c0 = (nb - GB + 1) * cpb
                S.dma('sp', hyY[:, c0:c0 + GB * cpb, :].rearrange("p c x -> p (c x)"),
                      stg[sb_][:, :, :].rearrange("p g x -> p (g x)"), r=[('hstg', sb_)])

    def _hy_f2(self, Yz, ncol, N1, tr, ti, nti, pz, f1):
        S = self.S
        calls = [('matmul', (pz[:, 0:ncol], tr[:, f1::N1], Yz[:, :, f1]), dict(start=True, stop=False)),
                 ('matmul', (pz[:, 0:ncol], nti[:, f1::N1], Yz[:, :, N1 + f1]), dict(start=False, stop=True)),
                 ('matmul', (pz[:, ncol:2 * ncol], ti[:, f1::N1], Yz[:, :, f1]), dict(start=True, stop=False)),
                 ('matmul', (pz[:, ncol:2 * ncol], tr[:, f1::N1], Yz[:, :, N1 + f1]), dict(start=False, stop=True))]
        return calls

    def _range_sin(self, dst, ps, fcol, fbcol, tl, n):
        S = self.S
        a, t, k = tl['a'], tl['t'], tl['k']
        S.op('dve', 'tensor_scalar', a[:, 0:n], ps, fcol, fbcol, ALU.mult, ALU.add, r=[tl['ps'], 'fcols'], w=['rs_a'])
        S.op('dve', 'tensor_scalar', t[:, 0:n], a[:, 0:n], 1.0 / (2 * math.pi), 12582912.0, ALU.mult, ALU.add, r=['rs_a'], w=['rs_t'])
        S.op('dve', 'tensor_scalar', k[:, 0:n], t[:, 0:n], -12582912.0, None, ALU.add, r=['rs_t'], w=['rs_k'])
        S.op('dve', 'scalar_tensor_tensor', a[:, 0:n], k[:, 0:n], -2 * math.pi, a[:, 0:n], ALU.mult, ALU.add, r=['rs_k', 'rs_a'], w=['rs_a'])
        S.op('dve', 'tensor_scalar', a[:, 0:n], a[:, 0:n], -3.14159, 3.14159, ALU.max, ALU.min, r=['rs_a'], w=['rs_a'])
        S.op('act', 'activation', dst, a[:, 0:n], AF.Sin, r=['rs_a'], w=[tl['dst']])

    def pk_filters(self, l):
        for nm in ('x', 'c'):
            if nm == 'c' and l == DEPTH - 1:
                continue
            self._filters(l, nm)

    def _filters(self, l, nm):
        nc, S = self.nc, self.S
        hy = self.hy[nm]; N1 = hy['N1']; Ls = hy['Ls']; N = hy['N']
        hyY = self.hyY[nm]
        with ExitStack() as es:
            sb = lambda n, shp, dt=F32: es.enter_context(nc.sbuf_tensor(self.uq(n), shp, dt))
            pst = lambda n, shp, dt=F32: es.enter_context(nc.psum_tensor(self.uq(n), shp, dt))
            ks = sb('ks', [N1, 128, 256], BF16)
            w1h = sb('w1h', [N1, 2 * N1], BF16); S.dma('sp', w1h[:], self.I('hy_w1' + nm), w=['w1h'])
            with ExitStack() as es2:
                sb2 = lambda n, shp, dt=F32: es2.enter_context(nc.sbuf_tensor(self.uq(n), shp, dt))
                ps2 = lambda n, shp, dt=F32: es2.enter_context(nc.psum_tensor(self.uq(n), shp, dt))
                wa = sb2('wa', [33, 64]); wb = sb2('wb', [64, 64]); wc = sb2('wc', [64, 512], BF16)
                S.dma('sp', wa[:], self.I('hy_w1')[l], w=['wa']); S.dma('sp', wb[:], self.I('hy_w2')[l], w=['wb'])
                S.dma('pool', wc[:], self.I('hy_w3')[l], w=['wc'])
                fc = sb2('fc', [64, 4])
                S.op('dve', 'tensor_copy', fc[:, 0:1], self.cols[0:64, C_HFR:C_HFR + 1], r=['cols'], w=['fcols'])
                S.op('dve', 'tensor_tensor', fc[:, 1:2], self.cols[0:64, C_HFR:C_HFR + 1], self.cols[0:64, C_HB1:C_HB1 + 1], ALU.mult, r=['cols'], w=['fcols'])
                S.op('dve', 'tensor_tensor', fc[:, 2:3], self.cols[0:64, C_HFR:C_HFR + 1], self.cols[0:64, C_HB2:C_HB2 + 1], ALU.mult, r=['cols'], w=['fcols'])
                zt = [sb2('zt%d' % i, [33, 512]) for i in range(2)]
                h1 = sb2('h1', [64, 512])
                hA = sb2('hA', [64, N], BF16); hB = sb2('hB', [64, N], BF16)
                tl = dict(a=sb2('rsa', [64, 512]), t=sb2('rst', [64, 512]), k=sb2('rsk', [64, 512]))
                tcol = sb2('tcol', [N1, 128]); S.dma('sp', tcol[:], self.I('hy_tcol' + nm), w=['tcol'])
                drow = sb2('drow', [128, 256]); S.dma('sp', drow[:], self.I('hy_drow' + nm), w=['drow'])
                dec = [sb2('dec%d' % i, [N1, 256]) for i in range(2)]
                kabs = [sb2('kabs%d' % i, [N1, 32 * 256], BF16) for i in range(2)]
                onb = sb2('onb', [128, 1], BF16); S.op('dve', 'memset', onb[:], 1.0, w=['onb'])
                sres = sb2('sres', [128, 2])
                pa = ps2('pa', [64, 512]); pb_ = ps2('pb', [64, 512])
                pk = [ps2('pk%d' % i, [128, 512]) for i in range(2)]
                pSs = [ps2('pS%d' % i, [128, 2]) for i in range(2)]
                nblk = N // 512
                for blk in range(nblk):
                    b = blk % 2
                    S.dma('sp', zt[b][:], self.I('hy_zT' + nm)[:, blk * 512:(blk + 1) * 512], w=[('zt', b)])
                    S.op('pe', 'matmul', pa[:, :], wa[:, :], zt[b][:, :], start=True, stop=True, r=['wa', ('zt', b)], w=['pa'])
                    tl.update(ps='pa', dst='h1')
                    self._range_sin(h1[:, :], pa[:, :], fc[:, 0:1], fc[:, 1:2], tl, 512)
                    S.op('pe', 'matmul', pb_[:, :], wb[:, :], h1[:, :], start=True, stop=True, r=['wb', 'h1'], w=['pb'])
                    tl.update(ps='pb', dst='hA')
                    self._range_sin(hA[:, blk * 512:(blk + 1) * 512], pb_[:, :], fc[:, 0:1], fc[:, 2:3], tl, 512)
                S.op('pool', 'tensor_copy', hB[:, Ls:N], hA[:, Ls:N], r=['hA'], w=['hB'])
                S.op('pool', 'memset', hB[:, 0:Ls], 0.0, w=['hB'])
                S.op('pool', 'memset', hA[:, Ls:N], 0.0, r=['hB'], w=['hA'])
                for s2 in range(128):
                    b = s2 % 2
                    S.multi('pe', [('matmul', (pk[b][0:N1, 0:256], hA[:, s2::128], wc[:, 0:256]), dict(start=True, stop=False)),
                                   ('matmul', (pk[b][0:N1, 0:256], hB[:, s2::128], wc[:, 256:512]), dict(start=False, stop=True))],
                            r=['hA', 'hB', 'wc'], w=[('pk', b)])
                    S.op('act', 'activation', dec[b][:, :], drow[0:N1, :], AF.Exp, scale=tcol[:, s2:s2 + 1],
                         r=['drow', 'tcol'], w=[('dec', b)])
                    S.op('dve', 'tensor_tensor', ks[:, s2, :], pk[b][0:N1, 0:256], dec[b][:, :], ALU.mult,
                         r=[('pk', b), ('dec', b)], w=['hsrc'])
                for q in range(4):
                    S.op('act', 'activation', kabs[q % 2][:, :], ks[:, q * 32:(q + 1) * 32, :].rearrange("p s c -> p (s c)"), AF.Abs,
                         r=['hsrc'], w=[('kabs', q % 2)])
                    for cc in range(2):
                        S.multi('pe', [('matmul', (pSs[cc][:, 0:1], kabs[q % 2][:, sl_ * 256 + cc * 128:sl_ * 256 + (cc + 1) * 128], onb[0:N1, 0:1]),
                                        dict(start=(q == 0 and sl_ == 0), stop=(q == 3 and sl_ == 31))) for sl_ in range(32)],
                                r=[('kabs', q % 2), 'onb'], w=[('pS', cc)])
                for cc in range(2):
                    S.op('dve', 'tensor_copy', sres[:, cc:cc + 1], pSs[cc][:, 0:1], r=[('pS', cc)], w=['sres'])
                S.dma('sp', hy['ssum'], sres[:, :], r=['sres'])
                self._hy_f1(ks, N1, N1, w1h, hyY, sb2, ps2)
                S.flush()
        with ExitStack() as es:
            sb = lambda n, shp, dt=F32: es.enter_context(nc.sbuf_tensor(self.uq(n), shp, dt))
            pst = lambda n, shp, dt=F32: es.enter_context(nc.psum_tensor(self.uq(n), shp, dt))
            Yz = sb('Yz', [128, 256, 2 * N1], BF16); S.dma('sp', Yz[:], hyY, w=['Yz'])
            tr = sb('tr', [128, N], BF16); ti = sb('ti', [128, N], BF16); nti = sb('nti', [128, N], BF16)
            S.dma('sp', tr[:], self.I('hy_tr' + nm), w=['tr']); S.dma('sp', ti[:], self.I('hy_ti' + nm), w=['ti'])
            S.dma('sp', nti[:], self.I('hy_nti' + nm), w=['nti'])
            FB = min(8, N1)
            kst = [sb('kst%d' % i, [128, FB, 512], BF16) for i in range(2)]
            pz = [pst('pz%d' % i, [128, 512]) for i in range(2)]
            for f1 in range(N1):
                b = f1 % 2; kb = (f1 // FB) % 2
                S.multi('pe', self._hy_f2(Yz, 256, N1, tr, ti, nti, pz[b], f1), r=['Yz', 'tr', 'ti', 'nti'], w=[('pz', b)])
                if b == 0:
                    S.op('act', 'activation', kst[kb][:, f1 % FB, :], pz[b][:, :], AF.Copy, r=[('pz', b)], w=[('kst', kb)])
                else:
                    S.op('dve', 'tensor_copy', kst[kb][:, f1 % FB, :], pz[b][:, :], r=[('pz', b)], w=[('kst', kb)])
                if f1 % FB == FB - 1:
                    f0 = f1 - FB + 1
                    S.dma('sp', hy['kf'][:, f0:f0 + FB, :, :].rearrange("p f r c -> p (f r c)"),
                          kst[kb][:, :, :].rearrange("p f x -> p (f x)"), r=[('kst', kb)])
            S.flush()

    def p4_hyena(self, l):
        for nm, t0 in (('x', 0), ('c', L)):
            if nm == 'c' and l == DEPTH - 1:
                continue
            self._hyena(l, nm, t0)

    def _hyena(self, l, nm, t0):
        nc, S = self.nc, self.S
        hy = self.hy[nm]; N1 = hy['N1']; Ls = hy['Ls']; N = hy['N']; K1 = N1 // 2
        hyY = self.hyY[nm]; hyF = self.hyF[nm]
        nblk = max(1, Ls // 512); bw = Ls // nblk
        with ExitStack() as es:
            sb = lambda n, shp, dt=F32: es.enter_context(nc.sbuf_tensor(self.uq(n), shp, dt))
            pst = lambda n, shp, dt=F32: es.enter_context(nc.psum_tensor(self.uq(n), shp, dt))
            uh = sb('uh', [128, 6, Ls], BF16)
            S.dma('sp', uh[:], self.pxT[7:13, :, t0:t0 + Ls].rearrange("c p t -> p c t"), w=['uh'])
            idb = sb('idb', [128, 128], BF16); S.dma('sp', idb[:], self.I('ident_bf'), w=['idb'])
            w1h = sb('w1h', [N1, 2 * N1], BF16); S.dma('sp', w1h[:], self.I('hy_w1' + nm), w=['w1h'])
            zT = sb('zT', [128, 2, Ls], BF16); x0 = sb('x0', [128, 2, Ls], BF16)
            o = [sb('o%d' % c, [128, bw]) for c in range(6)]
            zs = sb('zs', [K1, 128, 256], BF16)
            cw = lambda tap, c: self.cols[:, C_HCW + tap * 6 + c:C_HCW + tap * 6 + c + 1]
            for blk in range(nblk):
                c0 = blk * bw
                for c in range(6):
                    S.op('act', 'activation', o[c][:, :], uh[:, c, c0:c0 + bw], AF.Identity,
                         bias=self.cols[:, C_HCB + c:C_HCB + c + 1], scale=cw(1, c), r=['uh', 'cols'], w=[('o', c)])
                    lo = 1 if blk == 0 else 0
                    S.op('dve', 'scalar_tensor_tensor', o[c][:, lo:bw], uh[:, c, c0 + lo - 1:c0 + bw - 1], cw(0, c), o[c][:, lo:bw],
                         ALU.mult, ALU.add, r=['uh', 'cols', ('o', c)], w=[('o', c)])
                    hi = bw - 1 if blk == nblk - 1 else bw
                    S.op('dve', 'scalar_tensor_tensor', o[c][:, 0:hi], uh[:, c, c0 + 1:c0 + hi + 1], cw(2, c), o[c][:, 0:hi],
                         ALU.mult, ALU.add, r=['uh', 'cols', ('o', c)], w=[('o', c)])
                for cc in range(2):
                    S.op('pool', 'tensor_copy', x0[:, cc, c0:c0 + bw], o[cc][:, :], r=[('o', cc)], w=['x0'])
                    S.op('pool', 'tensor_tensor', zT[:, cc, c0:c0 + bw], o[2 + cc][:, :], o[4 + cc][:, :], ALU.mult,
                         r=[('o', 2 + cc), ('o', 4 + cc)], w=['zT'])
            S.dma('sp', self.hx0[:, :, t0:t0 + Ls].rearrange("c p t -> p c t"), x0[:], r=['x0'])
            S.dma('sp', self.hz[:, :, t0:t0 + Ls].rearrange("c p t -> p c t"), zT[:], r=['zT'])
            ptr = [pst('ptr%d' % i, [128, 1024], BF16) for i in range(2)]
            nb = 0
            for s20 in range(0, 128, 4):
                b = nb % 2; nb += 1
                calls = []
                for sl in range(4):
                    for cc in range(2):
                        calls.append(('transpose', (ptr[b][0:K1, (sl * 2 + cc) * 128:(sl * 2 + cc + 1) * 128],
                                                    zT[:, cc, s20 + sl::128], idb[:, :]), {}))
                S.multi('pe', calls, r=['zT', 'idb'], w=[('ptr', b)])
                dst = zs[:, s20:s20 + 4, :].rearrange("p a c -> p (a c)")
                if b == 0:
                    S.op('act', 'activation', dst, ptr[b][0:K1, :], AF.Copy, r=[('ptr', b)], w=['hsrc'])
                else:
                    S.op('dve', 'tensor_copy', dst, ptr[b][0:K1, :], r=[('ptr', b)], w=['hsrc'])
            self._hy_f1(zs, K1, N1, w1h, hyY, sb, pst)
            S.flush()
        for cc in range(2):
            with ExitStack() as es:
                sb = lambda n, shp, dt=F32: es.enter_context(nc.sbuf_tensor(self.uq(n), shp, dt))
                pst = lambda n, shp, dt=F32: es.enter_context(nc.psum_tensor(self.uq(n), shp, dt))
                Yz = sb('Yz', [128, 128, 2 * N1], BF16); S.dma('sp', Yz[:], hyY[:, cc * 128:(cc + 1) * 128, :], w=['Yz'])
                tr = sb('tr', [128, N], BF16); ti = sb('ti', [128, N], BF16); nti = sb('nti', [128, N], BF16)
                S.dma('sp', tr[:], self.I('hy_tr' + nm), w=['tr']); S.dma('sp', ti[:], self.I('hy_ti' + nm), w=['ti'])
                S.dma('sp', nti[:], self.I('hy_nti' + nm), w=['nti'])
                kf = sb('kf', [128, N1, 2, 256], BF16); S.dma('sp', kf[:], hy['kf'], w=['kf'])
                Yf = sb('Yf', [128, N1, 2, 128], BF16)
                FB = min(4, N1)
                ta = [sb('ta%d' % i, [128, FB, 2, 128]) for i in range(2)]; tb = [sb('tb%d' % i, [128, FB, 2, 128]) for i in range(2)]
                pz = [pst('pz%d' % i, [128, FB * 256]) for i in range(2)]
                ccs = slice(cc * 128, (cc + 1) * 128)
                for st_ in range(N1 // FB):
                    f0 = st_ * FB; b = st_ % 2
                    calls = []
                    for fl in range(FB):
                        calls += self._hy_f2(Yz, 128, N1, tr, ti, nti, pz[b][:, fl * 256:(fl + 1) * 256], f0 + fl)
                    S.multi('pe', calls, r=['Yz', 'tr', 'ti', 'nti'], w=[('pz', b)])
                    pzv = pz[b][:, :].rearrange("p (f r c) -> p f r c", f=FB, r=2)
                    S.op('dve', 'tensor_tensor', ta[b][:, :, :, :], pzv, kf[:, f0:f0 + FB, :, ccs], ALU.mult, r=[('pz', b), 'kf'], w=[('ta', b)])
                    S.op('dve', 'tensor_tensor', tb[b][:, :, 0, :], pzv[:, :, 0, :], kf[:, f0:f0 + FB, 1, ccs], ALU.mult, r=[('pz', b), 'kf'], w=[('tb', b)])
                    S.op('dve', 'tensor_tensor', tb[b][:, :, 1, :], pzv[:, :, 1, :], kf[:, f0:f0 + FB, 0, ccs], ALU.mult, r=[('pz', b), 'kf'], w=[('tb', b)])
                    S.op('pool', 'tensor_tensor', Yf[:, f0:f0 + FB, 0, :], ta[b][:, :, 0, :], ta[b][:, :, 1, :], ALU.subtract, r=[('ta', b)], w=['Yf'])
                    S.op('pool', 'tensor_tensor', Yf[:, f0:f0 + FB, 1, :], tb[b][:, :, 0, :], tb[b][:, :, 1, :], ALU.add, r=[('tb', b)], w=['Yf'])
                S.dma('sp', hyF, Yf[:], r=['Yf'])
                S.flush()
            with ExitStack() as es:
                sb = lambda n, shp, dt=F32: es.enter_context(nc.sbuf_tensor(self.uq(n), shp, dt))
                pst = lambda n, shp, dt=F32: es.enter_context(nc.psum_tensor(self.uq(n), shp, dt))
                Yf = sb('Yf', [128, N1, 2, 128], BF16); S.dma('sp', Yf[:], hyF, w=['Yf'])
                i1a = sb('i1a', [128, 256], BF16); i1b = sb('i1b', [128, 256], BF16)
                S.dma('sp', i1a[:], self.I('hy_i1a' + nm), w=['i1a']); S.dma('sp', i1b[:], self.I('hy_i1b' + nm), w=['i1b'])
                er = sb('er', [N1, Ls], BF16); nei = sb('nei', [N1, Ls], BF16)
                S.dma('sp', er[:], self.I('hy_er' + nm), w=['er']); S.dma('sp', nei[:], self.I('hy_nei' + nm), w=['nei'])
                V = sb('V', [N1, 128, 2, 128], BF16)
                yT = sb('yT', [128, Ls]); x0h = sb('x0h', [128, Ls], BF16); zh = sb('zh', [128, Ls], BF16)
                S.dma('sp', x0h[:], self.hx0[cc, :, t0:t0 + Ls], w=['x0h']); S.dma('sp', zh[:], self.hz[cc, :, t0:t0 + Ls], w=['zh'])
                ssb = sb('ssb', [128, 2]); S.dma('sp', ssb[:], hy['ssum'], w=['ssb'])
                S.op('dve', 'reciprocal', ssb[:, :], ssb[:, :], r=['ssb'], w=['ssb'])
                res = sb('res', [128, Ls], BF16)
                pv = [pst('pv%d' % i, [128, 512]) for i in range(2)]
                py = [pst('py%d' % i, [128, 512]) for i in range(2)]
                nb = 0
                for ch0 in range(0, 128, 2):
                    b = nb % 2; nb += 1
                    calls = []
                    for chl in range(2):
                        o_ = pv[b][0:N1, chl * 256:(chl + 1) * 256]
                        calls.append(('matmul', (o_, Yf[:, :, 0, ch0 + chl], i1a[:, :]), dict(start=True, stop=False)))
                        calls.append(('matmul', (o_, Yf[:, :, 1, ch0 + chl], i1b[:, :]), dict(start=False, stop=True)))
                    S.multi('pe', calls, r=['Yf', 'i1a', 'i1b'], w=[('pv', b)])
                    dst = V[:, ch0:ch0 + 2, :, :].rearrange("p c r a -> p (c r a)")
                    if b == 0:
                        S.op('act', 'activation', dst, pv[b][0:N1, :], AF.Copy, r=[('pv', b)], w=['V'])
                    else:
                        S.op('dve', 'tensor_copy', dst, pv[b][0:N1, :], r=[('pv', b)], w=['V'])
                nta = min(128, 512 // K1)
                nb = 0
                for ta0 in range(0, 128, nta):
                    b = nb % 2; nb += 1
                    calls = []
                    for al in range(nta):
                        ta_ = ta0 + al
                        o_ = py[b][:, al * K1:(al + 1) * K1]
                        calls.append(('matmul', (o_, V[:, :, 0, ta_], er[:, ta_::128]), dict(start=True, stop=False)))
                        calls.append(('matmul', (o_, V[:, :, 1, ta_], nei[:, ta_::128]), dict(start=False, stop=True)))
                    S.multi('pe', calls, r=['V', 'er', 'nei'], w=[('py', b)])
                    dst = yT[:, :].rearrange("p (b a) -> p a b", a=128)[:, ta0:ta0 + nta, :]
                    src = py[b][:, 0:nta * K1].rearrange("p (a b) -> p a b", b=K1)
                    if b == 0:
                        S.op('act', 'activation', dst, src, AF.Copy, r=[('py', b)], w=['yT'])
                    else:
                        S.op('dve', 'tensor_copy', dst, src, r=[('py', b)], w=['yT'])
                S.op('act', 'activation', yT[:, :], yT[:, :], AF.Identity, scale=ssb[:, cc:cc + 1], r=['yT', 'ssb'], w=['yT'])
                S.op('dve', 'scalar_tensor_tensor', yT[:, :], zh[:, :], self.cols[:, C_HD + cc:C_HD + cc + 1], yT[:, :],
                     ALU.mult, ALU.add, r=['zh', 'yT', 'cols'], w=['yT'])
                S.op('dve', 'tensor_tensor', res[:, :], yT[:, :], x0h[:, :], ALU.mult, r=['yT', 'x0h'], w=['res'])
                S.dma('sp', self.mixT[6 + cc, :, t0:t0 + Ls], res[:, :], r=['res'])
                S.flush()


def build_program(debug=False, stop_after=None):
    P = Prog(debug=debug)
    steps = []
    for l in range(DEPTH):
        steps += [('p0', l), ('p1', l), ('pk', l), ('p2', l), ('p3', l), ('p4', l), ('p5', l)]
    for (nm, l) in steps:
        fn = getattr(P, {'p0': 'p0_mod', 'p1': 'p1_inproj', 'pk': 'pk_filters', 'p2': 'p2_attn',
                         'p3': 'p3_fnet', 'p4': 'p4_hyena', 'p5': 'p5_out_mlp'}[nm])
        fn(l)
        if stop_after is not None and (nm, l) == stop_after:
            break
    P.es.close()
    P.nc.used_inputs = list(P._in.keys())
    return P.nc


_PROG = None


def kernel(**inputs):
    global _PROG
    per = _prep(inputs)
    if _PROG is None:
        _PROG = build_program()
    per = [{k: d[k] for k in _PROG.used_inputs} for d in per]
    res = run_bass_kernel_spmd(_PROG, per, core_ids=list(range(8)))
    return np.stack([np.asarray(r['y'], np.float32) for r in res.results], axis=0)
```

```python
import math
import numpy as np
import ml_dtypes
from contextlib import ExitStack
import concourse.bass as bass
import concourse.mybir as mybir
from concourse.bass_utils import run_bass_kernel_spmd

F32 = mybir.dt.float32
BF16 = mybir.dt.bfloat16
AF = mybir.ActivationFunctionType
ALU = mybir.AluOpType

D = 1024
L = 4096
LC = 256
T = L + LC
NT = T // 128
DEPTH = 2
H = 8
OFF_Q, OFF_KV, OFF_KR, OFF_F, OFF_H, IN_W = 0, 384, 640, 672, 928, 1696
EPS = 1e-6
NCOL = 64
C_N1G, C_N2G, C_QG, C_KVG, C_HCW, C_HCB, C_HD, C_HB1, C_HFR, C_HB2 = 0, 8, 16, 19, 21, 39, 45, 47, 48, 49


class _Op:
    __slots__ = ('eng', 'calls', 'deps', 'signal', 'sem', 'val', 'dma', 'waits', 'pre')


class Sched:
    NDMA = 8
    ATTR = [('pe', 'tensor'), ('act', 'scalar'), ('dve', 'vector'), ('pool', 'gpsimd'), ('sp', 'sync')]

    def __init__(self, nc, es):
        self.nc = nc
        self.sem = {e: es.enter_context(nc.semaphore('s_' + e)) for e in ['pe', 'act', 'dve', 'pool']}
        self.dsem = {q: [es.enter_context(nc.semaphore('d_%s%d' % (q, i))) for i in range(self.NDMA)]
                     for q in ['sp', 'pool', 'act']}
        self.cnt = {e: 0 for e in self.sem}
        self.dcnt = {q: 0 for q in self.dsem}
        self._reset()

    def _reset(self):
        self.ops = {e: [] for e, _ in self.ATTR}
        self.last_w = {}
        self.readers = {}

    def multi(self, eng, calls, r=(), w=(), dma=False):
        op = _Op()
        op.eng = eng; op.calls = calls; op.dma = dma; op.signal = False; op.pre = None
        deps = []
        for k in r:
            d = self.last_w.get(k)
            if d is not None: deps.append(d)
        for k in w:
            d = self.last_w.get(k)
            if d is not None: deps.append(d)
            deps.extend(self.readers.get(k, ()))
        if eng == 'pe':
            deps = [d for d in deps if d.eng != 'pe' or d.dma]
        op.deps = [d for d in deps if d is not op]
        for d in op.deps: d.signal = True
        for k in r: self.readers.setdefault(k, []).append(op)
        for k in w:
            self.last_w[k] = op
            self.readers[k] = []
        self.ops[eng].append(op)
        return op

    def op(self, eng, name, *args, r=(), w=(), **kw):
        return self.multi(eng, [(name, args, kw)], r=r, w=w)

    def dma(self, q, out, in_, r=(), w=(), **kw):
        return self.multi(q, [('dma_start', (out, in_), kw)], r=r, w=w, dma=True)

    def flush(self):
        nc = self.nc
        for e, _ in self.ATTR:
            ops = self.ops[e]
            for op in reversed(ops):
                if not op.dma:
                    op.signal = True
                    break
            for op in ops:
                if op.dma:
                    j = self.dcnt[e]; self.dcnt[e] += 1
                    op.sem = self.dsem[e][j % self.NDMA]
                    op.val = 16 * (j // self.NDMA + 1)
                    op.pre = (op.sem, op.val - 16) if op.val > 16 else None
                elif op.signal:
                    self.cnt[e] += 1
                    op.sem = self.sem[e]; op.val = self.cnt[e]
        finals = {}
        for e, _ in self.ATTR:
            seen = {}
            for op in self.ops[e]:
                need = {}
                if op.pre is not None: need[id(op.pre[0])] = op.pre
                for d in op.deps:
                    cur = need.get(id(d.sem))
                    if cur is None or cur[1] < d.val: need[id(d.sem)] = (d.sem, d.val)
                op.waits = []
                for k, (s, v) in need.items():
                    if seen.get(k, -1) < v:
                        seen[k] = v; op.waits.append((s, v))
                if op.dma or op.signal:
                    cur = finals.get(id(op.sem))
                    if cur is None or cur[1] < op.val: finals[id(op.sem)] = (op.sem, op.val)
        with nc.Block() as blk:
            for e, attr in self.ATTR:
                ops = self.ops[e]
                if not ops and e != 'sp': continue

                def body(eng, ops=ops, e=e):
                    for op in ops:
                        for (s, v) in op.waits: eng.wait_ge(s, v)
                        inst = None
                        for (name, args, kw) in op.calls:
                            inst = getattr(eng, name)(*args, **kw)
                        if op.dma: inst.then_inc(op.sem, 16)
                        elif op.signal: inst.then_inc(op.sem, 1)
                    if e == 'sp':
                        for (s, v) in finals.values(): eng.wait_ge(s, v)
                getattr(blk, attr)(body)
        self._reset()


def _bf(a):
    return np.ascontiguousarray(a.astype(ml_dtypes.bfloat16))


def _f32(a):
    return np.ascontiguousarray(a.astype(np.float32))


ROPE_PERM = np.array(list(range(8, 16)) + list(range(0, 8)) + list(range(24, 32)) + list(range(16, 24)))


def _rope_tables():
    rows = L // 64
    row = np.repeat(np.arange(rows, dtype=np.float64), 64)
    col = np.tile(np.arange(64, dtype=np.float64), rows)
    inv = 10000.0 ** (-np.arange(0, 16, 2, dtype=np.float64) / 16)
    ar = row[None, :] * inv[:, None]
    ac = col[None, :] * inv[:, None]
    cos = np.ones((32, T)); sin = np.zeros((32, T))
    cos[0:8, :L] = np.cos(ar); cos[8:16, :L] = np.cos(ar); cos[16:24, :L] = np.cos(ac); cos[24:32, :L] = np.cos(ac)
    sin[0:8, :L] = -np.sin(ar); sin[8:16, :L] = np.sin(ar); sin[16:24, :L] = -np.sin(ac); sin[24:32, :L] = np.sin(ac)
    return _f32(cos), _f32(sin)


def _fnet_tables(Ls, L1, L2):
    w = np.arange(64)
    ph = 2 * np.pi * np.outer(w, w) / 64
    cw = np.zeros((128, 128)); sw = np.zeros((128, 128))
    for g in range(2):
        cw[g * 64:(g + 1) * 64, g * 64:(g + 1) * 64] = np.cos(ph)
        sw[g * 64:(g + 1) * 64, g * 64:(g + 1) * 64] = np.sin(ph)
    csw = np.concatenate([cw, sw], axis=1)
    a = np.arange(L1)
    p1 = 2 * np.pi * np.outer(a, a) / L1
    wr, wi = np.cos(p1), -np.sin(p1)
    w1 = np.concatenate([wr, wi], axis=1)
    w2 = np.concatenate([wi, -wr], axis=1)
    p2 = 2 * np.pi * np.outer(np.arange(L2), np.arange(Ls)) / Ls
    sc = 1.0 / math.sqrt(Ls * 64)
    return _bf(csw), _bf(w1), _bf(w2), _bf(np.cos(p2) * sc), _bf(np.sin(p2) * sc)


def _hyena_tables(Ls):
    N = 2 * Ls
    N1 = N // 128
    a = np.arange(N1)
    p1 = 2 * np.pi * np.outer(a, a) / N1
    w1 = np.concatenate([np.cos(p1), -np.sin(p1)], axis=1)
    p2 = 2 * np.pi * np.outer(np.arange(128), np.arange(N)) / N
    tr, ti, nti = np.cos(p2), -np.sin(p2), np.sin(p2)
    p3 = 2 * np.pi * np.outer(np.arange(128), np.arange(128)) / 128
    i1a = np.concatenate([np.cos(p3), np.sin(p3)], axis=1)
    i1b = np.concatenate([-np.sin(p3), np.cos(p3)], axis=1)
    p4 = 2 * np.pi * np.outer(np.arange(N1), np.arange(Ls)) / N
    er, nei = np.cos(p4) / N, -np.sin(p4) / N
    j = np.arange(N)
    tau = np.where(j < Ls, j, N - j).astype(np.float64)
    tau[Ls] = 0
    tl = np.linspace(0.0, 1.0, Ls)
    t = tl[tau.astype(np.int64)]
    fr = np.linspace(1e-4, 15.0, 16)
    ang = 2.0 * np.pi * tau[:, None] / Ls * fr[None, :]
    z = np.concatenate([t[:, None], np.cos(ang), -np.sin(ang)], axis=1)
    negt = -t.copy()
    negt[Ls] = -1e4
    tcol = negt.reshape(N1, 128)
    deltas = np.abs(np.linspace(math.log(1e-2) / 1.5, math.log(1e-2) / 0.3, 256))
    drow = np.broadcast_to(deltas[None, :], (128, 256))
    return dict(N1=N1, w1=_bf(w1), tr=_bf(tr), ti=_bf(ti), nti=_bf(nti), i1a=_bf(i1a), i1b=_bf(i1b),
                er=_bf(er), nei=_bf(nei), zT=_f32(z.T), tcol=_f32(tcol), drow=_f32(drow))


_CONST = None


def _consts():
    global _CONST
    if _CONST is not None:
        return _CONST
    c = {}
    c['ident_bf'] = _bf(np.eye(128))
    c['ident_f'] = _f32(np.eye(128))
    c['ones_f'] = _f32(np.ones((128, 128)))
    c['ropeC'], c['ropeS'] = _rope_tables()
    for nm, Ls, L1, L2 in (('x', L, 64, 64), ('c', LC, 4, 64)):
        csw, w1, w2, tr, sn = _fnet_tables(Ls, L1, L2)
        c['fn_csw'] = csw
        c['fn_w1' + nm], c['fn_w2' + nm], c['fn_tr' + nm], c['fn_sn' + nm] = w1, w2, tr, sn
        ht = _hyena_tables(Ls)
        for k, v in ht.items():
            if k != 'N1':
                c['hy_%s%s' % (k, nm)] = v
    _CONST = c
    return c


def _colpack(v, n):
    return np.asarray(v, np.float32).reshape(n, 128).T


def _prep(inp):
    c = dict(_consts())
    sh = {}
    w_in = np.asarray(inp['w_in'], np.float32)
    sh['w_inx'] = np.ascontiguousarray(np.concatenate([w_in, w_in[:, :, OFF_KR + ROPE_PERM]], axis=2))
    w_uq = np.asarray(inp['w_uq'], np.float32)
    sh['w_uq'] = np.ascontiguousarray(w_uq)
    wq = w_uq.reshape(DEPTH, 384, H, 96)
    sh['w_uqp'] = np.ascontiguousarray(
        np.concatenate([wq[..., :64], wq[..., 64:][..., ROPE_PERM]], axis=-1).reshape(DEPTH, 384, 768))
    wkv = np.asarray(inp['w_ukv'], np.float32).reshape(DEPTH, 256, H, 128)
    sh['w_ukvx'] = np.ascontiguousarray(
        np.concatenate([wkv[..., :64].reshape(DEPTH, 256, 512), wkv[..., 64:].reshape(DEPTH, 256, 512)], axis=2))
    for k in ('w_mod', 'w_out', 'w_mlp1', 'w_mlp2', 'hy_w1', 'hy_w2', 'hy_w3'):
        sh[k] = np.ascontiguousarray(np.asarray(inp[k], np.float32))
    sh['b_mod2'] = np.ascontiguousarray(np.repeat(np.asarray(inp['b_mod'], np.float32)[:, None, :], 2, axis=1))
    cols = np.zeros((DEPTH, 128, NCOL), np.float32)
    for l in range(DEPTH):
        cols[l, :, C_N1G:C_N1G + 8] = _colpack(inp['norm1_g'][l], 8)
        cols[l, :, C_N2G:C_N2G + 8] = _colpack(inp['norm2_g'][l], 8)
        cols[l, :, C_QG:C_QG + 3] = _colpack(inp['q_norm_g'][l], 3)
        cols[l, :, C_KVG:C_KVG + 2] = _colpack(inp['kv_norm_g'][l], 2)
        for tap in range(3):
            cols[l, :, C_HCW + tap * 6:C_HCW + tap * 6 + 6] = _colpack(inp['hy_conv_w'][l, tap], 6)
        cols[l, :, C_HCB:C_HCB + 6] = _colpack(inp['hy_conv_b'][l], 6)
        cols[l, :, C_HD:C_HD + 2] = _colpack(inp['hy_d'][l], 2)
        cols[l, :64, C_HB1] = inp['hy_b1'][l]
        cols[l, :64, C_HFR] = inp['hy_freq'][l]
        cols[l, :64, C_HB2] = inp['hy_b2'][l]
    sh['cols'] = cols
    sh['fgrow'] = np.ascontiguousarray(np.broadcast_to(np.asarray(inp['final_norm_g'], np.float32)[None, :], (128, D)))
    sh.update(c)
    per = []
    x = np.asarray(inp['x'], np.float32); ctx = np.asarray(inp['ctx'], np.float32)
    cc = np.asarray(inp['c'], np.float32); c_ctx = np.asarray(inp['c_ctx'], np.float32)
    for b in range(8):
        d = dict(sh)
        d['xin'] = np.ascontiguousarray(np.concatenate([x[b], ctx[b]], axis=0))
        cv = np.stack([_colpack(cc[b], 8), _colpack(c_ctx, 8)], axis=-1)
        d['cvec'] = np.ascontiguousarray(cv)
        per.append(d)
    return per


GROUPS = [(g * 512, 512, 0) for g in range(8)] + [(L, LC, 1)]
SCALE = 1.0 / math.sqrt(96.0)


class Prog:
    def __init__(self, debug=False):
        self.debug = debug
        self.nc = nc = bass.Bass("TRN2", target_bir_lowering=False)
        self.es = ExitStack()
        self.S = Sched(nc, self.es)
        self._in = {}
        okind = dict(kind="ExternalOutput") if debug else {}
        scr = lambda n, shp, dt=F32: nc.dram_tensor(n, list(shp), dt, **okind).ap()
        self.spec = dict(
            xin=([T, D], F32), cvec=([128, 8, 2], F32), w_mod=([DEPTH, D, 6 * D], F32), b_mod2=([DEPTH, 2, 6 * D], F32),
            w_inx=([DEPTH, D, 1728], F32), w_uq=([DEPTH, 384, 768], F32), w_uqp=([DEPTH, 384, 768], F32),
            w_ukvx=([DEPTH, 256, 1024], F32), w_out=([DEPTH, D, D], F32), w_mlp1=([DEPTH, D, 4 * D], F32),
            w_mlp2=([DEPTH, 4 * D, D], F32), hy_w1=([DEPTH, 33, 64], F32), hy_w2=([DEPTH, 64, 64], F32),
            hy_w3=([DEPTH, 64, 512], F32), cols=([DEPTH, 128, NCOL], F32), fgrow=([128, D], F32),
            ident_bf=([128, 128], BF16), ident_f=([128, 128], F32), ones_f=([128, 128], F32),
            ropeC=([32, T], F32), ropeS=([32, T], F32), fn_csw=([128, 256], BF16))
        self.fn = {}; self.hy = {}
        for nm, Ls, L1 in (('x', L, 64), ('c', LC, 4)):
            N = 2 * Ls; N1 = N // 128
            self.spec.update({'fn_w1' + nm: ([L1, 2 * L1], BF16), 'fn_w2' + nm: ([L1, 2 * L1], BF16),
                              'fn_tr' + nm: ([64, Ls], BF16), 'fn_sn' + nm: ([64, Ls], BF16),
                              'hy_w1' + nm: ([N1, 2 * N1], BF16), 'hy_tr' + nm: ([128, N], BF16),
                              'hy_ti' + nm: ([128, N], BF16), 'hy_nti' + nm: ([128, N], BF16),
                              'hy_i1a' + nm: ([128, 256], BF16), 'hy_i1b' + nm: ([128, 256], BF16),
                              'hy_er' + nm: ([N1, Ls], BF16), 'hy_nei' + nm: ([N1, Ls], BF16),
                              'hy_zT' + nm: ([33, N], F32), 'hy_tcol' + nm: ([N1, 128], F32),
                              'hy_drow' + nm: ([128, 256], F32)})
            self.fn[nm] = dict(L1=L1, Ls=Ls)
            self.hy[nm] = dict(N1=N1, Ls=Ls, N=N, kf=scr('kf' + nm, [128, N1, 2, 256], BF16),
                               ssum=scr('ssum' + nm, [128, 2]))
        self.y = nc.dram_tensor('y', [L, D], F32, kind="ExternalOutput").ap()
        self.xres = scr('xres', [T, D]); self.mrow = scr('mrow', [2, 6 * D])
        self.pxT = scr('pxT', [14, 128, T], BF16); self.mixT = scr('mixT', [8, 128, T], BF16)
        self.hx0 = scr('hx0', [2, 128, T], BF16); self.hz = scr('hz', [2, 128, T], BF16)
        self.xmid = scr('xmid', [T, D]); self.h2T = scr('h2T', [8, 128, T], BF16)
        self.hyY = {}; self.hyF = {}
        for nm_ in ('x', 'c'):
            n1_ = self.hy[nm_]['N1']
            self.hyY[nm_] = scr('hyY' + nm_, [128, 256, 2 * n1_], BF16)
            self.hyF[nm_] = scr('hyF' + nm_, [2, 128, n1_, 2, 128], BF16)
        self.modc = self.es.enter_context(nc.sbuf_tensor('modc', [128, 2, 4, 8], F32))
        self.cols = self.es.enter_context(nc.sbuf_tensor('colsb', [128, NCOL], F32))

    def uq(self, n):
        self._uq = getattr(self, '_uq', 0) + 1
        return '%s_%d' % (n, self._uq)

    def I(self, name):
        if name not in self._in:
            shp, dt = self.spec[name]
            self._in[name] = self.nc.dram_tensor(name, list(shp), dt, kind="ExternalInput").ap()
        return self._in[name]

    def p0_mod(self, l):
        nc, S = self.nc, self.S
        with ExitStack() as es:
            sb = lambda n, shp, dt=F32: es.enter_context(nc.sbuf_tensor(self.uq(n), shp, dt))
            cv = sb('cv', [128, 8, 2]); sc = sb('sc', [128, 8, 2])
            wt = [sb('wt%d' % i, [128, 8, 512]) for i in range(2)]
            msb = sb('msb', [2, 6 * D]); bm = sb('bm', [2, 6 * D])
            ps = [es.enter_context(nc.psum_tensor(self.uq('ps%d' % i), [2, 512], F32)) for i in range(2)]
            S.dma('sp', cv[:], self.I('cvec'), w=['cv'])
            S.dma('sp', bm[:], self.I('b_mod2')[l], w=['bm'])
            S.dma('sp', self.cols[:], self.I('cols')[l], w=['cols'])
            S.op('act', 'activation', sc[:], cv[:], AF.Silu, r=['cv'], w=['sc'])
            wv = self.I('w_mod')[l].rearrange("(k p) n -> p k n", p=128)
            for n in range(12):
                b = n % 2
                S.dma('sp', wt[b][:], wv[:, :, n * 512:(n + 1) * 512], w=[('wt', b)])
                S.multi('pe', [('matmul', (ps[b][:], sc[:, k, :], wt[b][:, k, :]), dict(start=(k == 0), stop=(k == 7)))
                               for k in range(8)], r=['sc', ('wt', b)], w=[('ps', b)])
                S.op('dve', 'tensor_tensor', msb[:, n * 512:(n + 1) * 512], ps[b][:], bm[:, n * 512:(n + 1) * 512],
                     ALU.add, r=[('ps', b), 'bm'], w=['msb'])
            S.dma('sp', self.mrow, msb[:], r=['msb'])
            S.flush()
            mT = sb('mT', [96, 128]); idf = sb('idf', [128, 128]); mcol = sb('mcol', [128, 96])
            pm = es.enter_context(nc.psum_tensor(self.uq('pm'), [128, 96], F32))
            S.dma('sp', mT[:], self.mrow.rearrange("s (j p) -> (s j) p", p=128), w=['mT'])
            S.dma('sp', idf[:], self.I('ident_f'), w=['idf'])
            S.op('pe', 'transpose', pm[:], mT[:], idf[0:96, 0:96], r=['mT', 'idf'], w=['pm'])
            S.op('dve', 'tensor_copy', mcol[:], pm[:], r=['pm'], w=['mcol'])
            for s in range(2):
                o = s * 48
                S.op('dve', 'scalar_tensor_tensor', self.modc[:, s, 0, :], mcol[:, o + 8:o + 16], 1.0,
                     self.cols[:, C_N1G:C_N1G + 8], ALU.add, ALU.mult, r=['mcol', 'cols'], w=['modc'])
                S.op('dve', 'tensor_copy', self.modc[:, s, 1, :], mcol[:, o:o + 8], r=['mcol'], w=['modc'])
                S.op('dve', 'scalar_tensor_tensor', self.modc[:, s, 2, :], mcol[:, o + 32:o + 40], 1.0,
                     self.cols[:, C_N2G:C_N2G + 8], ALU.add, ALU.mult, r=['mcol', 'cols'], w=['modc'])
                S.op('dve', 'tensor_copy', self.modc[:, s, 3, :], mcol[:, o + 24:o + 32], r=['mcol'], w=['modc'])
            S.flush()

    def norm_tiles(self, xt, nj, s, which, tl, key, n):
        S = self.S
        ss, rstd, xn, junk, pt, hT, idb = tl['ss'], tl['rstd'], tl['xn'], tl['junk'], tl['pt'], tl['hT'], tl['idb']
        for j in range(nj):
            S.op('act', 'activation', junk[:], xt[:, j, :], AF.Square, accum_out=ss[:, j:j + 1],
                 r=[key], w=['junk', tl['k'] + 'ss'])
        S.op('act', 'activation', rstd[:, 0:nj], ss[:, 0:nj], AF.Sqrt, bias=tl['eps'][:, 0:1], scale=1.0 / D,
             r=[tl['k'] + 'ss', 'eps'], w=[tl['k'] + 'rstd'])
        S.op('dve', 'reciprocal', rstd[:, 0:nj], rstd[:, 0:nj], r=[tl['k'] + 'rstd'], w=[tl['k'] + 'rstd'])
        for j in range(nj):
            S.op('dve', 'tensor_scalar', xn[:, j, :], xt[:, j, :], rstd[:, j:j + 1], None, ALU.mult,
                 r=[key, tl['k'] + 'rstd'], w=[tl['k'] + 'xn'])
        for k in range(8):
            pb = k % 2
            S.multi('pe', [('transpose', (pt[pb][:, j * 128:(j + 1) * 128], xn[:, j, k * 128:(k + 1) * 128], idb[:]), {})
                           for j in range(nj)], r=[tl['k'] + 'xn', 'idb'], w=[('pt', pb)])
            if k % 2 == 0:
                S.op('dve', 'tensor_scalar', hT[:, k, 0:n], pt[pb][:, 0:n], self.modc[:, s, which, k:k + 1],
                     self.modc[:, s, which + 1, k:k + 1], ALU.mult, ALU.add, r=[('pt', pb), 'modc'], w=[tl['k'] + 'hT'])
            else:
                S.op('act', 'activation', hT[:, k, 0:n], pt[pb][:, 0:n], AF.Identity,
                     bias=self.modc[:, s, which + 1, k:k + 1], scale=self.modc[:, s, which, k:k + 1],
                     r=[('pt', pb), 'modc'], w=[tl['k'] + 'hT'])

    def p1_inproj(self, l):
        nc, S = self.nc, self.S
        src = self.I('xin') if l == 0 else self.xres
        with ExitStack() as es:
            sb = lambda n, shp, dt=F32: es.enter_context(nc.sbuf_tensor(self.uq(n), shp, dt))
            pst = lambda n, shp, dt=F32: es.enter_context(nc.psum_tensor(self.uq(n), shp, dt))
            win = sb('win', [128, 8, 1728], BF16)
            S.dma('pool', win[:], self.I('w_inx')[l].rearrange("(k p) n -> p k n", p=128), w=['win'])
            idb = sb('idb', [128, 128], BF16); S.dma('sp', idb[:], self.I('ident_bf'), w=['idb'])
            onesf = sb('onesf', [128, 128]); S.dma('sp', onesf[:], self.I('ones_f'), w=['onesf'])
            rc = sb('rc', [32, T]); rs = sb('rs', [32, T])
            S.dma('sp', rc[:], self.I('ropeC'), w=['rc']); S.dma('sp', rs[:], self.I('ropeS'), w=['rs'])
            eps = sb('eps', [128, 1]); S.op('dve', 'memset', eps[:], EPS, w=['eps'])
            xg = [sb('xg%d' % i, [128, 4, D]) for i in range(2)]
            tls = []
            junk = sb('junk', [128, D], BF16)
            pt = [pst('pt%d' % i, [128, 512], BF16) for i in range(2)]
            for i in range(2):
                tls.append(dict(k='t%d' % i, ss=sb('ss%d' % i, [128, 4]), rstd=sb('rstd%d' % i, [128, 4]),
                                xn=sb('xn%d' % i, [128, 4, D], BF16), junk=junk, pt=pt,
                                hT=sb('hT%d' % i, [128, 8, 512], BF16), idb=idb, eps=eps))
            ost = [sb('ost%d' % i, [128, 14, 512], BF16) for i in range(2)]
            for i in range(2):
                S.op('pool', 'memset', ost[i][:, 13, :], 0.0, w=[('ost', i)])
            c32 = sb('c32', [128, 5, 512]); sq = sb('sq', [128, 5, 512]); rq = sb('rq', [128, 512])
            t1 = sb('t1', [32, 512]); t2 = sb('t2', [32, 512])
            po = [pst('po%d' % i, [128, 512]) for i in range(3)]
            pn = pst('pn', [128, 512])
            npo = [0]

            def proj(col0, ncols, hT, n, b):
                i = npo[0] % 3; npo[0] += 1
                S.multi('pe', [('matmul', (po[i][0:ncols, 0:n], win[:, k, col0:col0 + ncols], hT[:, k, 0:n]),
                                dict(start=(k == 0), stop=(k == 7))) for k in range(8)],
                        r=['win', 't%dhT' % b], w=[('po', i)])
                return i

            def front(gi):
                t0, n, s = GROUPS[gi]
                b = gi % 2; nj = n // 128; tl = tls[b]
                S.dma('sp', xg[b][:, 0:nj, :], src[t0:t0 + n, :].rearrange("(j p) d -> p j d", p=128), w=[('xg', b)])
                self.norm_tiles(xg[b], nj, s, 0, tl, ('xg', b), n)

            def back(gi):
                t0, n, s = GROUPS[gi]
                b = gi % 2; nj = n // 128; tl = tls[b]
                hT = tl['hT']
                okey = ('ost', b)
                for (c0, nch, gcol, ci0, i0) in ((OFF_Q, 3, C_QG, 0, 0), (OFF_KV, 2, C_KVG, 3, 3)):
                    for c in range(nch):
                        i = proj(c0 + c * 128, 128, hT, n, b)
                        S.op('act', 'activation', c32[:, i0 + c, 0:n], po[i][:, 0:n], AF.Copy,
                             r=[('po', i)], w=[('c32', i0 + c)])
                        S.op('act', 'activation', sq[:, i0 + c, 0:n], po[i][:, 0:n], AF.Square,
                             r=[('po', i)], w=[('sq', i0 + c)])
                    S.multi('pe', [('matmul', (pn[:, 0:n], onesf[:], sq[:, i0 + c, 0:n]),
                                    dict(start=(c == 0), stop=(c == nch - 1))) for c in range(nch)],
                            r=['onesf'] + [('sq', i0 + c) for c in range(nch)], w=['pn'])
                    S.op('act', 'activation', rq[:, 0:n], pn[:, 0:n], AF.Sqrt, bias=eps[:, 0:1], scale=1.0 / (128 * nch),
                         r=['pn', 'eps'], w=['rq'])
                    S.op('dve', 'reciprocal', rq[:, 0:n], rq[:, 0:n], r=['rq'], w=['rq'])
                    for c in range(nch):
                        S.op('dve', 'scalar_tensor_tensor', ost[b][:, ci0 + c, 0:n], c32[:, i0 + c, 0:n],
                             self.cols[:, gcol + c:gcol + c + 1], rq[:, 0:n], ALU.mult, ALU.mult,
                             r=[('c32', i0 + c), 'rq', 'cols'], w=[okey])
                ia = proj(OFF_KR, 32, hT, n, b)
                ib = proj(IN_W, 32, hT, n, b)
                S.op('dve', 'tensor_tensor', t1[:, 0:n], po[ia][0:32, 0:n], rc[:, t0:t0 + n], ALU.mult,
                     r=[('po', ia), 'rc'], w=['t1'])
                S.op('dve', 'tensor_tensor', t2[:, 0:n], po[ib][0:32, 0:n], rs[:, t0:t0 + n], ALU.mult,
                     r=[('po', ib), 'rs'], w=['t2'])
                S.op('dve', 'tensor_tensor', ost[b][0:32, 13, 0:n], t1[:, 0:n], t2[:, 0:n], ALU.add,
                     r=['t1', 't2'], w=[okey])
                for c in range(8):
                    i = proj(OFF_F + c * 128, 128, hT, n, b)
                    if c % 2 == 0:
                        S.op('act', 'activation', ost[b][:, 5 + c, 0:n], po[i][:, 0:n], AF.Copy, r=[('po', i)], w=[okey])
                    else:
                        S.op('dve', 'tensor_copy', ost[b][:, 5 + c, 0:n], po[i][:, 0:n], r=[('po', i)], w=[okey])
                S.dma('sp', self.pxT[:, :, t0:t0 + n].rearrange("c p t -> p c t"), ost[b][:, :, 0:n], r=[okey])

            front(0)
            for gi in range(len(GROUPS)):
                if gi + 1 < len(GROUPS):
                    front(gi + 1)
                back(gi)
            S.flush()

    def p2_attn(self, l):
        nc, S = self.nc, self.S
        with ExitStack() as es:
            sb = lambda n, shp, dt=F32: es.enter_context(nc.sbuf_tensor(self.uq(n), shp, dt))
            pst = lambda n, shp, dt=F32: es.enter_context(nc.psum_tensor(self.uq(n), shp, dt))
            cqn = sb('cqn', [128, 3, T], BF16); ckvn = sb('ckvn', [128, 2, T], BF16)
            S.dma('sp', cqn[:], self.pxT[0:3].rearrange("c p t -> p c t"), w=['cqn'])
            S.dma('sp', ckvn[:], self.pxT[3:5].rearrange("c p t -> p c t"), w=['ckvn'])
            wuq = sb('wuq', [128, 3, 768], BF16); wuqp = sb('wuqp', [128, 3, 768], BF16)
            wukv = sb('wukv', [128, 2, 1024], BF16)
            S.dma('pool', wuq[:], self.I('w_uq')[l].rearrange("(k p) n -> p k n", p=128), w=['wuq'])
            S.dma('pool', wuqp[:], self.I('w_uqp')[l].rearrange("(k p) n -> p k n", p=128), w=['wuqp'])
            S.dma('pool', wukv[:], self.I('w_ukvx')[l].rearrange("(k p) n -> p k n", p=128), w=['wukv'])
            onesf = sb('onesf', [128, 128]); S.dma('sp', onesf[:], self.I('ones_f'), w=['onesf'])
            tc_ = sb('tabc', [96, T]); ts_ = sb('tabs', [96, T])
            S.dma('sp', tc_[64:96, :], self.I('ropeC'), w=['tabc']); S.dma('sp', ts_[64:96, :], self.I('ropeS'), w=['tabs'])
            kt = [sb('kt%d' % i, [96, T], BF16) for i in range(2)]
            qt = [sb('qt%d' % i, [96, T], BF16) for i in range(2)]
            for b in range(2):
                S.dma('sp', kt[b][64:96, :], self.pxT[13, 0:32, :], w=[('ktr', b)])
            vh = [sb('vh%d' % i, [128, NT, 128], BF16) for i in range(2)]
            for i in range(2):
                S.op('pool', 'memset', vh[i][:], 1.0, w=[('vh', i)])
            t1 = sb('t1', [96, 512]); t2 = sb('t2', [96, 512])
            ptl = [sb('ptl%d' % i, [128, 2, 512], BF16) for i in range(3)]
            rr = sb('rr', [96, 512]); osb = [sb('osb%d' % i, [64, 512]) for i in range(2)]; ost = sb('ost', [64, T], BF16)
            sel = sb('sel', [128, 128], BF16); rb = sb('rb', [64, 512])
            S.op('pool', 'memset', sel[:], 0.0, w=['sel'])
            S.op('pool', 'memset', sel[64:65, :], 1.0, w=['sel'])
            rhi = [sb('rhi%d' % i, [128, 512], BF16) for i in range(2)]; rlo = [sb('rlo%d' % i, [128, 512], BF16) for i in range(2)]
            for i in range(2):
                S.op('pool', 'memset', rhi[i][:], 0.0, w=[('rhi', i)])
                S.op('pool', 'memset', rlo[i][:], 0.0, w=[('rlo', i)])
            ps = [pst('ps%d' % i, [128, 1024]) for i in range(2)]
            po = [pst('po%d' % i, [128, 512]) for i in range(2)]
            pq = pst('pq', [128, 512]); pq2 = pst('pq2', [128, 512]); pk = pq; pb = pq2
            LA = 2

            def project_chunks(h):
                b = h % 2
                chunks = []
                for gi, (t0, n, s) in enumerate(GROUPS):
                    def cA(gi=gi, t0=t0, n=n):
                        S.multi('pe', [('matmul', (pq[0:96, 0:n], wuq[:, c, h * 96:(h + 1) * 96], cqn[:, c, t0:t0 + n]),
                                        dict(start=(c == 0), stop=(c == 2))) for c in range(3)],
                                r=['cqn', 'wuq'], w=['pq'])
                        S.multi('pe', [('matmul', (pq2[0:96, 0:n], wuqp[:, c, h * 96:(h + 1) * 96], cqn[:, c, t0:t0 + n]),
                                        dict(start=(c == 0), stop=(c == 2))) for c in range(3)],
                                r=['cqn', 'wuqp'], w=['pq2'])
                        S.op('dve', 'tensor_copy', qt[b][0:64, t0:t0 + n], pq[0:64, 0:n], r=['pq'], w=[('qt', b, gi)])
                        S.op('dve', 'tensor_tensor', t1[64:96, 0:n], pq[64:96, 0:n], tc_[64:96, t0:t0 + n], ALU.mult,
                             r=['pq', 'tabc'], w=['t1'])
                        S.op('dve', 'tensor_tensor', t2[64:96, 0:n], pq2[64:96, 0:n], ts_[64:96, t0:t0 + n], ALU.mult,
                             r=['pq2', 'tabs'], w=['t2'])
                        S.op('dve', 'tensor_tensor', qt[b][64:96, t0:t0 + n], t1[64:96, 0:n], t2[64:96, 0:n], ALU.add,
                             r=['t1', 't2'], w=[('qt', b, gi)])

                    def cB(gi=gi, t0=t0, n=n):
                        S.multi('pe', [('matmul', (pk[:, 0:n], wukv[:, c, h * 64:h * 64 + 128], ckvn[:, c, t0:t0 + n]),
                                        dict(start=(c == 0), stop=(c == 1))) for c in range(2)],
                                r=['ckvn', 'wukv'], w=['pq'])
                        S.op('dve', 'tensor_copy', kt[b][0:64, t0:t0 + n], pk[0:64, 0:n], r=['pq'], w=[('kt', b, gi)])
                    chunks += [cA, cB]
                for i0 in range(0, NT, 8):
                    def cV(i0=i0):
                        nt_ = min(8, NT - i0)
                        calls = []
                        for ii_ in range(nt_):
                            i_ = i0 + ii_
                            calls += [('matmul', (pq2[:, ii_ * 64:(ii_ + 1) * 64], ckvn[:, c, i_ * 128:(i_ + 1) * 128],
                                                  wukv[:, c, 512 + h * 64:512 + (h + 1) * 64]), dict(start=(c == 0), stop=(c == 1)))
                                      for c in range(2)]
                        S.multi('pe', calls, r=['ckvn', 'wukv'], w=['pq2'])
                        S.op('dve', 'tensor_copy', vh[b][:, i0:i0 + nt_, 0:64],
                             pq2[:, 0:nt_ * 64].rearrange("p (a d) -> p a d", d=64), r=['pq2'], w=[('vh', b)])
                    chunks.append(cV)
                return chunks

            def project(h):
                for c_ in project_chunks(h):
                    c_()

            project(0)
            for h in range(H):
                b = h % 2
                seq = []
                for gi, (q0, nq, s) in enumerate(GROUPS):
                    tiles = list(range(NT)) if s == 0 else [32, 33]
                    npair = len(tiles) // 2
                    for ii in range(npair):
                        seq.append((gi, q0, nq, ii, tiles[2 * ii], npair))

                def qk(e):
                    gi, q0, nq, ii, i, np_ = seq[e]
                    sl = e % 2
                    S.multi('pe', [('matmul', (ps[sl][:, a_ * 512:a_ * 512 + nq], kt[b][0:96, (i + a_) * 128:(i + a_ + 1) * 128],
                                               qt[b][0:96, q0:q0 + nq]), dict(start=True, stop=True)) for a_ in range(2)],
                            r=[('kt', b, i // 4), ('kt', b, (i + 1) // 4), ('ktr', b), ('qt', b, gi)], w=[('ps', sl)])

                pending = []
                nxt = project_chunks(h + 1) if h + 1 < H else []
                cstep = max(1, (len(seq) - 12) // max(1, len(nxt)))
                for e in range(min(LA, len(seq))):
                    qk(e)
                for e in range(len(seq)):
                    gi, q0, nq, ii, i, np_ = seq[e]
                    sl = e % 2; sl2 = e % 3; ob = gi % 2
                    S.op('act', 'activation', ptl[sl2][:, :, 0:nq], ps[sl][:, :].rearrange("p (a n) -> p a n", a=2)[:, :, 0:nq],
                         AF.Exp, scale=SCALE, r=[('ps', sl)], w=[('ptl', sl2)])
                    if e + LA < len(seq):
                        qk(e + LA)
                    S.multi('pe', [('matmul', (po[ob][:, 0:nq], vh[b][:, i + a_, :], ptl[sl2][:, a_, 0:nq]),
                                    dict(start=(ii == 0 and a_ == 0), stop=(ii == np_ - 1 and a_ == 1))) for a_ in range(2)],
                            r=[('ptl', sl2), ('vh', b)], w=[('po', ob)])
                    if ii == np_ - 1:
                        fb = gi % 2
                        S.op('dve', 'tensor_copy', rhi[fb][64:65, 0:nq], po[ob][64:65, 0:nq], r=[('po', ob)], w=[('rhi', fb)])
                        S.op('dve', 'tensor_tensor', rlo[fb][64:65, 0:nq], po[ob][64:65, 0:nq], rhi[fb][64:65, 0:nq], ALU.subtract,
                             r=[('po', ob), ('rhi', fb)], w=[('rlo', fb)])
                        S.op('dve', 'tensor_copy', osb[fb][:, 0:nq], po[ob][0:64, 0:nq], r=[('po', ob)], w=[('osb', fb)])
                        pending.append((e + 3, fb, q0, nq))
                    while pending and (pending[0][0] <= e or e == len(seq) - 1):
                        _, fb, fq0, fnq = pending.pop(0)
                        S.multi('pe', [('matmul', (pb[:, 0:fnq], sel[:, :], rhi[fb][:, 0:fnq]), dict(start=True, stop=False)),
                                       ('matmul', (pb[:, 0:fnq], sel[:, :], rlo[fb][:, 0:fnq]), dict(start=False, stop=True))],
                                r=[('rhi', fb), ('rlo', fb), 'sel'], w=['pq2'])
                        S.op('dve', 'reciprocal', rb[:, 0:fnq], pb[0:64, 0:fnq], r=['pq2'], w=['rb'])
                        S.op('dve', 'tensor_tensor', ost[:, fq0:fq0 + fnq], osb[fb][:, 0:fnq], rb[:, 0:fnq], ALU.mult,
                             r=[('osb', fb), 'rb'], w=['ost'])
                    if nxt and e >= 4 and (e - 4) % cstep == 0:
                        nxt.pop(0)()
                while nxt:
                    nxt.pop(0)()
                S.dma('sp', self.mixT[h // 2, (h % 2) * 64:(h % 2) * 64 + 64, :], ost[:, :], r=['ost'])
            S.flush()

    def p5_out_mlp(self, l):
        nc, S = self.nc, self.S
        last = (l == DEPTH - 1)
        src = self.I('xin') if l == 0 else self.xres
        groups = GROUPS[:8] if last else GROUPS
        with ExitStack() as es:
            sb = lambda n, shp, dt=F32: es.enter_context(nc.sbuf_tensor(self.uq(n), shp, dt))
            pst = lambda n, shp, dt=F32: es.enter_context(nc.psum_tensor(self.uq(n), shp, dt))
            wout = sb('wout', [128, 8, D], BF16)
            S.dma('pool', wout[:], self.I('w_out')[l].rearrange("(k p) n -> p k n", p=128), w=['wout'])
            idb = sb('idb', [128, 128], BF16); S.dma('sp', idb[:], self.I('ident_bf'), w=['idb'])
            eps = sb('eps', [128, 1]); S.op('dve', 'memset', eps[:], EPS, w=['eps'])
            g1r = sb('g1r', [128, D])
            mixt = [sb('mixt%d' % i, [128, 8, 512], BF16) for i in range(2)]
            xt = [sb('xt%d' % i, [128, 4, D]) for i in range(2)]
            tmp = [sb('tmp%d' % i, [128, 512]) for i in range(2)]
            junk = sb('junk', [128, D], BF16)
            pt = [pst('pt%d' % i, [128, 512], BF16) for i in range(2)]
            tls = [dict(k='n%d' % i, ss=sb('ss%d' % i, [128, 4]), rstd=sb('rstd%d' % i, [128, 4]),
                        xn=sb('xn%d' % i, [128, 4, D], BF16), junk=junk, pt=pt,
                        hT=sb('hT%d' % i, [128, 8, 512], BF16), idb=idb, eps=eps) for i in range(2)]
            pso = [pst('pso%d' % i, [128, 512]) for i in range(4)]
            st = dict(cur_s=-1, np_=0)

            def stA(gi):
                t0, n, s = groups[gi]
                b = gi % 2; nj = n // 128
                if s != st['cur_s']:
                    S.dma('sp', g1r[:], self.mrow[s, 2 * D:3 * D].partition_broadcast(128), w=['g1r']); st['cur_s'] = s
                S.dma('sp', mixt[b][:, :, 0:n], self.mixT[:, :, t0:t0 + n].rearrange("c p t -> p c t"), w=[('mixt', b)])
                S.dma('sp', xt[b][:, 0:nj, :], src[t0:t0 + n, :].rearrange("(j p) d -> p j d", p=128), w=[('xt', b)])
                for j in range(nj):
                    for hh in range(2):
                        pi = st['np_'] % 4; st['np_'] += 1
                        S.multi('pe', [('matmul', (pso[pi][:, :], mixt[b][:, c, j * 128:(j + 1) * 128], wout[:, c, hh * 512:(hh + 1) * 512]),
                                        dict(start=(c == 0), stop=(c == 7))) for c in range(8)],
                                r=[('mixt', b), 'wout'], w=[('pso', pi)])
                        S.op('dve', 'tensor_tensor', tmp[pi % 2][:, :], pso[pi][:, :], g1r[:, hh * 512:(hh + 1) * 512], ALU.mult,
                             r=[('pso', pi), 'g1r'], w=[('tmp', pi % 2)])
                        S.op('pool', 'tensor_tensor', xt[b][:, j, hh * 512:(hh + 1) * 512], tmp[pi % 2][:, :],
                             xt[b][:, j, hh * 512:(hh + 1) * 512], ALU.add, r=[('tmp', pi % 2), ('xt', b)], w=[('xt', b)])
                S.dma('sp', self.xmid[t0:t0 + n, :].rearrange("(j p) d -> p j d", p=128), xt[b][:, 0:nj, :], r=[('xt', b)])

            def stB(gi):
                t0, n, s = groups[gi]
                b = gi % 2; nj = n // 128
                self.norm_tiles(xt[b], nj, s, 2, tls[b], ('xt', b), n)
                S.dma('sp', self.h2T[:, :, t0:t0 + n].rearrange("c p t -> p c t"), tls[b]['hT'][:, :, 0:n], r=['n%dhT' % b])

            stA(0)
            for gi in range(len(groups)):
                if gi + 1 < len(groups):
                    stA(gi + 1)
                stB(gi)
            S.flush()
        with ExitStack() as es:
            sb = lambda n, shp, dt=F32: es.enter_context(nc.sbuf_tensor(self.uq(n), shp, dt))
            pst = lambda n, shp, dt=F32: es.enter_context(nc.psum_tensor(self.uq(n), shp, dt))
            w1 = sb('w1', [128, 8, 4 * D], BF16); w2 = sb('w2', [128, 32, D], BF16)
            for hh in range(2):
                S.dma('pool', w1[:, :, hh * 2048:(hh + 1) * 2048],
                      self.I('w_mlp1')[l].rearrange("(k p) n -> p k n", p=128)[:, :, hh * 2048:(hh + 1) * 2048], w=[('w1', hh)])
            for q in range(4):
                S.dma('pool', w2[:, q * 8:(q + 1) * 8, :],
                      self.I('w_mlp2')[l].rearrange("(k p) n -> p k n", p=128)[:, q * 8:(q + 1) * 8, :], w=[('w2', q)])
            g2r = sb('g2r', [128, D])
            if last:
                fg = sb('fg', [128, D]); S.dma('sp', fg[:], self.I('fgrow'), w=['fg'])
                eps = sb('eps', [128, 1]); S.op('dve', 'memset', eps[:], EPS, w=['eps'])
                junk = sb('junk', [128, D], BF16); fss = sb('fss', [128, 4]); frs = sb('frs', [128, 4])
            hT = sb('hT', [128, 8, 512], BF16); fT = sb('fT', [128, 32, 512], BF16)
            x1 = sb('x1', [128, 4, D]); sq = [sb('sq%d' % i, [128, 512]) for i in range(2)]
            tmp = [sb('tmp%d' % i, [128, 512]) for i in range(2)]
            psf = [pst('psf%d' % i, [128, 512]) for i in range(3)]
            ps2 = [pst('ps2%d' % i, [128, 512]) for i in range(3)]
            cur_s = -1; n2 = 0
            t0_, n_, s_ = groups[0]
            S.dma('sp', hT[:, :, 0:n_], self.h2T[:, :, t0_:t0_ + n_].rearrange("c p t -> p c t"), w=['hT'])
            for gi, (t0, n, s) in enumerate(groups):
                nj = n // 128
                if s != cur_s:
                    S.dma('sp', g2r[:], self.mrow[s, 5 * D:6 * D].partition_broadcast(128), w=['g2r']); cur_s = s
                S.dma('sp', x1[:, 0:nj, :], self.xmid[t0:t0 + n, :].rearrange("(j p) d -> p j d", p=128),
                      w=[('x1', j) for j in range(nj)])
                for j in range(32):
                    pf = psf[j % 3]
                    S.multi('pe', [('matmul', (pf[:, 0:n], w1[:, k, j * 128:(j + 1) * 128], hT[:, k, 0:n]),
                                    dict(start=(k == 0), stop=(k == 7))) for k in range(8)],
                            r=[('w1', j // 16), 'hT'], w=[('psf', j % 3)])
                    S.op('act', 'activation', sq[j % 2][:, 0:n], pf[:, 0:n], AF.Square, r=[('psf', j % 3)], w=[('sq', j % 2)])
                    S.op('dve', 'scalar_tensor_tensor', fT[:, j, 0:n], pf[:, 0:n], 0.0, sq[j % 2][:, 0:n],
                         ALU.is_gt, ALU.mult, r=[('psf', j % 3), ('sq', j % 2)], w=[('fT', j)])
                if gi + 1 < len(groups):
                    t0n, nn, sn_ = groups[gi + 1]
                    S.dma('sp', hT[:, :, 0:nn], self.h2T[:, :, t0n:t0n + nn].rearrange("c p t -> p c t"), w=['hT'])
                for j in range(nj):
                    for hh in range(2):
                        pi = n2 % 3; n2 += 1
                        S.multi('pe', [('matmul', (ps2[pi][:, :], fT[:, q, j * 128:(j + 1) * 128], w2[:, q, hh * 512:(hh + 1) * 512]),
                                        dict(start=(q == 0), stop=(q == 31))) for q in range(32)],
                                r=[('fT', q) for q in range(32)] + [('w2', q) for q in range(4)], w=[('ps2', pi)])
                        S.op('dve', 'tensor_tensor', tmp[pi % 2][:, :], ps2[pi][:, :], g2r[:, hh * 512:(hh + 1) * 512], ALU.mult,
                             r=[('ps2', pi), 'g2r'], w=[('tmp', pi % 2)])
                        S.op('pool', 'tensor_tensor', x1[:, j, hh * 512:(hh + 1) * 512], tmp[pi % 2][:, :],
                             x1[:, j, hh * 512:(hh + 1) * 512], ALU.add, r=[('tmp', pi % 2), ('x1', j)], w=[('x1', j)])
                    tt = t0 + j * 128
                    if not last:
                        S.dma('sp', self.xres[tt:tt + 128, :], x1[:, j, :], r=[('x1', j)])
                    else:
                        S.op('act', 'activation', junk[:], x1[:, j, :], AF.Square, accum_out=fss[:, j:j + 1],
                             r=[('x1', j)], w=['junk', ('fss', j)])
                        S.op('act', 'activation', frs[:, j:j + 1], fss[:, j:j + 1], AF.Sqrt, bias=eps[:, 0:1], scale=1.0 / D,
                             r=[('fss', j), 'eps'], w=[('frs', j)])
                        S.op('dve', 'reciprocal', frs[:, j:j + 1], frs[:, j:j + 1], r=[('frs', j)], w=[('frs', j)])
                        S.op('dve', 'scalar_tensor_tensor', x1[:, j, :], x1[:, j, :], frs[:, j:j + 1], fg[:], ALU.mult, ALU.mult,
                             r=[('x1', j), ('frs', j), 'fg'], w=[('x1', j)])
                        S.dma('sp', self.y[tt:tt + 128, :], x1[:, j, :], r=[('x1', j)])
            S.flush()

    def p3_fnet(self, l):
        for nm, t0 in (('x', 0), ('c', L)):
            if nm == 'c' and l == DEPTH - 1:
                continue
            self._fnet(l, nm, t0)

    def _fnet(self, l, nm, t0):
        nc, S = self.nc, self.S
        L1 = self.fn[nm]['L1']; Ls = self.fn[nm]['Ls']; L2 = 64
        with ExitStack() as es:
            sb = lambda n, shp, dt=F32: es.enter_context(nc.sbuf_tensor(self.uq(n), shp, dt))
            pst = lambda n, shp, dt=F32: es.enter_context(nc.psum_tensor(self.uq(n), shp, dt))
            uf = sb('uf', [128, 2, Ls], BF16)
            S.dma('sp', uf[:], self.pxT[5:7, :, t0:t0 + Ls].rearrange("c p t -> p c t"), w=['uf'])
            csw = sb('csw', [128, 256], BF16); S.dma('sp', csw[:], self.I('fn_csw'), w=['csw'])
            w1 = sb('w1', [L1, 2 * L1], BF16); w2 = sb('w2', [L1, 2 * L1], BF16)
            S.dma('sp', w1[:], self.I('fn_w1' + nm), w=['w1']); S.dma('sp', w2[:], self.I('fn_w2' + nm), w=['w2'])
            tr = sb('tr', [64, Ls], BF16); sn = sb('sn', [64, Ls], BF16)
            S.dma('sp', tr[:], self.I('fn_tr' + nm), w=['tr']); S.dma('sp', sn[:], self.I('fn_sn' + nm), w=['sn'])
            U = sb('U', [L1, L2, 512], BF16)
            Y = sb('Y', [64, 2, 256, L1], BF16)
            pu = [pst('pu%d' % i, [128, 512]) for i in range(2)]
            py = [pst('py%d' % i, [128, 512]) for i in range(2)]
            pf = [pst('pf%d' % i, [128, 512]) for i in range(2)]
            for l2 in range(L2):
                b = l2 % 2
                S.multi('pe', [('matmul', (pu[b][0:L1, c * 256:(c + 1) * 256], uf[:, c, l2::L2], csw[:, :]),
                                dict(start=True, stop=True)) for c in range(2)], r=['uf', 'csw'], w=[('pu', b)])
                if b == 0:
                    S.op('act', 'activation', U[:, l2, :], pu[b][0:L1, :], AF.Copy, r=[('pu', b)], w=['U'])
                else:
                    S.op('dve', 'tensor_copy', U[:, l2, :], pu[b][0:L1, :], r=[('pu', b)], w=['U'])
            cpb = 512 // (2 * L1)
            nb = 0
            for ch0 in range(0, 256, cpb):
                b = nb % 2; nb += 1
                calls = []
                for chl in range(cpb):
                    ch = ch0 + chl; c = ch // 128; i = ch % 128
                    o = py[b][0:64, chl * 2 * L1:(chl + 1) * 2 * L1]
                    calls.append(('matmul', (o, U[:, :, c * 256 + i], w1[:, :]), dict(start=True, stop=False)))
                    calls.append(('matmul', (o, U[:, :, c * 256 + 128 + i], w2[:, :]), dict(start=False, stop=True)))
                S.multi('pe', calls, r=['U', 'w1', 'w2'], w=[('py', b)])
                for r_ in range(2):
                    src = py[b][0:64, 0:cpb * 2 * L1].rearrange("p (c r a) -> p c r a", c=cpb, r=2)[:, :, r_, :]
                    dst = Y[:, r_, ch0:ch0 + cpb, :]
                    if r_ == 0:
                        S.op('act', 'activation', dst, src, AF.Copy, r=[('py', b)], w=['Y'])
                    else:
                        S.op('dve', 'tensor_copy', dst, src, r=[('py', b)], w=['Y'])
            na = min(8, L1)
            nb = 0
            for cc in range(2):
                for a0 in range(0, L1, na):
                    b = nb % 2; nb += 1
                    calls = []
                    for al in range(na):
                        a_ = a0 + al
                        o = pf[b][:, al * 64:(al + 1) * 64]
                        calls.append(('matmul', (o, Y[:, 0, cc * 128:(cc + 1) * 128, a_], tr[:, a_::L1]), dict(start=True, stop=False)))
                        calls.append(('matmul', (o, Y[:, 1, cc * 128:(cc + 1) * 128, a_], sn[:, a_::L1]), dict(start=False, stop=True)))
                    S.multi('pe', calls, r=['Y', 'tr', 'sn'], w=[('pf', b)])
                    dst = uf[:, cc, :].rearrange("p (b a) -> p a b", a=L1)[:, a0:a0 + na, :]
                    src = pf[b][:, 0:na * 64].rearrange("p (a b) -> p a b", a=na)
                    if b == 0:
                        S.op('act', 'activation', dst, src, AF.Copy, r=[('pf', b), 'U'], w=['uf'])
                    else:
                        S.op('dve', 'tensor_copy', dst, src, r=[('pf', b), 'U'], w=['uf'])
            S.dma('sp', self.mixT[4:6, :, t0:t0 + Ls].rearrange("c p t -> p c t"), uf[:], r=['uf'])
            S.flush()

    def _hy_f1(self, src, krows, N1, w1h, hyY, es_sb, pst):
        S = self.S
        cpb = 512 // (2 * N1)
        py = [pst('hpy%d' % i, [128, 512]) for i in range(2)]
        nbank = 256 // cpb
        GB = min(4, nbank)
        stg = [es_sb('hstg%d' % i, [128, GB, 512], BF16) for i in range(2)]
        for nb in range(nbank):
            ch0 = nb * cpb
            b = nb % 2; sb_ = (nb // GB) % 2; g = nb % GB
            S.multi('pe', [('matmul', (py[b][:, chl * 2 * N1:(chl + 1) * 2 * N1], src[0:krows, :, ch0 + chl], w1h[0:krows, :]),
                            dict(start=True, stop=True)) for chl in range(cpb)], r=['hsrc', 'w1h'], w=[('hpy', b)])
            if b == 0:
                S.op('act', 'activation', stg[sb_][:, g, :], py[b][:, :], AF.Copy, r=[('hpy', b)], w=[('hstg', sb_)])
            else:
                S.op('dve', 'tensor_copy', stg[sb_][:, g, :], py[b][:, :], r=[('hpy', b)], w=[('hstg', sb_)])
            if g == GB - 1:
                c0 = (nb - GB + 1) * cpb
                S.dma('sp', hyY[:, c0:c0 + GB * cpb, :].rearrange("p c x -> p (c x)"),
                      stg[sb_][:, :, :].rearrange("p g x -> p (g x)"), r=[('hstg', sb_)])

    def _hy_f2(self, Yz, ncol, N1, tr, ti, nti, pz, f1):
        S = self.S
        calls = [('matmul', (pz[:, 0:ncol], tr[:, f1::N1], Yz[:, :, f1]), dict(start=True, stop=False)),
                 ('matmul', (pz[:, 0:ncol], nti[:, f1::N1], Yz[:, :, N1 + f1]), dict(start=False, stop=True)),
                 ('matmul', (pz[:, ncol:2 * ncol], ti[:, f1::N1], Yz[:, :, f1]), dict(start=True, stop=False)),
                 ('matmul', (pz[:, ncol:2 * ncol], tr[:, f1::N1], Yz[:, :, N1 + f1]), dict(start=False, stop=True))]
        return calls

    def _range_sin(self, dst, ps, fcol, fbcol, tl, n):
        S = self.S
        a, t, k = tl['a'], tl['t'], tl['k']
        S.op('dve', 'tensor_scalar', a[:, 0:n], ps, fcol, fbcol, ALU.mult, ALU.add, r=[tl['ps'], 'fcols'], w=['rs_a'])
        S.op('dve', 'tensor_scalar', t[:, 0:n], a[:, 0:n], 1.0 / (2 * math.pi), 12582912.0, ALU.mult, ALU.add, r=['rs_a'], w=['rs_t'])
        S.op('dve', 'tensor_scalar', k[:, 0:n], t[:, 0:n], -12582912.0, None, ALU.add, r=['rs_t'], w=['rs_k'])
        S.op('dve', 'scalar_tensor_tensor', a[:, 0:n], k[:, 0:n], -2 * math.pi, a[:, 0:n], ALU.mult, ALU.add, r=['rs_k', 'rs_a'], w=['rs_a'])
        S.op('dve', 'tensor_scalar', a[:, 0:n], a[:, 0:n], -3.14159, 3.14159, ALU.max, ALU.min, r=['rs_a'], w=['rs_a'])
        S.op('act', 'activation', dst, a[:, 0:n], AF.Sin, r=['rs_a'], w=[tl['dst']])

    def pk_filters(self, l):
        for nm in ('x', 'c'):
            if nm == 'c' and l == DEPTH - 1:
                continue
            self._filters(l, nm)

    def _filters(self, l, nm):
        nc, S = self.nc, self.S
        hy = self.hy[nm]; N1 = hy['N1']; Ls = hy['Ls']; N = hy['N']
        hyY = self.hyY[nm]
        with ExitStack() as es:
            sb = lambda n, shp, dt=F32: es.enter_context(nc.sbuf_tensor(self.uq(n), shp, dt))
            pst = lambda n, shp, dt=F32: es.enter_context(nc.psum_tensor(self.uq(n), shp, dt))
            ks = sb('ks', [N1, 128, 256], BF16)
            w1h = sb('w1h', [N1, 2 * N1], BF16); S.dma('sp', w1h[:], self.I('hy_w1' + nm), w=['w1h'])
            with ExitStack() as es2:
                sb2 = lambda n, shp, dt=F32: es2.enter_context(nc.sbuf_tensor(self.uq(n), shp, dt))
                ps2 = lambda n, shp, dt=F32: es2.enter_context(nc.psum_tensor(self.uq(n), shp, dt))
                wa = sb2('wa', [33, 64]); wb = sb2('wb', [64, 64]); wc = sb2('wc', [64, 512], BF16)
                S.dma('sp', wa[:], self.I('hy_w1')[l], w=['wa']); S.dma('sp', wb[:], self.I('hy_w2')[l], w=['wb'])
                S.dma('pool', wc[:], self.I('hy_w3')[l], w=['wc'])
                fc = sb2('fc', [64, 4])
                S.op('dve', 'tensor_copy', fc[:, 0:1], self.cols[0:64, C_HFR:C_HFR + 1], r=['cols'], w=['fcols'])
                S.op('dve', 'tensor_tensor', fc[:, 1:2], self.cols[0:64, C_HFR:C_HFR + 1], self.cols[0:64, C_HB1:C_HB1 + 1], ALU.mult, r=['cols'], w=['fcols'])
                S.op('dve', 'tensor_tensor', fc[:, 2:3], self.cols[0:64, C_HFR:C_HFR + 1], self.cols[0:64, C_HB2:C_HB2 + 1], ALU.mult, r=['cols'], w=['fcols'])
                zt = [sb2('zt%d' % i, [33, 512]) for i in range(2)]
                h1 = sb2('h1', [64, 512])
                hA = sb2('hA', [64, N], BF16); hB = sb2('hB', [64, N], BF16)
                tl = dict(a=sb2('rsa', [64, 512]), t=sb2('rst', [64, 512]), k=sb2('rsk', [64, 512]))
                tcol = sb2('tcol', [N1, 128]); S.dma('sp', tcol[:], self.I('hy_tcol' + nm), w=['tcol'])
                drow = sb2('drow', [128, 256]); S.dma('sp', drow[:], self.I('hy_drow' + nm), w=['drow'])
                dec = [sb2('dec%d' % i, [N1, 256]) for i in range(2)]
                kabs = [sb2('kabs%d' % i, [N1, 32 * 256], BF16) for i in range(2)]
                onb = sb2('onb', [128, 1], BF16); S.op('dve', 'memset', onb[:], 1.0, w=['onb'])
                sres = sb2('sres', [128, 2])
                pa = ps2('pa', [64, 512]); pb_ = ps2('pb', [64, 512])
                pk = [ps2('pk%d' % i, [128, 512]) for i in range(2)]
                pSs = [ps2('pS%d' % i, [128, 2]) for i in range(2)]
                nblk = N // 512
                for blk in range(nblk):
                    b = blk % 2
                    S.dma('sp', zt[b][:], self.I('hy_zT' + nm)[:, blk * 512:(blk + 1) * 512], w=[('zt', b)])
                    S.op('pe', 'matmul', pa[:, :], wa[:, :], zt[b][:, :], start=True, stop=True, r=['wa', ('zt', b)], w=['pa'])
                    tl.update(ps='pa', dst='h1')
                    self._range_sin(h1[:, :], pa[:, :], fc[:, 0:1], fc[:, 1:2], tl, 512)
                    S.op('pe', 'matmul', pb_[:, :], wb[:, :], h1[:, :], start=True, stop=True, r=['wb', 'h1'], w=['pb'])
                    tl.update(ps='pb', dst='hA')
                    self._range_sin(hA[:, blk * 512:(blk + 1) * 512], pb_[:, :], fc[:, 0:1], fc[:, 2:3], tl, 512)
                S.op('pool', 'tensor_copy', hB[:, Ls:N], hA[:, Ls:N], r=['hA'], w=['hB'])
                S.op('pool', 'memset', hB[:, 0:Ls], 0.0, w=['hB'])
                S.op('pool', 'memset', hA[:, Ls:N], 0.0, r=['hB'], w=['hA'])
                for s2 in range(128):
                    b = s2 % 2
                    S.multi('pe', [('matmul', (pk[b][0:N1, 0:256], hA[:, s2::128], wc[:, 0:256]), dict(start=True, stop=False)),
                                   ('matmul', (pk[b][0:N1, 0:256], hB[:, s2::128], wc[:, 256:512]), dict(start=False, stop=True))],
                            r=['hA', 'hB', 'wc'], w=[('pk', b)])
                    S.op('act', 'activation', dec[b][:, :], drow[0:N1, :], AF.Exp, scale=tcol[:, s2:s2 + 1],
                         r=['drow', 'tcol'], w=[('dec', b)])
                    S.op('dve', 'tensor_tensor', ks[:, s2, :], pk[b][0:N1, 0:256], dec[b][:, :], ALU.mult,
                         r=[('pk', b), ('dec', b)], w=['hsrc'])
                for q in range(4):
                    S.op('act', 'activation', kabs[q % 2][:, :], ks[:, q * 32:(q + 1) * 32, :].rearrange("p s c -> p (s c)"), AF.Abs,
                         r=['hsrc'], w=[('kabs', q % 2)])
                    for cc in range(2):
                        S.multi('pe', [('matmul', (pSs[cc][:, 0:1], kabs[q % 2][:, sl_ * 256 + cc * 128:sl_ * 256 + (cc + 1) * 128], onb[0:N1, 0:1]),
                                        dict(start=(q == 0 and sl_ == 0), stop=(q == 3 and sl_ == 31))) for sl_ in range(32)],
                                r=[('kabs', q % 2), 'onb'], w=[('pS', cc)])
                for cc in range(2):
                    S.op('dve', 'tensor_copy', sres[:, cc:cc + 1], pSs[cc][:, 0:1], r=[('pS', cc)], w=['sres'])
                S.dma('sp', hy['ssum'], sres[:, :], r=['sres'])
                self._hy_f1(ks, N1, N1, w1h, hyY, sb2, ps2)
                S.flush()
        with ExitStack() as es:
            sb = lambda n, shp, dt=F32: es.enter_context(nc.sbuf_tensor(self.uq(n), shp, dt))
            pst = lambda n, shp, dt=F32: es.enter_context(nc.psum_tensor(self.uq(n), shp, dt))
            Yz = sb('Yz', [128, 256, 2 * N1], BF16); S.dma('sp', Yz[:], hyY, w=['Yz'])
            tr = sb('tr', [128, N], BF16); ti = sb('ti', [128, N], BF16); nti = sb('nti', [128, N], BF16)
            S.dma('sp', tr[:], self.I('hy_tr' + nm), w=['tr']); S.dma('sp', ti[:], self.I('hy_ti' + nm), w=['ti'])
            S.dma('sp', nti[:], self.I('hy_nti' + nm), w=['nti'])
            FB = min(8, N1)
            kst = [sb('kst%d' % i, [128, FB, 512], BF16) for i in range(2)]
            pz = [pst('pz%d' % i, [128, 512]) for i in range(2)]
            for f1 in range(N1):
                b = f1 % 2; kb = (f1 // FB) % 2
                S.multi('pe', self._hy_f2(Yz, 256, N1, tr, ti, nti, pz[b], f1), r=['Yz', 'tr', 'ti', 'nti'], w=[('pz', b)])
                if b == 0:
                    S.op('act', 'activation', kst[kb][:, f1 % FB, :], pz[b][:, :], AF.Copy, r=[('pz', b)], w=[('kst', kb)])
                else:
                    S.op('dve', 'tensor_copy', kst[kb][:, f1 % FB, :], pz[b][:, :], r=[('pz', b)], w=[('kst', kb)])
                if f1 % FB == FB - 1:
                    f0 = f1 - FB + 1
                    S.dma('sp', hy['kf'][:, f0:f0 + FB, :, :].rearrange("p f r c -> p (f r c)"),
                          kst[kb][:, :, :].rearrange("p f x -> p (f x)"), r=[('kst', kb)])
            S.flush()

    def p4_hyena(self, l):
        for nm, t0 in (('x', 0), ('c', L)):
            if nm == 'c' and l == DEPTH - 1:
                continue
            self._hyena(l, nm, t0)

    def _hyena(self, l, nm, t0):
        nc, S = self.nc, self.S
        hy = self.hy[nm]; N1 = hy['N1']; Ls = hy['Ls']; N = hy['N']; K1 = N1 // 2
        hyY = self.hyY[nm]; hyF = self.hyF[nm]
        nblk = max(1, Ls // 512); bw = Ls // nblk
        with ExitStack() as es:
            sb = lambda n, shp, dt=F32: es.enter_context(nc.sbuf_tensor(self.uq(n), shp, dt))
            pst = lambda n, shp, dt=F32: es.enter_context(nc.psum_tensor(self.uq(n), shp, dt))
            uh = sb('uh', [128, 6, Ls], BF16)
            S.dma('sp', uh[:], self.pxT[7:13, :, t0:t0 + Ls].rearrange("c p t -> p c t"), w=['uh'])
            idb = sb('idb', [128, 128], BF16); S.dma('sp', idb[:], self.I('ident_bf'), w=['idb'])
            w1h = sb('w1h', [N1, 2 * N1], BF16); S.dma('sp', w1h[:], self.I('hy_w1' + nm), w=['w1h'])
            zT = sb('zT', [128, 2, Ls], BF16); x0 = sb('x0', [128, 2, Ls], BF16)
            o = [sb('o%d' % c, [128, bw]) for c in range(6)]
            zs = sb('zs', [K1, 128, 256], BF16)
            cw = lambda tap, c: self.cols[:, C_HCW + tap * 6 + c:C_HCW + tap * 6 + c + 1]
            for blk in range(nblk):
                c0 = blk * bw
                for c in range(6):
                    S.op('act', 'activation', o[c][:, :], uh[:, c, c0:c0 + bw], AF.Identity,
                         bias=self.cols[:, C_HCB + c:C_HCB + c + 1], scale=cw(1, c), r=['uh', 'cols'], w=[('o', c)])
                    lo = 1 if blk == 0 else 0
                    S.op('dve', 'scalar_tensor_tensor', o[c][:, lo:bw], uh[:, c, c0 + lo - 1:c0 + bw - 1], cw(0, c), o[c][:, lo:bw],
                         ALU.mult, ALU.add, r=['uh', 'cols', ('o', c)], w=[('o', c)])
                    hi = bw - 1 if blk == nblk - 1 else bw
                    S.op('dve', 'scalar_tensor_tensor', o[c][:, 0:hi], uh[:, c, c0 + 1:c0 + hi + 1], cw(2, c), o[c][:, 0:hi],
                         ALU.mult, ALU.add, r=['uh', 'cols', ('o', c)], w=[('o', c)])
                for cc in range(2):
                    S.op('pool', 'tensor_copy', x0[:, cc, c0:c0 + bw], o[cc][:, :], r=[('o', cc)], w=['x0'])
                    S.op('pool', 'tensor_tensor', zT[:, cc, c0:c0 + bw], o[2 + cc][:, :], o[4 + cc][:, :], ALU.mult,
                         r=[('o', 2 + cc), ('o', 4 + cc)], w=['zT'])
            S.dma('sp', self.hx0[:, :, t0:t0 + Ls].rearrange("c p t -> p c t"), x0[:], r=['x0'])
            S.dma('sp', self.hz[:, :, t0:t0 + Ls].rearrange("c p t -> p c t"), zT[:], r=['zT'])
            ptr = [pst('ptr%d' % i, [128, 1024], BF16) for i in range(2)]
            nb = 0
            for s20 in range(0, 128, 4):
                b = nb % 2; nb += 1
                calls = []
                for sl in range(4):
                    for cc in range(2):
                        calls.append(('transpose', (ptr[b][0:K1, (sl * 2 + cc) * 128:(sl * 2 + cc + 1) * 128],
                                                    zT[:, cc, s20 + sl::128], idb[:, :]), {}))
                S.multi('pe', calls, r=['zT', 'idb'], w=[('ptr', b)])
                dst = zs[:, s20:s20 + 4, :].rearrange("p a c -> p (a c)")
                if b == 0:
                    S.op('act', 'activation', dst, ptr[b][0:K1, :], AF.Copy, r=[('ptr', b)], w=['hsrc'])
                else:
                    S.op('dve', 'tensor_copy', dst, ptr[b][0:K1, :], r=[('ptr', b)], w=['hsrc'])
            self._hy_f1(zs, K1, N1, w1h, hyY, sb, pst)
            S.flush()
        with ExitStack() as es:
            sb = lambda n, shp, dt=F32: es.enter_context(nc.sbuf_tensor(self.uq(n), shp, dt))
            pst = lambda n, shp, dt=F32: es.enter_context(nc.psum_tensor(self.uq(n), shp, dt))
            tr = sb('tr', [128, N], BF16); ti = sb('ti', [128, N], BF16); nti = sb('nti', [128, N], BF16)
            Yz = sb('Yz', [128, 128, 2 * N1], BF16); S.dma('sp', Yz[:], hyY[:, 0:128, :], w=['Yz'])
            S.dma('sp', tr[:], self.I('hy_tr' + nm), w=['tr']); S.dma('sp', ti[:], self.I('hy_ti' + nm), w=['ti'])
            S.dma('sp', nti[:], self.I('hy_nti' + nm), w=['nti'])
            kf = sb('kf', [128, N1, 2, 256], BF16); S.dma('sp', kf[:], hy['kf'], w=['kf'])
            Yf = sb('Yf', [128, N1, 2, 128], BF16)
            FB = min(4, N1)
            ta = [sb('ta%d' % i, [128, FB, 2, 128]) for i in range(2)]; tb = [sb('tb%d' % i, [128, FB, 2, 128]) for i in range(2)]
            pz = [pst('pz%d' % i, [128, FB * 256]) for i in range(2)]
            for cc in range(2):
                if cc == 1:
                    S.dma('sp', Yz[:], hyY[:, 128:256, :], w=['Yz'])
                ccs = slice(cc * 128, (cc + 1) * 128)
                for st_ in range(N1 // FB):
                    f0 = st_ * FB; b = st_ % 2
                    calls = []
                    for fl in range(FB):
                        calls += self._hy_f2(Yz, 128, N1, tr, ti, nti, pz[b][:, fl * 256:(fl + 1) * 256], f0 + fl)
                    S.multi('pe', calls, r=['Yz', 'tr', 'ti', 'nti'], w=[('pz', b)])
                    pzv = pz[b][:, :].rearrange("p (f r c) -> p f r c", f=FB, r=2)
                    S.op('dve', 'tensor_tensor', ta[b][:, :, :, :], pzv, kf[:, f0:f0 + FB, :, ccs], ALU.mult, r=[('pz', b), 'kf'], w=[('ta', b)])
                    S.op('dve', 'tensor_tensor', tb[b][:, :, 0, :], pzv[:, :, 0, :], kf[:, f0:f0 + FB, 1, ccs], ALU.mult, r=[('pz', b), 'kf'], w=[('tb', b)])
                    S.op('dve', 'tensor_tensor', tb[b][:, :, 1, :], pzv[:, :, 1, :], kf[:, f0:f0 + FB, 0, ccs], ALU.mult, r=[('pz', b), 'kf'], w=[('tb', b)])
                    S.op('pool', 'tensor_tensor', Yf[:, f0:f0 + FB, 0, :], ta[b][:, :, 0, :], ta[b][:, :, 1, :], ALU.subtract, r=[('ta', b)], w=['Yf'])
                    S.op('pool', 'tensor_tensor', Yf[:, f0:f0 + FB, 1, :], tb[b][:, :, 0, :], tb[b][:, :, 1, :], ALU.add, r=[('tb', b)], w=['Yf'])
                S.dma('sp', hyF[cc], Yf[:], r=['Yf'])
            S.flush()
        for cc in range(2):
            with ExitStack() as es:
                sb = lambda n, shp, dt=F32: es.enter_context(nc.sbuf_tensor(self.uq(n), shp, dt))
                pst = lambda n, shp, dt=F32: es.enter_context(nc.psum_tensor(self.uq(n), shp, dt))
                Yf = sb('Yf', [128, N1, 2, 128], BF16); S.dma('sp', Yf[:], hyF[cc], w=['Yf'])
                i1a = sb('i1a', [128, 256], BF16); i1b = sb('i1b', [128, 256], BF16)
                S.dma('sp', i1a[:], self.I('hy_i1a' + nm), w=['i1a']); S.dma('sp', i1b[:], self.I('hy_i1b' + nm), w=['i1b'])
                er = sb('er', [N1, Ls], BF16); nei = sb('nei', [N1, Ls], BF16)
                S.dma('sp', er[:], self.I('hy_er' + nm), w=['er']); S.dma('sp', nei[:], self.I('hy_nei' + nm), w=['nei'])
                V = sb('V', [N1, 128, 2, 128], BF16)
                yT = sb('yT', [128, Ls]); x0h = sb('x0h', [128, Ls], BF16); zh = sb('zh', [128, Ls], BF16)
                S.dma('sp', x0h[:], self.hx0[cc, :, t0:t0 + Ls], w=['x0h']); S.dma('sp', zh[:], self.hz[cc, :, t0:t0 + Ls], w=['zh'])
                ssb = sb('ssb', [128, 2]); S.dma('sp', ssb[:], hy['ssum'], w=['ssb'])
                S.op('dve', 'reciprocal', ssb[:, :], ssb[:, :], r=['ssb'], w=['ssb'])
                res = sb('res', [128, Ls], BF16)
                pv = [pst('pv%d' % i, [128, 512]) for i in range(2)]
                py = [pst('py%d' % i, [128, 512]) for i in range(2)]
                nb = 0
                for ch0 in range(0, 128, 2):
                    b = nb % 2; nb += 1
                    calls = []
                    for chl in range(2):
                        o_ = pv[b][0:N1, chl * 256:(chl + 1) * 256]
                        calls.append(('matmul', (o_, Yf[:, :, 0, ch0 + chl], i1a[:, :]), dict(start=True, stop=False)))
                        calls.append(('matmul', (o_, Yf[:, :, 1, ch0 + chl], i1b[:, :]), dict(start=False, stop=True)))
                    S.multi('pe', calls, r=['Yf', 'i1a', 'i1b'], w=[('pv', b)])
                    dst = V[:, ch0:ch0 + 2, :, :].rearrange("p c r a -> p (c r a)")
                    if b == 0:
                        S.op('act', 'activation', dst, pv[b][0:N1, :], AF.Copy, r=[('pv', b)], w=['V'])
                    else:
                        S.op('dve', 'tensor_copy', dst, pv[b][0:N1, :], r=[('pv', b)], w=['V'])
                nta = min(128, 512 // K1)
                nb = 0
                for ta0 in range(0, 128, nta):
                    b = nb % 2; nb += 1
                    calls = []
                    for al in range(nta):
                        ta_ = ta0 + al
                        o_ = py[b][:, al * K1:(al + 1) * K1]
                        calls.append(('matmul', (o_, V[:, :, 0, ta_], er[:, ta_::128]), dict(start=True, stop=False)))
                        calls.append(('matmul', (o_, V[:, :, 1, ta_], nei[:, ta_::128]), dict(start=False, stop=True)))
                    S.multi('pe', calls, r=['V', 'er', 'nei'], w=[('py', b)])
                    dst = yT[:, :].rearrange("p (b a) -> p a b", a=128)[:, ta0:ta0 + nta, :]
                    src = py[b][:, 0:nta * K1].rearrange("p (a b) -> p a b", b=K1)
                    if b == 0:
                        S.op('act', 'activation', dst, src, AF.Copy, r=[('py', b)], w=['yT'])
                    else:
                        S.op('dve', 'tensor_copy', dst, src, r=[('py', b)], w=['yT'])
                S.op('act', 'activation', yT[:, :], yT[:, :], AF.Identity, scale=ssb[:, cc:cc + 1], r=['yT', 'ssb'], w=['yT'])
                S.op('dve', 'scalar_tensor_tensor', yT[:, :], zh[:, :], self.cols[:, C_HD + cc:C_HD + cc + 1], yT[:, :],
                     ALU.mult, ALU.add, r=['zh', 'yT', 'cols'], w=['yT'])
                S.op('dve', 'tensor_tensor', res[:, :], yT[:, :], x0h[:, :], ALU.mult, r=['yT', 'x0h'], w=['res'])
                S.dma('sp', self.mixT[6 + cc, :, t0:t0 + Ls], res[:, :], r=['res'])
                S.flush()


def build_program(debug=False, stop_after=None):
    P = Prog(debug=debug)
    steps = []
    for l in range(DEPTH):
        steps += [('p0', l), ('p1', l), ('pk', l), ('p2', l), ('p3', l), ('p4', l), ('p5', l)]
    for (nm, l) in steps:
        fn = getattr(P, {'p0': 'p0_mod', 'p1': 'p1_inproj', 'pk': 'pk_filters', 'p2': 'p2_attn',
                         'p3': 'p3_fnet', 'p4': 'p4_hyena', 'p5': 'p5_out_mlp'}[nm])
        fn(l)
        if stop_after is not None and (nm, l) == stop_after:
            break
    P.es.close()
    P.nc.used_inputs = list(P._in.keys())
    return P.nc


_PROG = None


def kernel(**inputs):
    global _PROG
    per = _prep(inputs)
    if _PROG is None:
        _PROG = build_program()
    per = [{k: d[k] for k in _PROG.used_inputs} for d in per]
    res = run_bass_kernel_spmd(_PROG, per, core_ids=list(range(8)))
    return np.stack([np.asarray(r['y'], np.float32) for r in res.results], axis=0)
```

```python
import math
import numpy as np
import ml_dtypes
from contextlib import ExitStack
import concourse.bass as bass
import concourse.mybir as mybir
from concourse.bass_utils import run_bass_kernel_spmd

F32 = mybir.dt.float32
BF16 = mybir.dt.bfloat16
AF = mybir.ActivationFunctionType
ALU = mybir.AluOpType

D = 1024
L = 4096
LC = 256
T = L + LC
NT = T // 128
DEPTH = 2
H = 8
OFF_Q, OFF_KV, OFF_KR, OFF_F, OFF_H, IN_W = 0, 384, 640, 672, 928, 1696
EPS = 1e-6
NCOL = 64
C_N1G, C_N2G, C_QG, C_KVG, C_HCW, C_HCB, C_HD, C_HB1, C_HFR, C_HB2 = 0, 8, 16, 19, 21, 39, 45, 47, 48, 49


class _Op:
    __slots__ = ('eng', 'calls', 'deps', 'signal', 'sem', 'val', 'dma', 'waits', 'pre')


class Sched:
    NDMA = 8
    ATTR = [('pe', 'tensor'), ('act', 'scalar'), ('dve', 'vector'), ('pool', 'gpsimd'), ('sp', 'sync')]

    def __init__(self, nc, es):
        self.nc = nc
        self.sem = {e: es.enter_context(nc.semaphore('s_' + e)) for e in ['pe', 'act', 'dve', 'pool']}
        self.dsem = {q: [es.enter_context(nc.semaphore('d_%s%d' % (q, i))) for i in range(self.NDMA)]
                     for q in ['sp', 'pool', 'act']}
        self.cnt = {e: 0 for e in self.sem}
        self.dcnt = {q: 0 for q in self.dsem}
        self._reset()

    def _reset(self):
        self.ops = {e: [] for e, _ in self.ATTR}
        self.last_w = {}
        self.readers = {}

    def multi(self, eng, calls, r=(), w=(), dma=False):
        op = _Op()
        op.eng = eng; op.calls = calls; op.dma = dma; op.signal = False; op.pre = None
        deps = []
        for k in r:
            d = self.last_w.get(k)
            if d is not None: deps.append(d)
        for k in w:
            d = self.last_w.get(k)
            if d is not None: deps.append(d)
            deps.extend(self.readers.get(k, ()))
        if eng == 'pe':
            deps = [d for d in deps if d.eng != 'pe' or d.dma]
        op.deps = [d for d in deps if d is not op]
        for d in op.deps: d.signal = True
        for k in r: self.readers.setdefault(k, []).append(op)
        for k in w:
            self.last_w[k] = op
            self.readers[k] = []
        self.ops[eng].append(op)
        return op

    def op(self, eng, name, *args, r=(), w=(), **kw):
        return self.multi(eng, [(name, args, kw)], r=r, w=w)

    def dma(self, q, out, in_, r=(), w=(), **kw):
        return self.multi(q, [('dma_start', (out, in_), kw)], r=r, w=w, dma=True)

    def flush(self):
        nc = self.nc
        for e, _ in self.ATTR:
            ops = self.ops[e]
            for op in reversed(ops):
                if not op.dma:
                    op.signal = True
                    break
            for op in ops:
                if op.dma:
                    j = self.dcnt[e]; self.dcnt[e] += 1
                    op.sem = self.dsem[e][j % self.NDMA]
                    op.val = 16 * (j // self.NDMA + 1)
                    op.pre = (op.sem, op.val - 16) if op.val > 16 else None
                elif op.signal:
                    self.cnt[e] += 1
                    op.sem = self.sem[e]; op.val = self.cnt[e]
        finals = {}
        for e, _ in self.ATTR:
            seen = {}
            for op in self.ops[e]:
                need = {}
                if op.pre is not None: need[id(op.pre[0])] = op.pre
                for d in op.deps:
                    cur = need.get(id(d.sem))
                    if cur is None or cur[1] < d.val: need[id(d.sem)] = (d.sem, d.val)
                op.waits = []
                for k, (s, v) in need.items():
                    if seen.get(k, -1) < v:
                        seen[k] = v; op.waits.append((s, v))
                if op.dma or op.signal:
                    cur = finals.get(id(op.sem))
                    if cur is None or cur[1] < op.val: finals[id(op.sem)] = (op.sem, op.val)
        with nc.Block() as blk:
            for e, attr in self.ATTR:
                ops = self.ops[e]
                if not ops and e != 'sp': continue

                def body(eng, ops=ops, e=e):
                    for op in ops:
                        for (s, v) in op.waits: eng.wait_ge(s, v)
                        inst = None
                        for (name, args, kw) in op.calls:
                            inst = getattr(eng, name)(*args, **kw)
                        if op.dma: inst.then_inc(op.sem, 16)
                        elif op.signal: inst.then_inc(op.sem, 1)
                    if e == 'sp':
                        for (s, v) in finals.values(): eng.wait_ge(s, v)
                getattr(blk, attr)(body)
        self._reset()


def _bf(a):
    return np.ascontiguousarray(a.astype(ml_dtypes.bfloat16))


def _f32(a):
    return np.ascontiguousarray(a.astype(np.float32))


ROPE_PERM = np.array(list(range(8, 16)) + list(range(0, 8)) + list(range(24, 32)) + list(range(16, 24)))


def _rope_tables():
    rows = L // 64
    row = np.repeat(np.arange(rows, dtype=np.float64), 64)
    col = np.tile(np.arange(64, dtype=np.float64), rows)
    inv = 10000.0 ** (-np.arange(0, 16, 2, dtype=np.float64) / 16)
    ar = row[None, :] * inv[:, None]
    ac = col[None, :] * inv[:, None]
    cos = np.ones((32, T)); sin = np.zeros((32, T))
    cos[0:8, :L] = np.cos(ar); cos[8:16, :L] = np.cos(ar); cos[16:24, :L] = np.cos(ac); cos[24:32, :L] = np.cos(ac)
    sin[0:8, :L] = -np.sin(ar); sin[8:16, :L] = np.sin(ar); sin[16:24, :L] = -np.sin(ac); sin[24:32, :L] = np.sin(ac)
    return _f32(cos), _f32(sin)


def _fnet_tables(Ls, L1, L2):
    w = np.arange(64)
    ph = 2 * np.pi * np.outer(w, w) / 64
    cw = np.zeros((128, 128)); sw = np.zeros((128, 128))
    for g in range(2):
        cw[g * 64:(g + 1) * 64, g * 64:(g + 1) * 64] = np.cos(ph)
        sw[g * 64:(g + 1) * 64, g * 64:(g + 1) * 64] = np.sin(ph)
    csw = np.concatenate([cw, sw], axis=1)
    a = np.arange(L1)
    p1 = 2 * np.pi * np.outer(a, a) / L1
    wr, wi = np.cos(p1), -np.sin(p1)
    w1 = np.concatenate([wr, wi], axis=1)
    w2 = np.concatenate([wi, -wr], axis=1)
    p2 = 2 * np.pi * np.outer(np.arange(L2), np.arange(Ls)) / Ls
    sc = 1.0 / math.sqrt(Ls * 64)
    return _bf(csw), _bf(w1), _bf(w2), _bf(np.cos(p2) * sc), _bf(np.sin(p2) * sc)


def _hyena_tables(Ls):
    N = 2 * Ls
    N1 = N // 128
    a = np.arange(N1)
    p1 = 2 * np.pi * np.outer(a, a) / N1
    w1 = np.concatenate([np.cos(p1), -np.sin(p1)], axis=1)
    p2 = 2 * np.pi * np.outer(np.arange(128), np.arange(N)) / N
    tr, ti, nti = np.cos(p2), -np.sin(p2), np.sin(p2)
    p3 = 2 * np.pi * np.outer(np.arange(128), np.arange(128)) / 128
    i1a = np.concatenate([np.cos(p3), np.sin(p3)], axis=1)
    i1b = np.concatenate([-np.sin(p3), np.cos(p3)], axis=1)
    p4 = 2 * np.pi * np.outer(np.arange(N1), np.arange(Ls)) / N
    er, nei = np.cos(p4) / N, -np.sin(p4) / N
    j = np.arange(N)
    tau = np.where(j < Ls, j, N - j).astype(np.float64)
    tau[Ls] = 0
    tl = np.linspace(0.0, 1.0, Ls)
    t = tl[tau.astype(np.int64)]
    fr = np.linspace(1e-4, 15.0, 16)
    ang = 2.0 * np.pi * tau[:, None] / Ls * fr[None, :]
    z = np.concatenate([t[:, None], np.cos(ang), -np.sin(ang)], axis=1)
    negt = -t.copy()
    negt[Ls] = -1e4
    tcol = negt.reshape(N1, 128)
    deltas = np.abs(np.linspace(math.log(1e-2) / 1.5, math.log(1e-2) / 0.3, 256))
    drow = np.broadcast_to(deltas[None, :], (128, 256))
    return dict(N1=N1, w1=_bf(w1), tr=_bf(tr), ti=_bf(ti), nti=_bf(nti), i1a=_bf(i1a), i1b=_bf(i1b),
                er=_bf(er), nei=_bf(nei), zT=_f32(z.T), tcol=_f32(tcol), drow=_f32(drow))


_CONST = None


def _consts():
    global _CONST
    if _CONST is not None:
        return _CONST
    c = {}
    c['ident_bf'] = _bf(np.eye(128))
    c['ident_f'] = _f32(np.eye(128))
    c['ones_f'] = _f32(np.ones((128, 128)))
    c['ropeC'], c['ropeS'] = _rope_tables()
    for nm, Ls, L1, L2 in (('x', L, 64, 64), ('c', LC, 4, 64)):
        csw, w1, w2, tr, sn = _fnet_tables(Ls, L1, L2)
        c['fn_csw'] = csw
        c['fn_w1' + nm], c['fn_w2' + nm], c['fn_tr' + nm], c['fn_sn' + nm] = w1, w2, tr, sn
        ht = _hyena_tables(Ls)
        for k, v in ht.items():
            if k != 'N1':
                c['hy_%s%s' % (k, nm)] = v
    _CONST = c
    return c


def _colpack(v, n):
    return np.asarray(v, np.float32).reshape(n, 128).T


def _prep(inp):
    c = dict(_consts())
    sh = {}
    w_in = np.asarray(inp['w_in'], np.float32)
    sh['w_inx'] = np.ascontiguousarray(np.concatenate([w_in, w_in[:, :, OFF_KR + ROPE_PERM]], axis=2))
    w_uq = np.asarray(inp['w_uq'], np.float32)
    sh['w_uq'] = np.ascontiguousarray(w_uq)
    wq = w_uq.reshape(DEPTH, 384, H, 96)
    sh['w_uqp'] = np.ascontiguousarray(
        np.concatenate([wq[..., :64], wq[..., 64:][..., ROPE_PERM]], axis=-1).reshape(DEPTH, 384, 768))
    wkv = np.asarray(inp['w_ukv'], np.float32).reshape(DEPTH, 256, H, 128)
    sh['w_ukvx'] = np.ascontiguousarray(
        np.concatenate([wkv[..., :64].reshape(DEPTH, 256, 512), wkv[..., 64:].reshape(DEPTH, 256, 512)], axis=2))
    for k in ('w_mod', 'w_out', 'w_mlp1', 'w_mlp2', 'hy_w1', 'hy_w2', 'hy_w3'):
        sh[k] = np.ascontiguousarray(np.asarray(inp[k], np.float32))
    sh['b_mod2'] = np.ascontiguousarray(np.repeat(np.asarray(inp['b_mod'], np.float32)[:, None, :], 2, axis=1))
    cols = np.zeros((DEPTH, 128, NCOL), np.float32)
    for l in range(DEPTH):
        cols[l, :, C_N1G:C_N1G + 8] = _colpack(inp['norm1_g'][l], 8)
        cols[l, :, C_N2G:C_N2G + 8] = _colpack(inp['norm2_g'][l], 8)
        cols[l, :, C_QG:C_QG + 3] = _colpack(inp['q_norm_g'][l], 3)
        cols[l, :, C_KVG:C_KVG + 2] = _colpack(inp['kv_norm_g'][l], 2)
        for tap in range(3):
            cols[l, :, C_HCW + tap * 6:C_HCW + tap * 6 + 6] = _colpack(inp['hy_conv_w'][l, tap], 6)
        cols[l, :, C_HCB:C_HCB + 6] = _colpack(inp['hy_conv_b'][l], 6)
        cols[l, :, C_HD:C_HD + 2] = _colpack(inp['hy_d'][l], 2)
        cols[l, :64, C_HB1] = inp['hy_b1'][l]
        cols[l, :64, C_HFR] = inp['hy_freq'][l]
        cols[l, :64, C_HB2] = inp['hy_b2'][l]
    sh['cols'] = cols
    sh['fgrow'] = np.ascontiguousarray(np.broadcast_to(np.asarray(inp['final_norm_g'], np.float32)[None, :], (128, D)))
    sh.update(c)
    per = []
    x = np.asarray(inp['x'], np.float32); ctx = np.asarray(inp['ctx'], np.float32)
    cc = np.asarray(inp['c'], np.float32); c_ctx = np.asarray(inp['c_ctx'], np.float32)
    for b in range(8):
        d = dict(sh)
        d['xin'] = np.ascontiguousarray(np.concatenate([x[b], ctx[b]], axis=0))
        cv = np.stack([_colpack(cc[b], 8), _colpack(c_ctx, 8)], axis=-1)
        d['cvec'] = np.ascontiguousarray(cv)
        per.append(d)
    return per


GROUPS = [(g * 512, 512, 0) for g in range(8)] + [(L, LC, 1)]
SCALE = 1.0 / math.sqrt(96.0)


class Prog:
    def __init__(self, debug=False):
        self.debug = debug
        self.nc = nc = bass.Bass("TRN2", target_bir_lowering=False)
        self.es = ExitStack()
        self.S = Sched(nc, self.es)
        self._in = {}
        okind = dict(kind="ExternalOutput") if debug else {}
        scr = lambda n, shp, dt=F32: nc.dram_tensor(n, list(shp), dt, **okind).ap()
        self.spec = dict(
            xin=([T, D], F32), cvec=([128, 8, 2], F32), w_mod=([DEPTH, D, 6 * D], F32), b_mod2=([DEPTH, 2, 6 * D], F32),
            w_inx=([DEPTH, D, 1728], F32), w_uq=([DEPTH, 384, 768], F32), w_uqp=([DEPTH, 384, 768], F32),
            w_ukvx=([DEPTH, 256, 1024], F32), w_out=([DEPTH, D, D], F32), w_mlp1=([DEPTH, D, 4 * D], F32),
            w_mlp2=([DEPTH, 4 * D, D], F32), hy_w1=([DEPTH, 33, 64], F32), hy_w2=([DEPTH, 64, 64], F32),
            hy_w3=([DEPTH, 64, 512], F32), cols=([DEPTH, 128, NCOL], F32), fgrow=([128, D], F32),
            ident_bf=([128, 128], BF16), ident_f=([128, 128], F32), ones_f=([128, 128], F32),
            ropeC=([32, T], F32), ropeS=([32, T], F32), fn_csw=([128, 256], BF16))
        self.fn = {}; self.hy = {}
        for nm, Ls, L1 in (('x', L, 64), ('c', LC, 4)):
            N = 2 * Ls; N1 = N // 128
            self.spec.update({'fn_w1' + nm: ([L1, 2 * L1], BF16), 'fn_w2' + nm: ([L1, 2 * L1], BF16),
                              'fn_tr' + nm: ([64, Ls], BF16), 'fn_sn' + nm: ([64, Ls], BF16),
                              'hy_w1' + nm: ([N1, 2 * N1], BF16), 'hy_tr' + nm: ([128, N], BF16),
                              'hy_ti' + nm: ([128, N], BF16), 'hy_nti' + nm: ([128, N], BF16),
                              'hy_i1a' + nm: ([128, 256], BF16), 'hy_i1b' + nm: ([128, 256], BF16),
                              'hy_er' + nm: ([N1, Ls], BF16), 'hy_nei' + nm: ([N1, Ls], BF16),
                              'hy_zT' + nm: ([33, N], F32), 'hy_tcol' + nm: ([N1, 128], F32),
                              'hy_drow' + nm: ([128, 256], F32)})
            self.fn[nm] = dict(L1=L1, Ls=Ls)
            self.hy[nm] = dict(N1=N1, Ls=Ls, N=N, kf=scr('kf' + nm, [128, N1, 2, 256], BF16),
                               ssum=scr('ssum' + nm, [128, 2]))
        self.y = nc.dram_tensor('y', [L, D], F32, kind="ExternalOutput").ap()
        self.xres = scr('xres', [T, D]); self.mrow = scr('mrow', [2, 6 * D])
        self.pxT = scr('pxT', [14, 128, T], BF16); self.mixT = scr('mixT', [8, 128, T], BF16)
        self.hx0 = scr('hx0', [2, 128, T], BF16); self.hz = scr('hz', [2, 128, T], BF16)
        self.xmid = scr('xmid', [T, D]); self.h2T = scr('h2T', [8, 128, T], BF16)
        self.hyY = {}; self.hyF = {}
        for nm_ in ('x', 'c'):
            n1_ = self.hy[nm_]['N1']
            self.hyY[nm_] = scr('hyY' + nm_, [128, 256, 2 * n1_], BF16)
            self.hyF[nm_] = scr('hyF' + nm_, [2, 128, n1_, 2, 128], BF16)
        self.modc = self.es.enter_context(nc.sbuf_tensor('modc', [128, 2, 4, 8], F32))
        self.cols = self.es.enter_context(nc.sbuf_tensor('colsb', [128, NCOL], F32))

    def uq(self, n):
        self._uq = getattr(self, '_uq', 0) + 1
        return '%s_%d' % (n, self._uq)

    def I(self, name):
        if name not in self._in:
            shp, dt = self.spec[name]
            self._in[name] = self.nc.dram_tensor(name, list(shp), dt, kind="ExternalInput").ap()
        return self._in[name]

    def p0_mod(self, l):
        nc, S = self.nc, self.S
        with ExitStack() as es:
            sb = lambda n, shp, dt=F32: es.enter_context(nc.sbuf_tensor(self.uq(n), shp, dt))
            cv = sb('cv', [128, 8, 2]); sc = sb('sc', [128, 8, 2])
            wt = [sb('wt%d' % i, [128, 8, 512]) for i in range(2)]
            msb = sb('msb', [2, 6 * D]); bm = sb('bm', [2, 6 * D])
            ps = [es.enter_context(nc.psum_tensor(self.uq('ps%d' % i), [2, 512], F32)) for i in range(2)]
            S.dma('sp', cv[:], self.I('cvec'), w=['cv'])
            S.dma('sp', bm[:], self.I('b_mod2')[l], w=['bm'])
            S.dma('sp', self.cols[:], self.I('cols')[l], w=['cols'])
            S.op('act', 'activation', sc[:], cv[:], AF.Silu, r=['cv'], w=['sc'])
            wv = self.I('w_mod')[l].rearrange("(k p) n -> p k n", p=128)
            for n in range(12):
                b = n % 2
                S.dma('sp', wt[b][:], wv[:, :, n * 512:(n + 1) * 512], w=[('wt', b)])
                S.multi('pe', [('matmul', (ps[b][:], sc[:, k, :], wt[b][:, k, :]), dict(start=(k == 0), stop=(k == 7)))
                               for k in range(8)], r=['sc', ('wt', b)], w=[('ps', b)])
                S.op('dve', 'tensor_tensor', msb[:, n * 512:(n + 1) * 512], ps[b][:], bm[:, n * 512:(n + 1) * 512],
                     ALU.add, r=[('ps', b), 'bm'], w=['msb'])
            S.dma('sp', self.mrow, msb[:], r=['msb'])
            S.flush()
            mT = sb('mT', [96, 128]); idf = sb('idf', [128, 128]); mcol = sb('mcol', [128, 96])
            pm = es.enter_context(nc.psum_tensor(self.uq('pm'), [128, 96], F32))
            S.dma('sp', mT[:], self.mrow.rearrange("s (j p) -> (s j) p", p=128), w=['mT'])
            S.dma('sp', idf[:], self.I('ident_f'), w=['idf'])
            S.op('pe', 'transpose', pm[:], mT[:], idf[0:96, 0:96], r=['mT', 'idf'], w=['pm'])
            S.op('dve', 'tensor_copy', mcol[:], pm[:], r=['pm'], w=['mcol'])
            for s in range(2):
                o = s * 48
                S.op('dve', 'scalar_tensor_tensor', self.modc[:, s, 0, :], mcol[:, o + 8:o + 16], 1.0,
                     self.cols[:, C_N1G:C_N1G + 8], ALU.add, ALU.mult, r=['mcol', 'cols'], w=['modc'])
                S.op('dve', 'tensor_copy', self.modc[:, s, 1, :], mcol[:, o:o + 8], r=['mcol'], w=['modc'])
                S.op('dve', 'scalar_tensor_tensor', self.modc[:, s, 2, :], mcol[:, o + 32:o + 40], 1.0,
                     self.cols[:, C_N2G:C_N2G + 8], ALU.add, ALU.mult, r=['mcol', 'cols'], w=['modc'])
                S.op('dve', 'tensor_copy', self.modc[:, s, 3, :], mcol[:, o + 24:o + 32], r=['mcol'], w=['modc'])
            S.flush()

    def norm_tiles(self, xt, nj, s, which, tl, key, n):
        S = self.S
        ss, rstd, xn, junk, pt, hT, idb = tl['ss'], tl['rstd'], tl['xn'], tl['junk'], tl['pt'], tl['hT'], tl['idb']
        for j in range(nj):
            S.op('act', 'activation', junk[:], xt[:, j, :], AF.Square, accum_out=ss[:, j:j + 1],
                 r=[key], w=['junk', tl['k'] + 'ss'])
        S.op('act', 'activation', rstd[:, 0:nj], ss[:, 0:nj], AF.Sqrt, bias=tl['eps'][:, 0:1], scale=1.0 / D,
             r=[tl['k'] + 'ss', 'eps'], w=[tl['k'] + 'rstd'])
        S.op('dve', 'reciprocal', rstd[:, 0:nj], rstd[:, 0:nj], r=[tl['k'] + 'rstd'], w=[tl['k'] + 'rstd'])
        for j in range(nj):
            S.op('dve', 'tensor_scalar', xn[:, j, :], xt[:, j, :], rstd[:, j:j + 1], None, ALU.mult,
                 r=[key, tl['k'] + 'rstd'], w=[tl['k'] + 'xn'])
        for k in range(8):
            pb = k % 2
            S.multi('pe', [('transpose', (pt[pb][:, j * 128:(j + 1) * 128], xn[:, j, k * 128:(k + 1) * 128], idb[:]), {})
                           for j in range(nj)], r=[tl['k'] + 'xn', 'idb'], w=[('pt', pb)])
            if k % 2 == 0:
                S.op('dve', 'tensor_scalar', hT[:, k, 0:n], pt[pb][:, 0:n], self.modc[:, s, which, k:k + 1],
                     self.modc[:, s, which + 1, k:k + 1], ALU.mult, ALU.add, r=[('pt', pb), 'modc'], w=[tl['k'] + 'hT'])
            else:
                S.op('act', 'activation', hT[:, k, 0:n], pt[pb][:, 0:n], AF.Identity,
                     bias=self.modc[:, s, which + 1, k:k + 1], scale=self.modc[:, s, which, k:k + 1],
                     r=[('pt', pb), 'modc'], w=[tl['k'] + 'hT'])

    def p1_inproj(self, l):
        nc, S = self.nc, self.S
        src = self.I('xin') if l == 0 else self.xres
        with ExitStack() as es:
            sb = lambda n, shp, dt=F32: es.enter_context(nc.sbuf_tensor(self.uq(n), shp, dt))
            pst = lambda n, shp, dt=F32: es.enter_context(nc.psum_tensor(self.uq(n), shp, dt))
            win = sb('win', [128, 8, 1728], BF16)
            S.dma('pool', win[:], self.I('w_inx')[l].rearrange("(k p) n -> p k n", p=128), w=['win'])
            idb = sb('idb', [128, 128], BF16); S.dma('sp', idb[:], self.I('ident_bf'), w=['idb'])
            onesf = sb('onesf', [128, 128]); S.dma('sp', onesf[:], self.I('ones_f'), w=['onesf'])
            rc = sb('rc', [32, T]); rs = sb('rs', [32, T])
            S.dma('sp', rc[:], self.I('ropeC'), w=['rc']); S.dma('sp', rs[:], self.I('ropeS'), w=['rs'])
            eps = sb('eps', [128, 1]); S.op('dve', 'memset', eps[:], EPS, w=['eps'])
            xg = [sb('xg%d' % i, [128, 4, D]) for i in range(2)]
            tls = []
            junk = sb('junk', [128, D], BF16)
            pt = [pst('pt%d' % i, [128, 512], BF16) for i in range(2)]
            for i in range(2):
                tls.append(dict(k='t%d' % i, ss=sb('ss%d' % i, [128, 4]), rstd=sb('rstd%d' % i, [128, 4]),
                                xn=sb('xn%d' % i, [128, 4, D], BF16), junk=junk, pt=pt,
                                hT=sb('hT%d' % i, [128, 8, 512], BF16), idb=idb, eps=eps))
            ost = [sb('ost%d' % i, [128, 14, 512], BF16) for i in range(2)]
            for i in range(2):
                S.op('pool', 'memset', ost[i][:, 13, :], 0.0, w=[('ost', i)])
            c32 = sb('c32', [128, 5, 512]); sq = sb('sq', [128, 5, 512]); rq = sb('rq', [128, 512])
            t1 = sb('t1', [32, 512]); t2 = sb('t2', [32, 512])
            po = [pst('po%d' % i, [128, 512]) for i in range(3)]
            pn = pst('pn', [128, 512])
            npo = [0]

            def proj(col0, ncols, hT, n, b):
                i = npo[0] % 3; npo[0] += 1
                S.multi('pe', [('matmul', (po[i][0:ncols, 0:n], win[:, k, col0:col0 + ncols], hT[:, k, 0:n]),
                                dict(start=(k == 0), stop=(k == 7))) for k in range(8)],
                        r=['win', 't%dhT' % b], w=[('po', i)])
                return i

            def front(gi):
                t0, n, s = GROUPS[gi]
                b = gi % 2; nj = n // 128; tl = tls[b]
                S.dma('sp', xg[b][:, 0:nj, :], src[t0:t0 + n, :].rearrange("(j p) d -> p j d", p=128), w=[('xg', b)])
                self.norm_tiles(xg[b], nj, s, 0, tl, ('xg', b), n)

            def back(gi):
                t0, n, s = GROUPS[gi]
                b = gi % 2; nj = n // 128; tl = tls[b]
                hT = tl['hT']
                okey = ('ost', b)
                for (c0, nch, gcol, ci0, i0) in ((OFF_Q, 3, C_QG, 0, 0), (OFF_KV, 2, C_KVG, 3, 3)):
                    for c in range(nch):
                        i = proj(c0 + c * 128, 128, hT, n, b)
                        S.op('act', 'activation', c32[:, i0 + c, 0:n], po[i][:, 0:n], AF.Copy,
                             r=[('po', i)], w=[('c32', i0 + c)])
                        S.op('act', 'activation', sq[:, i0 + c, 0:n], po[i][:, 0:n], AF.Square,
                             r=[('po', i)], w=[('sq', i0 + c)])
                    S.multi('pe', [('matmul', (pn[:, 0:n], onesf[:], sq[:, i0 + c, 0:n]),
                                    dict(start=(c == 0), stop=(c == nch - 1))) for c in range(nch)],
                            r=['onesf'] + [('sq', i0 + c) for c in range(nch)], w=['pn'])
                    S.op('act', 'activation', rq[:, 0:n], pn[:, 0:n], AF.Sqrt, bias=eps[:, 0:1], scale=1.0 / (128 * nch),
                         r=['pn', 'eps'], w=['rq'])
                    S.op('dve', 'reciprocal', rq[:, 0:n], rq[:, 0:n], r=['rq'], w=['rq'])
                    for c in range(nch):
                        S.op('dve', 'scalar_tensor_tensor', ost[b][:, ci0 + c, 0:n], c32[:, i0 + c, 0:n],
                             self.cols[:, gcol + c:gcol + c + 1], rq[:, 0:n], ALU.mult, ALU.mult,
                             r=[('c32', i0 + c), 'rq', 'cols'], w=[okey])
                ia = proj(OFF_KR, 32, hT, n, b)
                ib = proj(IN_W, 32, hT, n, b)
                S.op('dve', 'tensor_tensor', t1[:, 0:n], po[ia][0:32, 0:n], rc[:, t0:t0 + n], ALU.mult,
                     r=[('po', ia), 'rc'], w=['t1'])
                S.op('dve', 'tensor_tensor', t2[:, 0:n], po[ib][0:32, 0:n], rs[:, t0:t0 + n], ALU.mult,
                     r=[('po', ib), 'rs'], w=['t2'])
                S.op('dve', 'tensor_tensor', ost[b][0:32, 13, 0:n], t1[:, 0:n], t2[:, 0:n], ALU.add,
                     r=['t1', 't2'], w=[okey])
                for c in range(8):
                    i = proj(OFF_F + c * 128, 128, hT, n, b)
                    if c % 2 == 0:
                        S.op('act', 'activation', ost[b][:, 5 + c, 0:n], po[i][:, 0:n], AF.Copy, r=[('po', i)], w=[okey])
                    else:
                        S.op('dve', 'tensor_copy', ost[b][:, 5 + c, 0:n], po[i][:, 0:n], r=[('po', i)], w=[okey])
                S.dma('sp', self.pxT[:, :, t0:t0 + n].rearrange("c p t -> p c t"), ost[b][:, :, 0:n], r=[okey])

            front(0)
            for gi in range(len(GROUPS)):
                if gi + 1 < len(GROUPS):
                    front(gi + 1)
                back(gi)
            S.flush()

    def p2_attn(self, l):
        nc, S = self.nc, self.S
        with ExitStack() as es:
            sb = lambda n, shp, dt=F32: es.enter_context(nc.sbuf_tensor(self.uq(n), shp, dt))
            pst = lambda n, shp, dt=F32: es.enter_context(nc.psum_tensor(self.uq(n), shp, dt))
            cqn = sb('cqn', [128, 3, T], BF16); ckvn = sb('ckvn', [128, 2, T], BF16)
            S.dma('sp', cqn[:], self.pxT[0:3].rearrange("c p t -> p c t"), w=['cqn'])
            S.dma('sp', ckvn[:], self.pxT[3:5].rearrange("c p t -> p c t"), w=['ckvn'])
            wuq = sb('wuq', [128, 3, 768], BF16); wuqp = sb('wuqp', [128, 3, 768], BF16)
            wukv = sb('wukv', [128, 2, 1024], BF16)
            S.dma('pool', wuq[:], self.I('w_uq')[l].rearrange("(k p) n -> p k n", p=128), w=['wuq'])
            S.dma('pool', wuqp[:], self.I('w_uqp')[l].rearrange("(k p) n -> p k n", p=128), w=['wuqp'])
            S.dma('pool', wukv[:], self.I('w_ukvx')[l].rearrange("(k p) n -> p k n", p=128), w=['wukv'])
            onesf = sb('onesf', [128, 128]); S.dma('sp', onesf[:], self.I('ones_f'), w=['onesf'])
            tc_ = sb('tabc', [96, T]); ts_ = sb('tabs', [96, T])
            S.dma('sp', tc_[64:96, :], self.I('ropeC'), w=['tabc']); S.dma('sp', ts_[64:96, :], self.I('ropeS'), w=['tabs'])
            kt = [sb('kt%d' % i, [96, T], BF16) for i in range(2)]
            qt = [sb('qt%d' % i, [96, T], BF16) for i in range(2)]
            for b in range(2):
                S.dma('sp', kt[b][64:96, :], self.pxT[13, 0:32, :], w=[('ktr', b)])
            vh = [sb('vh%d' % i, [128, NT, 128], BF16) for i in range(2)]
            for i in range(2):
                S.op('pool', 'memset', vh[i][:], 1.0, w=[('vh', i)])
            t1 = sb('t1', [96, 512]); t2 = sb('t2', [96, 512])
            ptl = [sb('ptl%d' % i, [128, 2, 512], BF16) for i in range(3)]
            rr = sb('rr', [96, 512]); osb = [sb('osb%d' % i, [64, 512]) for i in range(2)]; ost = sb('ost', [64, T], BF16)
            sel = sb('sel', [128, 128], BF16); rb = sb('rb', [64, 512])
            S.op('pool', 'memset', sel[:], 0.0, w=['sel'])
            S.op('pool', 'memset', sel[64:65, :], 1.0, w=['sel'])
            rhi = [sb('rhi%d' % i, [128, 512], BF16) for i in range(2)]; rlo = [sb('rlo%d' % i, [128, 512], BF16) for i in range(2)]
            for i in range(2):
                S.op('pool', 'memset', rhi[i][:], 0.0, w=[('rhi', i)])
                S.op('pool', 'memset', rlo[i][:], 0.0, w=[('rlo', i)])
            ps = [pst('ps%d' % i, [128, 1024]) for i in range(2)]
            po = [pst('po%d' % i, [128, 512]) for i in range(2)]
            pq = pst('pq', [128, 512]); pq2 = pst('pq2', [128, 512]); pk = pq; pb = pq2
            LA = 2

            def project_chunks(h):
                b = h % 2
                chunks = []
                for gi, (t0, n, s) in enumerate(GROUPS):
                    def cA(gi=gi, t0=t0, n=n):
                        S.multi('pe', [('matmul', (pq[0:96, 0:n], wuq[:, c, h * 96:(h + 1) * 96], cqn[:, c, t0:t0 + n]),
                                        dict(start=(c == 0), stop=(c == 2))) for c in range(3)],
                                r=['cqn', 'wuq'], w=['pq'])
                        S.multi('pe', [('matmul', (pq2[0:96, 0:n], wuqp[:, c, h * 96:(h + 1) * 96], cqn[:, c, t0:t0 + n]),
                                        dict(start=(c == 0), stop=(c == 2))) for c in range(3)],
                                r=['cqn', 'wuqp'], w=['pq2'])
                        S.op('dve', 'tensor_copy', qt[b][0:64, t0:t0 + n], pq[0:64, 0:n], r=['pq'], w=[('qt', b, gi)])
                        S.op('dve', 'tensor_tensor', t1[64:96, 0:n], pq[64:96, 0:n], tc_[64:96, t0:t0 + n], ALU.mult,
                             r=['pq', 'tabc'], w=['t1'])
                        S.op('dve', 'tensor_tensor', t2[64:96, 0:n], pq2[64:96, 0:n], ts_[64:96, t0:t0 + n], ALU.mult,
                             r=['pq2', 'tabs'], w=['t2'])
                        S.op('dve', 'tensor_tensor', qt[b][64:96, t0:t0 + n], t1[64:96, 0:n], t2[64:96, 0:n], ALU.add,
                             r=['t1', 't2'], w=[('qt', b, gi)])

                    def cB(gi=gi, t0=t0, n=n):
                        S.multi('pe', [('matmul', (pk[:, 0:n], wukv[:, c, h * 64:h * 64 + 128], ckvn[:, c, t0:t0 + n]),
                                        dict(start=(c == 0), stop=(c == 1))) for c in range(2)],
                                r=['ckvn', 'wukv'], w=['pq'])
                        S.op('dve', 'tensor_copy', kt[b][0:64, t0:t0 + n], pk[0:64, 0:n], r=['pq'], w=[('kt', b, gi)])
                    chunks += [cA, cB]
                for i0 in range(0, NT, 8):
                    def cV(i0=i0):
                        nt_ = min(8, NT - i0)
                        calls = []
                        for ii_ in range(nt_):
                            i_ = i0 + ii_
                            calls += [('matmul', (pq2[:, ii_ * 64:(ii_ + 1) * 64], ckvn[:, c, i_ * 128:(i_ + 1) * 128],
                                                  wukv[:, c, 512 + h * 64:512 + (h + 1) * 64]), dict(start=(c == 0), stop=(c == 1)))
                                      for c in range(2)]
                        S.multi('pe', calls, r=['ckvn', 'wukv'], w=['pq2'])
                        S.op('dve', 'tensor_copy', vh[b][:, i0:i0 + nt_, 0:64],
                             pq2[:, 0:nt_ * 64].rearrange("p (a d) -> p a d", d=64), r=['pq2'], w=[('vh', b)])
                    chunks.append(cV)
                return chunks

            def project(h):
                for c_ in project_chunks(h):
                    c_()

            project(0)
            for h in range(H):
                b = h % 2
                seq = []
                for gi, (q0, nq, s) in enumerate(GROUPS):
                    tiles = list(range(NT)) if s == 0 else [32, 33]
                    npair = len(tiles) // 2
                    for ii in range(npair):
                        seq.append((gi, q0, nq, ii, tiles[2 * ii], npair))

                def qk(e):
                    gi, q0, nq, ii, i, np_ = seq[e]
                    sl = e % 2
                    S.multi('pe', [('matmul', (ps[sl][:, a_ * 512:a_ * 512 + nq], kt[b][0:96, (i + a_) * 128:(i + a_ + 1) * 128],
                                               qt[b][0:96, q0:q0 + nq]), dict(start=True, stop=True)) for a_ in range(2)],
                            r=[('kt', b, i // 4), ('kt', b, (i + 1) // 4), ('ktr', b), ('qt', b, gi)], w=[('ps', sl)])

                pending = []
                nxt = project_chunks(h + 1) if h + 1 < H else []
                cstep = max(1, (len(seq) - 12) // max(1, len(nxt)))
                for e in range(min(LA, len(seq))):
                    qk(e)
                for e in range(len(seq)):
                    gi, q0, nq, ii, i, np_ = seq[e]
                    sl = e % 2; sl2 = e % 3; ob = gi % 2
                    S.op('act', 'activation', ptl[sl2][:, :, 0:nq], ps[sl][:, :].rearrange("p (a n) -> p a n", a=2)[:, :, 0:nq],
                         AF.Exp, scale=SCALE, r=[('ps', sl)], w=[('ptl', sl2)])
                    if e + LA < len(seq):
                        qk(e + LA)
                    S.multi('pe', [('matmul', (po[ob][:, 0:nq], vh[b][:, i + a_, :], ptl[sl2][:, a_, 0:nq]),
                                    dict(start=(ii == 0 and a_ == 0), stop=(ii == np_ - 1 and a_ == 1))) for a_ in range(2)],
                            r=[('ptl', sl2), ('vh', b)], w=[('po', ob)])
                    if ii == np_ - 1:
                        fb = gi % 2
                        S.op('dve', 'tensor_copy', rhi[fb][64:65, 0:nq], po[ob][64:65, 0:nq], r=[('po', ob)], w=[('rhi', fb)])
                        S.op('dve', 'tensor_tensor', rlo[fb][64:65, 0:nq], po[ob][64:65, 0:nq], rhi[fb][64:65, 0:nq], ALU.subtract,
                             r=[('po', ob), ('rhi', fb)], w=[('rlo', fb)])
                        S.op('dve', 'tensor_copy', osb[fb][:, 0:nq], po[ob][0:64, 0:nq], r=[('po', ob)], w=[('osb', fb)])
                        pending.append((e + 3, fb, q0, nq))
                    while pending and (pending[0][0] <= e or e == len(seq) - 1):
                        _, fb, fq0, fnq = pending.pop(0)
                        S.multi('pe', [('matmul', (pb[:, 0:fnq], sel[:, :], rhi[fb][:, 0:fnq]), dict(start=True, stop=False)),
                                       ('matmul', (pb[:, 0:fnq], sel[:, :], rlo[fb][:, 0:fnq]), dict(start=False, stop=True))],
                                r=[('rhi', fb), ('rlo', fb), 'sel'], w=['pq2'])
                        S.op('dve', 'reciprocal', rb[:, 0:fnq], pb[0:64, 0:fnq], r=['pq2'], w=['rb'])
                        S.op('dve', 'tensor_tensor', ost[:, fq0:fq0 + fnq], osb[fb][:, 0:fnq], rb[:, 0:fnq], ALU.mult,
                             r=[('osb', fb), 'rb'], w=['ost'])
                    if nxt and e >= 4 and (e - 4) % cstep == 0:
                        nxt.pop(0)()
                while nxt:
                    nxt.pop(0)()
                S.dma('sp', self.mixT[h // 2, (h % 2) * 64:(h % 2) * 64 + 64, :], ost[:, :], r=['ost'])
            S.flush()

    def p5_out_mlp(self, l):
        nc, S = self.nc, self.S
        last = (l == DEPTH - 1)
        src = self.I('xin') if l == 0 else self.xres
        groups = GROUPS[:8] if last else GROUPS
        with ExitStack() as es:
            sb = lambda n, shp, dt=F32: es.enter_context(nc.sbuf_tensor(self.uq(n), shp, dt))
            pst = lambda n, shp, dt=F32: es.enter_context(nc.psum_tensor(self.uq(n), shp, dt))
            wout = sb('wout', [128, 8, D], BF16)
            S.dma('pool', wout[:], self.I('w_out')[l].rearrange("(k p) n -> p k n", p=128), w=['wout'])
            idb = sb('idb', [128, 128], BF16); S.dma('sp', idb[:], self.I('ident_bf'), w=['idb'])
            eps = sb('eps', [128, 1]); S.op('dve', 'memset', eps[:], EPS, w=['eps'])
            g1r = sb('g1r', [128, D])
            mixt = [sb('mixt%d' % i, [128, 8, 512], BF16) for i in range(2)]
            xt = [sb('xt%d' % i, [128, 4, D]) for i in range(2)]
            tmp = [sb('tmp%d' % i, [128, 512]) for i in range(2)]
            junk = sb('junk', [128, D], BF16)
            pt = [pst('pt%d' % i, [128, 512], BF16) for i in range(2)]
            tls = [dict(k='n%d' % i, ss=sb('ss%d' % i, [128, 4]), rstd=sb('rstd%d' % i, [128, 4]),
                        xn=sb('xn%d' % i, [128, 4, D], BF16), junk=junk, pt=pt,
                        hT=sb('hT%d' % i, [128, 8, 512], BF16), idb=idb, eps=eps) for i in range(2)]
            pso = [pst('pso%d' % i, [128, 512]) for i in range(4)]
            st = dict(cur_s=-1, np_=0)

            def stA(gi):
                t0, n, s = groups[gi]
                b = gi % 2; nj = n // 128
                if s != st['cur_s']:
                    S.dma('sp', g1r[:], self.mrow[s, 2 * D:3 * D].partition_broadcast(128), w=['g1r']); st['cur_s'] = s
                S.dma('sp', mixt[b][:, :, 0:n], self.mixT[:, :, t0:t0 + n].rearrange("c p t -> p c t"), w=[('mixt', b)])
                S.dma('sp', xt[b][:, 0:nj, :], src[t0:t0 + n, :].rearrange("(j p) d -> p j d", p=128), w=[('xt', b)])
                for j in range(nj):
                    for hh in range(2):
                        pi = st['np_'] % 4; st['np_'] += 1
                        S.multi('pe', [('matmul', (pso[pi][:, :], mixt[b][:, c, j * 128:(j + 1) * 128], wout[:, c, hh * 512:(hh + 1) * 512]),
                                        dict(start=(c == 0), stop=(c == 7))) for c in range(8)],
                                r=[('mixt', b), 'wout'], w=[('pso', pi)])
                        S.op('dve', 'tensor_tensor', tmp[pi % 2][:, :], pso[pi][:, :], g1r[:, hh * 512:(hh + 1) * 512], ALU.mult,
                             r=[('pso', pi), 'g1r'], w=[('tmp', pi % 2)])
                        S.op('pool', 'tensor_tensor', xt[b][:, j, hh * 512:(hh + 1) * 512], tmp[pi % 2][:, :],
                             xt[b][:, j, hh * 512:(hh + 1) * 512], ALU.add, r=[('tmp', pi % 2), ('xt', b)], w=[('xt', b)])
                S.dma('sp', self.xmid[t0:t0 + n, :].rearrange("(j p) d -> p j d", p=128), xt[b][:, 0:nj, :], r=[('xt', b)])

            def stB(gi):
                t0, n, s = groups[gi]
                b = gi % 2; nj = n // 128
                self.norm_tiles(xt[b], nj, s, 2, tls[b], ('xt', b), n)
                S.dma('sp', self.h2T[:, :, t0:t0 + n].rearrange("c p t -> p c t"), tls[b]['hT'][:, :, 0:n], r=['n%dhT' % b])

            stA(0)
            for gi in range(len(groups)):
                if gi + 1 < len(groups):
                    stA(gi + 1)
                stB(gi)
            S.flush()
        with ExitStack() as es:
            sb = lambda n, shp, dt=F32: es.enter_context(nc.sbuf_tensor(self.uq(n), shp, dt))
            pst = lambda n, shp, dt=F32: es.enter_context(nc.psum_tensor(self.uq(n), shp, dt))
            w1 = sb('w1', [128, 8, 4 * D], BF16); w2 = sb('w2', [128, 32, D], BF16)
            for hh in range(2):
                S.dma('pool', w1[:, :, hh * 2048:(hh + 1) * 2048],
                      self.I('w_mlp1')[l].rearrange("(k p) n -> p k n", p=128)[:, :, hh * 2048:(hh + 1) * 2048], w=[('w1', hh)])
            for q in range(4):
                S.dma('pool', w2[:, q * 8:(q + 1) * 8, :],
                      self.I('w_mlp2')[l].rearrange("(k p) n -> p k n", p=128)[:, q * 8:(q + 1) * 8, :], w=[('w2', q)])
            g2r = sb('g2r', [128, D])
            if last:
                fg = sb('fg', [128, D]); S.dma('sp', fg[:], self.I('fgrow'), w=['fg'])
                eps = sb('eps', [128, 1]); S.op('dve', 'memset', eps[:], EPS, w=['eps'])
                junk = sb('junk', [128, D], BF16); fss = sb('fss', [128, 4]); frs = sb('frs', [128, 4])
            hT = sb('hT', [128, 8, 512], BF16); fT = sb('fT', [128, 32, 512], BF16)
            x1 = sb('x1', [128, 4, D]); sq = [sb('sq%d' % i, [128, 512]) for i in range(2)]
            tmp = [sb('tmp%d' % i, [128, 512]) for i in range(2)]
            psf = [pst('psf%d' % i, [128, 512]) for i in range(3)]
            ps2 = [pst('ps2%d' % i, [128, 512]) for i in range(3)]
            cur_s = -1; n2 = 0
            t0_, n_, s_ = groups[0]
            S.dma('sp', hT[:, :, 0:n_], self.h2T[:, :, t0_:t0_ + n_].rearrange("c p t -> p c t"), w=['hT'])
            for gi, (t0, n, s) in enumerate(groups):
                nj = n // 128
                if s != cur_s:
                    S.dma('sp', g2r[:], self.mrow[s, 5 * D:6 * D].partition_broadcast(128), w=['g2r']); cur_s = s
                S.dma('sp', x1[:, 0:nj, :], self.xmid[t0:t0 + n, :].rearrange("(j p) d -> p j d", p=128),
                      w=[('x1', j) for j in range(nj)])
                for j in range(32):
                    pf = psf[j % 3]
                    S.multi('pe', [('matmul', (pf[:, 0:n], w1[:, k, j * 128:(j + 1) * 128], hT[:, k, 0:n]),
                                    dict(start=(k == 0), stop=(k == 7))) for k in range(8)],
                            r=[('w1', j // 16), 'hT'], w=[('psf', j % 3)])
                    S.op('act', 'activation', sq[j % 2][:, 0:n], pf[:, 0:n], AF.Square, r=[('psf', j % 3)], w=[('sq', j % 2)])
                    S.op('dve', 'scalar_tensor_tensor', fT[:, j, 0:n], pf[:, 0:n], 0.0, sq[j % 2][:, 0:n],
                         ALU.is_gt, ALU.mult, r=[('psf', j % 3), ('sq', j % 2)], w=[('fT', j)])
                if gi + 1 < len(groups):
                    t0n, nn, sn_ = groups[gi + 1]
                    S.dma('sp', hT[:, :, 0:nn], self.h2T[:, :, t0n:t0n + nn].rearrange("c p t -> p c t"), w=['hT'])
                for j in range(nj):
                    for hh in range(2):
                        pi = n2 % 3; n2 += 1
                        S.multi('pe', [('matmul', (ps2[pi][:, :], fT[:, q, j * 128:(j + 1) * 128], w2[:, q, hh * 512:(hh + 1) * 512]),
                                        dict(start=(q == 0), stop=(q == 31))) for q in range(32)],
                                r=[('fT', q) for q in range(32)] + [('w2', q) for q in range(4)], w=[('ps2', pi)])
                        S.op('dve', 'tensor_tensor', tmp[pi % 2][:, :], ps2[pi][:, :], g2r[:, hh * 512:(hh + 1) * 512], ALU.mult,
                             r=[('ps2', pi), 'g2r'], w=[('tmp', pi % 2)])
                        S.op('pool', 'tensor_tensor', x1[:, j, hh * 512:(hh + 1) * 512], tmp[pi % 2][:, :],
                             x1[:, j, hh * 512:(hh + 1) * 512], ALU.add, r=[('tmp', pi % 2), ('x1', j)], w=[('x1', j)])
                    tt = t0 + j * 128
                    if not last:
                        S.dma('sp', self.xres[tt:tt + 128, :], x1[:, j, :], r=[('x1', j)])
                    else:
                        S.op('act', 'activation', junk[:], x1[:, j, :], AF.Square, accum_out=fss[:, j:j + 1],
                             r=[('x1', j)], w=['junk', ('fss', j)])
                        S.op('act', 'activation', frs[:, j:j + 1], fss[:, j:j + 1], AF.Sqrt, bias=eps[:, 0:1], scale=1.0 / D,
                             r=[('fss', j), 'eps'], w=[('frs', j)])
                        S.op('dve', 'reciprocal', frs[:, j:j + 1], frs[:, j:j + 1], r=[('frs', j)], w=[('frs', j)])
                        S.op('dve', 'scalar_tensor_tensor', x1[:, j, :], x1[:, j, :], frs[:, j:j + 1], fg[:], ALU.mult, ALU.mult,
                             r=[('x1', j), ('frs', j), 'fg'], w=[('x1', j)])
                        S.dma('sp', self.y[tt:tt + 128, :], x1[:, j, :], r=[('x1', j)])
            S.flush()

    def p3_fnet(self, l):
        for nm, t0 in (('x', 0), ('c', L)):
            if nm == 'c' and l == DEPTH - 1:
                continue
            self._fnet(l, nm, t0)

    def _fnet(self, l, nm, t0):
        nc, S = self.nc, self.S
        L1 = self.fn[nm]['L1']; Ls = self.fn[nm]['Ls']; L2 = 64
        with ExitStack() as es:
            sb = lambda n, shp, dt=F32: es.enter_context(nc.sbuf_tensor(self.uq(n), shp, dt))
            pst = lambda n, shp, dt=F32: es.enter_context(nc.psum_tensor(self.uq(n), shp, dt))
            uf = sb('uf', [128, 2, Ls], BF16)
            S.dma('sp', uf[:], self.pxT[5:7, :, t0:t0 + Ls].rearrange("c p t -> p c t"), w=['uf'])
            csw = sb('csw', [128, 256], BF16); S.dma('sp', csw[:], self.I('fn_csw'), w=['csw'])
            w1 = sb('w1', [L1, 2 * L1], BF16); w2 = sb('w2', [L1, 2 * L1], BF16)
            S.dma('sp', w1[:], self.I('fn_w1' + nm), w=['w1']); S.dma('sp', w2[:], self.I('fn_w2' + nm), w=['w2'])
            tr = sb('tr', [64, Ls], BF16); sn = sb('sn', [64, Ls], BF16)
            S.dma('sp', tr[:], self.I('fn_tr' + nm), w=['tr']); S.dma('sp', sn[:], self.I('fn_sn' + nm), w=['sn'])
            U = sb('U', [L1, L2, 512], BF16)
            Y = sb('Y', [64, 2, 256, L1], BF16)
            pu = [pst('pu%d' % i, [128, 512]) for i in range(2)]
            py = [pst('py%d' % i, [128, 512]) for i in range(2)]
            pf = [pst('pf%d' % i, [128, 512]) for i in range(2)]
            for l2 in range(L2):
                b = l2 % 2
                S.multi('pe', [('matmul', (pu[b][0:L1, c * 256:(c + 1) * 256], uf[:, c, l2::L2], csw[:, :]),
                                dict(start=True, stop=True)) for c in range(2)], r=['uf', 'csw'], w=[('pu', b)])
                S.op('dve', 'tensor_copy', U[:, l2, :], pu[b][0:L1, :], r=[('pu', b)], w=['U'])
            cpb = 512 // (2 * L1)
            nb = 0
            for ch0 in range(0, 256, cpb):
                b = nb % 2; nb += 1
                calls = []
                for chl in range(cpb):
                    ch = ch0 + chl; c = ch // 128; i = ch % 128
                    o = py[b][0:64, chl * 2 * L1:(chl + 1) * 2 * L1]
                    calls.append(('matmul', (o, U[:, :, c * 256 + i], w1[:, :]), dict(start=True, stop=False)))
                    calls.append(('matmul', (o, U[:, :, c * 256 + 128 + i], w2[:, :]), dict(start=False, stop=True)))
                S.multi('pe', calls, r=['U', 'w1', 'w2'], w=[('py', b)])
                for r_ in range(2):
                    src = py[b][0:64, 0:cpb * 2 * L1].rearrange("p (c r a) -> p c r a", c=cpb, r=2)[:, :, r_, :]
                    dst = Y[:, r_, ch0:ch0 + cpb, :]
                    if r_ == 0:
                        S.op('act', 'activation', dst, src, AF.Copy, r=[('py', b)], w=['Y'])
                    else:
                        S.op('dve', 'tensor_copy', dst, src, r=[('py', b)], w=['Y'])
            na = min(8, L1)
            nb = 0
            for cc in range(2):
                for a0 in range(0, L1, na):
                    b = nb % 2; nb += 1
                    calls = []
                    for al in range(na):
                        a_ = a0 + al
                        o = pf[b][:, al * 64:(al + 1) * 64]
                        calls.append(('matmul', (o, Y[:, 0, cc * 128:(cc + 1) * 128, a_], tr[:, a_::L1]), dict(start=True, stop=False)))
                        calls.append(('matmul', (o, Y[:, 1, cc * 128:(cc + 1) * 128, a_], sn[:, a_::L1]), dict(start=False, stop=True)))
                    S.multi('pe', calls, r=['Y', 'tr', 'sn'], w=[('pf', b)])
                    dst = uf[:, cc, :].rearrange("p (b a) -> p a b", a=L1)[:, a0:a0 + na, :]
                    src = pf[b][:, 0:na * 64].rearrange("p (a b) -> p a b", a=na)
                    if b == 0:
                        S.op('act', 'activation', dst, src, AF.Copy, r=[('pf', b), 'U'], w=['uf'])
                    else:
                        S.op('dve', 'tensor_copy', dst, src, r=[('pf', b), 'U'], w=['uf'])
            S.dma('sp', self.mixT[4:6, :, t0:t0 + Ls].rearrange("c p t -> p c t"), uf[:], r=['uf'])
            S.flush()

    def _hy_f1(self, src, krows, N1, w1h, hyY, es_sb, pst):
        S = self.S
        cpb = 512 // (2 * N1)
        py = [pst('hpy%d' % i, [128, 512]) for i in range(2)]
        nbank = 256 // cpb
        GB = min(4, nbank)
        stg = [es_sb('hstg%d' % i, [128, GB, 512], BF16) for i in range(2)]
        for nb in range(nbank):
            ch0 = nb * cpb
            b = nb % 2; sb_ = (nb // GB) % 2; g = nb % GB
            S.multi('pe', [('matmul', (py[b][:, chl * 2 * N1:(chl + 1) * 2 * N1], src[0:krows, :, ch0 + chl], w1h[0:krows, :]),
                            dict(start=True, stop=True)) for chl in range(cpb)], r=['hsrc', 'w1h'], w=[('hpy', b)])
            if b == 0:
                S.op('act', 'activation', stg[sb_][:, g, :], py[b][:, :], AF.Copy, r=[('hpy', b)], w=[('hstg', sb_)])
            else:
                S.op('dve', 'tensor_copy', stg[sb_][:, g, :], py[b][:, :], r=[('hpy', b)], w=[('hstg', sb_)])
            if g == GB - 1:
                c0 = (nb - GB + 1) * cpb
                S.dma('sp', hyY[:, c0:c0 + GB * cpb, :].rearrange("p c x -> p (c x)"),
                      stg[sb_][:, :, :].rearrange("p g x -> p (g x)"), r=[('hstg', sb_)])

    def _hy_f2(self, Yz, ncol, N1, tr, ti, nti, pz, f1):
        S = self.S
        calls = [('matmul', (pz[:, 0:ncol], tr[:, f1::N1], Yz[:, :, f1]), dict(start=True, stop=False)),
                 ('matmul', (pz[:, 0:ncol], nti[:, f1::N1], Yz[:, :, N1 + f1]), dict(start=False, stop=True)),
                 ('matmul', (pz[:, ncol:2 * ncol], ti[:, f1::N1], Yz[:, :, f1]), dict(start=True, stop=False)),
                 ('matmul', (pz[:, ncol:2 * ncol], tr[:, f1::N1], Yz[:, :, N1 + f1]), dict(start=False, stop=True))]
        return calls

    def _range_sin(self, dst, ps, fcol, fbcol, tl, n):
        S = self.S
        a, t, k = tl['a'], tl['t'], tl['k']
        S.op('dve', 'tensor_scalar', a[:, 0:n], ps, fcol, fbcol, ALU.mult, ALU.add, r=[tl['ps'], 'fcols'], w=['rs_a'])
        S.op('dve', 'tensor_scalar', t[:, 0:n], a[:, 0:n], 1.0 / (2 * math.pi), 12582912.0, ALU.mult, ALU.add, r=['rs_a'], w=['rs_t'])
        S.op('dve', 'tensor_scalar', k[:, 0:n], t[:, 0:n], -12582912.0, None, ALU.add, r=['rs_t'], w=['rs_k'])
        S.op('dve', 'scalar_tensor_tensor', a[:, 0:n], k[:, 0:n], -2 * math.pi, a[:, 0:n], ALU.mult, ALU.add, r=['rs_k', 'rs_a'], w=['rs_a'])
        S.op('dve', 'tensor_scalar', a[:, 0:n], a[:, 0:n], -3.14159, 3.14159, ALU.max, ALU.min, r=['rs_a'], w=['rs_a'])
        S.op('act', 'activation', dst, a[:, 0:n], AF.Sin, r=['rs_a'], w=[tl['dst']])

    def pk_filters(self, l):
        for nm in ('x', 'c'):
            if nm == 'c' and l == DEPTH - 1:
                continue
            self._filters(l, nm)

    def _filters(self, l, nm):
        nc, S = self.nc, self.S
        hy = self.hy[nm]; N1 = hy['N1']; Ls = hy['Ls']; N = hy['N']
        hyY = self.hyY[nm]
        with ExitStack() as es:
            sb = lambda n, shp, dt=F32: es.enter_context(nc.sbuf_tensor(self.uq(n), shp, dt))
            pst = lambda n, shp, dt=F32: es.enter_context(nc.psum_tensor(self.uq(n), shp, dt))
            ks = sb('ks', [N1, 128, 256], BF16)
            w1h = sb('w1h', [N1, 2 * N1], BF16); S.dma('sp', w1h[:], self.I('hy_w1' + nm), w=['w1h'])
            with ExitStack() as es2:
                sb2 = lambda n, shp, dt=F32: es2.enter_context(nc.sbuf_tensor(self.uq(n), shp, dt))
                ps2 = lambda n, shp, dt=F32: es2.enter_context(nc.psum_tensor(self.uq(n), shp, dt))
                wa = sb2('wa', [33, 64]); wb = sb2('wb', [64, 64]); wc = sb2('wc', [64, 512], BF16)
                S.dma('sp', wa[:], self.I('hy_w1')[l], w=['wa']); S.dma('sp', wb[:], self.I('hy_w2')[l], w=['wb'])
                S.dma('pool', wc[:], self.I('hy_w3')[l], w=['wc'])
                fc = sb2('fc', [64, 4])
                S.op('dve', 'tensor_copy', fc[:, 0:1], self.cols[0:64, C_HFR:C_HFR + 1], r=['cols'], w=['fcols'])
                S.op('dve', 'tensor_tensor', fc[:, 1:2], self.cols[0:64, C_HFR:C_HFR + 1], self.cols[0:64, C_HB1:C_HB1 + 1], ALU.mult, r=['cols'], w=['fcols'])
                S.op('dve', 'tensor_tensor', fc[:, 2:3], self.cols[0:64, C_HFR:C_HFR + 1], self.cols[0:64, C_HB2:C_HB2 + 1], ALU.mult, r=['cols'], w=['fcols'])
                zt = [sb2('zt%d' % i, [33, 512]) for i in range(2)]
                h1 = sb2('h1', [64, 512])
                hA = sb2('hA', [64, N], BF16); hB = sb2('hB', [64, N], BF16)
                tl = dict(a=sb2('rsa', [64, 512]), t=sb2('rst', [64, 512]), k=sb2('rsk', [64, 512]))
                tcol = sb2('tcol', [N1, 128]); S.dma('sp', tcol[:], self.I('hy_tcol' + nm), w=['tcol'])
                drow = sb2('drow', [128, 256]); S.dma('sp', drow[:], self.I('hy_drow' + nm), w=['drow'])
                dec = [sb2('dec%d' % i, [N1, 256]) for i in range(2)]
                kabs = [sb2('kabs%d' % i, [N1, 32 * 256], BF16) for i in range(2)]
                onb = sb2('onb', [128, 1], BF16); S.op('dve', 'memset', onb[:], 1.0, w=['onb'])
                sres = sb2('sres', [128, 2])
                pa = ps2('pa', [64, 512]); pb_ = ps2('pb', [64, 512])
                pk = [ps2('pk%d' % i, [128, 512]) for i in range(2)]
                pSs = [ps2('pS%d' % i, [128, 2]) for i in range(2)]
                nblk = N // 512
                for blk in range(nblk):
                    b = blk % 2
                    S.dma('sp', zt[b][:], self.I('hy_zT' + nm)[:, blk * 512:(blk + 1) * 512], w=[('zt', b)])
                    S.op('pe', 'matmul', pa[:, :], wa[:, :], zt[b][:, :], start=True, stop=True, r=['wa', ('zt', b)], w=['pa'])
                    tl.update(ps='pa', dst='h1')
                    self._range_sin(h1[:, :], pa[:, :], fc[:, 0:1], fc[:, 1:2], tl, 512)
                    S.op('pe', 'matmul', pb_[:, :], wb[:, :], h1[:, :], start=True, stop=True, r=['wb', 'h1'], w=['pb'])
                    tl.update(ps='pb', dst='hA')
                    self._range_sin(hA[:, blk * 512:(blk + 1) * 512], pb_[:, :], fc[:, 0:1], fc[:, 2:3], tl, 512)
                S.op('pool', 'tensor_copy', hB[:, Ls:N], hA[:, Ls:N], r=['hA'], w=['hB'])
                S.op('pool', 'memset', hB[:, 0:Ls], 0.0, w=['hB'])
                S.op('pool', 'memset', hA[:, Ls:N], 0.0, r=['hB'], w=['hA'])
                for s2 in range(128):
                    b = s2 % 2
                    S.multi('pe', [('matmul', (pk[b][0:N1, 0:256], hA[:, s2::128], wc[:, 0:256]), dict(start=True, stop=False)),
                                   ('matmul', (pk[b][0:N1, 0:256], hB[:, s2::128], wc[:, 256:512]), dict(start=False, stop=True))],
                            r=['hA', 'hB', 'wc'], w=[('pk', b)])
                    S.op('act', 'activation', dec[b][:, :], drow[0:N1, :], AF.Exp, scale=tcol[:, s2:s2 + 1],
                         r=['drow', 'tcol'], w=[('dec', b)])
                    S.op('dve', 'tensor_tensor', ks[:, s2, :], pk[b][0:N1, 0:256], dec[b][:, :], ALU.mult,
                         r=[('pk', b), ('dec', b)], w=['hsrc'])
                for q in range(4):
                    S.op('act', 'activation', kabs[q % 2][:, :], ks[:, q * 32:(q + 1) * 32, :].rearrange("p s c -> p (s c)"), AF.Abs,
                         r=['hsrc'], w=[('kabs', q % 2)])
                    for cc in range(2):
                        S.multi('pe', [('matmul', (pSs[cc][:, 0:1], kabs[q % 2][:, sl_ * 256 + cc * 128:sl_ * 256 + (cc + 1) * 128], onb[0:N1, 0:1]),
                                        dict(start=(q == 0 and sl_ == 0), stop=(q == 3 and sl_ == 31))) for sl_ in range(32)],
                                r=[('kabs', q % 2), 'onb'], w=[('pS', cc)])
                for cc in range(2):
                    S.op('dve', 'tensor_copy', sres[:, cc:cc + 1], pSs[cc][:, 0:1], r=[('pS', cc)], w=['sres'])
                S.dma('sp', hy['ssum'], sres[:, :], r=['sres'])
                self._hy_f1(ks, N1, N1, w1h, hyY, sb2, ps2)
                S.flush()
        with ExitStack() as es:
            sb = lambda n, shp, dt=F32: es.enter_context(nc.sbuf_tensor(self.uq(n), shp, dt))
            pst = lambda n, shp, dt=F32: es.enter_context(nc.psum_tensor(self.uq(n), shp, dt))
            Yz = sb('Yz', [128, 256, 2 * N1], BF16); S.dma('sp', Yz[:], hyY, w=['Yz'])
            tr = sb('tr', [128, N], BF16); ti = sb('ti', [128, N], BF16); nti = sb('nti', [128, N], BF16)
            S.dma('sp', tr[:], self.I('hy_tr' + nm), w=['tr']); S.dma('sp', ti[:], self.I('hy_ti' + nm), w=['ti'])
            S.dma('sp', nti[:], self.I('hy_nti' + nm), w=['nti'])
            FB = min(8, N1)
            kst = [sb('kst%d' % i, [128, FB, 512], BF16) for i in range(2)]
            pz = [pst('pz%d' % i, [128, 512]) for i in range(2)]
            for f1 in range(N1):
                b = f1 % 2; kb = (f1 // FB) % 2
                S.multi('pe', self._hy_f2(Yz, 256, N1, tr, ti, nti, pz[b], f1), r=['Yz', 'tr', 'ti', 'nti'], w=[('pz', b)])
                if b == 0:
                    S.op('act', 'activation', kst[kb][:, f1 % FB, :], pz[b][:, :], AF.Copy, r=[('pz', b)], w=[('kst', kb)])
                else:
                    S.op('dve', 'tensor_copy', kst[kb][:, f1 % FB, :], pz[b][:, :], r=[('pz', b)], w=[('kst', kb)])
                if f1 % FB == FB - 1:
                    f0 = f1 - FB + 1
                    S.dma('sp', hy['kf'][:, f0:f0 + FB, :, :].rearrange("p f r c -> p (f r c)"),
                          kst[kb][:, :, :].rearrange("p f x -> p (f x)"), r=[('kst', kb)])
            S.flush()

    def p4_hyena(self, l):
        for nm, t0 in (('x', 0), ('c', L)):
            if nm == 'c' and l == DEPTH - 1:
                continue
            self._hyena(l, nm, t0)

    def _hyena(self, l, nm, t0):
        nc, S = self.nc, self.S
        hy = self.hy[nm]; N1 = hy['N1']; Ls = hy['Ls']; N = hy['N']; K1 = N1 // 2
        hyY = self.hyY[nm]; hyF = self.hyF[nm]
        nblk = max(1, Ls // 512); bw = Ls // nblk
        with ExitStack() as es:
            sb = lambda n, shp, dt=F32: es.enter_context(nc.sbuf_tensor(self.uq(n), shp, dt))
            pst = lambda n, shp, dt=F32: es.enter_context(nc.psum_tensor(self.uq(n), shp, dt))
            uh = sb('uh', [128, 6, Ls], BF16)
            S.dma('sp', uh[:], self.pxT[7:13, :, t0:t0 + Ls].rearrange("c p t -> p c t"), w=['uh'])
            idb = sb('idb', [128, 128], BF16); S.dma('sp', idb[:], self.I('ident_bf'), w=['idb'])
            w1h = sb('w1h', [N1, 2 * N1], BF16); S.dma('sp', w1h[:], self.I('hy_w1' + nm), w=['w1h'])
            zT = sb('zT', [128, 2, Ls], BF16); x0 = sb('x0', [128, 2, Ls], BF16)
            o = [sb('o%d' % c, [128, bw]) for c in range(6)]
            zs = sb('zs', [K1, 128, 256], BF16)
            cw = lambda tap, c: self.cols[:, C_HCW + tap * 6 + c:C_HCW + tap * 6 + c + 1]
            for blk in range(nblk):
                c0 = blk * bw
                for c in range(6):
                    S.op('act', 'activation', o[c][:, :], uh[:, c, c0:c0 + bw], AF.Identity,
                         bias=self.cols[:, C_HCB + c:C_HCB + c + 1], scale=cw(1, c), r=['uh', 'cols'], w=[('o', c)])
                    lo = 1 if blk == 0 else 0
                    S.op('dve', 'scalar_tensor_tensor', o[c][:, lo:bw], uh[:, c, c0 + lo - 1:c0 + bw - 1], cw(0, c), o[c][:, lo:bw],
                         ALU.mult, ALU.add, r=['uh', 'cols', ('o', c)], w=[('o', c)])
                    hi = bw - 1 if blk == nblk - 1 else bw
                    S.op('dve', 'scalar_tensor_tensor', o[c][:, 0:hi], uh[:, c, c0 + 1:c0 + hi + 1], cw(2, c), o[c][:, 0:hi],
                         ALU.mult, ALU.add, r=['uh', 'cols', ('o', c)], w=[('o', c)])
                for cc in range(2):
                    S.op('pool', 'tensor_copy', x0[:, cc, c0:c0 + bw], o[cc][:, :], r=[('o', cc)], w=['x0'])
                    S.op('pool', 'tensor_tensor', zT[:, cc, c0:c0 + bw], o[2 + cc][:, :], o[4 + cc][:, :], ALU.mult,
                         r=[('o', 2 + cc), ('o', 4 + cc)], w=['zT'])
            S.dma('sp', self.hx0[:, :, t0:t0 + Ls].rearrange("c p t -> p c t"), x0[:], r=['x0'])
            S.dma('sp', self.hz[:, :, t0:t0 + Ls].rearrange("c p t -> p c t"), zT[:], r=['zT'])
            ptr = [pst('ptr%d' % i, [128, 1024], BF16) for i in range(2)]
            nb = 0
            for s20 in range(0, 128, 4):
                b = nb % 2; nb += 1
                calls = []
                for sl in range(4):
                    for cc in range(2):
                        calls.append(('transpose', (ptr[b][0:K1, (sl * 2 + cc) * 128:(sl * 2 + cc + 1) * 128],
                                                    zT[:, cc, s20 + sl::128], idb[:, :]), {}))
                S.multi('pe', calls, r=['zT', 'idb'], w=[('ptr', b)])
                dst = zs[:, s20:s20 + 4, :].rearrange("p a c -> p (a c)")
                S.op('act', 'activation', dst, ptr[b][0:K1, :], AF.Copy, r=[('ptr', b)], w=['hsrc'])
            self._hy_f1(zs, K1, N1, w1h, hyY, sb, pst)
            S.flush()
        with ExitStack() as es:
            sb = lambda n, shp, dt=F32: es.enter_context(nc.sbuf_tensor(self.uq(n), shp, dt))
            pst = lambda n, shp, dt=F32: es.enter_context(nc.psum_tensor(self.uq(n), shp, dt))
            tr = sb('tr', [128, N], BF16); ti = sb('ti', [128, N], BF16); nti = sb('nti', [128, N], BF16)
            Yz = sb('Yz', [128, 128, 2 * N1], BF16); S.dma('sp', Yz[:], hyY[:, 0:128, :], w=['Yz'])
            S.dma('sp', tr[:], self.I('hy_tr' + nm), w=['tr']); S.dma('sp', ti[:], self.I('hy_ti' + nm), w=['ti'])
            S.dma('sp', nti[:], self.I('hy_nti' + nm), w=['nti'])
            kf = sb('kf', [128, N1, 2, 256], BF16); S.dma('sp', kf[:], hy['kf'], w=['kf'])
            Yf = sb('Yf', [128, N1, 2, 128], BF16)
            FB = min(4, N1)
            ta = [sb('ta%d' % i, [128, FB, 2, 128]) for i in range(2)]; tb = [sb('tb%d' % i, [128, FB, 2, 128]) for i in range(2)]
            pz = [pst('pz%d' % i, [128, FB * 256]) for i in range(2)]
            for cc in range(2):
                if cc == 1:
                    S.dma('sp', Yz[:], hyY[:, 128:256, :], w=['Yz'])
                ccs = slice(cc * 128, (cc + 1) * 128)
                for st_ in range(N1 // FB):
                    f0 = st_ * FB; b = st_ % 2
                    calls = []
                    for fl in range(FB):
                        calls += self._hy_f2(Yz, 128, N1, tr, ti, nti, pz[b][:, fl * 256:(fl + 1) * 256], f0 + fl)
                    S.multi('pe', calls, r=['Yz', 'tr', 'ti', 'nti'], w=[('pz', b)])
                    pzv = pz[b][:, :].rearrange("p (f r c) -> p f r c", f=FB, r=2)
                    S.op('dve', 'tensor_tensor', ta[b][:, :, :, :], pzv, kf[:, f0:f0 + FB, :, ccs], ALU.mult, r=[('pz', b), 'kf'], w=[('ta', b)])
                    S.op('dve', 'tensor_tensor', tb[b][:, :, 0, :], pzv[:, :, 0, :], kf[:, f0:f0 + FB, 1, ccs], ALU.mult, r=[('pz', b), 'kf'], w=[('tb', b)])
                    S.op('dve', 'tensor_tensor', tb[b][:, :, 1, :], pzv[:, :, 1, :], kf[:, f0:f0 + FB, 0, ccs], ALU.mult, r=[('pz', b), 'kf'], w=[('tb', b)])
                    S.op('pool', 'tensor_tensor', Yf[:, f0:f0 + FB, 0, :], ta[b][:, :, 0, :], ta[b][:, :, 1, :], ALU.subtract, r=[('ta', b)], w=['Yf'])
                    S.op('pool', 'tensor_tensor', Yf[:, f0:f0 + FB, 1, :], tb[b][:, :, 0, :], tb[b][:, :, 1, :], ALU.add, r=[('tb', b)], w=['Yf'])
                S.dma('sp', hyF[cc], Yf[:], r=['Yf'])
            S.flush()
        for cc in range(2):
            with ExitStack() as es:
                sb = lambda n, shp, dt=F32: es.enter_context(nc.sbuf_tensor(self.uq(n), shp, dt))
                pst = lambda n, shp, dt=F32: es.enter_context(nc.psum_tensor(self.uq(n), shp, dt))
                Yf = sb('Yf', [128, N1, 2, 128], BF16); S.dma('sp', Yf[:], hyF[cc], w=['Yf'])
                i1a = sb('i1a', [128, 256], BF16); i1b = sb('i1b', [128, 256], BF16)
                S.dma('sp', i1a[:], self.I('hy_i1a' + nm), w=['i1a']); S.dma('sp', i1b[:], self.I('hy_i1b' + nm), w=['i1b'])
                er = sb('er', [N1, Ls], BF16); nei = sb('nei', [N1, Ls], BF16)
                S.dma('sp', er[:], self.I('hy_er' + nm), w=['er']); S.dma('sp', nei[:], self.I('hy_nei' + nm), w=['nei'])
                V = sb('V', [N1, 128, 2, 128], BF16)
                yT = sb('yT', [128, Ls]); x0h = sb('x0h', [128, Ls], BF16); zh = sb('zh', [128, Ls], BF16)
                S.dma('sp', x0h[:], self.hx0[cc, :, t0:t0 + Ls], w=['x0h']); S.dma('sp', zh[:], self.hz[cc, :, t0:t0 + Ls], w=['zh'])
                ssb = sb('ssb', [128, 2]); S.dma('sp', ssb[:], hy['ssum'], w=['ssb'])
                S.op('dve', 'reciprocal', ssb[:, :], ssb[:, :], r=['ssb'], w=['ssb'])
                res = sb('res', [128, Ls], BF16)
                pv = [pst('pv%d' % i, [128, 512]) for i in range(2)]
                py = [pst('py%d' % i, [128, 512]) for i in range(2)]
                nb = 0
                for ch0 in range(0, 128, 2):
                    b = nb % 2; nb += 1
                    calls = []
                    for chl in range(2):
                        o_ = pv[b][0:N1, chl * 256:(chl + 1) * 256]
                        calls.append(('matmul', (o_, Yf[:, :, 0, ch0 + chl], i1a[:, :]), dict(start=True, stop=False)))
                        calls.append(('matmul', (o_, Yf[:, :, 1, ch0 + chl], i1b[:, :]), dict(start=False, stop=True)))
                    S.multi('pe', calls, r=['Yf', 'i1a', 'i1b'], w=[('pv', b)])
                    dst = V[:, ch0:ch0 + 2, :, :].rearrange("p c r a -> p (c r a)")
                    S.op('dve', 'tensor_copy', dst, pv[b][0:N1, :], r=[('pv', b)], w=['V'])
                nta = min(128, 512 // K1)
                nb = 0
                for ta0 in range(0, 128, nta):
                    b = nb % 2; nb += 1
                    calls = []
                    for al in range(nta):
                        ta_ = ta0 + al
                        o_ = py[b][:, al * K1:(al + 1) * K1]
                        calls.append(('matmul', (o_, V[:, :, 0, ta_], er[:, ta_::128]), dict(start=True, stop=False)))
                        calls.append(('matmul', (o_, V[:, :, 1, ta_], nei[:, ta_::128]), dict(start=False, stop=True)))
                    S.multi('pe', calls, r=['V', 'er', 'nei'], w=[('py', b)])
                    dst = yT[:, :].rearrange("p (b a) -> p a b", a=128)[:, ta0:ta0 + nta, :]
                    src = py[b][:, 0:nta * K1].rearrange("p (a b) -> p a b", b=K1)
                    if b == 0:
                        S.op('act', 'activation', dst, src, AF.Copy, r=[('py', b)], w=['yT'])
                    else:
                        S.op('dve', 'tensor_copy', dst, src, r=[('py', b)], w=['yT'])
                S.op('act', 'activation', yT[:, :], yT[:, :], AF.Identity, scale=ssb[:, cc:cc + 1], r=['yT', 'ssb'], w=['yT'])
                S.op('dve', 'scalar_tensor_tensor', yT[:, :], zh[:, :], self.cols[:, C_HD + cc:C_HD + cc + 1], yT[:, :],
                     ALU.mult, ALU.add, r=['zh', 'yT', 'cols'], w=['yT'])
                S.op('dve', 'tensor_tensor', res[:, :], yT[:, :], x0h[:, :], ALU.mult, r=['yT', 'x0h'], w=['res'])
                S.dma('sp', self.mixT[6 + cc, :, t0:t0 + Ls], res[:, :], r=['res'])
                S.flush()


def build_program(debug=False, stop_after=None):
    P = Prog(debug=debug)
    steps = []
    for l in range(DEPTH):
        steps += [('p0', l), ('p1', l), ('pk', l), ('p2', l), ('p3', l), ('p4', l), ('p5', l)]
    for (nm, l) in steps:
        fn = getattr(P, {'p0': 'p0_mod', 'p1': 'p1_inproj', 'pk': 'pk_filters', 'p2': 'p2_attn',
                         'p3': 'p3_fnet', 'p4': 'p4_hyena', 'p5': 'p5_out_mlp'}[nm])
        fn(l)
        if stop_after is not None and (nm, l) == stop_after:
            break
    P.es.close()
    P.nc.used_inputs = list(P._in.keys())
    return P.nc


_PROG = None


def kernel(**inputs):
    global _PROG
    per = _prep(inputs)
    if _PROG is None:
        _PROG = build_program()
    per = [{k: d[k] for k in _PROG.used_inputs} for d in per]
    res = run_bass_kernel_spmd(_PROG, per, core_ids=list(range(8)))
    return np.stack([np.asarray(r['y'], np.float32) for r in res.results], axis=0)
```

```python
import math
import numpy as np
import ml_dtypes
from contextlib import ExitStack
import concourse.bass as bass
import concourse.mybir as mybir
from concourse.bass_utils import run_bass_kernel_spmd

F32 = mybir.dt.float32
BF16 = mybir.dt.bfloat16
AF = mybir.ActivationFunctionType
ALU = mybir.AluOpType

D = 1024
L = 4096
LC = 256
T = L + LC
NT = T // 128
DEPTH = 2
H = 8
OFF_Q, OFF_KV, OFF_KR, OFF_F, OFF_H, IN_W = 0, 384, 640, 672, 928, 1696
EPS = 1e-6
NCOL = 64
C_N1G, C_N2G, C_QG, C_KVG, C_HCW, C_HCB, C_HD, C_HB1, C_HFR, C_HB2 = 0, 8, 16, 19, 21, 39, 45, 47, 48, 49


class _Op:
    __slots__ = ('eng', 'calls', 'deps', 'signal', 'sem', 'val', 'dma', 'waits', 'pre')


class Sched:
    NDMA = 8
    ATTR = [('pe', 'tensor'), ('act', 'scalar'), ('dve', 'vector'), ('pool', 'gpsimd'), ('sp', 'sync')]

    def __init__(self, nc, es):
        self.nc = nc
        self.sem = {e: es.enter_context(nc.semaphore('s_' + e)) for e in ['pe', 'act', 'dve', 'pool']}
        self.dsem = {q: [es.enter_context(nc.semaphore('d_%s%d' % (q, i))) for i in range(self.NDMA)]
                     for q in ['sp', 'pool', 'act']}
        self.cnt = {e: 0 for e in self.sem}
        self.dcnt = {q: 0 for q in self.dsem}
        self._reset()

    def _reset(self):
        self.ops = {e: [] for e, _ in self.ATTR}
        self.last_w = {}
        self.readers = {}

    def multi(self, eng, calls, r=(), w=(), dma=False):
        op = _Op()
        op.eng = eng; op.calls = calls; op.dma = dma; op.signal = False; op.pre = None
        deps = []
        for k in r:
            d = self.last_w.get(k)
            if d is not None: deps.append(d)
        for k in w:
            d = self.last_w.get(k)
            if d is not None: deps.append(d)
            deps.extend(self.readers.get(k, ()))
        if eng == 'pe':
            deps = [d for d in deps if d.eng != 'pe' or d.dma]
        op.deps = [d for d in deps if d is not op]
        for d in op.deps: d.signal = True
        for k in r: self.readers.setdefault(k, []).append(op)
        for k in w:
            self.last_w[k] = op
            self.readers[k] = []
        self.ops[eng].append(op)
        return op

    def op(self, eng, name, *args, r=(), w=(), **kw):
        return self.multi(eng, [(name, args, kw)], r=r, w=w)

    def dma(self, q, out, in_, r=(), w=(), **kw):
        return self.multi(q, [('dma_start', (out, in_), kw)], r=r, w=w, dma=True)

    def flush(self):
        nc = self.nc
        for e, _ in self.ATTR:
            ops = self.ops[e]
            for op in reversed(ops):
                if not op.dma:
                    op.signal = True
                    break
            for op in ops:
                if op.dma:
                    j = self.dcnt[e]; self.dcnt[e] += 1
                    op.sem = self.dsem[e][j % self.NDMA]
                    op.val = 16 * (j // self.NDMA + 1)
                    op.pre = (op.sem, op.val - 16) if op.val > 16 else None
                elif op.signal:
                    self.cnt[e] += 1
                    op.sem = self.sem[e]; op.val = self.cnt[e]
        finals = {}
        for e, _ in self.ATTR:
            seen = {}
            for op in self.ops[e]:
                need = {}
                if op.pre is not None: need[id(op.pre[0])] = op.pre
                for d in op.deps:
                    cur = need.get(id(d.sem))
                    if cur is None or cur[1] < d.val: need[id(d.sem)] = (d.sem, d.val)
                op.waits = []
                for k, (s, v) in need.items():
                    if seen.get(k, -1) < v:
                        seen[k] = v; op.waits.append((s, v))
                if op.dma or op.signal:
                    cur = finals.get(id(op.sem))
                    if cur is None or cur[1] < op.val: finals[id(op.sem)] = (op.sem, op.val)
        with nc.Block() as blk:
            for e, attr in self.ATTR:
                ops = self.ops[e]
                if not ops and e != 'sp': continue

                def body(eng, ops=ops, e=e):
                    for op in ops:
                        for (s, v) in op.waits: eng.wait_ge(s, v)
                        inst = None
                        for (name, args, kw) in op.calls:
                            inst = getattr(eng, name)(*args, **kw)
                        if op.dma: inst.then_inc(op.sem, 16)
                        elif op.signal: inst.then_inc(op.sem, 1)
                    if e == 'sp':
                        for (s, v) in finals.values(): eng.wait_ge(s, v)
                getattr(blk, attr)(body)
        self._reset()


def _bf(a):
    return np.ascontiguousarray(a.astype(ml_dtypes.bfloat16))


def _f32(a):
    return np.ascontiguousarray(a.astype(np.float32))


ROPE_PERM = np.array(list(range(8, 16)) + list(range(0, 8)) + list(range(24, 32)) + list(range(16, 24)))


def _rope_tables():
    rows = L // 64
    row = np.repeat(np.arange(rows, dtype=np.float64), 64)
    col = np.tile(np.arange(64, dtype=np.float64), rows)
    inv = 10000.0 ** (-np.arange(0, 16, 2, dtype=np.float64) / 16)
    ar = row[None, :] * inv[:, None]
    ac = col[None, :] * inv[:, None]
    cos = np.ones((32, T)); sin = np.zeros((32, T))
    cos[0:8, :L] = np.cos(ar); cos[8:16, :L] = np.cos(ar); cos[16:24, :L] = np.cos(ac); cos[24:32, :L] = np.cos(ac)
    sin[0:8, :L] = -np.sin(ar); sin[8:16, :L] = np.sin(ar); sin[16:24, :L] = -np.sin(ac); sin[24:32, :L] = np.sin(ac)
    return _f32(cos), _f32(sin)


def _fnet_tables(Ls, L1, L2):
    w = np.arange(64)
    ph = 2 * np.pi * np.outer(w, w) / 64
    cw = np.zeros((128, 128)); sw = np.zeros((128, 128))
    for g in range(2):
        cw[g * 64:(g + 1) * 64, g * 64:(g + 1) * 64] = np.cos(ph)
        sw[g * 64:(g + 1) * 64, g * 64:(g + 1) * 64] = np.sin(ph)
    csw = np.concatenate([cw, sw], axis=1)
    a = np.arange(L1)
    p1 = 2 * np.pi * np.outer(a, a) / L1
    wr, wi = np.cos(p1), -np.sin(p1)
    w1 = np.concatenate([wr, wi], axis=1)
    w2 = np.concatenate([wi, -wr], axis=1)
    p2 = 2 * np.pi * np.outer(np.arange(L2), np.arange(Ls)) / Ls
    sc = 1.0 / math.sqrt(Ls * 64)
    return _bf(csw), _bf(w1), _bf(w2), _bf(np.cos(p2) * sc), _bf(np.sin(p2) * sc)


def _hyena_tables(Ls):
    N = 2 * Ls
    N1 = N // 128
    a = np.arange(N1)
    p1 = 2 * np.pi * np.outer(a, a) / N1
    w1 = np.concatenate([np.cos(p1), -np.sin(p1)], axis=1)
    p2 = 2 * np.pi * np.outer(np.arange(128), np.arange(N)) / N
    tr, ti, nti = np.cos(p2), -np.sin(p2), np.sin(p2)
    p3 = 2 * np.pi * np.outer(np.arange(128), np.arange(128)) / 128
    i1a = np.concatenate([np.cos(p3), np.sin(p3)], axis=1)
    i1b = np.concatenate([-np.sin(p3), np.cos(p3)], axis=1)
    p4 = 2 * np.pi * np.outer(np.arange(N1), np.arange(Ls)) / N
    er, nei = np.cos(p4) / N, -np.sin(p4) / N
    j = np.arange(N)
    tau = np.where(j < Ls, j, N - j).astype(np.float64)
    tau[Ls] = 0
    tl = np.linspace(0.0, 1.0, Ls)
    t = tl[tau.astype(np.int64)]
    fr = np.linspace(1e-4, 15.0, 16)
    ang = 2.0 * np.pi * tau[:, None] / Ls * fr[None, :]
    z = np.concatenate([t[:, None], np.cos(ang), -np.sin(ang)], axis=1)
    negt = -t.copy()
    negt[Ls] = -1e4
    tcol = negt.reshape(N1, 128)
    deltas = np.abs(np.linspace(math.log(1e-2) / 1.5, math.log(1e-2) / 0.3, 256))
    drow = np.broadcast_to(deltas[None, :], (128, 256))
    return dict(N1=N1, w1=_bf(w1), tr=_bf(tr), ti=_bf(ti), nti=_bf(nti), i1a=_bf(i1a), i1b=_bf(i1b),
                er=_bf(er), nei=_bf(nei), zT=_f32(z.T), tcol=_f32(tcol), drow=_f32(drow))


_CONST = None


def _consts():
    global _CONST
    if _CONST is not None:
        return _CONST
    c = {}
    c['ident_bf'] = _bf(np.eye(128))
    c['ident_f'] = _f32(np.eye(128))
    c['ones_f'] = _f32(np.ones((128, 128)))
    c['ropeC'], c['ropeS'] = _rope_tables()
    for nm, Ls, L1, L2 in (('x', L, 64, 64), ('c', LC, 4, 64)):
        csw, w1, w2, tr, sn = _fnet_tables(Ls, L1, L2)
        c['fn_csw'] = csw
        c['fn_w1' + nm], c['fn_w2' + nm], c['fn_tr' + nm], c['fn_sn' + nm] = w1, w2, tr, sn
        ht = _hyena_tables(Ls)
        for k, v in ht.items():
            if k != 'N1':
                c['hy_%s%s' % (k, nm)] = v
    _CONST = c
    return c


def _colpack(v, n):
    return np.asarray(v, np.float32).reshape(n, 128).T


def _prep(inp):
    c = dict(_consts())
    sh = {}
    w_in = np.asarray(inp['w_in'], np.float32)
    sh['w_inx'] = np.ascontiguousarray(np.concatenate([w_in, w_in[:, :, OFF_KR + ROPE_PERM]], axis=2))
    w_uq = np.asarray(inp['w_uq'], np.float32)
    sh['w_uq'] = np.ascontiguousarray(w_uq)
    wq = w_uq.reshape(DEPTH, 384, H, 96)
    sh['w_uqp'] = np.ascontiguousarray(
        np.concatenate([wq[..., :64], wq[..., 64:][..., ROPE_PERM]], axis=-1).reshape(DEPTH, 384, 768))
    wkv = np.asarray(inp['w_ukv'], np.float32).reshape(DEPTH, 256, H, 128)
    sh['w_ukvx'] = np.ascontiguousarray(
        np.concatenate([wkv[..., :64].reshape(DEPTH, 256, 512), wkv[..., 64:].reshape(DEPTH, 256, 512)], axis=2))
    for k in ('w_mod', 'w_out', 'w_mlp1', 'w_mlp2', 'hy_w1', 'hy_w2', 'hy_w3'):
        sh[k] = np.ascontiguousarray(np.asarray(inp[k], np.float32))
    sh['b_mod2'] = np.ascontiguousarray(np.repeat(np.asarray(inp['b_mod'], np.float32)[:, None, :], 2, axis=1))
    cols = np.zeros((DEPTH, 128, NCOL), np.float32)
    for l in range(DEPTH):
        cols[l, :, C_N1G:C_N1G + 8] = _colpack(inp['norm1_g'][l], 8)
        cols[l, :, C_N2G:C_N2G + 8] = _colpack(inp['norm2_g'][l], 8)
        cols[l, :, C_QG:C_QG + 3] = _colpack(inp['q_norm_g'][l], 3)
        cols[l, :, C_KVG:C_KVG + 2] = _colpack(inp['kv_norm_g'][l], 2)
        for tap in range(3):
            cols[l, :, C_HCW + tap * 6:C_HCW + tap * 6 + 6] = _colpack(inp['hy_conv_w'][l, tap], 6)
        cols[l, :, C_HCB:C_HCB + 6] = _colpack(inp['hy_conv_b'][l], 6)
        cols[l, :, C_HD:C_HD + 2] = _colpack(inp['hy_d'][l], 2)
        cols[l, :64, C_HB1] = inp['hy_b1'][l]
        cols[l, :64, C_HFR] = inp['hy_freq'][l]
        cols[l, :64, C_HB2] = inp['hy_b2'][l]
    sh['cols'] = cols
    sh['fgrow'] = np.ascontiguousarray(np.broadcast_to(np.asarray(inp['final_norm_g'], np.float32)[None, :], (128, D)))
    sh.update(c)
    per = []
    x = np.asarray(inp['x'], np.float32); ctx = np.asarray(inp['ctx'], np.float32)
    cc = np.asarray(inp['c'], np.float32); c_ctx = np.asarray(inp['c_ctx'], np.float32)
    for b in range(8):
        d = dict(sh)
        d['xin'] = np.ascontiguousarray(np.concatenate([x[b], ctx[b]], axis=0))
        cv = np.stack([_colpack(cc[b], 8), _colpack(c_ctx, 8)], axis=-1)
        d['cvec'] = np.ascontiguousarray(cv)
        per.append(d)
    return per


GROUPS = [(g * 512, 512, 0) for g in range(8)] + [(L, LC, 1)]
SCALE = 1.0 / math.sqrt(96.0)


class Prog:
    def __init__(self, debug=False):
        self.debug = debug
        self.nc = nc = bass.Bass("TRN2", target_bir_lowering=False)
        self.es = ExitStack()
        self.S = Sched(nc, self.es)
        self._in = {}
        okind = dict(kind="ExternalOutput") if debug else {}
        scr = lambda n, shp, dt=F32: nc.dram_tensor(n, list(shp), dt, **okind).ap()
        self.spec = dict(
            xin=([T, D], F32), cvec=([128, 8, 2], F32), w_mod=([DEPTH, D, 6 * D], F32), b_mod2=([DEPTH, 2, 6 * D], F32),
            w_inx=([DEPTH, D, 1728], F32), w_uq=([DEPTH, 384, 768], F32), w_uqp=([DEPTH, 384, 768], F32),
            w_ukvx=([DEPTH, 256, 1024], F32), w_out=([DEPTH, D, D], F32), w_mlp1=([DEPTH, D, 4 * D], F32),
            w_mlp2=([DEPTH, 4 * D, D], F32), hy_w1=([DEPTH, 33, 64], F32), hy_w2=([DEPTH, 64, 64], F32),
            hy_w3=([DEPTH, 64, 512], F32), cols=([DEPTH, 128, NCOL], F32), fgrow=([128, D], F32),
            ident_bf=([128, 128], BF16), ident_f=([128, 128], F32), ones_f=([128, 128], F32),
            ropeC=([32, T], F32), ropeS=([32, T], F32), fn_csw=([128, 256], BF16))
        self.fn = {}; self.hy = {}
        for nm, Ls, L1 in (('x', L, 64), ('c', LC, 4)):
            N = 2 * Ls; N1 = N // 128
            self.spec.update({'fn_w1' + nm: ([L1, 2 * L1], BF16), 'fn_w2' + nm: ([L1, 2 * L1], BF16),
                              'fn_tr' + nm: ([64, Ls], BF16), 'fn_sn' + nm: ([64, Ls], BF16),
                              'hy_w1' + nm: ([N1, 2 * N1], BF16), 'hy_tr' + nm: ([128, N], BF16),
                              'hy_ti' + nm: ([128, N], BF16), 'hy_nti' + nm: ([128, N], BF16),
                              'hy_i1a' + nm: ([128, 256], BF16), 'hy_i1b' + nm: ([128, 256], BF16),
                              'hy_er' + nm: ([N1, Ls], BF16), 'hy_nei' + nm: ([N1, Ls], BF16),
                              'hy_zT' + nm: ([33, N], F32), 'hy_tcol' + nm: ([N1, 128], F32),
                              'hy_drow' + nm: ([128, 256], F32)})
            self.fn[nm] = dict(L1=L1, Ls=Ls)
            self.hy[nm] = dict(N1=N1, Ls=Ls, N=N, kf=scr('kf' + nm, [128, N1, 2, 256], BF16),
                               ssum=scr('ssum' + nm, [128, 2]))
        self.y = nc.dram_tensor('y', [L, D], F32, kind="ExternalOutput").ap()
        self.xres = scr('xres', [T, D]); self.mrow = scr('mrow', [2, 6 * D])
        self.pxT = scr('pxT', [14, 128, T], BF16); self.mixT = scr('mixT', [8, 128, T], BF16)
        self.hx0 = scr('hx0', [2, 128, T], BF16); self.hz = scr('hz', [2, 128, T], BF16)
        self.xmid = scr('xmid', [T, D]); self.h2T = scr('h2T', [8, 128, T], BF16)
        self.hyY = {}; self.hyF = {}
        for nm_ in ('x', 'c'):
            n1_ = self.hy[nm_]['N1']
            self.hyY[nm_] = scr('hyY' + nm_, [128, 256, 2 * n1_], BF16)
            self.hyF[nm_] = scr('hyF' + nm_, [2, 128, n1_, 2, 128], BF16)
        self.modc = self.es.enter_context(nc.sbuf_tensor('modc', [128, 2, 4, 8], F32))
        self.cols = self.es.enter_context(nc.sbuf_tensor('colsb', [128, NCOL], F32))

    def uq(self, n):
        self._uq = getattr(self, '_uq', 0) + 1
        return '%s_%d' % (n, self._uq)

    def I(self, name):
        if name not in self._in:
            shp, dt = self.spec[name]
            self._in[name] = self.nc.dram_tensor(name, list(shp), dt, kind="ExternalInput").ap()
        return self._in[name]

    def p0_mod(self, l):
        nc, S = self.nc, self.S
        with ExitStack() as es:
            sb = lambda n, shp, dt=F32: es.enter_context(nc.sbuf_tensor(self.uq(n), shp, dt))
            cv = sb('cv', [128, 8, 2]); sc = sb('sc', [128, 8, 2])
            wt = [sb('wt%d' % i, [128, 8, 512]) for i in range(2)]
            msb = sb('msb', [2, 6 * D]); bm = sb('bm', [2, 6 * D])
            ps = [es.enter_context(nc.psum_tensor(self.uq('ps%d' % i), [2, 512], F32)) for i in range(2)]
            S.dma('sp', cv[:], self.I('cvec'), w=['cv'])
            S.dma('sp', bm[:], self.I('b_mod2')[l], w=['bm'])
            S.dma('sp', self.cols[:], self.I('cols')[l], w=['cols'])
            S.op('act', 'activation', sc[:], cv[:], AF.Silu, r=['cv'], w=['sc'])
            wv = self.I('w_mod')[l].rearrange("(k p) n -> p k n", p=128)
            for n in range(12):
                b = n % 2
                S.dma('sp', wt[b][:], wv[:, :, n * 512:(n + 1) * 512], w=[('wt', b)])
                S.multi('pe', [('matmul', (ps[b][:], sc[:, k, :], wt[b][:, k, :]), dict(start=(k == 0), stop=(k == 7)))
                               for k in range(8)], r=['sc', ('wt', b)], w=[('ps', b)])
                S.op('dve', 'tensor_tensor', msb[:, n * 512:(n + 1) * 512], ps[b][:], bm[:, n * 512:(n + 1) * 512],
                     ALU.add, r=[('ps', b), 'bm'], w=['msb'])
            S.dma('sp', self.mrow, msb[:], r=['msb'])
            S.flush()
            mT = sb('mT', [96, 128]); idf = sb('idf', [128, 128]); mcol = sb('mcol', [128, 96])
            pm = es.enter_context(nc.psum_tensor(self.uq('pm'), [128, 96], F32))
            S.dma('sp', mT[:], self.mrow.rearrange("s (j p) -> (s j) p", p=128), w=['mT'])
            S.dma('sp', idf[:], self.I('ident_f'), w=['idf'])
            S.op('pe', 'transpose', pm[:], mT[:], idf[0:96, 0:96], r=['mT', 'idf'], w=['pm'])
            S.op('dve', 'tensor_copy', mcol[:], pm[:], r=['pm'], w=['mcol'])
            for s in range(2):
                o = s * 48
                S.op('dve', 'scalar_tensor_tensor', self.modc[:, s, 0, :], mcol[:, o + 8:o + 16], 1.0,
                     self.cols[:, C_N1G:C_N1G + 8], ALU.add, ALU.mult, r=['mcol', 'cols'], w=['modc'])
                S.op('dve', 'tensor_copy', self.modc[:, s, 1, :], mcol[:, o:o + 8], r=['mcol'], w=['modc'])
                S.op('dve', 'scalar_tensor_tensor', self.modc[:, s, 2, :], mcol[:, o + 32:o + 40], 1.0,
                     self.cols[:, C_N2G:C_N2G + 8], ALU.add, ALU.mult, r=['mcol', 'cols'], w=['modc'])
                S.op('dve', 'tensor_copy', self.modc[:, s, 3, :], mcol[:, o + 24:o + 32], r=['mcol'], w=['modc'])
            S.flush()

    def norm_tiles(self, xt, nj, s, which, tl, key, n, part='ab'):
        S = self.S
        ss, rstd, xn, junk, pt, hT, idb = tl['ss'], tl['rstd'], tl['xn'], tl['junk'], tl['pt'], tl['hT'], tl['idb']
        for j in (range(nj) if 'a' in part else ()):
            S.op('act', 'activation', junk[:], xt[:, j, :], AF.Square, accum_out=ss[:, j:j + 1],
                 r=[key], w=['junk', tl['k'] + 'ss'])
        if 'a' in part:
            S.op('act', 'activation', rstd[:, 0:nj], ss[:, 0:nj], AF.Sqrt, bias=tl['eps'][:, 0:1], scale=1.0 / D,
                 r=[tl['k'] + 'ss', 'eps'], w=[tl['k'] + 'rstd'])
            S.op('dve', 'reciprocal', rstd[:, 0:nj], rstd[:, 0:nj], r=[tl['k'] + 'rstd'], w=[tl['k'] + 'rstd'])
        for j in (range(nj) if 'a' in part else ()):
            S.op('dve', 'tensor_scalar', xn[:, j, :], xt[:, j, :], rstd[:, j:j + 1], None, ALU.mult,
                 r=[key, tl['k'] + 'rstd'], w=[tl['k'] + 'xn'])
        for k in (range(8) if 'b' in part else ()):
            pb = k % 2
            S.multi('pe', [('transpose', (pt[pb][:, j * 128:(j + 1) * 128], xn[:, j, k * 128:(k + 1) * 128], idb[:]), {})
                           for j in range(nj)], r=[tl['k'] + 'xn', 'idb'], w=[('pt', pb)])
            if k % 2 == 0:
                S.op('dve', 'tensor_scalar', hT[:, k, 0:n], pt[pb][:, 0:n], self.modc[:, s, which, k:k + 1],
                     self.modc[:, s, which + 1, k:k + 1], ALU.mult, ALU.add, r=[('pt', pb), 'modc'], w=[tl['k'] + 'hT'])
            else:
                S.op('act', 'activation', hT[:, k, 0:n], pt[pb][:, 0:n], AF.Identity,
                     bias=self.modc[:, s, which + 1, k:k + 1], scale=self.modc[:, s, which, k:k + 1],
                     r=[('pt', pb), 'modc'], w=[tl['k'] + 'hT'])

    def p1_inproj(self, l):
        nc, S = self.nc, self.S
        src = self.I('xin') if l == 0 else self.xres
        with ExitStack() as es:
            sb = lambda n, shp, dt=F32: es.enter_context(nc.sbuf_tensor(self.uq(n), shp, dt))
            pst = lambda n, shp, dt=F32: es.enter_context(nc.psum_tensor(self.uq(n), shp, dt))
            win = sb('win', [128, 8, 1728], BF16)
            S.dma('pool', win[:], self.I('w_inx')[l].rearrange("(k p) n -> p k n", p=128), w=['win'])
            idb = sb('idb', [128, 128], BF16); S.dma('sp', idb[:], self.I('ident_bf'), w=['idb'])
            onesf = sb('onesf', [128, 128]); S.dma('sp', onesf[:], self.I('ones_f'), w=['onesf'])
            rc = sb('rc', [32, T]); rs = sb('rs', [32, T])
            S.dma('sp', rc[:], self.I('ropeC'), w=['rc']); S.dma('sp', rs[:], self.I('ropeS'), w=['rs'])
            eps = sb('eps', [128, 1]); S.op('dve', 'memset', eps[:], EPS, w=['eps'])
            xg = [sb('xg%d' % i, [128, 4, D]) for i in range(3)]
            tls = []
            junk = sb('junk', [128, D], BF16)
            pt = [pst('pt%d' % i, [128, 512], BF16) for i in range(2)]
            for i in range(2):
                tls.append(dict(k='t%d' % i, ss=sb('ss%d' % i, [128, 4]), rstd=sb('rstd%d' % i, [128, 4]),
                                xn=sb('xn%d' % i, [128, 4, D], BF16), junk=junk, pt=pt,
                                hT=sb('hT%d' % i, [128, 8, 512], BF16), idb=idb, eps=eps))
            ost = [sb('ost%d' % i, [128, 14, 512], BF16) for i in range(2)]
            for i in range(2):
                S.op('pool', 'memset', ost[i][:, 13, :], 0.0, w=[('ost', i)])
            c32 = sb('c32', [128, 5, 512]); sq = sb('sq', [128, 5, 512]); rq = sb('rq', [128, 512])
            t1 = sb('t1', [32, 512]); t2 = sb('t2', [32, 512])
            po = [pst('po%d' % i, [128, 512]) for i in range(3)]
            pn = pst('pn', [128, 512])
            npo = [0]

            def proj(col0, ncols, hT, n, b):
                i = npo[0] % 3; npo[0] += 1
                S.multi('pe', [('matmul', (po[i][0:ncols, 0:n], win[:, k, col0:col0 + ncols], hT[:, k, 0:n]),
                                dict(start=(k == 0), stop=(k == 7))) for k in range(8)],
                        r=['win', 't%dhT' % b], w=[('po', i)])
                return i

            def load(gi):
                t0, n, s = GROUPS[gi]
                x3 = gi % 3; nj = n // 128
                S.dma('sp', xg[x3][:, 0:nj, :], src[t0:t0 + n, :].rearrange("(j p) d -> p j d", p=128), w=[('xg', x3)])

            def front(gi, part):
                t0, n, s = GROUPS[gi]
                nj = n // 128
                self.norm_tiles(xg[gi % 3], nj, s, 0, tls[gi % 2], ('xg', gi % 3), n, part=part)

            def back(gi):
                t0, n, s = GROUPS[gi]
                b = gi % 2; nj = n // 128; tl = tls[b]
                hT = tl['hT']
                okey = ('ost', b)
                for (c0, nch, gcol, ci0, i0) in ((OFF_Q, 3, C_QG, 0, 0), (OFF_KV, 2, C_KVG, 3, 3)):
                    for c in range(nch):
                        i = proj(c0 + c * 128, 128, hT, n, b)
                        S.op('act', 'activation', c32[:, i0 + c, 0:n], po[i][:, 0:n], AF.Copy,
                             r=[('po', i)], w=[('c32', i0 + c)])
                        S.op('act', 'activation', sq[:, i0 + c, 0:n], po[i][:, 0:n], AF.Square,
                             r=[('po', i)], w=[('sq', i0 + c)])
                    S.multi('pe', [('matmul', (pn[:, 0:n], onesf[:], sq[:, i0 + c, 0:n]),
                                    dict(start=(c == 0), stop=(c == nch - 1))) for c in range(nch)],
                            r=['onesf'] + [('sq', i0 + c) for c in range(nch)], w=['pn'])
                    S.op('act', 'activation', rq[:, 0:n], pn[:, 0:n], AF.Sqrt, bias=eps[:, 0:1], scale=1.0 / (128 * nch),
                         r=['pn', 'eps'], w=['rq'])
                    S.op('dve', 'reciprocal', rq[:, 0:n], rq[:, 0:n], r=['rq'], w=['rq'])
                    for c in range(nch):
                        S.op('dve', 'scalar_tensor_tensor', ost[b][:, ci0 + c, 0:n], c32[:, i0 + c, 0:n],
                             self.cols[:, gcol + c:gcol + c + 1], rq[:, 0:n], ALU.mult, ALU.mult,
                             r=[('c32', i0 + c), 'rq', 'cols'], w=[okey])
                ia = proj(OFF_KR, 32, hT, n, b)
                ib = proj(IN_W, 32, hT, n, b)
                S.op('dve', 'tensor_tensor', t1[:, 0:n], po[ia][0:32, 0:n], rc[:, t0:t0 + n], ALU.mult,
                     r=[('po', ia), 'rc'], w=['t1'])
                S.op('dve', 'tensor_tensor', t2[:, 0:n], po[ib][0:32, 0:n], rs[:, t0:t0 + n], ALU.mult,
                     r=[('po', ib), 'rs'], w=['t2'])
                S.op('dve', 'tensor_tensor', ost[b][0:32, 13, 0:n], t1[:, 0:n], t2[:, 0:n], ALU.add,
                     r=['t1', 't2'], w=[okey])
                for c in range(8):
                    i = proj(OFF_F + c * 128, 128, hT, n, b)
                    if c % 2 == 0:
                        S.op('act', 'activation', ost[b][:, 5 + c, 0:n], po[i][:, 0:n], AF.Copy, r=[('po', i)], w=[okey])
                    else:
                        S.op('dve', 'tensor_copy', ost[b][:, 5 + c, 0:n], po[i][:, 0:n], r=[('po', i)], w=[okey])
                S.dma('sp', self.pxT[:, :, t0:t0 + n].rearrange("c p t -> p c t"), ost[b][:, :, 0:n], r=[okey])

            NG = len(GROUPS)
            load(0); load(1)
            front(0, 'a'); front(0, 'b'); front(1, 'a')
            for gi in range(NG):
                if gi + 2 < NG:
                    load(gi + 2)
                back(gi)
                if gi + 1 < NG:
                    front(gi + 1, 'b')
                if gi + 2 < NG:
                    front(gi + 2, 'a')
            S.flush()

    def p2_attn(self, l):
        nc, S = self.nc, self.S
        with ExitStack() as es:
            sb = lambda n, shp, dt=F32: es.enter_context(nc.sbuf_tensor(self.uq(n), shp, dt))
            pst = lambda n, shp, dt=F32: es.enter_context(nc.psum_tensor(self.uq(n), shp, dt))
            cqn = sb('cqn', [128, 3, T], BF16); ckvn = sb('ckvn', [128, 2, T], BF16)
            S.dma('sp', cqn[:], self.pxT[0:3].rearrange("c p t -> p c t"), w=['cqn'])
            S.dma('sp', ckvn[:], self.pxT[3:5].rearrange("c p t -> p c t"), w=['ckvn'])
            wuq = sb('wuq', [128, 3, 768], BF16); wuqp = sb('wuqp', [128, 3, 768], BF16)
            wukv = sb('wukv', [128, 2, 1024], BF16)
            S.dma('pool', wuq[:], self.I('w_uq')[l].rearrange("(k p) n -> p k n", p=128), w=['wuq'])
            S.dma('pool', wuqp[:], self.I('w_uqp')[l].rearrange("(k p) n -> p k n", p=128), w=['wuqp'])
            S.dma('pool', wukv[:], self.I('w_ukvx')[l].rearrange("(k p) n -> p k n", p=128), w=['wukv'])
            onesf = sb('onesf', [128, 128]); S.dma('sp', onesf[:], self.I('ones_f'), w=['onesf'])
            tc_ = sb('tabc', [96, T]); ts_ = sb('tabs', [96, T])
            S.dma('sp', tc_[64:96, :], self.I('ropeC'), w=['tabc']); S.dma('sp', ts_[64:96, :], self.I('ropeS'), w=['tabs'])
            kt = [sb('kt%d' % i, [96, T], BF16) for i in range(2)]
            qt = [sb('qt%d' % i, [96, T], BF16) for i in range(2)]
            for b in range(2):
                S.dma('sp', kt[b][64:96, :], self.pxT[13, 0:32, :], w=[('ktr', b)])
            vh = [sb('vh%d' % i, [128, NT, 128], BF16) for i in range(2)]
            for i in range(2):
                S.op('pool', 'memset', vh[i][:], 1.0, w=[('vh', i)])
            t1 = sb('t1', [96, 512]); t2 = sb('t2', [96, 512])
            ptl = [sb('ptl%d' % i, [128, 2, 512], BF16) for i in range(3)]
            rr = sb('rr', [96, 512]); osb = [sb('osb%d' % i, [64, 512]) for i in range(2)]; ost = sb('ost', [64, T], BF16)
            sel = sb('sel', [128, 128], BF16); rb = sb('rb', [64, 512])
            S.op('pool', 'memset', sel[:], 0.0, w=['sel'])
            S.op('pool', 'memset', sel[64:65, :], 1.0, w=['sel'])
            rhi = [sb('rhi%d' % i, [128, 512], BF16) for i in range(2)]; rlo = [sb('rlo%d' % i, [128, 512], BF16) for i in range(2)]
            for i in range(2):
                S.op('pool', 'memset', rhi[i][:], 0.0, w=[('rhi', i)])
                S.op('pool', 'memset', rlo[i][:], 0.0, w=[('rlo', i)])
            ps = [pst('ps%d' % i, [128, 1024]) for i in range(2)]
            po = [pst('po%d' % i, [128, 512]) for i in range(2)]
            pq = pst('pq', [128, 512]); pq2 = pst('pq2', [128, 512]); pk = pq; pb = pq2
            LA = 2

            def project_chunks(h):
                b = h % 2
                chunks = []
                for gi, (t0, n, s) in enumerate(GROUPS):
                    def cA(gi=gi, t0=t0, n=n):
                        S.multi('pe', [('matmul', (pq[0:96, 0:n], wuq[:, c, h * 96:(h + 1) * 96], cqn[:, c, t0:t0 + n]),
                                        dict(start=(c == 0), stop=(c == 2))) for c in range(3)],
                                r=['cqn', 'wuq'], w=['pq'])
                        S.multi('pe', [('matmul', (pq2[0:96, 0:n], wuqp[:, c, h * 96:(h + 1) * 96], cqn[:, c, t0:t0 + n]),
                                        dict(start=(c == 0), stop=(c == 2))) for c in range(3)],
                                r=['cqn', 'wuqp'], w=['pq2'])
                        S.op('dve', 'tensor_copy', qt[b][0:64, t0:t0 + n], pq[0:64, 0:n], r=['pq'], w=[('qt', b, gi)])
                        S.op('dve', 'tensor_tensor', t1[64:96, 0:n], pq[64:96, 0:n], tc_[64:96, t0:t0 + n], ALU.mult,
                             r=['pq', 'tabc'], w=['t1'])
                        S.op('dve', 'tensor_tensor', t2[64:96, 0:n], pq2[64:96, 0:n], ts_[64:96, t0:t0 + n], ALU.mult,
                             r=['pq2', 'tabs'], w=['t2'])
                        S.op('dve', 'tensor_tensor', qt[b][64:96, t0:t0 + n], t1[64:96, 0:n], t2[64:96, 0:n], ALU.add,
                             r=['t1', 't2'], w=[('qt', b, gi)])

                    def cB(gi=gi, t0=t0, n=n):
                        S.multi('pe', [('matmul', (pk[:, 0:n], wukv[:, c, h * 64:h * 64 + 128], ckvn[:, c, t0:t0 + n]),
                                        dict(start=(c == 0), stop=(c == 1))) for c in range(2)],
                                r=['ckvn', 'wukv'], w=['pq'])
                        S.op('dve', 'tensor_copy', kt[b][0:64, t0:t0 + n], pk[0:64, 0:n], r=['pq'], w=[('kt', b, gi)])
                    chunks += [cA, cB]
                for i0 in range(0, NT, 8):
                    def cV(i0=i0):
                        nt_ = min(8, NT - i0)
                        calls = []
                        for ii_ in range(nt_):
                            i_ = i0 + ii_
                            calls += [('matmul', (pq2[:, ii_ * 64:(ii_ + 1) * 64], ckvn[:, c, i_ * 128:(i_ + 1) * 128],
                                                  wukv[:, c, 512 + h * 64:512 + (h + 1) * 64]), dict(start=(c == 0), stop=(c == 1)))
                                      for c in range(2)]
                        S.multi('pe', calls, r=['ckvn', 'wukv'], w=['pq2'])
                        S.op('dve', 'tensor_copy', vh[b][:, i0:i0 + nt_, 0:64],
                             pq2[:, 0:nt_ * 64].rearrange("p (a d) -> p a d", d=64), r=['pq2'], w=[('vh', b)])
                    chunks.append(cV)
                return chunks

            def project(h):
                for c_ in project_chunks(h):
                    c_()

            project(0)
            for h in range(H):
                b = h % 2
                seq = []
                for gi, (q0, nq, s) in enumerate(GROUPS):
                    tiles = list(range(NT)) if s == 0 else [32, 33]
                    npair = len(tiles) // 2
                    for ii in range(npair):
                        seq.append((gi, q0, nq, ii, tiles[2 * ii], npair))

                def qk(e):
                    gi, q0, nq, ii, i, np_ = seq[e]
                    sl = e % 2
                    S.multi('pe', [('matmul', (ps[sl][:, a_ * 512:a_ * 512 + nq], kt[b][0:96, (i + a_) * 128:(i + a_ + 1) * 128],
                                               qt[b][0:96, q0:q0 + nq]), dict(start=True, stop=True)) for a_ in range(2)],
                            r=[('kt', b, i // 4), ('kt', b, (i + 1) // 4), ('ktr', b), ('qt', b, gi)], w=[('ps', sl)])

                pending = []
                nxt = project_chunks(h + 1) if h + 1 < H else []
                cstep = max(1, (len(seq) - 12) // max(1, len(nxt)))
                for e in range(min(LA, len(seq))):
                    qk(e)
                for e in range(len(seq)):
                    gi, q0, nq, ii, i, np_ = seq[e]
                    sl = e % 2; sl2 = e % 3; ob = gi % 2
                    S.op('act', 'activation', ptl[sl2][:, :, 0:nq], ps[sl][:, :].rearrange("p (a n) -> p a n", a=2)[:, :, 0:nq],
                         AF.Exp, scale=SCALE, r=[('ps', sl)], w=[('ptl', sl2)])
                    if e + LA < len(seq):
                        qk(e + LA)
                    S.multi('pe', [('matmul', (po[ob][:, 0:nq], vh[b][:, i + a_, :], ptl[sl2][:, a_, 0:nq]),
                                    dict(start=(ii == 0 and a_ == 0), stop=(ii == np_ - 1 and a_ == 1))) for a_ in range(2)],
                            r=[('ptl', sl2), ('vh', b)], w=[('po', ob)])
                    if ii == np_ - 1:
                        fb = gi % 2
                        S.op('dve', 'tensor_copy', rhi[fb][64:65, 0:nq], po[ob][64:65, 0:nq], r=[('po', ob)], w=[('rhi', fb)])
                        S.op('dve', 'tensor_tensor', rlo[fb][64:65, 0:nq], po[ob][64:65, 0:nq], rhi[fb][64:65, 0:nq], ALU.subtract,
                             r=[('po', ob), ('rhi', fb)], w=[('rlo', fb)])
                        S.op('dve', 'tensor_copy', osb[fb][:, 0:nq], po[ob][0:64, 0:nq], r=[('po', ob)], w=[('osb', fb)])
                        pending.append((e + 3, fb, q0, nq))
                    while pending and (pending[0][0] <= e or e == len(seq) - 1):
                        _, fb, fq0, fnq = pending.pop(0)
                        S.multi('pe', [('matmul', (pb[:, 0:fnq], sel[:, :], rhi[fb][:, 0:fnq]), dict(start=True, stop=False)),
                                       ('matmul', (pb[:, 0:fnq], sel[:, :], rlo[fb][:, 0:fnq]), dict(start=False, stop=True))],
                                r=[('rhi', fb), ('rlo', fb), 'sel'], w=['pq2'])
                        S.op('dve', 'reciprocal', rb[:, 0:fnq], pb[0:64, 0:fnq], r=['pq2'], w=['rb'])
                        S.op('dve', 'tensor_tensor', ost[:, fq0:fq0 + fnq], osb[fb][:, 0:fnq], rb[:, 0:fnq], ALU.mult,
                             r=[('osb', fb), 'rb'], w=['ost'])
                    if nxt and e >= 4 and (e - 4) % cstep == 0:
                        nxt.pop(0)()
                while nxt:
                    nxt.pop(0)()
                S.dma('sp', self.mixT[h // 2, (h % 2) * 64:(h % 2) * 64 + 64, :], ost[:, :], r=['ost'])
            S.flush()

    def p5_out_mlp(self, l):
        nc, S = self.nc, self.S
        last = (l == DEPTH - 1)
        src = self.I('xin') if l == 0 else self.xres
        groups = GROUPS[:8] if last else GROUPS
        with ExitStack() as es:
            sb = lambda n, shp, dt=F32: es.enter_context(nc.sbuf_tensor(self.uq(n), shp, dt))
            pst = lambda n, shp, dt=F32: es.enter_context(nc.psum_tensor(self.uq(n), shp, dt))
            wout = sb('wout', [128, 8, D], BF16)
            S.dma('pool', wout[:], self.I('w_out')[l].rearrange("(k p) n -> p k n", p=128), w=['wout'])
            idb = sb('idb', [128, 128], BF16); S.dma('sp', idb[:], self.I('ident_bf'), w=['idb'])
            eps = sb('eps', [128, 1]); S.op('dve', 'memset', eps[:], EPS, w=['eps'])
            g1r = sb('g1r', [128, D])
            mixt = [sb('mixt%d' % i, [128, 8, 512], BF16) for i in range(2)]
            xt = [sb('xt%d' % i, [128, 4, D]) for i in range(2)]
            tmp = [sb('tmp%d' % i, [128, 512]) for i in range(2)]
            junk = sb('junk', [128, D], BF16)
            pt = [pst('pt%d' % i, [128, 512], BF16) for i in range(2)]
            tls = [dict(k='n%d' % i, ss=sb('ss%d' % i, [128, 4]), rstd=sb('rstd%d' % i, [128, 4]),
                        xn=sb('xn%d' % i, [128, 4, D], BF16), junk=junk, pt=pt,
                        hT=sb('hT%d' % i, [128, 8, 512], BF16), idb=idb, eps=eps) for i in range(2)]
            pso = [pst('pso%d' % i, [128, 512]) for i in range(4)]
            st = dict(cur_s=-1, np_=0)

            def stA(gi):
                t0, n, s = groups[gi]
                b = gi % 2; nj = n // 128
                if s != st['cur_s']:
                    S.dma('sp', g1r[:], self.mrow[s, 2 * D:3 * D].partition_broadcast(128), w=['g1r']); st['cur_s'] = s
                S.dma('sp', mixt[b][:, :, 0:n], self.mixT[:, :, t0:t0 + n].rearrange("c p t -> p c t"), w=[('mixt', b)])
                S.dma('sp', xt[b][:, 0:nj, :], src[t0:t0 + n, :].rearrange("(j p) d -> p j d", p=128), w=[('xt', b)])
                for j in range(nj):
                    for hh in range(2):
                        pi = st['np_'] % 4; st['np_'] += 1
                        S.multi('pe', [('matmul', (pso[pi][:, :], mixt[b][:, c, j * 128:(j + 1) * 128], wout[:, c, hh * 512:(hh + 1) * 512]),
                                        dict(start=(c == 0), stop=(c == 7))) for c in range(8)],
                                r=[('mixt', b), 'wout'], w=[('pso', pi)])
                        S.op('dve', 'tensor_tensor', tmp[pi % 2][:, :], pso[pi][:, :], g1r[:, hh * 512:(hh + 1) * 512], ALU.mult,
                             r=[('pso', pi), 'g1r'], w=[('tmp', pi % 2)])
                        S.op('pool', 'tensor_tensor', xt[b][:, j, hh * 512:(hh + 1) * 512], tmp[pi % 2][:, :],
                             xt[b][:, j, hh * 512:(hh + 1) * 512], ALU.add, r=[('tmp', pi % 2), ('xt', b)], w=[('xt', b)])
                S.dma('sp', self.xmid[t0:t0 + n, :].rearrange("(j p) d -> p j d", p=128), xt[b][:, 0:nj, :], r=[('xt', b)])

            def stB(gi):
                t0, n, s = groups[gi]
                b = gi % 2; nj = n // 128
                self.norm_tiles(xt[b], nj, s, 2, tls[b], ('xt', b), n)
                S.dma('sp', self.h2T[:, :, t0:t0 + n].rearrange("c p t -> p c t"), tls[b]['hT'][:, :, 0:n], r=['n%dhT' % b])

            stA(0)
            for gi in range(len(groups)):
                if gi + 1 < len(groups):
                    stA(gi + 1)
                stB(gi)
            S.flush()
        with ExitStack() as es:
            sb = lambda n, shp, dt=F32: es.enter_context(nc.sbuf_tensor(self.uq(n), shp, dt))
            pst = lambda n, shp, dt=F32: es.enter_context(nc.psum_tensor(self.uq(n), shp, dt))
            w1 = sb('w1', [128, 8, 4 * D], BF16); w2 = sb('w2', [128, 32, D], BF16)
            for hh in range(2):
                S.dma('pool', w1[:, :, hh * 2048:(hh + 1) * 2048],
                      self.I('w_mlp1')[l].rearrange("(k p) n -> p k n", p=128)[:, :, hh * 2048:(hh + 1) * 2048], w=[('w1', hh)])
            for q in range(4):
                S.dma('pool', w2[:, q * 8:(q + 1) * 8, :],
                      self.I('w_mlp2')[l].rearrange("(k p) n -> p k n", p=128)[:, q * 8:(q + 1) * 8, :], w=[('w2', q)])
            g2r = sb('g2r', [128, D])
            if last:
                fg = sb('fg', [128, D]); S.dma('sp', fg[:], self.I('fgrow'), w=['fg'])
                eps = sb('eps', [128, 1]); S.op('dve', 'memset', eps[:], EPS, w=['eps'])
                junk = sb('junk', [128, D], BF16); fss = sb('fss', [128, 4]); frs = sb('frs', [128, 4])
            hT = sb('hT', [128, 8, 512], BF16); fT = sb('fT', [128, 32, 512], BF16)
            x1 = sb('x1', [128, 4, D]); sq = [sb('sq%d' % i, [128, 512]) for i in range(2)]
            tmp = [sb('tmp%d' % i, [128, 512]) for i in range(2)]
            psf = [pst('psf%d' % i, [128, 512]) for i in range(3)]
            ps2 = [pst('ps2%d' % i, [128, 512]) for i in range(3)]
            cur_s = -1; n2 = 0
            t0_, n_, s_ = groups[0]
            S.dma('sp', hT[:, :, 0:n_], self.h2T[:, :, t0_:t0_ + n_].rearrange("c p t -> p c t"), w=['hT'])
            for gi, (t0, n, s) in enumerate(groups):
                nj = n // 128
                if s != cur_s:
                    S.dma('sp', g2r[:], self.mrow[s, 5 * D:6 * D].partition_broadcast(128), w=['g2r']); cur_s = s
                S.dma('sp', x1[:, 0:nj, :], self.xmid[t0:t0 + n, :].rearrange("(j p) d -> p j d", p=128),
                      w=[('x1', j) for j in range(nj)])
                for j in range(32):
                    pf = psf[j % 3]
                    S.multi('pe', [('matmul', (pf[:, 0:n], w1[:, k, j * 128:(j + 1) * 128], hT[:, k, 0:n]),
                                    dict(start=(k == 0), stop=(k == 7))) for k in range(8)],
                            r=[('w1', j // 16), 'hT'], w=[('psf', j % 3)])
                    S.op('act', 'activation', sq[j % 2][:, 0:n], pf[:, 0:n], AF.Square, r=[('psf', j % 3)], w=[('sq', j % 2)])
                    S.op('dve', 'scalar_tensor_tensor', fT[:, j, 0:n], pf[:, 0:n], 0.0, sq[j % 2][:, 0:n],
                         ALU.is_gt, ALU.mult, r=[('psf', j % 3), ('sq', j % 2)], w=[('fT', j)])
                if gi + 1 < len(groups):
                    t0n, nn, sn_ = groups[gi + 1]
                    S.dma('sp', hT[:, :, 0:nn], self.h2T[:, :, t0n:t0n + nn].rearrange("c p t -> p c t"), w=['hT'])
                for j in range(nj):
                    for hh in range(2):
                        pi = n2 % 3; n2 += 1
                        S.multi('pe', [('matmul', (ps2[pi][:, :], fT[:, q, j * 128:(j + 1) * 128], w2[:, q, hh * 512:(hh + 1) * 512]),
                                        dict(start=(q == 0), stop=(q == 31))) for q in range(32)],
                                r=[('fT', q) for q in range(32)] + [('w2', q) for q in range(4)], w=[('ps2', pi)])
                        S.op('dve', 'tensor_tensor', tmp[pi % 2][:, :], ps2[pi][:, :], g2r[:, hh * 512:(hh + 1) * 512], ALU.mult,
                             r=[('ps2', pi), 'g2r'], w=[('tmp', pi % 2)])
                        S.op('pool', 'tensor_tensor', x1[:, j, hh * 512:(hh + 1) * 512], tmp[pi % 2][:, :],
                             x1[:, j, hh * 512:(hh + 1) * 512], ALU.add, r=[('tmp', pi % 2), ('x1', j)], w=[('x1', j)])
                    tt = t0 + j * 128
                    if not last:
                        S.dma('sp', self.xres[tt:tt + 128, :], x1[:, j, :], r=[('x1', j)])
                    else:
                        S.op('act', 'activation', junk[:], x1[:, j, :], AF.Square, accum_out=fss[:, j:j + 1],
                             r=[('x1', j)], w=['junk', ('fss', j)])
                        S.op('act', 'activation', frs[:, j:j + 1], fss[:, j:j + 1], AF.Sqrt, bias=eps[:, 0:1], scale=1.0 / D,
                             r=[('fss', j), 'eps'], w=[('frs', j)])
                        S.op('dve', 'reciprocal', frs[:, j:j + 1], frs[:, j:j + 1], r=[('frs', j)], w=[('frs', j)])
                        S.op('dve', 'scalar_tensor_tensor', x1[:, j, :], x1[:, j, :], frs[:, j:j + 1], fg[:], ALU.mult, ALU.mult,
                             r=[('x1', j), ('frs', j), 'fg'], w=[('x1', j)])
                        S.dma('sp', self.y[tt:tt + 128, :], x1[:, j, :], r=[('x1', j)])
            S.flush()

    def p3_fnet(self, l):
        for nm, t0 in (('x', 0), ('c', L)):
            if nm == 'c' and l == DEPTH - 1:
                continue
            self._fnet(l, nm, t0)

    def _fnet(self, l, nm, t0):
        nc, S = self.nc, self.S
        L1 = self.fn[nm]['L1']; Ls = self.fn[nm]['Ls']; L2 = 64
        with ExitStack() as es:
            sb = lambda n, shp, dt=F32: es.enter_context(nc.sbuf_tensor(self.uq(n), shp, dt))
            pst = lambda n, shp, dt=F32: es.enter_context(nc.psum_tensor(self.uq(n), shp, dt))
            uf = sb('uf', [128, 2, Ls], BF16)
            S.dma('sp', uf[:], self.pxT[5:7, :, t0:t0 + Ls].rearrange("c p t -> p c t"), w=['uf'])
            csw = sb('csw', [128, 256], BF16); S.dma('sp', csw[:], self.I('fn_csw'), w=['csw'])
            w1 = sb('w1', [L1, 2 * L1], BF16); w2 = sb('w2', [L1, 2 * L1], BF16)
            S.dma('sp', w1[:], self.I('fn_w1' + nm), w=['w1']); S.dma('sp', w2[:], self.I('fn_w2' + nm), w=['w2'])
            tr = sb('tr', [64, Ls], BF16); sn = sb('sn', [64, Ls], BF16)
            S.dma('sp', tr[:], self.I('fn_tr' + nm), w=['tr']); S.dma('sp', sn[:], self.I('fn_sn' + nm), w=['sn'])
            U = sb('U', [L1, L2, 512], BF16)
            Y = sb('Y', [64, 2, 256, L1], BF16)
            pu = [pst('pu%d' % i, [128, 512]) for i in range(2)]
            py = [pst('py%d' % i, [128, 512]) for i in range(2)]
            pf = [pst('pf%d' % i, [128, 512]) for i in range(2)]
            for l2 in range(L2):
                b = l2 % 2
                S.multi('pe', [('matmul', (pu[b][0:L1, c * 256:(c + 1) * 256], uf[:, c, l2::L2], csw[:, :]),
                                dict(start=True, stop=True)) for c in range(2)], r=['uf', 'csw'], w=[('pu', b)])
                if b == 0:
                    S.op('act', 'activation', U[:, l2, :], pu[b][0:L1, :], AF.Copy, r=[('pu', b)], w=['U'])
                else:
                    S.op('dve', 'tensor_copy', U[:, l2, :], pu[b][0:L1, :], r=[('pu', b)], w=['U'])
            cpb = 512 // (2 * L1)
            nb = 0
            for ch0 in range(0, 256, cpb):
                b = nb % 2; nb += 1
                calls = []
                for chl in range(cpb):
                    ch = ch0 + chl; c = ch // 128; i = ch % 128
                    o = py[b][0:64, chl * 2 * L1:(chl + 1) * 2 * L1]
                    calls.append(('matmul', (o, U[:, :, c * 256 + i], w1[:, :]), dict(start=True, stop=False)))
                    calls.append(('matmul', (o, U[:, :, c * 256 + 128 + i], w2[:, :]), dict(start=False, stop=True)))
                S.multi('pe', calls, r=['U', 'w1', 'w2'], w=[('py', b)])
                for r_ in range(2):
                    src = py[b][0:64, 0:cpb * 2 * L1].rearrange("p (c r a) -> p c r a", c=cpb, r=2)[:, :, r_, :]
                    dst = Y[:, r_, ch0:ch0 + cpb, :]
                    if r_ == 0:
                        S.op('act', 'activation', dst, src, AF.Copy, r=[('py', b)], w=['Y'])
                    else:
                        S.op('dve', 'tensor_copy', dst, src, r=[('py', b)], w=['Y'])
            na = min(8, L1)
            nb = 0
            for cc in range(2):
                for a0 in range(0, L1, na):
                    b = nb % 2; nb += 1
                    calls = []
                    for al in range(na):
                        a_ = a0 + al
                        o = pf[b][:, al * 64:(al + 1) * 64]
                        calls.append(('matmul', (o, Y[:, 0, cc * 128:(cc + 1) * 128, a_], tr[:, a_::L1]), dict(start=True, stop=False)))
                        calls.append(('matmul', (o, Y[:, 1, cc * 128:(cc + 1) * 128, a_], sn[:, a_::L1]), dict(start=False, stop=True)))
                    S.multi('pe', calls, r=['Y', 'tr', 'sn'], w=[('pf', b)])
                    dst = uf[:, cc, :].rearrange("p (b a) -> p a b", a=L1)[:, a0:a0 + na, :]
                    src = pf[b][:, 0:na * 64].rearrange("p (a b) -> p a b", a=na)
                    if b == 0:
                        S.op('act', 'activation', dst, src, AF.Copy, r=[('pf', b), 'U'], w=['uf'])
                    else:
                        S.op('dve', 'tensor_copy', dst, src, r=[('pf', b), 'U'], w=['uf'])
            S.dma('sp', self.mixT[4:6, :, t0:t0 + Ls].rearrange("c p t -> p c t"), uf[:], r=['uf'])
            S.flush()

    def _hy_f1(self, src, krows, N1, w1h, hyY, es_sb, pst):
        S = self.S
        cpb = 512 // (2 * N1)
        py = [pst('hpy%d' % i, [128, 512]) for i in range(2)]
        nbank = 256 // cpb
        GB = min(4, nbank)
        stg = [es_sb('hstg%d' % i, [128, GB, 512], BF16) for i in range(2)]
        for nb in range(nbank):
            ch0 = nb * cpb
            b = nb % 2; sb_ = (nb // GB) % 2; g = nb % GB
            S.multi('pe', [('matmul', (py[b][:, chl * 2 * N1:(chl + 1) * 2 * N1], src[0:krows, :, ch0 + chl], w1h[0:krows, :]),
                            dict(start=True, stop=True)) for chl in range(cpb)], r=['hsrc', 'w1h'], w=[('hpy', b)])
            if b == 0:
                S.op('act', 'activation', stg[sb_][:, g, :], py[b][:, :], AF.Copy, r=[('hpy', b)], w=[('hstg', sb_)])
            else:
                S.op('dve', 'tensor_copy', stg[sb_][:, g, :], py[b][:, :], r=[('hpy', b)], w=[('hstg', sb_)])
            if g == GB - 1:
                c0 = (nb - GB + 1) * cpb
                S.dma('sp', hyY[:, c0:c0 + GB * cpb, :].rearrange("p c x -> p (c x)"),
                      stg[sb_][:, :, :].rearrange("p g x -> p (g x)"), r=[('hstg', sb_)])

    def _hy_f2(self, Yz, ncol, N1, tr, ti, nti, pz, f1):
        S = self.S
        calls = [('matmul', (pz[:, 0:ncol], tr[:, f1::N1], Yz[:, :, f1]), dict(start=True, stop=False)),
                 ('matmul', (pz[:, 0:ncol], nti[:, f1::N1], Yz[:, :, N1 + f1]), dict(start=False, stop=True)),
                 ('matmul', (pz[:, ncol:2 * ncol], ti[:, f1::N1], Yz[:, :, f1]), dict(start=True, stop=False)),
                 ('matmul', (pz[:, ncol:2 * ncol], tr[:, f1::N1], Yz[:, :, N1 + f1]), dict(start=False, stop=True))]
        return calls

    def _range_sin(self, dst, ps, fcol, fbcol, tl, n):
        S = self.S
        a, t, k = tl['a'], tl['t'], tl['k']
        S.op('dve', 'tensor_scalar', a[:, 0:n], ps, fcol, fbcol, ALU.mult, ALU.add, r=[tl['ps'], 'fcols'], w=['rs_a'])
        S.op('dve', 'tensor_scalar', t[:, 0:n], a[:, 0:n], 1.0 / (2 * math.pi), 12582912.0, ALU.mult, ALU.add, r=['rs_a'], w=['rs_t'])
        S.op('dve', 'tensor_scalar', k[:, 0:n], t[:, 0:n], -12582912.0, None, ALU.add, r=['rs_t'], w=['rs_k'])
        S.op('dve', 'scalar_tensor_tensor', a[:, 0:n], k[:, 0:n], -2 * math.pi, a[:, 0:n], ALU.mult, ALU.add, r=['rs_k', 'rs_a'], w=['rs_a'])
        S.op('dve', 'tensor_scalar', a[:, 0:n], a[:, 0:n], -3.14159, 3.14159, ALU.max, ALU.min, r=['rs_a'], w=['rs_a'])
        S.op('act', 'activation', dst, a[:, 0:n], AF.Sin, r=['rs_a'], w=[tl['dst']])

    def pk_filters(self, l):
        for nm in ('x', 'c'):
            if nm == 'c' and l == DEPTH - 1:
                continue
            self._filters(l, nm)

    def _filters(self, l, nm):
        nc, S = self.nc, self.S
        hy = self.hy[nm]; N1 = hy['N1']; Ls = hy['Ls']; N = hy['N']
        hyY = self.hyY[nm]
        with ExitStack() as es:
            sb = lambda n, shp, dt=F32: es.enter_context(nc.sbuf_tensor(self.uq(n), shp, dt))
            pst = lambda n, shp, dt=F32: es.enter_context(nc.psum_tensor(self.uq(n), shp, dt))
            ks = sb('ks', [N1, 128, 256], BF16)
            w1h = sb('w1h', [N1, 2 * N1], BF16); S.dma('sp', w1h[:], self.I('hy_w1' + nm), w=['w1h'])
            with ExitStack() as es2:
                sb2 = lambda n, shp, dt=F32: es2.enter_context(nc.sbuf_tensor(self.uq(n), shp, dt))
                ps2 = lambda n, shp, dt=F32: es2.enter_context(nc.psum_tensor(self.uq(n), shp, dt))
                wa = sb2('wa', [33, 64]); wb = sb2('wb', [64, 64]); wc = sb2('wc', [64, 512], BF16)
                S.dma('sp', wa[:], self.I('hy_w1')[l], w=['wa']); S.dma('sp', wb[:], self.I('hy_w2')[l], w=['wb'])
                S.dma('pool', wc[:], self.I('hy_w3')[l], w=['wc'])
                fc = sb2('fc', [64, 4])
                S.op('dve', 'tensor_copy', fc[:, 0:1], self.cols[0:64, C_HFR:C_HFR + 1], r=['cols'], w=['fcols'])
                S.op('dve', 'tensor_tensor', fc[:, 1:2], self.cols[0:64, C_HFR:C_HFR + 1], self.cols[0:64, C_HB1:C_HB1 + 1], ALU.mult, r=['cols'], w=['fcols'])
                S.op('dve', 'tensor_tensor', fc[:, 2:3], self.cols[0:64, C_HFR:C_HFR + 1], self.cols[0:64, C_HB2:C_HB2 + 1], ALU.mult, r=['cols'], w=['fcols'])
                zt = [sb2('zt%d' % i, [33, 512]) for i in range(2)]
                h1 = sb2('h1', [64, 512])
                hA = sb2('hA', [64, N], BF16); hB = sb2('hB', [64, N], BF16)
                tl = dict(a=sb2('rsa', [64, 512]), t=sb2('rst', [64, 512]), k=sb2('rsk', [64, 512]))
                tcol = sb2('tcol', [N1, 128]); S.dma('sp', tcol[:], self.I('hy_tcol' + nm), w=['tcol'])
                drow = sb2('drow', [128, 256]); S.dma('sp', drow[:], self.I('hy_drow' + nm), w=['drow'])
                dec = [sb2('dec%d' % i, [N1, 256]) for i in range(2)]
                kabs = [sb2('kabs%d' % i, [N1, 32 * 256], BF16) for i in range(2)]
                onb = sb2('onb', [128, 1], BF16); S.op('dve', 'memset', onb[:], 1.0, w=['onb'])
                sres = sb2('sres', [128, 2])
                pa = ps2('pa', [64, 512]); pb_ = ps2('pb', [64, 512])
                pk = [ps2('pk%d' % i, [128, 512]) for i in range(2)]
                pSs = [ps2('pS%d' % i, [128, 2]) for i in range(2)]
                nblk = N // 512
                for blk in range(nblk):
                    b = blk % 2
                    S.dma('sp', zt[b][:], self.I('hy_zT' + nm)[:, blk * 512:(blk + 1) * 512], w=[('zt', b)])
                    S.op('pe', 'matmul', pa[:, :], wa[:, :], zt[b][:, :], start=True, stop=True, r=['wa', ('zt', b)], w=['pa'])
                    tl.update(ps='pa', dst='h1')
                    self._range_sin(h1[:, :], pa[:, :], fc[:, 0:1], fc[:, 1:2], tl, 512)
                    S.op('pe', 'matmul', pb_[:, :], wb[:, :], h1[:, :], start=True, stop=True, r=['wb', 'h1'], w=['pb'])
                    tl.update(ps='pb', dst='hA')
                    self._range_sin(hA[:, blk * 512:(blk + 1) * 512], pb_[:, :], fc[:, 0:1], fc[:, 2:3], tl, 512)
                S.op('pool', 'tensor_copy', hB[:, Ls:N], hA[:, Ls:N], r=['hA'], w=['hB'])
                S.op('pool', 'memset', hB[:, 0:Ls], 0.0, w=['hB'])
                S.op('pool', 'memset', hA[:, Ls:N], 0.0, r=['hB'], w=['hA'])
                for s2 in range(128):
                    b = s2 % 2
                    S.multi('pe', [('matmul', (pk[b][0:N1, 0:256], hA[:, s2::128], wc[:, 0:256]), dict(start=True, stop=False)),
                                   ('matmul', (pk[b][0:N1, 0:256], hB[:, s2::128], wc[:, 256:512]), dict(start=False, stop=True))],
                            r=['hA', 'hB', 'wc'], w=[('pk', b)])
                    S.op('act', 'activation', dec[b][:, :], drow[0:N1, :], AF.Exp, scale=tcol[:, s2:s2 + 1],
                         r=['drow', 'tcol'], w=[('dec', b)])
                    S.op('dve', 'tensor_tensor', ks[:, s2, :], pk[b][0:N1, 0:256], dec[b][:, :], ALU.mult,
                         r=[('pk', b), ('dec', b)], w=['hsrc'])
                for q in range(4):
                    S.op('act', 'activation', kabs[q % 2][:, :], ks[:, q * 32:(q + 1) * 32, :].rearrange("p s c -> p (s c)"), AF.Abs,
                         r=['hsrc'], w=[('kabs', q % 2)])
                    for cc in range(2):
                        S.multi('pe', [('matmul', (pSs[cc][:, 0:1], kabs[q % 2][:, sl_ * 256 + cc * 128:sl_ * 256 + (cc + 1) * 128], onb[0:N1, 0:1]),
                                        dict(start=(q == 0 and sl_ == 0), stop=(q == 3 and sl_ == 31))) for sl_ in range(32)],
                                r=[('kabs', q % 2), 'onb'], w=[('pS', cc)])
                for cc in range(2):
                    S.op('dve', 'tensor_copy', sres[:, cc:cc + 1], pSs[cc][:, 0:1], r=[('pS', cc)], w=['sres'])
                S.dma('sp', hy['ssum'], sres[:, :], r=['sres'])
                self._hy_f1(ks, N1, N1, w1h, hyY, sb2, ps2)
                S.flush()
        with ExitStack() as es:
            sb = lambda n, shp, dt=F32: es.enter_context(nc.sbuf_tensor(self.uq(n), shp, dt))
            pst = lambda n, shp, dt=F32: es.enter_context(nc.psum_tensor(self.uq(n), shp, dt))
            Yz = sb('Yz', [128, 256, 2 * N1], BF16); S.dma('sp', Yz[:], hyY, w=['Yz'])
            tr = sb('tr', [128, N], BF16); ti = sb('ti', [128, N], BF16); nti = sb('nti', [128, N], BF16)
            S.dma('sp', tr[:], self.I('hy_tr' + nm), w=['tr']); S.dma('sp', ti[:], self.I('hy_ti' + nm), w=['ti'])
            S.dma('sp', nti[:], self.I('hy_nti' + nm), w=['nti'])
            FB = min(8, N1)
            kst = [sb('kst%d' % i, [128, FB, 512], BF16) for i in range(2)]
            pz = [pst('pz%d' % i, [128, 512]) for i in range(2)]
            for f1 in range(N1):
                b = f1 % 2; kb = (f1 // FB) % 2
                S.multi('pe', self._hy_f2(Yz, 256, N1, tr, ti, nti, pz[b], f1), r=['Yz', 'tr', 'ti', 'nti'], w=[('pz', b)])
                if b == 0:
                    S.op('act', 'activation', kst[kb][:, f1 % FB, :], pz[b][:, :], AF.Copy, r=[('pz', b)], w=[('kst', kb)])
                else:
                    S.op('dve', 'tensor_copy', kst[kb][:, f1 % FB, :], pz[b][:, :], r=[('pz', b)], w=[('kst', kb)])
                if f1 % FB == FB - 1:
                    f0 = f1 - FB + 1
                    S.dma('sp', hy['kf'][:, f0:f0 + FB, :, :].rearrange("p f r c -> p (f r c)"),
                          kst[kb][:, :, :].rearrange("p f x -> p (f x)"), r=[('kst', kb)])
            S.flush()

    def p4_hyena(self, l):
        for nm, t0 in (('x', 0), ('c', L)):
            if nm == 'c' and l == DEPTH - 1:
                continue
            self._hyena(l, nm, t0)

    def _hyena(self, l, nm, t0):
        nc, S = self.nc, self.S
        hy = self.hy[nm]; N1 = hy['N1']; Ls = hy['Ls']; N = hy['N']; K1 = N1 // 2
        hyY = self.hyY[nm]; hyF = self.hyF[nm]
        nblk = max(1, Ls // 512); bw = Ls // nblk
        with ExitStack() as es:
            sb = lambda n, shp, dt=F32: es.enter_context(nc.sbuf_tensor(self.uq(n), shp, dt))
            pst = lambda n, shp, dt=F32: es.enter_context(nc.psum_tensor(self.uq(n), shp, dt))
            uh = sb('uh', [128, 6, Ls], BF16)
            S.dma('sp', uh[:], self.pxT[7:13, :, t0:t0 + Ls].rearrange("c p t -> p c t"), w=['uh'])
            idb = sb('idb', [128, 128], BF16); S.dma('sp', idb[:], self.I('ident_bf'), w=['idb'])
            w1h = sb('w1h', [N1, 2 * N1], BF16); S.dma('sp', w1h[:], self.I('hy_w1' + nm), w=['w1h'])
            zT = sb('zT', [128, 2, Ls], BF16); x0 = sb('x0', [128, 2, Ls], BF16)
            o = [sb('o%d' % c, [128, bw]) for c in range(6)]
            zs = sb('zs', [K1, 128, 256], BF16)
            cw = lambda tap, c: self.cols[:, C_HCW + tap * 6 + c:C_HCW + tap * 6 + c + 1]
            for blk in range(nblk):
                c0 = blk * bw
                for c in range(6):
                    S.op('act', 'activation', o[c][:, :], uh[:, c, c0:c0 + bw], AF.Identity,
                         bias=self.cols[:, C_HCB + c:C_HCB + c + 1], scale=cw(1, c), r=['uh', 'cols'], w=[('o', c)])
                    lo = 1 if blk == 0 else 0
                    S.op('dve', 'scalar_tensor_tensor', o[c][:, lo:bw], uh[:, c, c0 + lo - 1:c0 + bw - 1], cw(0, c), o[c][:, lo:bw],
                         ALU.mult, ALU.add, r=['uh', 'cols', ('o', c)], w=[('o', c)])
                    hi = bw - 1 if blk == nblk - 1 else bw
                    S.op('dve', 'scalar_tensor_tensor', o[c][:, 0:hi], uh[:, c, c0 + 1:c0 + hi + 1], cw(2, c), o[c][:, 0:hi],
                         ALU.mult, ALU.add, r=['uh', 'cols', ('o', c)], w=[('o', c)])
                for cc in range(2):
                    S.op('pool', 'tensor_copy', x0[:, cc, c0:c0 + bw], o[cc][:, :], r=[('o', cc)], w=['x0'])
                    S.op('pool', 'tensor_tensor', zT[:, cc, c0:c0 + bw], o[2 + cc][:, :], o[4 + cc][:, :], ALU.mult,
                         r=[('o', 2 + cc), ('o', 4 + cc)], w=['zT'])
            S.dma('sp', self.hx0[:, :, t0:t0 + Ls].rearrange("c p t -> p c t"), x0[:], r=['x0'])
            S.dma('sp', self.hz[:, :, t0:t0 + Ls].rearrange("c p t -> p c t"), zT[:], r=['zT'])
            ptr = [pst('ptr%d' % i, [128, 1024], BF16) for i in range(2)]
            nb = 0
            for s20 in range(0, 128, 4):
                b = nb % 2; nb += 1
                calls = []
                for sl in range(4):
                    for cc in range(2):
                        calls.append(('transpose', (ptr[b][0:K1, (sl * 2 + cc) * 128:(sl * 2 + cc + 1) * 128],
                                                    zT[:, cc, s20 + sl::128], idb[:, :]), {}))
                S.multi('pe', calls, r=['zT', 'idb'], w=[('ptr', b)])
                dst = zs[:, s20:s20 + 4, :].rearrange("p a c -> p (a c)")
                if b == 0:
                    S.op('act', 'activation', dst, ptr[b][0:K1, :], AF.Copy, r=[('ptr', b)], w=['hsrc'])
                else:
                    S.op('dve', 'tensor_copy', dst, ptr[b][0:K1, :], r=[('ptr', b)], w=['hsrc'])
            self._hy_f1(zs, K1, N1, w1h, hyY, sb, pst)
            S.flush()
        with ExitStack() as es:
            sb = lambda n, shp, dt=F32: es.enter_context(nc.sbuf_tensor(self.uq(n), shp, dt))
            pst = lambda n, shp, dt=F32: es.enter_context(nc.psum_tensor(self.uq(n), shp, dt))
            tr = sb('tr', [128, N], BF16); ti = sb('ti', [128, N], BF16); nti = sb('nti', [128, N], BF16)
            Yz = sb('Yz', [128, 128, 2 * N1], BF16); S.dma('sp', Yz[:], hyY[:, 0:128, :], w=['Yz'])
            S.dma('sp', tr[:], self.I('hy_tr' + nm), w=['tr']); S.dma('sp', ti[:], self.I('hy_ti' + nm), w=['ti'])
            S.dma('sp', nti[:], self.I('hy_nti' + nm), w=['nti'])
            kf = sb('kf', [128, N1, 2, 256], BF16); S.dma('sp', kf[:], hy['kf'], w=['kf'])
            Yf = sb('Yf', [128, N1, 2, 128], BF16)
            FB = min(4, N1)
            ta = [sb('ta%d' % i, [128, FB, 2, 128]) for i in range(2)]; tb = [sb('tb%d' % i, [128, FB, 2, 128]) for i in range(2)]
            pz = [pst('pz%d' % i, [128, FB * 256]) for i in range(2)]
            for cc in range(2):
                if cc == 1:
                    S.dma('sp', Yz[:], hyY[:, 128:256, :], w=['Yz'])
                ccs = slice(cc * 128, (cc + 1) * 128)
                for st_ in range(N1 // FB):
                    f0 = st_ * FB; b = st_ % 2
                    calls = []
                    for fl in range(FB):
                        calls += self._hy_f2(Yz, 128, N1, tr, ti, nti, pz[b][:, fl * 256:(fl + 1) * 256], f0 + fl)
                    S.multi('pe', calls, r=['Yz', 'tr', 'ti', 'nti'], w=[('pz', b)])
                    pzv = pz[b][:, :].rearrange("p (f r c) -> p f r c", f=FB, r=2)
                    S.op('dve', 'tensor_tensor', ta[b][:, :, :, :], pzv, kf[:, f0:f0 + FB, :, ccs], ALU.mult, r=[('pz', b), 'kf'], w=[('ta', b)])
                    S.op('dve', 'tensor_tensor', tb[b][:, :, 0, :], pzv[:, :, 0, :], kf[:, f0:f0 + FB, 1, ccs], ALU.mult, r=[('pz', b), 'kf'], w=[('tb', b)])
                    S.op('dve', 'tensor_tensor', tb[b][:, :, 1, :], pzv[:, :, 1, :], kf[:, f0:f0 + FB, 0, ccs], ALU.mult, r=[('pz', b), 'kf'], w=[('tb', b)])
                    S.op('pool', 'tensor_tensor', Yf[:, f0:f0 + FB, 0, :], ta[b][:, :, 0, :], ta[b][:, :, 1, :], ALU.subtract, r=[('ta', b)], w=['Yf'])
                    S.op('pool', 'tensor_tensor', Yf[:, f0:f0 + FB, 1, :], tb[b][:, :, 0, :], tb[b][:, :, 1, :], ALU.add, r=[('tb', b)], w=['Yf'])
                S.dma('sp', hyF[cc], Yf[:], r=['Yf'])
            S.flush()
        for cc in range(2):
            with ExitStack() as es:
                sb = lambda n, shp, dt=F32: es.enter_context(nc.sbuf_tensor(self.uq(n), shp, dt))
                pst = lambda n, shp, dt=F32: es.enter_context(nc.psum_tensor(self.uq(n), shp, dt))
                Yf = sb('Yf', [128, N1, 2, 128], BF16); S.dma('sp', Yf[:], hyF[cc], w=['Yf'])
                i1a = sb('i1a', [128, 256], BF16); i1b = sb('i1b', [128, 256], BF16)
                S.dma('sp', i1a[:], self.I('hy_i1a' + nm), w=['i1a']); S.dma('sp', i1b[:], self.I('hy_i1b' + nm), w=['i1b'])
                er = sb('er', [N1, Ls], BF16); nei = sb('nei', [N1, Ls], BF16)
                S.dma('sp', er[:], self.I('hy_er' + nm), w=['er']); S.dma('sp', nei[:], self.I('hy_nei' + nm), w=['nei'])
                V = sb('V', [N1, 128, 2, 128], BF16)
                yT = sb('yT', [128, Ls]); x0h = sb('x0h', [128, Ls], BF16); zh = sb('zh', [128, Ls], BF16)
                S.dma('sp', x0h[:], self.hx0[cc, :, t0:t0 + Ls], w=['x0h']); S.dma('sp', zh[:], self.hz[cc, :, t0:t0 + Ls], w=['zh'])
                ssb = sb('ssb', [128, 2]); S.dma('sp', ssb[:], hy['ssum'], w=['ssb'])
                S.op('dve', 'reciprocal', ssb[:, :], ssb[:, :], r=['ssb'], w=['ssb'])
                res = sb('res', [128, Ls], BF16)
                pv = [pst('pv%d' % i, [128, 512]) for i in range(2)]
                py = [pst('py%d' % i, [128, 512]) for i in range(2)]
                nb = 0
                for ch0 in range(0, 128, 2):
                    b = nb % 2; nb += 1
                    calls = []
                    for chl in range(2):
                        o_ = pv[b][0:N1, chl * 256:(chl + 1) * 256]
                        calls.append(('matmul', (o_, Yf[:, :, 0, ch0 + chl], i1a[:, :]), dict(start=True, stop=False)))
                        calls.append(('matmul', (o_, Yf[:, :, 1, ch0 + chl], i1b[:, :]), dict(start=False, stop=True)))
                    S.multi('pe', calls, r=['Yf', 'i1a', 'i1b'], w=[('pv', b)])
                    dst = V[:, ch0:ch0 + 2, :, :].rearrange("p c r a -> p (c r a)")
                    if b == 0:
                        S.op('act', 'activation', dst, pv[b][0:N1, :], AF.Copy, r=[('pv', b)], w=['V'])
                    else:
                        S.op('dve', 'tensor_copy', dst, pv[b][0:N1, :], r=[('pv', b)], w=['V'])
                nta = min(128, 512 // K1)
                nb = 0
                for ta0 in range(0, 128, nta):
                    b = nb % 2; nb += 1
                    calls = []
                    for al in range(nta):
                        ta_ = ta0 + al
                        o_ = py[b][:, al * K1:(al + 1) * K1]
                        calls.append(('matmul', (o_, V[:, :, 0, ta_], er[:, ta_::128]), dict(start=True, stop=False)))
                        calls.append(('matmul', (o_, V[:, :, 1, ta_], nei[:, ta_::128]), dict(start=False, stop=True)))
                    S.multi('pe', calls, r=['V', 'er', 'nei'], w=[('py', b)])
                    dst = yT[:, :].rearrange("p (b a) -> p a b", a=128)[:, ta0:ta0 + nta, :]
                    src = py[b][:, 0:nta * K1].rearrange("p (a b) -> p a b", b=K1)
                    if b == 0:
                        S.op('act', 'activation', dst, src, AF.Copy, r=[('py', b)], w=['yT'])
                    else:
                        S.op('dve', 'tensor_copy', dst, src, r=[('py', b)], w=['yT'])
                S.op('act', 'activation', yT[:, :], yT[:, :], AF.Identity, scale=ssb[:, cc:cc + 1], r=['yT', 'ssb'], w=['yT'])
                S.op('dve', 'scalar_tensor_tensor', yT[:, :], zh[:, :], self.cols[:, C_HD + cc:C_HD + cc + 1], yT[:, :],
                     ALU.mult, ALU.add, r=['zh', 'yT', 'cols'], w=['yT'])
                S.op('dve', 'tensor_tensor', res[:, :], yT[:, :], x0h[:, :], ALU.mult, r=['yT', 'x0h'], w=['res'])
                S.dma('sp', self.mixT[6 + cc, :, t0:t0 + Ls], res[:, :], r=['res'])
                S.flush()


def build_program(debug=False, stop_after=None):
    P = Prog(debug=debug)
    steps = []
    for l in range(DEPTH):
        steps += [('p0', l), ('p1', l), ('pk', l), ('p2', l), ('p3', l), ('p4', l), ('p5', l)]
    for (nm, l) in steps:
        fn = getattr(P, {'p0': 'p0_mod', 'p1': 'p1_inproj', 'pk': 'pk_filters', 'p2': 'p2_attn',
                         'p3': 'p3_fnet', 'p4': 'p4_hyena', 'p5': 'p5_out_mlp'}[nm])
        fn(l)
        if stop_after is not None and (nm, l) == stop_after:
            break
    P.es.close()
    P.nc.used_inputs = list(P._in.keys())
    return P.nc


_PROG = None


def kernel(**inputs):
    global _PROG
    per = _prep(inputs)
    if _PROG is None:
        _PROG = build_program()
    per = [{k: d[k] for k in _PROG.used_inputs} for d in per]
    res = run_bass_kernel_spmd(_PROG, per, core_ids=list(range(8)))
    return np.stack([np.asarray(r['y'], np.float32) for r in res.results], axis=0)
```

```python
import math
import numpy as np
import ml_dtypes
from contextlib import ExitStack
import concourse.bass as bass
import concourse.mybir as mybir
from concourse.bass_utils import run_bass_kernel_spmd

F32 = mybir.dt.float32
BF16 = mybir.dt.bfloat16
AF = mybir.ActivationFunctionType
ALU = mybir.AluOpType

D = 1024
L = 4096
LC = 256
T = L + LC
NT = T // 128
DEPTH = 2
H = 8
OFF_Q, OFF_KV, OFF_KR, OFF_F, OFF_H, IN_W = 0, 384, 640, 672, 928, 1696
EPS = 1e-6
NCOL = 64
C_N1G, C_N2G, C_QG, C_KVG, C_HCW, C_HCB, C_HD, C_HB1, C_HFR, C_HB2 = 0, 8, 16, 19, 21, 39, 45, 47, 48, 49


class _Op:
    __slots__ = ('eng', 'calls', 'deps', 'signal', 'sem', 'val', 'dma', 'waits', 'pre')


class Sched:
    NDMA = 8
    ATTR = [('pe', 'tensor'), ('act', 'scalar'), ('dve', 'vector'), ('pool', 'gpsimd'), ('sp', 'sync')]

    def __init__(self, nc, es):
        self.nc = nc
        self.sem = {e: es.enter_context(nc.semaphore('s_' + e)) for e in ['pe', 'act', 'dve', 'pool']}
        self.dsem = {q: [es.enter_context(nc.semaphore('d_%s%d' % (q, i))) for i in range(self.NDMA)]
                     for q in ['sp', 'pool', 'act']}
        self.cnt = {e: 0 for e in self.sem}
        self.dcnt = {q: 0 for q in self.dsem}
        self._reset()

    def _reset(self):
        self.ops = {e: [] for e, _ in self.ATTR}
        self.last_w = {}
        self.readers = {}

    def multi(self, eng, calls, r=(), w=(), dma=False):
        op = _Op()
        op.eng = eng; op.calls = calls; op.dma = dma; op.signal = False; op.pre = None
        deps = []
        for k in r:
            d = self.last_w.get(k)
            if d is not None: deps.append(d)
        for k in w:
            d = self.last_w.get(k)
            if d is not None: deps.append(d)
            deps.extend(self.readers.get(k, ()))
        if eng == 'pe':
            deps = [d for d in deps if d.eng != 'pe' or d.dma]
        op.deps = [d for d in deps if d is not op]
        for d in op.deps: d.signal = True
        for k in r: self.readers.setdefault(k, []).append(op)
        for k in w:
            self.last_w[k] = op
            self.readers[k] = []
        self.ops[eng].append(op)
        return op

    def op(self, eng, name, *args, r=(), w=(), **kw):
        return self.multi(eng, [(name, args, kw)], r=r, w=w)

    def dma(self, q, out, in_, r=(), w=(), **kw):
        return self.multi(q, [('dma_start', (out, in_), kw)], r=r, w=w, dma=True)

    def flush(self):
        nc = self.nc
        for e, _ in self.ATTR:
            ops = self.ops[e]
            for op in reversed(ops):
                if not op.dma:
                    op.signal = True
                    break
            for op in ops:
                if op.dma:
                    j = self.dcnt[e]; self.dcnt[e] += 1
                    op.sem = self.dsem[e][j % self.NDMA]
                    op.val = 16 * (j // self.NDMA + 1)
                    op.pre = (op.sem, op.val - 16) if op.val > 16 else None
                elif op.signal:
                    self.cnt[e] += 1
                    op.sem = self.sem[e]; op.val = self.cnt[e]
        finals = {}
        for e, _ in self.ATTR:
            seen = {}
            for op in self.ops[e]:
                need = {}
                if op.pre is not None: need[id(op.pre[0])] = op.pre
                for d in op.deps:
                    cur = need.get(id(d.sem))
                    if cur is None or cur[1] < d.val: need[id(d.sem)] = (d.sem, d.val)
                op.waits = []
                for k, (s, v) in need.items():
                    if seen.get(k, -1) < v:
                        seen[k] = v; op.waits.append((s, v))
                if op.dma or op.signal:
                    cur = finals.get(id(op.sem))
                    if cur is None or cur[1] < op.val: finals[id(op.sem)] = (op.sem, op.val)
        with nc.Block() as blk:
            for e, attr in self.ATTR:
                ops = self.ops[e]
                if not ops and e != 'sp': continue

                def body(eng, ops=ops, e=e):
                    for op in ops:
                        for (s, v) in op.waits: eng.wait_ge(s, v)
                        inst = None
                        for (name, args, kw) in op.calls:
                            inst = getattr(eng, name)(*args, **kw)
                        if op.dma: inst.then_inc(op.sem, 16)
                        elif op.signal: inst.then_inc(op.sem, 1)
                    if e == 'sp':
                        for (s, v) in finals.values(): eng.wait_ge(s, v)
                getattr(blk, attr)(body)
        self._reset()


def _bf(a):
    return np.ascontiguousarray(a.astype(ml_dtypes.bfloat16))


def _f32(a):
    return np.ascontiguousarray(a.astype(np.float32))


ROPE_PERM = np.array(list(range(8, 16)) + list(range(0, 8)) + list(range(24, 32)) + list(range(16, 24)))


def _rope_tables():
    rows = L // 64
    row = np.repeat(np.arange(rows, dtype=np.float64), 64)
    col = np.tile(np.arange(64, dtype=np.float64), rows)
    inv = 10000.0 ** (-np.arange(0, 16, 2, dtype=np.float64) / 16)
    ar = row[None, :] * inv[:, None]
    ac = col[None, :] * inv[:, None]
    cos = np.ones((32, T)); sin = np.zeros((32, T))
    cos[0:8, :L] = np.cos(ar); cos[8:16, :L] = np.cos(ar); cos[16:24, :L] = np.cos(ac); cos[24:32, :L] = np.cos(ac)
    sin[0:8, :L] = -np.sin(ar); sin[8:16, :L] = np.sin(ar); sin[16:24, :L] = -np.sin(ac); sin[24:32, :L] = np.sin(ac)
    return _f32(cos), _f32(sin)


def _fnet_tables(Ls, L1, L2):
    w = np.arange(64)
    ph = 2 * np.pi * np.outer(w, w) / 64
    cw = np.zeros((128, 128)); sw = np.zeros((128, 128))
    for g in range(2):
        cw[g * 64:(g + 1) * 64, g * 64:(g + 1) * 64] = np.cos(ph)
        sw[g * 64:(g + 1) * 64, g * 64:(g + 1) * 64] = np.sin(ph)
    csw = np.concatenate([cw, sw], axis=1)
    a = np.arange(L1)
    p1 = 2 * np.pi * np.outer(a, a) / L1
    wr, wi = np.cos(p1), -np.sin(p1)
    w1 = np.concatenate([wr, wi], axis=1)
    w2 = np.concatenate([wi, -wr], axis=1)
    p2 = 2 * np.pi * np.outer(np.arange(L2), np.arange(Ls)) / Ls
    sc = 1.0 / math.sqrt(Ls * 64)
    return _bf(csw), _bf(w1), _bf(w2), _bf(np.cos(p2) * sc), _bf(np.sin(p2) * sc)


def _hyena_tables(Ls):
    N = 2 * Ls
    N1 = N // 128
    a = np.arange(N1)
    p1 = 2 * np.pi * np.outer(a, a) / N1
    w1 = np.concatenate([np.cos(p1), -np.sin(p1)], axis=1)
    p2 = 2 * np.pi * np.outer(np.arange(128), np.arange(N)) / N
    tr, ti, nti = np.cos(p2), -np.sin(p2), np.sin(p2)
    p3 = 2 * np.pi * np.outer(np.arange(128), np.arange(128)) / 128
    i1a = np.concatenate([np.cos(p3), np.sin(p3)], axis=1)
    i1b = np.concatenate([-np.sin(p3), np.cos(p3)], axis=1)
    p4 = 2 * np.pi * np.outer(np.arange(N1), np.arange(Ls)) / N
    er, nei = np.cos(p4) / N, -np.sin(p4) / N
    j = np.arange(N)
    tau = np.where(j < Ls, j, N - j).astype(np.float64)
    tau[Ls] = 0
    tl = np.linspace(0.0, 1.0, Ls)
    t = tl[tau.astype(np.int64)]
    fr = np.linspace(1e-4, 15.0, 16)
    ang = 2.0 * np.pi * tau[:, None] / Ls * fr[None, :]
    z = np.concatenate([t[:, None], np.cos(ang), -np.sin(ang)], axis=1)
    negt = -t.copy()
    negt[Ls] = -1e4
    tcol = negt.reshape(N1, 128)
    deltas = np.abs(np.linspace(math.log(1e-2) / 1.5, math.log(1e-2) / 0.3, 256))
    drow = np.broadcast_to(deltas[None, :], (128, 256))
    return dict(N1=N1, w1=_bf(w1), tr=_bf(tr), ti=_bf(ti), nti=_bf(nti), i1a=_bf(i1a), i1b=_bf(i1b),
                er=_bf(er), nei=_bf(nei), zT=_f32(z.T), tcol=_f32(tcol), drow=_f32(drow))


_CONST = None


def _consts():
    global _CONST
    if _CONST is not None:
        return _CONST
    c = {}
    c['ident_bf'] = _bf(np.eye(128))
    c['ident_f'] = _f32(np.eye(128))
    c['ones_f'] = _f32(np.ones((128, 128)))
    c['ropeC'], c['ropeS'] = _rope_tables()
    for nm, Ls, L1, L2 in (('x', L, 64, 64), ('c', LC, 4, 64)):
        csw, w1, w2, tr, sn = _fnet_tables(Ls, L1, L2)
        c['fn_csw'] = csw
        c['fn_w1' + nm], c['fn_w2' + nm], c['fn_tr' + nm], c['fn_sn' + nm] = w1, w2, tr, sn
        ht = _hyena_tables(Ls)
        for k, v in ht.items():
            if k != 'N1':
                c['hy_%s%s' % (k, nm)] = v
    _CONST = c
    return c


def _colpack(v, n):
    return np.asarray(v, np.float32).reshape(n, 128).T


def _prep(inp):
    c = dict(_consts())
    sh = {}
    w_in = np.asarray(inp['w_in'], np.float32)
    sh['w_inx'] = np.ascontiguousarray(np.concatenate([w_in, w_in[:, :, OFF_KR + ROPE_PERM]], axis=2))
    w_uq = np.asarray(inp['w_uq'], np.float32)
    sh['w_uq'] = np.ascontiguousarray(w_uq)
    wq = w_uq.reshape(DEPTH, 384, H, 96)
    sh['w_uqp'] = np.ascontiguousarray(
        np.concatenate([wq[..., :64], wq[..., 64:][..., ROPE_PERM]], axis=-1).reshape(DEPTH, 384, 768))
    wkv = np.asarray(inp['w_ukv'], np.float32).reshape(DEPTH, 256, H, 128)
    sh['w_ukvx'] = np.ascontiguousarray(
        np.concatenate([wkv[..., :64].reshape(DEPTH, 256, 512), wkv[..., 64:].reshape(DEPTH, 256, 512)], axis=2))
    for k in ('w_mod', 'w_out', 'w_mlp1', 'w_mlp2', 'hy_w1', 'hy_w2', 'hy_w3'):
        sh[k] = np.ascontiguousarray(np.asarray(inp[k], np.float32))
    sh['b_mod2'] = np.ascontiguousarray(np.repeat(np.asarray(inp['b_mod'], np.float32)[:, None, :], 2, axis=1))
    cols = np.zeros((DEPTH, 128, NCOL), np.float32)
    for l in range(DEPTH):
        cols[l, :, C_N1G:C_N1G + 8] = _colpack(inp['norm1_g'][l], 8)
        cols[l, :, C_N2G:C_N2G + 8] = _colpack(inp['norm2_g'][l], 8)
        cols[l, :, C_QG:C_QG + 3] = _colpack(inp['q_norm_g'][l], 3)
        cols[l, :, C_KVG:C_KVG + 2] = _colpack(inp['kv_norm_g'][l], 2)
        for tap in range(3):
            cols[l, :, C_HCW + tap * 6:C_HCW + tap * 6 + 6] = _colpack(inp['hy_conv_w'][l, tap], 6)
        cols[l, :, C_HCB:C_HCB + 6] = _colpack(inp['hy_conv_b'][l], 6)
        cols[l, :, C_HD:C_HD + 2] = _colpack(inp['hy_d'][l], 2)
        cols[l, :64, C_HB1] = inp['hy_b1'][l]
        cols[l, :64, C_HFR] = inp['hy_freq'][l]
        cols[l, :64, C_HB2] = inp['hy_b2'][l]
    sh['cols'] = cols
    sh['fgrow'] = np.ascontiguousarray(np.broadcast_to(np.asarray(inp['final_norm_g'], np.float32)[None, :], (128, D)))
    sh.update(c)
    per = []
    x = np.asarray(inp['x'], np.float32); ctx = np.asarray(inp['ctx'], np.float32)
    cc = np.asarray(inp['c'], np.float32); c_ctx = np.asarray(inp['c_ctx'], np.float32)
    for b in range(8):
        d = dict(sh)
        d['xin'] = np.ascontiguousarray(np.concatenate([x[b], ctx[b]], axis=0))
        cv = np.stack([_colpack(cc[b], 8), _colpack(c_ctx, 8)], axis=-1)
        d['cvec'] = np.ascontiguousarray(cv)
        per.append(d)
    return per


GROUPS = [(g * 512, 512, 0) for g in range(8)] + [(L, LC, 1)]
SCALE = 1.0 / math.sqrt(96.0)


class Prog:
    def __init__(self, debug=False):
        self.debug = debug
        self.nc = nc = bass.Bass("TRN2", target_bir_lowering=False)
        self.es = ExitStack()
        self.S = Sched(nc, self.es)
        self._in = {}
        okind = dict(kind="ExternalOutput") if debug else {}
        scr = lambda n, shp, dt=F32: nc.dram_tensor(n, list(shp), dt, **okind).ap()
        self.spec = dict(
            xin=([T, D], F32), cvec=([128, 8, 2], F32), w_mod=([DEPTH, D, 6 * D], F32), b_mod2=([DEPTH, 2, 6 * D], F32),
            w_inx=([DEPTH, D, 1728], F32), w_uq=([DEPTH, 384, 768], F32), w_uqp=([DEPTH, 384, 768], F32),
            w_ukvx=([DEPTH, 256, 1024], F32), w_out=([DEPTH, D, D], F32), w_mlp1=([DEPTH, D, 4 * D], F32),
            w_mlp2=([DEPTH, 4 * D, D], F32), hy_w1=([DEPTH, 33, 64], F32), hy_w2=([DEPTH, 64, 64], F32),
            hy_w3=([DEPTH, 64, 512], F32), cols=([DEPTH, 128, NCOL], F32), fgrow=([128, D], F32),
            ident_bf=([128, 128], BF16), ident_f=([128, 128], F32), ones_f=([128, 128], F32),
            ropeC=([32, T], F32), ropeS=([32, T], F32), fn_csw=([128, 256], BF16))
        self.fn = {}; self.hy = {}
        for nm, Ls, L1 in (('x', L, 64), ('c', LC, 4)):
            N = 2 * Ls; N1 = N // 128
            self.spec.update({'fn_w1' + nm: ([L1, 2 * L1], BF16), 'fn_w2' + nm: ([L1, 2 * L1], BF16),
                              'fn_tr' + nm: ([64, Ls], BF16), 'fn_sn' + nm: ([64, Ls], BF16),
                              'hy_w1' + nm: ([N1, 2 * N1], BF16), 'hy_tr' + nm: ([128, N], BF16),
                              'hy_ti' + nm: ([128, N], BF16), 'hy_nti' + nm: ([128, N], BF16),
                              'hy_i1a' + nm: ([128, 256], BF16), 'hy_i1b' + nm: ([128, 256], BF16),
                              'hy_er' + nm: ([N1, Ls], BF16), 'hy_nei' + nm: ([N1, Ls], BF16),
                              'hy_zT' + nm: ([33, N], F32), 'hy_tcol' + nm: ([N1, 128], F32),
                              'hy_drow' + nm: ([128, 256], F32)})
            self.fn[nm] = dict(L1=L1, Ls=Ls)
            self.hy[nm] = dict(N1=N1, Ls=Ls, N=N, kf=scr('kf' + nm, [128, N1, 2, 256], BF16),
                               ssum=scr('ssum' + nm, [128, 2]))
        self.y = nc.dram_tensor('y', [L, D], F32, kind="ExternalOutput").ap()
        self.xres = scr('xres', [T, D]); self.mrow = scr('mrow', [2, 6 * D])
        self.pxT = scr('pxT', [14, 128, T], BF16); self.mixT = scr('mixT', [8, 128, T], BF16)
        self.hx0 = scr('hx0', [2, 128, T], BF16); self.hz = scr('hz', [2, 128, T], BF16)
        self.xmid = scr('xmid', [T, D]); self.h2T = scr('h2T', [8, 128, T], BF16)
        self.hyY = {}; self.hyF = {}
        for nm_ in ('x', 'c'):
            n1_ = self.hy[nm_]['N1']
            self.hyY[nm_] = scr('hyY' + nm_, [128, 256, 2 * n1_], BF16)
            self.hyF[nm_] = scr('hyF' + nm_, [2, 128, n1_, 2, 128], BF16)
        self.modc = self.es.enter_context(nc.sbuf_tensor('modc', [128, 2, 4, 8], F32))
        self.cols = self.es.enter_context(nc.sbuf_tensor('colsb', [128, NCOL], F32))

    def uq(self, n):
        self._uq = getattr(self, '_uq', 0) + 1
        return '%s_%d' % (n, self._uq)

    def I(self, name):
        if name not in self._in:
            shp, dt = self.spec[name]
            self._in[name] = self.nc.dram_tensor(name, list(shp), dt, kind="ExternalInput").ap()
        return self._in[name]

    def p0_mod(self, l):
        nc, S = self.nc, self.S
        with ExitStack() as es:
            sb = lambda n, shp, dt=F32: es.enter_context(nc.sbuf_tensor(self.uq(n), shp, dt))
            cv = sb('cv', [128, 8, 2]); sc = sb('sc', [128, 8, 2])
            wt = [sb('wt%d' % i, [128, 8, 512]) for i in range(2)]
            msb = sb('msb', [2, 6 * D]); bm = sb('bm', [2, 6 * D])
            ps = [es.enter_context(nc.psum_tensor(self.uq('ps%d' % i), [2, 512], F32)) for i in range(2)]
            S.dma('sp', cv[:], self.I('cvec'), w=['cv'])
            S.dma('sp', bm[:], self.I('b_mod2')[l], w=['bm'])
            S.dma('sp', self.cols[:], self.I('cols')[l], w=['cols'])
            S.op('act', 'activation', sc[:], cv[:], AF.Silu, r=['cv'], w=['sc'])
            wv = self.I('w_mod')[l].rearrange("(k p) n -> p k n", p=128)
            for n in range(12):
                b = n % 2
                S.dma('sp', wt[b][:], wv[:, :, n * 512:(n + 1) * 512], w=[('wt', b)])
                S.multi('pe', [('matmul', (ps[b][:], sc[:, k, :], wt[b][:, k, :]), dict(start=(k == 0), stop=(k == 7)))
                               for k in range(8)], r=['sc', ('wt', b)], w=[('ps', b)])
                S.op('dve', 'tensor_tensor', msb[:, n * 512:(n + 1) * 512], ps[b][:], bm[:, n * 512:(n + 1) * 512],
                     ALU.add, r=[('ps', b), 'bm'], w=['msb'])
            S.dma('sp', self.mrow, msb[:], r=['msb'])
            S.flush()
            mT = sb('mT', [96, 128]); idf = sb('idf', [128, 128]); mcol = sb('mcol', [128, 96])
            pm = es.enter_context(nc.psum_tensor(self.uq('pm'), [128, 96], F32))
            S.dma('sp', mT[:], self.mrow.rearrange("s (j p) -> (s j) p", p=128), w=['mT'])
            S.dma('sp', idf[:], self.I('ident_f'), w=['idf'])
            S.op('pe', 'transpose', pm[:], mT[:], idf[0:96, 0:96], r=['mT', 'idf'], w=['pm'])
            S.op('dve', 'tensor_copy', mcol[:], pm[:], r=['pm'], w=['mcol'])
            for s in range(2):
                o = s * 48
                S.op('dve', 'scalar_tensor_tensor', self.modc[:, s, 0, :], mcol[:, o + 8:o + 16], 1.0,
                     self.cols[:, C_N1G:C_N1G + 8], ALU.add, ALU.mult, r=['mcol', 'cols'], w=['modc'])
                S.op('dve', 'tensor_copy', self.modc[:, s, 1, :], mcol[:, o:o + 8], r=['mcol'], w=['modc'])
                S.op('dve', 'scalar_tensor_tensor', self.modc[:, s, 2, :], mcol[:, o + 32:o + 40], 1.0,
                     self.cols[:, C_N2G:C_N2G + 8], ALU.add, ALU.mult, r=['mcol', 'cols'], w=['modc'])
                S.op('dve', 'tensor_copy', self.modc[:, s, 3, :], mcol[:, o + 24:o + 32], r=['mcol'], w=['modc'])
            S.flush()

    def norm_tiles(self, xt, nj, s, which, tl, key, n, part='ab'):
        S = self.S
        ss, rstd, xn, junk, pt, hT, idb = tl['ss'], tl['rstd'], tl['xn'], tl['junk'], tl['pt'], tl['hT'], tl['idb']
        for j in (range(nj) if 'a' in part else ()):
            S.op('act', 'activation', junk[:], xt[:, j, :], AF.Square, accum_out=ss[:, j:j + 1],
                 r=[key], w=['junk', tl['k'] + 'ss'])
        if 'a' in part:
            S.op('act', 'activation', rstd[:, 0:nj], ss[:, 0:nj], AF.Sqrt, bias=tl['eps'][:, 0:1], scale=1.0 / D,
                 r=[tl['k'] + 'ss', 'eps'], w=[tl['k'] + 'rstd'])
            S.op('dve', 'reciprocal', rstd[:, 0:nj], rstd[:, 0:nj], r=[tl['k'] + 'rstd'], w=[tl['k'] + 'rstd'])
        for j in (range(nj) if 'a' in part else ()):
            S.op('dve', 'tensor_scalar', xn[:, j, :], xt[:, j, :], rstd[:, j:j + 1], None, ALU.mult,
                 r=[key, tl['k'] + 'rstd'], w=[tl['k'] + 'xn'])
        for k in (range(8) if 'b' in part else ()):
            pb = k % 2
            S.multi('pe', [('transpose', (pt[pb][:, j * 128:(j + 1) * 128], xn[:, j, k * 128:(k + 1) * 128], idb[:]), {})
                           for j in range(nj)], r=[tl['k'] + 'xn', 'idb'], w=[('pt', pb)])
            if k % 2 == 0:
                S.op('dve', 'tensor_scalar', hT[:, k, 0:n], pt[pb][:, 0:n], self.modc[:, s, which, k:k + 1],
                     self.modc[:, s, which + 1, k:k + 1], ALU.mult, ALU.add, r=[('pt', pb), 'modc'], w=[tl['k'] + 'hT'])
            else:
                S.op('act', 'activation', hT[:, k, 0:n], pt[pb][:, 0:n], AF.Identity,
                     bias=self.modc[:, s, which + 1, k:k + 1], scale=self.modc[:, s, which, k:k + 1],
                     r=[('pt', pb), 'modc'], w=[tl['k'] + 'hT'])

    def p1_inproj(self, l):
        nc, S = self.nc, self.S
        src = self.I('xin') if l == 0 else self.xres
        with ExitStack() as es:
            sb = lambda n, shp, dt=F32: es.enter_context(nc.sbuf_tensor(self.uq(n), shp, dt))
            pst = lambda n, shp, dt=F32: es.enter_context(nc.psum_tensor(self.uq(n), shp, dt))
            win = sb('win', [128, 8, 1728], BF16)
            S.dma('pool', win[:], self.I('w_inx')[l].rearrange("(k p) n -> p k n", p=128), w=['win'])
            idb = sb('idb', [128, 128], BF16); S.dma('sp', idb[:], self.I('ident_bf'), w=['idb'])
            onesf = sb('onesf', [128, 128]); S.dma('sp', onesf[:], self.I('ones_f'), w=['onesf'])
            rc = sb('rc', [32, T]); rs = sb('rs', [32, T])
            S.dma('sp', rc[:], self.I('ropeC'), w=['rc']); S.dma('sp', rs[:], self.I('ropeS'), w=['rs'])
            eps = sb('eps', [128, 1]); S.op('dve', 'memset', eps[:], EPS, w=['eps'])
            xg = [sb('xg%d' % i, [128, 4, D]) for i in range(3)]
            tls = []
            junk = sb('junk', [128, D], BF16)
            pt = [pst('pt%d' % i, [128, 512], BF16) for i in range(2)]
            for i in range(2):
                tls.append(dict(k='t%d' % i, ss=sb('ss%d' % i, [128, 4]), rstd=sb('rstd%d' % i, [128, 4]),
                                xn=sb('xn%d' % i, [128, 4, D], BF16), junk=junk, pt=pt,
                                hT=sb('hT%d' % i, [128, 8, 512], BF16), idb=idb, eps=eps))
            ost = [sb('ost%d' % i, [128, 14, 512], BF16) for i in range(2)]
            for i in range(2):
                S.op('pool', 'memset', ost[i][:, 13, :], 0.0, w=[('ost', i)])
            c32 = sb('c32', [128, 5, 512]); sq = sb('sq', [128, 5, 512]); rq = sb('rq', [128, 512])
            t1 = sb('t1', [32, 512]); t2 = sb('t2', [32, 512])
            po = [pst('po%d' % i, [128, 512]) for i in range(3)]
            pn = pst('pn', [128, 512])
            npo = [0]

            def proj(col0, ncols, hT, n, b):
                i = npo[0] % 3; npo[0] += 1
                S.multi('pe', [('matmul', (po[i][0:ncols, 0:n], win[:, k, col0:col0 + ncols], hT[:, k, 0:n]),
                                dict(start=(k == 0), stop=(k == 7))) for k in range(8)],
                        r=['win', 't%dhT' % b], w=[('po', i)])
                return i

            def load(gi):
                t0, n, s = GROUPS[gi]
                x3 = gi % 3; nj = n // 128
                S.dma('sp', xg[x3][:, 0:nj, :], src[t0:t0 + n, :].rearrange("(j p) d -> p j d", p=128), w=[('xg', x3)])

            def front(gi, part):
                t0, n, s = GROUPS[gi]
                nj = n // 128
                self.norm_tiles(xg[gi % 3], nj, s, 0, tls[gi % 2], ('xg', gi % 3), n, part=part)

            def back(gi):
                t0, n, s = GROUPS[gi]
                b = gi % 2; nj = n // 128; tl = tls[b]
                hT = tl['hT']
                okey = ('ost', b)
                for (c0, nch, gcol, ci0, i0) in ((OFF_Q, 3, C_QG, 0, 0), (OFF_KV, 2, C_KVG, 3, 3)):
                    for c in range(nch):
                        i = proj(c0 + c * 128, 128, hT, n, b)
                        S.op('act', 'activation', c32[:, i0 + c, 0:n], po[i][:, 0:n], AF.Copy,
                             r=[('po', i)], w=[('c32', i0 + c)])
                        S.op('act', 'activation', sq[:, i0 + c, 0:n], po[i][:, 0:n], AF.Square,
                             r=[('po', i)], w=[('sq', i0 + c)])
                    S.multi('pe', [('matmul', (pn[:, 0:n], onesf[:], sq[:, i0 + c, 0:n]),
                                    dict(start=(c == 0), stop=(c == nch - 1))) for c in range(nch)],
                            r=['onesf'] + [('sq', i0 + c) for c in range(nch)], w=['pn'])
                    S.op('act', 'activation', rq[:, 0:n], pn[:, 0:n], AF.Sqrt, bias=eps[:, 0:1], scale=1.0 / (128 * nch),
                         r=['pn', 'eps'], w=['rq'])
                    S.op('dve', 'reciprocal', rq[:, 0:n], rq[:, 0:n], r=['rq'], w=['rq'])
                    for c in range(nch):
                        S.op('dve', 'scalar_tensor_tensor', ost[b][:, ci0 + c, 0:n], c32[:, i0 + c, 0:n],
                             self.cols[:, gcol + c:gcol + c + 1], rq[:, 0:n], ALU.mult, ALU.mult,
                             r=[('c32', i0 + c), 'rq', 'cols'], w=[okey])
                ia = proj(OFF_KR, 32, hT, n, b)
                ib = proj(IN_W, 32, hT, n, b)
                S.op('dve', 'tensor_tensor', t1[:, 0:n], po[ia][0:32, 0:n], rc[:, t0:t0 + n], ALU.mult,
                     r=[('po', ia), 'rc'], w=['t1'])
                S.op('dve', 'tensor_tensor', t2[:, 0:n], po[ib][0:32, 0:n], rs[:, t0:t0 + n], ALU.mult,
                     r=[('po', ib), 'rs'], w=['t2'])
                S.op('dve', 'tensor_tensor', ost[b][0:32, 13, 0:n], t1[:, 0:n], t2[:, 0:n], ALU.add,
                     r=['t1', 't2'], w=[okey])
                for c in range(8):
                    i = proj(OFF_F + c * 128, 128, hT, n, b)
                    if c % 2 == 0:
                        S.op('act', 'activation', ost[b][:, 5 + c, 0:n], po[i][:, 0:n], AF.Copy, r=[('po', i)], w=[okey])
                    else:
                        S.op('dve', 'tensor_copy', ost[b][:, 5 + c, 0:n], po[i][:, 0:n], r=[('po', i)], w=[okey])
                S.dma('sp', self.pxT[:, :, t0:t0 + n].rearrange("c p t -> p c t"), ost[b][:, :, 0:n], r=[okey])

            NG = len(GROUPS)
            load(0); load(1)
            front(0, 'a'); front(0, 'b'); front(1, 'a')
            for gi in range(NG):
                if gi + 2 < NG:
                    load(gi + 2)
                back(gi)
                if gi + 1 < NG:
                    front(gi + 1, 'b')
                if gi + 2 < NG:
                    front(gi + 2, 'a')
            S.flush()

    def p2_attn(self, l):
        nc, S = self.nc, self.S
        with ExitStack() as es:
            sb = lambda n, shp, dt=F32: es.enter_context(nc.sbuf_tensor(self.uq(n), shp, dt))
            pst = lambda n, shp, dt=F32: es.enter_context(nc.psum_tensor(self.uq(n), shp, dt))
            cqn = sb('cqn', [128, 3, T], BF16); ckvn = sb('ckvn', [128, 2, T], BF16)
            S.dma('sp', cqn[:], self.pxT[0:3].rearrange("c p t -> p c t"), w=['cqn'])
            S.dma('sp', ckvn[:], self.pxT[3:5].rearrange("c p t -> p c t"), w=['ckvn'])
            wuq = sb('wuq', [128, 3, 768], BF16); wuqp = sb('wuqp', [128, 3, 768], BF16)
            wukv = sb('wukv', [128, 2, 1024], BF16)
            S.dma('pool', wuq[:], self.I('w_uq')[l].rearrange("(k p) n -> p k n", p=128), w=['wuq'])
            S.dma('pool', wuqp[:], self.I('w_uqp')[l].rearrange("(k p) n -> p k n", p=128), w=['wuqp'])
            S.dma('pool', wukv[:], self.I('w_ukvx')[l].rearrange("(k p) n -> p k n", p=128), w=['wukv'])
            onesf = sb('onesf', [128, 128]); S.dma('sp', onesf[:], self.I('ones_f'), w=['onesf'])
            tc_ = sb('tabc', [96, T]); ts_ = sb('tabs', [96, T])
            S.dma('sp', tc_[64:96, :], self.I('ropeC'), w=['tabc']); S.dma('sp', ts_[64:96, :], self.I('ropeS'), w=['tabs'])
            kt = [sb('kt%d' % i, [96, T], BF16) for i in range(2)]
            qt = [sb('qt%d' % i, [96, T], BF16) for i in range(2)]
            for b in range(2):
                S.dma('sp', kt[b][64:96, :], self.pxT[13, 0:32, :], w=[('ktr', b)])
            vh = [sb('vh%d' % i, [128, NT, 128], BF16) for i in range(2)]
            for i in range(2):
                S.op('pool', 'memset', vh[i][:], 1.0, w=[('vh', i)])
            t1 = sb('t1', [96, 512]); t2 = sb('t2', [96, 512])
            ptl = [sb('ptl%d' % i, [128, 2, 512], BF16) for i in range(3)]
            rr = sb('rr', [96, 512]); osb = [sb('osb%d' % i, [64, 512]) for i in range(2)]; ost = sb('ost', [64, T], BF16)
            sel = sb('sel', [128, 128], BF16); rb = sb('rb', [64, 512])
            S.op('pool', 'memset', sel[:], 0.0, w=['sel'])
            S.op('pool', 'memset', sel[64:65, :], 1.0, w=['sel'])
            rhi = [sb('rhi%d' % i, [128, 512], BF16) for i in range(2)]; rlo = [sb('rlo%d' % i, [128, 512], BF16) for i in range(2)]
            for i in range(2):
                S.op('pool', 'memset', rhi[i][:], 0.0, w=[('rhi', i)])
                S.op('pool', 'memset', rlo[i][:], 0.0, w=[('rlo', i)])
            ps = [pst('ps%d' % i, [128, 1024]) for i in range(2)]
            po = [pst('po%d' % i, [128, 512]) for i in range(2)]
            pq = pst('pq', [128, 512]); pq2 = pst('pq2', [128, 512]); pk = pq; pb = pq2
            LA = 2

            def project_chunks(h):
                b = h % 2
                chunks = []
                for gi, (t0, n, s) in enumerate(GROUPS):
                    def cA(gi=gi, t0=t0, n=n):
                        S.multi('pe', [('matmul', (pq[0:96, 0:n], wuq[:, c, h * 96:(h + 1) * 96], cqn[:, c, t0:t0 + n]),
                                        dict(start=(c == 0), stop=(c == 2))) for c in range(3)],
                                r=['cqn', 'wuq'], w=['pq'])
                        S.multi('pe', [('matmul', (pq2[0:96, 0:n], wuqp[:, c, h * 96:(h + 1) * 96], cqn[:, c, t0:t0 + n]),
                                        dict(start=(c == 0), stop=(c == 2))) for c in range(3)],
                                r=['cqn', 'wuqp'], w=['pq2'])
                        S.op('dve', 'tensor_copy', qt[b][0:64, t0:t0 + n], pq[0:64, 0:n], r=['pq'], w=[('qt', b, gi)])
                        S.op('dve', 'tensor_tensor', t1[64:96, 0:n], pq[64:96, 0:n], tc_[64:96, t0:t0 + n], ALU.mult,
                             r=['pq', 'tabc'], w=['t1'])
                        S.op('dve', 'tensor_tensor', t2[64:96, 0:n], pq2[64:96, 0:n], ts_[64:96, t0:t0 + n], ALU.mult,
                             r=['pq2', 'tabs'], w=['t2'])
                        S.op('dve', 'tensor_tensor', qt[b][64:96, t0:t0 + n], t1[64:96, 0:n], t2[64:96, 0:n], ALU.add,
                             r=['t1', 't2'], w=[('qt', b, gi)])

                    def cB(gi=gi, t0=t0, n=n):
                        S.multi('pe', [('matmul', (pk[:, 0:n], wukv[:, c, h * 64:h * 64 + 128], ckvn[:, c, t0:t0 + n]),
                                        dict(start=(c == 0), stop=(c == 1))) for c in range(2)],
                                r=['ckvn', 'wukv'], w=['pq'])
                        S.op('dve', 'tensor_copy', kt[b][0:64, t0:t0 + n], pk[0:64, 0:n], r=['pq'], w=[('kt', b, gi)])
                    chunks += [cA, cB]
                for i0 in range(0, NT, 8):
                    def cV(i0=i0):
                        nt_ = min(8, NT - i0)
                        calls = []
                        for ii_ in range(nt_):
                            i_ = i0 + ii_
                            calls += [('matmul', (pq2[:, ii_ * 64:(ii_ + 1) * 64], ckvn[:, c, i_ * 128:(i_ + 1) * 128],
                                                  wukv[:, c, 512 + h * 64:512 + (h + 1) * 64]), dict(start=(c == 0), stop=(c == 1)))
                                      for c in range(2)]
                        S.multi('pe', calls, r=['ckvn', 'wukv'], w=['pq2'])
                        S.op('dve', 'tensor_copy', vh[b][:, i0:i0 + nt_, 0:64],
                             pq2[:, 0:nt_ * 64].rearrange("p (a d) -> p a d", d=64), r=['pq2'], w=[('vh', b)])
                    chunks.append(cV)
                return chunks

            def project(h):
                for c_ in project_chunks(h):
                    c_()

            project(0)
            for h in range(H):
                b = h % 2
                seq = []
                for gi, (q0, nq, s) in enumerate(GROUPS):
                    tiles = list(range(NT)) if s == 0 else [32, 33]
                    npair = len(tiles) // 2
                    for ii in range(npair):
                        seq.append((gi, q0, nq, ii, tiles[2 * ii], npair))

                def qk(e):
                    gi, q0, nq, ii, i, np_ = seq[e]
                    sl = e % 2
                    S.multi('pe', [('matmul', (ps[sl][:, a_ * 512:a_ * 512 + nq], kt[b][0:96, (i + a_) * 128:(i + a_ + 1) * 128],
                                               qt[b][0:96, q0:q0 + nq]), dict(start=True, stop=True)) for a_ in range(2)],
                            r=[('kt', b, i // 4), ('kt', b, (i + 1) // 4), ('ktr', b), ('qt', b, gi)], w=[('ps', sl)])

                pending = []
                nxt = project_chunks(h + 1) if h + 1 < H else []
                cstep = max(1, (len(seq) - 12) // max(1, len(nxt)))
                for e in range(min(LA, len(seq))):
                    qk(e)
                for e in range(len(seq)):
                    gi, q0, nq, ii, i, np_ = seq[e]
                    sl = e % 2; sl2 = e % 3; ob = gi % 2
                    S.op('act', 'activation', ptl[sl2][:, :, 0:nq], ps[sl][:, :].rearrange("p (a n) -> p a n", a=2)[:, :, 0:nq],
                         AF.Exp, scale=SCALE, r=[('ps', sl)], w=[('ptl', sl2)])
                    if e + LA < len(seq):
                        qk(e + LA)
                    S.multi('pe', [('matmul', (po[ob][:, 0:nq], vh[b][:, i + a_, :], ptl[sl2][:, a_, 0:nq]),
                                    dict(start=(ii == 0 and a_ == 0), stop=(ii == np_ - 1 and a_ == 1))) for a_ in range(2)],
                            r=[('ptl', sl2), ('vh', b)], w=[('po', ob)])
                    if ii == np_ - 1:
                        fb = gi % 2
                        S.op('dve', 'tensor_copy', rhi[fb][64:65, 0:nq], po[ob][64:65, 0:nq], r=[('po', ob)], w=[('rhi', fb)])
                        S.op('dve', 'tensor_tensor', rlo[fb][64:65, 0:nq], po[ob][64:65, 0:nq], rhi[fb][64:65, 0:nq], ALU.subtract,
                             r=[('po', ob), ('rhi', fb)], w=[('rlo', fb)])
                        S.op('dve', 'tensor_copy', osb[fb][:, 0:nq], po[ob][0:64, 0:nq], r=[('po', ob)], w=[('osb', fb)])
                        pending.append((e + 3, fb, q0, nq))
                    while pending and (pending[0][0] <= e or e == len(seq) - 1):
                        _, fb, fq0, fnq = pending.pop(0)
                        S.multi('pe', [('matmul', (pb[:, 0:fnq], sel[:, :], rhi[fb][:, 0:fnq]), dict(start=True, stop=False)),
                                       ('matmul', (pb[:, 0:fnq], sel[:, :], rlo[fb][:, 0:fnq]), dict(start=False, stop=True))],
                                r=[('rhi', fb), ('rlo', fb), 'sel'], w=['pq2'])
                        S.op('dve', 'reciprocal', rb[:, 0:fnq], pb[0:64, 0:fnq], r=['pq2'], w=['rb'])
                        S.op('dve', 'tensor_tensor', ost[:, fq0:fq0 + fnq], osb[fb][:, 0:fnq], rb[:, 0:fnq], ALU.mult,
                             r=[('osb', fb), 'rb'], w=['ost'])
                    if nxt and e >= 4 and (e - 4) % cstep == 0:
                        nxt.pop(0)()
                while nxt:
                    nxt.pop(0)()
                S.dma('sp', self.mixT[h // 2, (h % 2) * 64:(h % 2) * 64 + 64, :], ost[:, :], r=['ost'])
            S.flush()

    def p5_out_mlp(self, l):
        nc, S = self.nc, self.S
        last = (l == DEPTH - 1)
        src = self.I('xin') if l == 0 else self.xres
        groups = GROUPS[:8] if last else GROUPS
        with ExitStack() as es:
            sb = lambda n, shp, dt=F32: es.enter_context(nc.sbuf_tensor(self.uq(n), shp, dt))
            pst = lambda n, shp, dt=F32: es.enter_context(nc.psum_tensor(self.uq(n), shp, dt))
            wout = sb('wout', [128, 8, D], BF16)
            S.dma('pool', wout[:], self.I('w_out')[l].rearrange("(k p) n -> p k n", p=128), w=['wout'])
            idb = sb('idb', [128, 128], BF16); S.dma('sp', idb[:], self.I('ident_bf'), w=['idb'])
            eps = sb('eps', [128, 1]); S.op('dve', 'memset', eps[:], EPS, w=['eps'])
            g1r = sb('g1r', [128, D])
            mixt = [sb('mixt%d' % i, [128, 8, 512], BF16) for i in range(3)]
            xt = [sb('xt%d' % i, [128, 4, D]) for i in range(3)]
            tmp = [sb('tmp%d' % i, [128, 512]) for i in range(2)]
            junk = sb('junk', [128, D], BF16)
            pt = [pst('pt%d' % i, [128, 512], BF16) for i in range(2)]
            tls = [dict(k='n%d' % i, ss=sb('ss%d' % i, [128, 4]), rstd=sb('rstd%d' % i, [128, 4]),
                        xn=sb('xn%d' % i, [128, 4, D], BF16), junk=junk, pt=pt,
                        hT=sb('hT%d' % i, [128, 8, 512], BF16), idb=idb, eps=eps) for i in range(2)]
            pso = [pst('pso%d' % i, [128, 512]) for i in range(4)]
            st = dict(cur_s=-1, np_=0)

            def ldA(gi):
                t0, n, s = groups[gi]
                b = gi % 3; nj = n // 128
                S.dma('sp', mixt[b][:, :, 0:n], self.mixT[:, :, t0:t0 + n].rearrange("c p t -> p c t"), w=[('mixt', b)])
                S.dma('sp', xt[b][:, 0:nj, :], src[t0:t0 + n, :].rearrange("(j p) d -> p j d", p=128), w=[('xt', b)])

            def stA(gi):
                t0, n, s = groups[gi]
                b = gi % 3; nj = n // 128
                if s != st['cur_s']:
                    S.dma('sp', g1r[:], self.mrow[s, 2 * D:3 * D].partition_broadcast(128), w=['g1r']); st['cur_s'] = s
                for j in range(nj):
                    for hh in range(2):
                        pi = st['np_'] % 4; st['np_'] += 1
                        S.multi('pe', [('matmul', (pso[pi][:, :], mixt[b][:, c, j * 128:(j + 1) * 128], wout[:, c, hh * 512:(hh + 1) * 512]),
                                        dict(start=(c == 0), stop=(c == 7))) for c in range(8)],
                                r=[('mixt', b), 'wout'], w=[('pso', pi)])
                        S.op('dve', 'tensor_tensor', tmp[pi % 2][:, :], pso[pi][:, :], g1r[:, hh * 512:(hh + 1) * 512], ALU.mult,
                             r=[('pso', pi), 'g1r'], w=[('tmp', pi % 2)])
                        S.op('pool', 'tensor_tensor', xt[b][:, j, hh * 512:(hh + 1) * 512], tmp[pi % 2][:, :],
                             xt[b][:, j, hh * 512:(hh + 1) * 512], ALU.add, r=[('tmp', pi % 2), ('xt', b)], w=[('xt', b)])
                S.dma('sp', self.xmid[t0:t0 + n, :].rearrange("(j p) d -> p j d", p=128), xt[b][:, 0:nj, :], r=[('xt', b)])

            def stB(gi):
                t0, n, s = groups[gi]
                b = gi % 3; b2 = gi % 2; nj = n // 128
                self.norm_tiles(xt[b], nj, s, 2, tls[b2], ('xt', b), n)
                S.dma('sp', self.h2T[:, :, t0:t0 + n].rearrange("c p t -> p c t"), tls[b2]['hT'][:, :, 0:n], r=['n%dhT' % b2])

            NGa = len(groups)
            ldA(0)
            if NGa > 1:
                ldA(1)
            stA(0)
            for gi in range(NGa):
                if gi + 2 < NGa:
                    ldA(gi + 2)
                if gi + 1 < NGa:
                    stA(gi + 1)
                stB(gi)
            S.flush()
        with ExitStack() as es:
            sb = lambda n, shp, dt=F32: es.enter_context(nc.sbuf_tensor(self.uq(n), shp, dt))
            pst = lambda n, shp, dt=F32: es.enter_context(nc.psum_tensor(self.uq(n), shp, dt))
            w1 = sb('w1', [128, 8, 4 * D], BF16); w2 = sb('w2', [128, 32, D], BF16)
            for hh in range(2):
                S.dma('pool', w1[:, :, hh * 2048:(hh + 1) * 2048],
                      self.I('w_mlp1')[l].rearrange("(k p) n -> p k n", p=128)[:, :, hh * 2048:(hh + 1) * 2048], w=[('w1', hh)])
            for q in range(4):
                S.dma('pool', w2[:, q * 8:(q + 1) * 8, :],
                      self.I('w_mlp2')[l].rearrange("(k p) n -> p k n", p=128)[:, q * 8:(q + 1) * 8, :], w=[('w2', q)])
            g2r = sb('g2r', [128, D])
            if last:
                fg = sb('fg', [128, D]); S.dma('sp', fg[:], self.I('fgrow'), w=['fg'])
                eps = sb('eps', [128, 1]); S.op('dve', 'memset', eps[:], EPS, w=['eps'])
                junk = sb('junk', [128, D], BF16); fss = sb('fss', [128, 4]); frs = sb('frs', [128, 4])
            hT = sb('hT', [128, 8, 512], BF16); fT = sb('fT', [128, 32, 512], BF16)
            x1 = sb('x1', [128, 4, D]); sq = [sb('sq%d' % i, [128, 512]) for i in range(2)]
            tmp = [sb('tmp%d' % i, [128, 512]) for i in range(2)]
            psf = [pst('psf%d' % i, [128, 512]) for i in range(3)]
            ps2 = [pst('ps2%d' % i, [128, 512]) for i in range(3)]
            cur_s = -1; n2 = 0
            t0_, n_, s_ = groups[0]
            S.dma('sp', hT[:, :, 0:n_], self.h2T[:, :, t0_:t0_ + n_].rearrange("c p t -> p c t"), w=['hT'])
            for gi, (t0, n, s) in enumerate(groups):
                nj = n // 128
                if s != cur_s:
                    S.dma('sp', g2r[:], self.mrow[s, 5 * D:6 * D].partition_broadcast(128), w=['g2r']); cur_s = s
                S.dma('sp', x1[:, 0:nj, :], self.xmid[t0:t0 + n, :].rearrange("(j p) d -> p j d", p=128),
                      w=[('x1', j) for j in range(nj)])
                for j in range(32):
                    pf = psf[j % 3]
                    S.multi('pe', [('matmul', (pf[:, 0:n], w1[:, k, j * 128:(j + 1) * 128], hT[:, k, 0:n]),
                                    dict(start=(k == 0), stop=(k == 7))) for k in range(8)],
                            r=[('w1', j // 16), 'hT'], w=[('psf', j % 3)])
                    S.op('act', 'activation', sq[j % 2][:, 0:n], pf[:, 0:n], AF.Square, r=[('psf', j % 3)], w=[('sq', j % 2)])
                    S.op('dve', 'scalar_tensor_tensor', fT[:, j, 0:n], pf[:, 0:n], 0.0, sq[j % 2][:, 0:n],
                         ALU.is_gt, ALU.mult, r=[('psf', j % 3), ('sq', j % 2)], w=[('fT', j)])
                if gi + 1 < len(groups):
                    t0n, nn, sn_ = groups[gi + 1]
                    S.dma('sp', hT[:, :, 0:nn], self.h2T[:, :, t0n:t0n + nn].rearrange("c p t -> p c t"), w=['hT'])
                for j in range(nj):
                    for hh in range(2):
                        pi = n2 % 3; n2 += 1
                        S.multi('pe', [('matmul', (ps2[pi][:, :], fT[:, q, j * 128:(j + 1) * 128], w2[:, q, hh * 512:(hh + 1) * 512]),
                                        dict(start=(q == 0), stop=(q == 31))) for q in range(32)],
                                r=[('fT', q) for q in range(32)] + [('w2', q) for q in range(4)], w=[('ps2', pi)])
                        S.op('dve', 'tensor_tensor', tmp[pi % 2][:, :], ps2[pi][:, :], g2r[:, hh * 512:(hh + 1) * 512], ALU.mult,
                             r=[('ps2', pi), 'g2r'], w=[('tmp', pi % 2)])
                        S.op('pool', 'tensor_tensor', x1[:, j, hh * 512:(hh + 1) * 512], tmp[pi % 2][:, :],
                             x1[:, j, hh * 512:(hh + 1) * 512], ALU.add, r=[('tmp', pi % 2), ('x1', j)], w=[('x1', j)])
                    tt = t0 + j * 128
                    if not last:
                        S.dma('sp', self.xres[tt:tt + 128, :], x1[:, j, :], r=[('x1', j)])
                    else:
                        S.op('act', 'activation', junk[:], x1[:, j, :], AF.Square, accum_out=fss[:, j:j + 1],
                             r=[('x1', j)], w=['junk', ('fss', j)])
                        S.op('act', 'activation', frs[:, j:j + 1], fss[:, j:j + 1], AF.Sqrt, bias=eps[:, 0:1], scale=1.0 / D,
                             r=[('fss', j), 'eps'], w=[('frs', j)])
                        S.op('dve', 'reciprocal', frs[:, j:j + 1], frs[:, j:j + 1], r=[('frs', j)], w=[('frs', j)])
                        S.op('dve', 'scalar_tensor_tensor', x1[:, j, :], x1[:, j, :], frs[:, j:j + 1], fg[:], ALU.mult, ALU.mult,
                             r=[('x1', j), ('frs', j), 'fg'], w=[('x1', j)])
                        S.dma('sp', self.y[tt:tt + 128, :], x1[:, j, :], r=[('x1', j)])
            S.flush()

    def p3_fnet(self, l):
        for nm, t0 in (('x', 0), ('c', L)):
            if nm == 'c' and l == DEPTH - 1:
                continue
            self._fnet(l, nm, t0)

    def _fnet(self, l, nm, t0):
        nc, S = self.nc, self.S
        L1 = self.fn[nm]['L1']; Ls = self.fn[nm]['Ls']; L2 = 64
        with ExitStack() as es:
            sb = lambda n, shp, dt=F32: es.enter_context(nc.sbuf_tensor(self.uq(n), shp, dt))
            pst = lambda n, shp, dt=F32: es.enter_context(nc.psum_tensor(self.uq(n), shp, dt))
            uf = sb('uf', [128, 2, Ls], BF16)
            S.dma('sp', uf[:], self.pxT[5:7, :, t0:t0 + Ls].rearrange("c p t -> p c t"), w=['uf'])
            csw = sb('csw', [128, 256], BF16); S.dma('sp', csw[:], self.I('fn_csw'), w=['csw'])
            w1 = sb('w1', [L1, 2 * L1], BF16); w2 = sb('w2', [L1, 2 * L1], BF16)
            S.dma('sp', w1[:], self.I('fn_w1' + nm), w=['w1']); S.dma('sp', w2[:], self.I('fn_w2' + nm), w=['w2'])
            tr = sb('tr', [64, Ls], BF16); sn = sb('sn', [64, Ls], BF16)
            S.dma('sp', tr[:], self.I('fn_tr' + nm), w=['tr']); S.dma('sp', sn[:], self.I('fn_sn' + nm), w=['sn'])
            U = sb('U', [L1, L2, 512], BF16)
            Y = sb('Y', [64, 2, 256, L1], BF16)
            pu = [pst('pu%d' % i, [128, 512]) for i in range(2)]
            py = [pst('py%d' % i, [128, 512]) for i in range(2)]
            pf = [pst('pf%d' % i, [128, 512]) for i in range(2)]
            for l2 in range(L2):
                b = l2 % 2
                S.multi('pe', [('matmul', (pu[b][0:L1, c * 256:(c + 1) * 256], uf[:, c, l2::L2], csw[:, :]),
                                dict(start=True, stop=True)) for c in range(2)], r=['uf', 'csw'], w=[('pu', b)])
                if b == 0:
                    S.op('act', 'activation', U[:, l2, :], pu[b][0:L1, :], AF.Copy, r=[('pu', b)], w=['U'])
                else:
                    S.op('dve', 'tensor_copy', U[:, l2, :], pu[b][0:L1, :], r=[('pu', b)], w=['U'])
            cpb = 512 // (2 * L1)
            nb = 0
            for ch0 in range(0, 256, cpb):
                b = nb % 2; nb += 1
                calls = []
                for chl in range(cpb):
                    ch = ch0 + chl; c = ch // 128; i = ch % 128
                    o = py[b][0:64, chl * 2 * L1:(chl + 1) * 2 * L1]
                    calls.append(('matmul', (o, U[:, :, c * 256 + i], w1[:, :]), dict(start=True, stop=False)))
                    calls.append(('matmul', (o, U[:, :, c * 256 + 128 + i], w2[:, :]), dict(start=False, stop=True)))
                S.multi('pe', calls, r=['U', 'w1', 'w2'], w=[('py', b)])
                for r_ in range(2):
                    src = py[b][0:64, 0:cpb * 2 * L1].rearrange("p (c r a) -> p c r a", c=cpb, r=2)[:, :, r_, :]
                    dst = Y[:, r_, ch0:ch0 + cpb, :]
                    if r_ == 0:
                        S.op('act', 'activation', dst, src, AF.Copy, r=[('py', b)], w=['Y'])
                    else:
                        S.op('dve', 'tensor_copy', dst, src, r=[('py', b)], w=['Y'])
            na = min(8, L1)
            nb = 0
            for cc in range(2):
                for a0 in range(0, L1, na):
                    b = nb % 2; nb += 1
                    calls = []
                    for al in range(na):
                        a_ = a0 + al
                        o = pf[b][:, al * 64:(al + 1) * 64]
                        calls.append(('matmul', (o, Y[:, 0, cc * 128:(cc + 1) * 128, a_], tr[:, a_::L1]), dict(start=True, stop=False)))
                        calls.append(('matmul', (o, Y[:, 1, cc * 128:(cc + 1) * 128, a_], sn[:, a_::L1]), dict(start=False, stop=True)))
                    S.multi('pe', calls, r=['Y', 'tr', 'sn'], w=[('pf', b)])
                    dst = uf[:, cc, :].rearrange("p (b a) -> p a b", a=L1)[:, a0:a0 + na, :]
                    src = pf[b][:, 0:na * 64].rearrange("p (a b) -> p a b", a=na)
                    if b == 0:
                        S.op('act', 'activation', dst, src, AF.Copy, r=[('pf', b), 'U'], w=['uf'])
                    else:
                        S.op('dve', 'tensor_copy', dst, src, r=[('pf', b), 'U'], w=['uf'])
            S.dma('sp', self.mixT[4:6, :, t0:t0 + Ls].rearrange("c p t -> p c t"), uf[:], r=['uf'])
            S.flush()

    def _hy_f1(self, src, krows, N1, w1h, hyY, es_sb, pst):
        S = self.S
        cpb = 512 // (2 * N1)
        py = [pst('hpy%d' % i, [128, 512]) for i in range(2)]
        nbank = 256 // cpb
        GB = min(4, nbank)
        stg = [es_sb('hstg%d' % i, [128, GB, 512], BF16) for i in range(2)]
        for nb in range(nbank):
            ch0 = nb * cpb
            b = nb % 2; sb_ = (nb // GB) % 2; g = nb % GB
            S.multi('pe', [('matmul', (py[b][:, chl * 2 * N1:(chl + 1) * 2 * N1], src[0:krows, :, ch0 + chl], w1h[0:krows, :]),
                            dict(start=True, stop=True)) for chl in range(cpb)], r=['hsrc', 'w1h'], w=[('hpy', b)])
            if b == 0:
                S.op('act', 'activation', stg[sb_][:, g, :], py[b][:, :], AF.Copy, r=[('hpy', b)], w=[('hstg', sb_)])
            else:
                S.op('dve', 'tensor_copy', stg[sb_][:, g, :], py[b][:, :], r=[('hpy', b)], w=[('hstg', sb_)])
            if g == GB - 1:
                c0 = (nb - GB + 1) * cpb
                S.dma('sp', hyY[:, c0:c0 + GB * cpb, :].rearrange("p c x -> p (c x)"),
                      stg[sb_][:, :, :].rearrange("p g x -> p (g x)"), r=[('hstg', sb_)])

    def _hy_f2(self, Yz, ncol, N1, tr, ti, nti, pz, f1):
        S = self.S
        calls = [('matmul', (pz[:, 0:ncol], tr[:, f1::N1], Yz[:, :, f1]), dict(start=True, stop=False)),
                 ('matmul', (pz[:, 0:ncol], nti[:, f1::N1], Yz[:, :, N1 + f1]), dict(start=False, stop=True)),
                 ('matmul', (pz[:, ncol:2 * ncol], ti[:, f1::N1], Yz[:, :, f1]), dict(start=True, stop=False)),
                 ('matmul', (pz[:, ncol:2 * ncol], tr[:, f1::N1], Yz[:, :, N1 + f1]), dict(start=False, stop=True))]
        return calls

    def _range_sin(self, dst, ps, fcol, fbcol, tl, n):
        S = self.S
        a, t, k = tl['a'], tl['t'], tl['k']
        S.op('dve', 'tensor_scalar', a[:, 0:n], ps, fcol, fbcol, ALU.mult, ALU.add, r=[tl['ps'], 'fcols'], w=['rs_a'])
        S.op('dve', 'tensor_scalar', t[:, 0:n], a[:, 0:n], 1.0 / (2 * math.pi), 12582912.0, ALU.mult, ALU.add, r=['rs_a'], w=['rs_t'])
        S.op('dve', 'tensor_scalar', k[:, 0:n], t[:, 0:n], -12582912.0, None, ALU.add, r=['rs_t'], w=['rs_k'])
        S.op('dve', 'scalar_tensor_tensor', a[:, 0:n], k[:, 0:n], -2 * math.pi, a[:, 0:n], ALU.mult, ALU.add, r=['rs_k', 'rs_a'], w=['rs_a'])
        S.op('dve', 'tensor_scalar', a[:, 0:n], a[:, 0:n], -3.14159, 3.14159, ALU.max, ALU.min, r=['rs_a'], w=['rs_a'])
        S.op('act', 'activation', dst, a[:, 0:n], AF.Sin, r=['rs_a'], w=[tl['dst']])

    def pk_filters(self, l):
        for nm in ('x', 'c'):
            if nm == 'c' and l == DEPTH - 1:
                continue
            self._filters(l, nm)

    def _filters(self, l, nm):
        nc, S = self.nc, self.S
        hy = self.hy[nm]; N1 = hy['N1']; Ls = hy['Ls']; N = hy['N']
        hyY = self.hyY[nm]
        with ExitStack() as es:
            sb = lambda n, shp, dt=F32: es.enter_context(nc.sbuf_tensor(self.uq(n), shp, dt))
            pst = lambda n, shp, dt=F32: es.enter_context(nc.psum_tensor(self.uq(n), shp, dt))
            ks = sb('ks', [N1, 128, 256], BF16)
            w1h = sb('w1h', [N1, 2 * N1], BF16); S.dma('sp', w1h[:], self.I('hy_w1' + nm), w=['w1h'])
            with ExitStack() as es2:
                sb2 = lambda n, shp, dt=F32: es2.enter_context(nc.sbuf_tensor(self.uq(n), shp, dt))
                ps2 = lambda n, shp, dt=F32: es2.enter_context(nc.psum_tensor(self.uq(n), shp, dt))
                wa = sb2('wa', [33, 64]); wb = sb2('wb', [64, 64]); wc = sb2('wc', [64, 512], BF16)
                S.dma('sp', wa[:], self.I('hy_w1')[l], w=['wa']); S.dma('sp', wb[:], self.I('hy_w2')[l], w=['wb'])
                S.dma('pool', wc[:], self.I('hy_w3')[l], w=['wc'])
                fc = sb2('fc', [64, 4])
                S.op('dve', 'tensor_copy', fc[:, 0:1], self.cols[0:64, C_HFR:C_HFR + 1], r=['cols'], w=['fcols'])
                S.op('dve', 'tensor_tensor', fc[:, 1:2], self.cols[0:64, C_HFR:C_HFR + 1], self.cols[0:64, C_HB1:C_HB1 + 1], ALU.mult, r=['cols'], w=['fcols'])
                S.op('dve', 'tensor_tensor', fc[:, 2:3], self.cols[0:64, C_HFR:C_HFR + 1], self.cols[0:64, C_HB2:C_HB2 + 1], ALU.mult, r=['cols'], w=['fcols'])
                zt = [sb2('zt%d' % i, [33, 512]) for i in range(2)]
                h1 = sb2('h1', [64, 512])
                hA = sb2('hA', [64, N], BF16); hB = sb2('hB', [64, N], BF16)
                tl = dict(a=sb2('rsa', [64, 512]), t=sb2('rst', [64, 512]), k=sb2('rsk', [64, 512]))
                tcol = sb2('tcol', [N1, 128]); S.dma('sp', tcol[:], self.I('hy_tcol' + nm), w=['tcol'])
                drow = sb2('drow', [128, 256]); S.dma('sp', drow[:], self.I('hy_drow' + nm), w=['drow'])
                dec = [sb2('dec%d' % i, [N1, 256]) for i in range(2)]
                kabs = [sb2('kabs%d' % i, [N1, 32 * 256], BF16) for i in range(2)]
                onb = sb2('onb', [128, 1], BF16); S.op('dve', 'memset', onb[:], 1.0, w=['onb'])
                sres = sb2('sres', [128, 2])
                pa = ps2('pa', [64, 512]); pb_ = ps2('pb', [64, 512])
                pk = [ps2('pk%d' % i, [128, 512]) for i in range(2)]
                pSs = [ps2('pS%d' % i, [128, 2]) for i in range(2)]
                nblk = N // 512
                for blk in range(nblk):
                    b = blk % 2
                    S.dma('sp', zt[b][:], self.I('hy_zT' + nm)[:, blk * 512:(blk + 1) * 512], w=[('zt', b)])
                    S.op('pe', 'matmul', pa[:, :], wa[:, :], zt[b][:, :], start=True, stop=True, r=['wa', ('zt', b)], w=['pa'])
                    tl.update(ps='pa', dst='h1')
                    self._range_sin(h1[:, :], pa[:, :], fc[:, 0:1], fc[:, 1:2], tl, 512)
                    S.op('pe', 'matmul', pb_[:, :], wb[:, :], h1[:, :], start=True, stop=True, r=['wb', 'h1'], w=['pb'])
                    tl.update(ps='pb', dst='hA')
                    self._range_sin(hA[:, blk * 512:(blk + 1) * 512], pb_[:, :], fc[:, 0:1], fc[:, 2:3], tl, 512)
                S.op('pool', 'tensor_copy', hB[:, Ls:N], hA[:, Ls:N], r=['hA'], w=['hB'])
                S.op('pool', 'memset', hB[:, 0:Ls], 0.0, w=['hB'])
                S.op('pool', 'memset', hA[:, Ls:N], 0.0, r=['hB'], w=['hA'])
                for s2 in range(128):
                    b = s2 % 2
                    S.multi('pe', [('matmul', (pk[b][0:N1, 0:256], hA[:, s2::128], wc[:, 0:256]), dict(start=True, stop=False)),
                                   ('matmul', (pk[b][0:N1, 0:256], hB[:, s2::128], wc[:, 256:512]), dict(start=False, stop=True))],
                            r=['hA', 'hB', 'wc'], w=[('pk', b)])
                    S.op('act', 'activation', dec[b][:, :], drow[0:N1, :], AF.Exp, scale=tcol[:, s2:s2 + 1],
                         r=['drow', 'tcol'], w=[('dec', b)])
                    S.op('dve', 'tensor_tensor', ks[:, s2, :], pk[b][0:N1, 0:256], dec[b][:, :], ALU.mult,
                         r=[('pk', b), ('dec', b)], w=['hsrc'])
                for q in range(4):
                    S.op('act', 'activation', kabs[q % 2][:, :], ks[:, q * 32:(q + 1) * 32, :].rearrange("p s c -> p (s c)"), AF.Abs,
                         r=['hsrc'], w=[('kabs', q % 2)])
                    for cc in range(2):
                        S.multi('pe', [('matmul', (pSs[cc][:, 0:1], kabs[q % 2][:, sl_ * 256 + cc * 128:sl_ * 256 + (cc + 1) * 128], onb[0:N1, 0:1]),
                                        dict(start=(q == 0 and sl_ == 0), stop=(q == 3 and sl_ == 31))) for sl_ in range(32)],
                                r=[('kabs', q % 2), 'onb'], w=[('pS', cc)])
                for cc in range(2):
                    S.op('dve', 'tensor_copy', sres[:, cc:cc + 1], pSs[cc][:, 0:1], r=[('pS', cc)], w=['sres'])
                S.dma('sp', hy['ssum'], sres[:, :], r=['sres'])
                self._hy_f1(ks, N1, N1, w1h, hyY, sb2, ps2)
                S.flush()
        with ExitStack() as es:
            sb = lambda n, shp, dt=F32: es.enter_context(nc.sbuf_tensor(self.uq(n), shp, dt))
            pst = lambda n, shp, dt=F32: es.enter_context(nc.psum_tensor(self.uq(n), shp, dt))
            Yz = sb('Yz', [128, 256, 2 * N1], BF16); S.dma('sp', Yz[:], hyY, w=['Yz'])
            tr = sb('tr', [128, N], BF16); ti = sb('ti', [128, N], BF16); nti = sb('nti', [128, N], BF16)
            S.dma('sp', tr[:], self.I('hy_tr' + nm), w=['tr']); S.dma('sp', ti[:], self.I('hy_ti' + nm), w=['ti'])
            S.dma('sp', nti[:], self.I('hy_nti' + nm), w=['nti'])
            FB = min(8, N1)
            kst = [sb('kst%d' % i, [128, FB, 512], BF16) for i in range(2)]
            pz = [pst('pz%d' % i, [128, 512]) for i in range(2)]
            for f1 in range(N1):
                b = f1 % 2; kb = (f1 // FB) % 2
                S.multi('pe', self._hy_f2(Yz, 256, N1, tr, ti, nti, pz[b], f1), r=['Yz', 'tr', 'ti', 'nti'], w=[('pz', b)])
                if b == 0:
                    S.op('act', 'activation', kst[kb][:, f1 % FB, :], pz[b][:, :], AF.Copy, r=[('pz', b)], w=[('kst', kb)])
                else:
                    S.op('dve', 'tensor_copy', kst[kb][:, f1 % FB, :], pz[b][:, :], r=[('pz', b)], w=[('kst', kb)])
                if f1 % FB == FB - 1:
                    f0 = f1 - FB + 1
                    S.dma('sp', hy['kf'][:, f0:f0 + FB, :, :].rearrange("p f r c -> p (f r c)"),
                          kst[kb][:, :, :].rearrange("p f x -> p (f x)"), r=[('kst', kb)])
            S.flush()

    def p4_hyena(self, l):
        for nm, t0 in (('x', 0), ('c', L)):
            if nm == 'c' and l == DEPTH - 1:
                continue
            self._hyena(l, nm, t0)

    def _hyena(self, l, nm, t0):
        nc, S = self.nc, self.S
        hy = self.hy[nm]; N1 = hy['N1']; Ls = hy['Ls']; N = hy['N']; K1 = N1 // 2
        hyY = self.hyY[nm]; hyF = self.hyF[nm]
        nblk = max(1, Ls // 512); bw = Ls // nblk
        with ExitStack() as es:
            sb = lambda n, shp, dt=F32: es.enter_context(nc.sbuf_tensor(self.uq(n), shp, dt))
            pst = lambda n, shp, dt=F32: es.enter_context(nc.psum_tensor(self.uq(n), shp, dt))
            uh = sb('uh', [128, 6, Ls], BF16)
            S.dma('sp', uh[:], self.pxT[7:13, :, t0:t0 + Ls].rearrange("c p t -> p c t"), w=['uh'])
            idb = sb('idb', [128, 128], BF16); S.dma('sp', idb[:], self.I('ident_bf'), w=['idb'])
            w1h = sb('w1h', [N1, 2 * N1], BF16); S.dma('sp', w1h[:], self.I('hy_w1' + nm), w=['w1h'])
            zT = sb('zT', [128, 2, Ls], BF16); x0 = sb('x0', [128, 2, Ls], BF16)
            o = [sb('o%d' % c, [128, bw]) for c in range(6)]
            zs = sb('zs', [K1, 128, 256], BF16)
            cw = lambda tap, c: self.cols[:, C_HCW + tap * 6 + c:C_HCW + tap * 6 + c + 1]
            for blk in range(nblk):
                c0 = blk * bw
                for c in range(6):
                    S.op('act', 'activation', o[c][:, :], uh[:, c, c0:c0 + bw], AF.Identity,
                         bias=self.cols[:, C_HCB + c:C_HCB + c + 1], scale=cw(1, c), r=['uh', 'cols'], w=[('o', c)])
                    lo = 1 if blk == 0 else 0
                    S.op('dve', 'scalar_tensor_tensor', o[c][:, lo:bw], uh[:, c, c0 + lo - 1:c0 + bw - 1], cw(0, c), o[c][:, lo:bw],
                         ALU.mult, ALU.add, r=['uh', 'cols', ('o', c)], w=[('o', c)])
                    hi = bw - 1 if blk == nblk - 1 else bw
                    S.op('dve', 'scalar_tensor_tensor', o[c][:, 0:hi], uh[:, c, c0 + 1:c0 + hi + 1], cw(2, c), o[c][:, 0:hi],
                         ALU.mult, ALU.add, r=['uh', 'cols', ('o', c)], w=[('o', c)])
                for cc in range(2):
                    S.op('pool', 'tensor_copy', x0[:, cc, c0:c0 + bw], o[cc][:, :], r=[('o', cc)], w=['x0'])
                    S.op('pool', 'tensor_tensor', zT[:, cc, c0:c0 + bw], o[2 + cc][:, :], o[4 + cc][:, :], ALU.mult,
                         r=[('o', 2 + cc), ('o', 4 + cc)], w=['zT'])
            S.dma('sp', self.hx0[:, :, t0:t0 + Ls].rearrange("c p t -> p c t"), x0[:], r=['x0'])
            S.dma('sp', self.hz[:, :, t0:t0 + Ls].rearrange("c p t -> p c t"), zT[:], r=['zT'])
            ptr = [pst('ptr%d' % i, [128, 1024], BF16) for i in range(2)]
            nb = 0
            for s20 in range(0, 128, 4):
                b = nb % 2; nb += 1
                calls = []
                for sl in range(4):
                    for cc in range(2):
                        calls.append(('transpose', (ptr[b][0:K1, (sl * 2 + cc) * 128:(sl * 2 + cc + 1) * 128],
                                                    zT[:, cc, s20 + sl::128], idb[:, :]), {}))
                S.multi('pe', calls, r=['zT', 'idb'], w=[('ptr', b)])
                dst = zs[:, s20:s20 + 4, :].rearrange("p a c -> p (a c)")
                if b == 0:
                    S.op('act', 'activation', dst, ptr[b][0:K1, :], AF.Copy, r=[('ptr', b)], w=['hsrc'])
                else:
                    S.op('dve', 'tensor_copy', dst, ptr[b][0:K1, :], r=[('ptr', b)], w=['hsrc'])
            self._hy_f1(zs, K1, N1, w1h, hyY, sb, pst)
            S.flush()
        with ExitStack() as es:
            sb = lambda n, shp, dt=F32: es.enter_context(nc.sbuf_tensor(self.uq(n), shp, dt))
            pst = lambda n, shp, dt=F32: es.enter_context(nc.psum_tensor(self.uq(n), shp, dt))
            tr = sb('tr', [128, N], BF16); ti = sb('ti', [128, N], BF16); nti = sb('nti', [128, N], BF16)
            Yz = sb('Yz', [128, 128, 2 * N1], BF16); S.dma('sp', Yz[:], hyY[:, 0:128, :], w=['Yz'])
            S.dma('sp', tr[:], self.I('hy_tr' + nm), w=['tr']); S.dma('sp', ti[:], self.I('hy_ti' + nm), w=['ti'])
            S.dma('sp', nti[:], self.I('hy_nti' + nm), w=['nti'])
            kf = sb('kf', [128, N1, 2, 256], BF16); S.dma('sp', kf[:], hy['kf'], w=['kf'])
            Yf = sb('Yf', [128, N1, 2, 128], BF16)
            FB = min(4, N1)
            ta = [sb('ta%d' % i, [128, FB, 2, 128]) for i in range(2)]; tb = [sb('tb%d' % i, [128, FB, 2, 128]) for i in range(2)]
            pz = [pst('pz%d' % i, [128, FB * 256]) for i in range(2)]
            for cc in range(2):
                if cc == 1:
                    S.dma('sp', Yz[:], hyY[:, 128:256, :], w=['Yz'])
                ccs = slice(cc * 128, (cc + 1) * 128)
                for st_ in range(N1 // FB):
                    f0 = st_ * FB; b = st_ % 2
                    calls = []
                    for fl in range(FB):
                        calls += self._hy_f2(Yz, 128, N1, tr, ti, nti, pz[b][:, fl * 256:(fl + 1) * 256], f0 + fl)
                    S.multi('pe', calls, r=['Yz', 'tr', 'ti', 'nti'], w=[('pz', b)])
                    pzv = pz[b][:, :].rearrange("p (f r c) -> p f r c", f=FB, r=2)
                    S.op('dve', 'tensor_tensor', ta[b][:, :, :, :], pzv, kf[:, f0:f0 + FB, :, ccs], ALU.mult, r=[('pz', b), 'kf'], w=[('ta', b)])
                    S.op('dve', 'tensor_tensor', tb[b][:, :, 0, :], pzv[:, :, 0, :], kf[:, f0:f0 + FB, 1, ccs], ALU.mult, r=[('pz', b), 'kf'], w=[('tb', b)])
                    S.op('dve', 'tensor_tensor', tb[b][:, :, 1, :], pzv[:, :, 1, :], kf[:, f0:f0 + FB, 0, ccs], ALU.mult, r=[('pz', b), 'kf'], w=[('tb', b)])
                    S.op('pool', 'tensor_tensor', Yf[:, f0:f0 + FB, 0, :], ta[b][:, :, 0, :], ta[b][:, :, 1, :], ALU.subtract, r=[('ta', b)], w=['Yf'])
                    S.op('pool', 'tensor_tensor', Yf[:, f0:f0 + FB, 1, :], tb[b][:, :, 0, :], tb[b][:, :, 1, :], ALU.add, r=[('tb', b)], w=['Yf'])
                S.dma('sp', hyF[cc], Yf[:], r=['Yf'])
            S.flush()
        for cc in range(2):
            with ExitStack() as es:
                sb = lambda n, shp, dt=F32: es.enter_context(nc.sbuf_tensor(self.uq(n), shp, dt))
                pst = lambda n, shp, dt=F32: es.enter_context(nc.psum_tensor(self.uq(n), shp, dt))
                Yf = sb('Yf', [128, N1, 2, 128], BF16); S.dma('sp', Yf[:], hyF[cc], w=['Yf'])
                i1a = sb('i1a', [128, 256], BF16); i1b = sb('i1b', [128, 256], BF16)
                S.dma('sp', i1a[:], self.I('hy_i1a' + nm), w=['i1a']); S.dma('sp', i1b[:], self.I('hy_i1b' + nm), w=['i1b'])
                er = sb('er', [N1, Ls], BF16); nei = sb('nei', [N1, Ls], BF16)
                S.dma('sp', er[:], self.I('hy_er' + nm), w=['er']); S.dma('sp', nei[:], self.I('hy_nei' + nm), w=['nei'])
                V = sb('V', [N1, 128, 2, 128], BF16)
                yT = sb('yT', [128, Ls]); x0h = sb('x0h', [128, Ls], BF16); zh = sb('zh', [128, Ls], BF16)
                S.dma('sp', x0h[:], self.hx0[cc, :, t0:t0 + Ls], w=['x0h']); S.dma('sp', zh[:], self.hz[cc, :, t0:t0 + Ls], w=['zh'])
                ssb = sb('ssb', [128, 2]); S.dma('sp', ssb[:], hy['ssum'], w=['ssb'])
                S.op('dve', 'reciprocal', ssb[:, :], ssb[:, :], r=['ssb'], w=['ssb'])
                res = sb('res', [128, Ls], BF16)
                pv = [pst('pv%d' % i, [128, 512]) for i in range(2)]
                py = [pst('py%d' % i, [128, 512]) for i in range(2)]
                nb = 0
                for ch0 in range(0, 128, 2):
                    b = nb % 2; nb += 1
                    calls = []
                    for chl in range(2):
                        o_ = pv[b][0:N1, chl * 256:(chl + 1) * 256]
                        calls.append(('matmul', (o_, Yf[:, :, 0, ch0 + chl], i1a[:, :]), dict(start=True, stop=False)))
                        calls.append(('matmul', (o_, Yf[:, :, 1, ch0 + chl], i1b[:, :]), dict(start=False, stop=True)))
                    S.multi('pe', calls, r=['Yf', 'i1a', 'i1b'], w=[('pv', b)])
                    dst = V[:, ch0:ch0 + 2, :, :].rearrange("p c r a -> p (c r a)")
                    if b == 0:
                        S.op('act', 'activation', dst, pv[b][0:N1, :], AF.Copy, r=[('pv', b)], w=['V'])
                    else:
                        S.op('dve', 'tensor_copy', dst, pv[b][0:N1, :], r=[('pv', b)], w=['V'])
                nta = min(128, 512 // K1)
                nb = 0
                for ta0 in range(0, 128, nta):
                    b = nb % 2; nb += 1
                    calls = []
                    for al in range(nta):
                        ta_ = ta0 + al
                        o_ = py[b][:, al * K1:(al + 1) * K1]
                        calls.append(('matmul', (o_, V[:, :, 0, ta_], er[:, ta_::128]), dict(start=True, stop=False)))
                        calls.append(('matmul', (o_, V[:, :, 1, ta_], nei[:, ta_::128]), dict(start=False, stop=True)))
                    S.multi('pe', calls, r=['V', 'er', 'nei'], w=[('py', b)])
                    dst = yT[:, :].rearrange("p (b a) -> p a b", a=128)[:, ta0:ta0 + nta, :]
                    src = py[b][:, 0:nta * K1].rearrange("p (a b) -> p a b", b=K1)
                    if b == 0:
                        S.op('act', 'activation', dst, src, AF.Copy, r=[('py', b)], w=['yT'])
                    else:
                        S.op('dve', 'tensor_copy', dst, src, r=[('py', b)], w=['yT'])
                S.op('act', 'activation', yT[:, :], yT[:, :], AF.Identity, scale=ssb[:, cc:cc + 1], r=['yT', 'ssb'], w=['yT'])
                S.op('dve', 'scalar_tensor_tensor', yT[:, :], zh[:, :], self.cols[:, C_HD + cc:C_HD + cc + 1], yT[:, :],
                     ALU.mult, ALU.add, r=['zh', 'yT', 'cols'], w=['yT'])
                S.op('dve', 'tensor_tensor', res[:, :], yT[:, :], x0h[:, :], ALU.mult, r=['yT', 'x0h'], w=['res'])
                S.dma('sp', self.mixT[6 + cc, :, t0:t0 + Ls], res[:, :], r=['res'])
                S.flush()


def build_program(debug=False, stop_after=None):
    P = Prog(debug=debug)
    steps = []
    for l in range(DEPTH):
        steps += [('p0', l), ('p1', l), ('pk', l), ('p2', l), ('p3', l), ('p4', l), ('p5', l)]
    for (nm, l) in steps:
        fn = getattr(P, {'p0': 'p0_mod', 'p1': 'p1_inproj', 'pk': 'pk_filters', 'p2': 'p2_attn',
                         'p3': 'p3_fnet', 'p4': 'p4_hyena', 'p5': 'p5_out_mlp'}[nm])
        fn(l)
        if stop_after is not None and (nm, l) == stop_after:
            break
    P.es.close()
    P.nc.used_inputs = list(P._in.keys())
    return P.nc


_PROG = None


def kernel(**inputs):
    global _PROG
    per = _prep(inputs)
    if _PROG is None:
        _PROG = build_program()
    per = [{k: d[k] for k in _PROG.used_inputs} for d in per]
    res = run_bass_kernel_spmd(_PROG, per, core_ids=list(range(8)))
    return np.stack([np.asarray(r['y'], np.float32) for r in res.results], axis=0)
```
